# Optimizing a Trainium2 kernel written in Bass

```python
import math
import jax, jax.numpy as jnp
from jax import lax
import numpy as np

D_MODEL = 1024
BATCH = 32
SEQ = 256
DEPTH = 4
DEC_BATCH = 4
DEC_SEQ = 2048
PAST_LEN = 512

GRID_W = 64
HEAD_DIM = 64
NA_HEADS = 4
NA_WIN_ROWS = 8
NA_WIN_COLS = 16
NA_KEY_COLS = 2 * NA_WIN_COLS
SW_HEADS = 8
SW_KV_HEADS = 2
SW_WINDOW = 128
DA_HEADS = 4
DA_DIM = 32
DA_VDIM = 2 * DA_DIM
QBLK = 128
D_FF = 2816
CONV_W = 3
ROPE_BASE = 10000.0
EPS = 1e-6
NA_W = NA_HEADS * HEAD_DIM
SW_QW = SW_HEADS * HEAD_DIM
SW_KVW = SW_KV_HEADS * HEAD_DIM
DA_QW = DA_HEADS * 2 * DA_DIM
DA_VW = DA_HEADS * DA_VDIM
D_MIX = NA_W + SW_QW + DA_VW
D_IN = 3 * NA_W + SW_QW + 2 * SW_KVW + 2 * DA_QW + DA_VW

kernel_name = "hybrid_dit_natten_swa_diffattn_step"

F32 = jnp.float32


def rmsnorm(x, g):
    xf = x.astype(F32)
    y = xf * lax.rsqrt(jnp.mean(xf * xf, axis=-1, keepdims=True) + EPS)
    return (y * g.astype(F32)).astype(x.dtype)


def modulation(cvec, w_mod_l, b_mod_l):
    m = jax.nn.silu(cvec) @ w_mod_l + b_mod_l
    return jnp.split(m, 6, axis=-1)


def adaln(x, g, shift, scale):
    return rmsnorm(x, g) * (1 + scale) + shift


def axial_rope(T, dim):
    n = dim // 4
    inv = 1.0 / (ROPE_BASE ** (jnp.arange(n, dtype=F32) / n))
    t = jnp.arange(T)
    row = (t // GRID_W).astype(F32)
    col = (t % GRID_W).astype(F32)
    ang = jnp.concatenate([row[:, None] * inv, col[:, None] * inv], axis=-1)
    return jnp.cos(ang), jnp.sin(ang)


def apply_rope(x, cos, sin):
    shp = x.shape
    xf = x.reshape(shp[0], shp[1], -1, shp[-1]).astype(F32)
    x1, x2 = xf[..., 0::2], xf[..., 1::2]
    c = cos[None, :, None, :]
    s = sin[None, :, None, :]
    out = jnp.stack([x1 * c - x2 * s, x1 * s + x2 * c], axis=-1)
    return out.reshape(shp).astype(x.dtype)


def softmax_with_sink(s, sink_b):
    sc = jnp.broadcast_to(sink_b, s.shape[:-1] + (1,))
    p = jax.nn.softmax(jnp.concatenate([s, sc], axis=-1), axis=-1)
    return p[..., :-1]


def split_qkv(h, w_in_l, na_g, sw_g, da_g):
    B, T, _ = h.shape
    sizes = (NA_W, NA_W, NA_W, SW_QW, SW_KVW, SW_KVW, DA_QW, DA_QW, DA_VW)
    cuts = [int(i) for i in np.cumsum(sizes)[:-1]]
    p = jnp.split(h @ w_in_l, cuts, axis=-1)
    qa = rmsnorm(p[0].reshape(B, T, NA_HEADS, HEAD_DIM), na_g[0])
    ka = rmsnorm(p[1].reshape(B, T, NA_HEADS, HEAD_DIM), na_g[1])
    va = p[2].reshape(B, T, NA_HEADS, HEAD_DIM)
    qb = rmsnorm(p[3].reshape(B, T, SW_HEADS, HEAD_DIM), sw_g[0])
    kb = rmsnorm(p[4].reshape(B, T, SW_KV_HEADS, HEAD_DIM), sw_g[1])
    vb = p[5].reshape(B, T, SW_KV_HEADS, HEAD_DIM)
    qc = rmsnorm(p[6].reshape(B, T, DA_HEADS, 2, DA_DIM), da_g[0])
    kc = rmsnorm(p[7].reshape(B, T, DA_HEADS, 2, DA_DIM), da_g[1])
    vc = p[8].reshape(B, T, DA_HEADS, DA_VDIM)
    return qa, ka, va, qb, kb, vb, qc, kc, vc


def dense_attn(q, k, v, sink=None):
    B, T, HQ, D = q.shape
    HKV = k.shape[2]
    G = HQ // HKV
    DV = v.shape[-1]
    nb = T // QBLK
    qb = jnp.moveaxis(q.reshape(B, nb, QBLK, HKV, G, D), 1, 0)
    scale = D ** -0.5

    def one(qblk):
        s = jnp.einsum('bqhgd,bkhd->bhgqk', qblk, k).astype(F32) * scale
        if sink is None:
            p = jax.nn.softmax(s, axis=-1)
        else:
            p = softmax_with_sink(s, sink.reshape(HKV, G)[None, :, :, None, None].astype(F32))
        return jnp.einsum('bhgqk,bkhd->bqhgd', p.astype(v.dtype), v)

    o = lax.map(one, qb)
    return jnp.moveaxis(o, 0, 1).reshape(B, T, HQ, DV)


def diff_lambda(lp, lam_init):
    lpf = lp.astype(F32)
    return jnp.exp(jnp.sum(lpf[0] * lpf[1])) - jnp.exp(jnp.sum(lpf[2] * lpf[3])) + lam_init


def diff_attn(q, k, v, lam, subln_g, lam_init):
    o1 = dense_attn(q[..., 0, :], k[..., 0, :], v).astype(F32)
    o2 = dense_attn(q[..., 1, :], k[..., 1, :], v).astype(F32)
    o = rmsnorm(o1 - lam * o2, subln_g) * (1.0 - lam_init)
    return o.astype(v.dtype)


def neighborhood_attn(q, k, v, kc, vc, rpb):
    B, T, H, D = q.shape
    rows = T // GRID_W
    kr = min(NA_WIN_ROWS, rows)
    ncb = GRID_W // NA_WIN_COLS
    r = jnp.arange(rows)
    r0 = jnp.clip(r - kr // 2, 0, rows - kr)
    ridx = r0[:, None] + jnp.arange(kr)[None, :]
    roff = ridx - r[:, None] + (NA_WIN_ROWS - 1)
    cb = jnp.arange(ncb)
    c0 = jnp.clip(cb * NA_WIN_COLS - NA_WIN_COLS // 2, 0, GRID_W - NA_KEY_COLS)
    cidx = c0[:, None] + jnp.arange(NA_KEY_COLS)[None, :]
    qcol = cb[:, None] * NA_WIN_COLS + jnp.arange(NA_WIN_COLS)[None, :]
    qs = jnp.clip(qcol - NA_WIN_COLS // 2, 0, GRID_W - NA_WIN_COLS)
    kcol = cidx[:, None, :]
    cmask = (kcol >= qs[..., None]) & (kcol < qs[..., None] + NA_WIN_COLS)
    coff = jnp.clip(kcol - qcol[..., None], -(NA_WIN_COLS - 1), NA_WIN_COLS - 1) + (NA_WIN_COLS - 1)
    qg = q.reshape(B, rows, ncb, NA_WIN_COLS, H, D)
    gi = (ridx[:, None, :, None], cidx[None, :, None, :])
    kg = k.reshape(B, rows, GRID_W, H, D)[:, gi[0], gi[1]]
    vg = v.reshape(B, rows, GRID_W, H, D)[:, gi[0], gi[1]]
    scale = D ** -0.5
    s_loc = jnp.einsum('brcqhd,brcakhd->brchqak', qg, kg).astype(F32) * scale
    bias = rpb[:, roff[:, None, None, :, None], coff[None, :, :, None, :]]
    s_loc = s_loc + jnp.moveaxis(bias, 0, 2).astype(F32)
    s_loc = jnp.where(cmask[None, None, :, None, :, None, :], s_loc, -jnp.inf)
    nl = kr * NA_KEY_COLS
    s_loc = s_loc.reshape(B, rows, ncb, H, NA_WIN_COLS, nl)
    s_ctx = jnp.einsum('brcqhd,bphd->brchqp', qg, kc).astype(F32) * scale
    p = jax.nn.softmax(jnp.concatenate([s_loc, s_ctx], axis=-1), axis=-1).astype(v.dtype)
    p_loc = p[..., :nl].reshape(B, rows, ncb, H, NA_WIN_COLS, kr, NA_KEY_COLS)
    o = (jnp.einsum('brchqak,brcakhd->brcqhd', p_loc, vg)
         + jnp.einsum('brchqp,bphd->brcqhd', p[..., nl:], vc))
    return o.reshape(B, T, H, D)


def window_attn(q, k, v, kc, vc, sink):
    B, T, HQ, D = q.shape
    HKV = k.shape[2]
    G = HQ // HKV
    nb = T // QBLK
    qb = q.reshape(B, nb, QBLK, HKV, G, D)
    pad = ((0, 0), (QBLK, QBLK), (0, 0), (0, 0))
    kp = jnp.pad(k, pad).reshape(B, nb + 2, QBLK, HKV, D)
    vp = jnp.pad(v, pad).reshape(B, nb + 2, QBLK, HKV, D)
    kband = jnp.concatenate([kp[:, :-2], kp[:, 1:-1], kp[:, 2:]], axis=2)
    vband = jnp.concatenate([vp[:, :-2], vp[:, 1:-1], vp[:, 2:]], axis=2)
    qi = jnp.arange(QBLK)
    kj = jnp.arange(3 * QBLK)
    rel = kj[None, :] - QBLK - qi[:, None]
    kpos = jnp.arange(nb)[:, None] * QBLK - QBLK + kj[None, :]
    mask = (jnp.abs(rel) <= SW_WINDOW)[None] & ((kpos >= 0) & (kpos < T))[:, None, :]
    scale = D ** -0.5
    s_loc = jnp.einsum('bnqhgd,bnkhd->bnhgqk', qb, kband).astype(F32) * scale
    s_loc = jnp.where(mask[None, :, None, None], s_loc, -jnp.inf)
    s_ctx = jnp.einsum('bnqhgd,bphd->bnhgqp', qb, kc).astype(F32) * scale
    p = softmax_with_sink(jnp.concatenate([s_loc, s_ctx], axis=-1),
                          sink.reshape(HKV, G)[None, None, :, :, None, None].astype(F32)).astype(v.dtype)
    nl = 3 * QBLK
    o = (jnp.einsum('bnhgqk,bnkhd->bnqhgd', p[..., :nl], vband)
         + jnp.einsum('bnhgqp,bphd->bnqhgd', p[..., nl:], vc))
    return o.reshape(B, T, HQ, D)


def merge_heads(oa, ob, oc, w_out_l):
    B, T = oa.shape[:2]
    o = jnp.concatenate([oa.reshape(B, T, NA_W), ob.reshape(B, T, SW_QW), oc.reshape(B, T, DA_VW)], axis=-1)
    return o @ w_out_l


def conv_ffn(h, w_up_l, cw, cb, w_down_l):
    u = h @ w_up_l
    T = u.shape[1]
    half = CONV_W // 2
    up = jnp.pad(u, ((0, 0), (half, half), (0, 0)))
    y = cb
    for j in range(CONV_W):
        y = y + up[:, j:j + T] * cw[j]
    g, val = jnp.split(y, 2, axis=-1)
    return (jax.nn.silu(g) * val) @ w_down_l


def setup_inputs(seed: int = 0) -> dict:
    key = jax.random.key(seed)
    ks = jax.random.split(key, 27)
    n = jax.random.normal
    d = {}
    d["x_prompt"] = n(ks[0], (BATCH, SEQ, D_MODEL), F32)
    d["x_sample"] = n(ks[1], (DEC_BATCH, DEC_SEQ, D_MODEL), F32)
    d["cache_na_k"] = n(ks[2], (DEC_BATCH, DEPTH, PAST_LEN, NA_HEADS, HEAD_DIM), F32)
    d["cache_na_v"] = n(ks[3], (DEC_BATCH, DEPTH, PAST_LEN, NA_HEADS, HEAD_DIM), F32)
    d["cache_sw_k"] = n(ks[4], (DEC_BATCH, DEPTH, PAST_LEN, SW_KV_HEADS, HEAD_DIM), F32)
    d["cache_sw_v"] = n(ks[5], (DEC_BATCH, DEPTH, PAST_LEN, SW_KV_HEADS, HEAD_DIM), F32)
    d["cache_da_k"] = n(ks[6], (DEC_BATCH, DEPTH, PAST_LEN, DA_HEADS, 2, DA_DIM), F32)
    d["cache_da_v"] = n(ks[7], (DEC_BATCH, DEPTH, PAST_LEN, DA_HEADS, DA_VDIM), F32)
    d["c"] = n(ks[8], (DEC_BATCH, D_MODEL), F32)
    d["c_ctx"] = n(ks[9], (D_MODEL,), F32)
    d["g_attn"] = 1.0 + 0.1 * n(ks[10], (DEPTH, D_MODEL), F32)
    d["g_ffn"] = 1.0 + 0.1 * n(ks[11], (DEPTH, D_MODEL), F32)
    d["w_mod"] = 0.5 * D_MODEL ** -0.5 * n(ks[12], (DEPTH, D_MODEL, 6 * D_MODEL), F32)
    d["b_mod"] = 0.02 * n(ks[13], (DEPTH, 6 * D_MODEL), F32)
    d["w_in"] = D_MODEL ** -0.5 * n(ks[14], (DEPTH, D_MODEL, D_IN), F32)
    d["na_qk_g"] = 1.0 + 0.1 * n(ks[15], (DEPTH, 2, HEAD_DIM), F32)
    d["na_rpb"] = 0.2 * n(ks[16], (DEPTH, NA_HEADS, 2 * NA_WIN_ROWS - 1, 2 * NA_WIN_COLS - 1), F32)
    d["sw_qk_g"] = 1.0 + 0.1 * n(ks[17], (DEPTH, 2, HEAD_DIM), F32)
    d["sw_sink"] = 0.5 * n(ks[18], (DEPTH, SW_HEADS), F32)
    d["da_qk_g"] = 1.0 + 0.1 * n(ks[19], (DEPTH, 2, DA_DIM), F32)
    d["da_lambda"] = 0.1 * n(ks[20], (DEPTH, 4, DA_DIM), F32)
    d["da_subln_g"] = 1.0 + 0.1 * n(ks[21], (DEPTH, DA_VDIM), F32)
    d["w_out"] = D_MIX ** -0.5 * n(ks[22], (DEPTH, D_MIX, D_MODEL), F32)
    d["w_up"] = D_MODEL ** -0.5 * n(ks[23], (DEPTH, D_MODEL, 2 * D_FF), F32)
    d["conv_w"] = CONV_W ** -0.5 * n(ks[24], (DEPTH, CONV_W, 2 * D_FF), F32)
    d["conv_b"] = 0.02 * n(ks[25], (DEPTH, 2 * D_FF), F32)
    d["w_down"] = D_FF ** -0.5 * n(ks[26], (DEPTH, D_FF, D_MODEL), F32)
    return d


def reference(x_prompt, x_sample, cache_na_k, cache_na_v, cache_sw_k, cache_sw_v, cache_da_k, cache_da_v,
              c, c_ctx, g_attn, g_ffn, w_mod, b_mod, w_in, na_qk_g, na_rpb, sw_qk_g, sw_sink,
              da_qk_g, da_lambda, da_subln_g, w_out, w_up, conv_w, conv_b, w_down):
    T = x_sample.shape[1]
    cos_b, sin_b = axial_rope(T, HEAD_DIM)
    cos_c, sin_c = axial_rope(T, DA_DIM)
    xp = x_prompt
    xs = x_sample
    na_k_l, na_v_l, sw_k_l, sw_v_l, da_k_l, da_v_l = [], [], [], [], [], []
    for l in range(DEPTH):
        lam_init = 0.8 - 0.6 * math.exp(-0.3 * l)
        lam = diff_lambda(da_lambda[l], lam_init)
        m = modulation(c_ctx[None, None, :], w_mod[l], b_mod[l])
        h = adaln(xp, g_attn[l], m[0], m[1])
        qa, ka, va, qb, kb, vb, qc, kc, vc = split_qkv(h, w_in[l], na_qk_g[l], sw_qk_g[l], da_qk_g[l])
        oa = dense_attn(qa, ka, va)
        ob = dense_attn(qb, kb, vb, sw_sink[l])
        oc = diff_attn(qc, kc, vc, lam, da_subln_g[l], lam_init)
        xp = xp + m[2] * merge_heads(oa, ob, oc, w_out[l])
        h = adaln(xp, g_ffn[l], m[3], m[4])
        xp = xp + m[5] * conv_ffn(h, w_up[l], conv_w[l], conv_b[l], w_down[l])
        na_k_l.append(ka)
        na_v_l.append(va)
        sw_k_l.append(kb)
        sw_v_l.append(vb)
        da_k_l.append(kc)
        da_v_l.append(vc)
        m = modulation(c[:, None, :], w_mod[l], b_mod[l])
        h = adaln(xs, g_attn[l], m[0], m[1])
        qa, ka, va, qb, kb, vb, qc, kc, vc = split_qkv(h, w_in[l], na_qk_g[l], sw_qk_g[l], da_qk_g[l])
        qb = apply_rope(qb, cos_b, sin_b)
        kb = apply_rope(kb, cos_b, sin_b)
        qc = apply_rope(qc, cos_c, sin_c)
        kc = apply_rope(kc, cos_c, sin_c)
        oa = neighborhood_attn(qa, ka, va, cache_na_k[:, l], cache_na_v[:, l], na_rpb[l])
        ob = window_attn(qb, kb, vb, cache_sw_k[:, l], cache_sw_v[:, l], sw_sink[l])
        oc = diff_attn(qc, jnp.concatenate([kc, cache_da_k[:, l]], axis=1),
                       jnp.concatenate([vc, cache_da_v[:, l]], axis=1), lam, da_subln_g[l], lam_init)
        xs = xs + m[2] * merge_heads(oa, ob, oc, w_out[l])
        h = adaln(xs, g_ffn[l], m[3], m[4])
        xs = xs + m[5] * conv_ffn(h, w_up[l], conv_w[l], conv_b[l], w_down[l])
    return (xp, xs, jnp.stack(na_k_l, axis=1), jnp.stack(na_v_l, axis=1), jnp.stack(sw_k_l, axis=1),
            jnp.stack(sw_v_l, axis=1), jnp.stack(da_k_l, axis=1), jnp.stack(da_v_l, axis=1))
```

```python
import math
from contextlib import ExitStack

import numpy as np
import ml_dtypes

import concourse.bass as bass
import concourse.mybir as mybir
from concourse.bass_utils import run_bass_kernel_spmd

F32 = mybir.dt.float32
BF16 = mybir.dt.bfloat16
AF = mybir.ActivationFunctionType
ALU = mybir.AluOpType
AX = mybir.AxisListType

L = 4
NB = 16
D = 1024
KC = 8
DFF = 2816
NJ = 22
EPS = 1e-6
NEG = -30000.0
PADOFF = 128
RPBLEN = 2176
ENGS = ("pe", "act", "dve", "pool", "sp")


class _Op:
    __slots__ = ("eng", "fn", "reads", "writes", "is_dma", "deps", "needs_inc",
                 "sem", "val", "idx", "final_wait", "barrier")

    def __init__(self, eng, fn, reads, writes, is_dma, final_wait):
        self.eng = eng
        self.fn = fn
        self.reads = reads
        self.writes = writes
        self.is_dma = is_dma
        self.deps = []
        self.needs_inc = False
        self.sem = None
        self.val = 0
        self.final_wait = final_wait
        self.barrier = False


class Sched:
    def __init__(self, same_engine_sync=True, n_dma_sems=32):
        self.ops = []
        self.same_engine_sync = same_engine_sync
        self.n_dma_sems = n_dma_sems

    def op(self, eng, fn, reads=(), writes=()):
        o = _Op(eng, fn, tuple(reads), tuple(writes), False, False)
        self.ops.append(o)
        return o

    def dma(self, eng, fn, reads=(), writes=(), final_wait=False):
        o = _Op(eng, fn, tuple(reads), tuple(writes), True, final_wait)
        self.ops.append(o)
        return o

    def barrier(self):
        for e in ENGS:
            o = _Op(e, None, (), (), False, False)
            o.barrier = True
            self.ops.append(o)

    def analyze(self):
        last_w = {}
        readers = {}
        waited = {e: {s: -1 for s in ENGS} for e in ENGS}
        waited_dma = {e: set() for e in ENGS}
        dma_slot_last = [None] * self.n_dma_sems
        dma_ctr = {"sp": 0, "pool": 0, "act": 0}
        half = self.n_dma_sems // 2
        last_compute = {e: None for e in ENGS}
        all_dma = []
        for idx, o in enumerate(self.ops):
            o.idx = idx
            deps = set()
            if o.barrier:
                for e in ENGS:
                    if last_compute[e] is not None and e != o.eng:
                        deps.add(last_compute[e])
                    if e == o.eng and last_compute[e] is not None and e != "pe":
                        deps.add(last_compute[e])
                for d in all_dma:
                    if d not in waited_dma[o.eng]:
                        deps.add(d)
            raw = set()
            for r in o.reads:
                w = last_w.get(r)
                if w is not None:
                    deps.add(w)
                    raw.add(w)
            for wkey in o.writes:
                w = last_w.get(wkey)
                if w is not None:
                    deps.add(w)
                for rd in readers.get(wkey, ()):
                    deps.add(rd)
            if o.is_dma:
                if o.eng == "pool":
                    slot = half + dma_ctr["pool"] % (self.n_dma_sems - half)
                else:
                    slot = dma_ctr["sp"] % half
                dma_ctr["pool" if o.eng == "pool" else "sp"] += 1
                prev = dma_slot_last[slot]
                if prev is not None:
                    deps.add(prev)
                dma_slot_last[slot] = idx
                o.sem = ("dma", slot)
                all_dma.append(idx)
            deps.discard(idx)
            best = {}
            out = []
            for d in sorted(deps):
                p = self.ops[d]
                if p.is_dma:
                    if d in waited_dma[o.eng]:
                        continue
                    waited_dma[o.eng].add(d)
                    out.append(d)
                else:
                    if p.eng == o.eng and not o.is_dma and not o.barrier and \
                            (p.eng == "pe" or not self.same_engine_sync):
                        continue
                    if waited[o.eng][p.eng] >= d:
                        continue
                    best[p.eng] = max(best.get(p.eng, -1), d)
            for e, d in best.items():
                waited[o.eng][e] = d
                out.append(d)
            o.deps = out
            for d in out:
                self.ops[d].needs_inc = True
            if not o.barrier:
                for r in o.reads:
                    readers.setdefault(r, []).append(idx)
                for wkey in o.writes:
                    last_w[wkey] = idx
                    readers[wkey] = []
                if not o.is_dma:
                    last_compute[o.eng] = idx
        self.final = [o.idx for o in self.ops if o.final_wait]
        cnt = {e: 0 for e in ENGS}
        dma_cnt = [0] * self.n_dma_sems
        for o in self.ops:
            if o.barrier:
                continue
            if o.is_dma:
                slot = o.sem[1]
                dma_cnt[slot] += 16
                o.val = dma_cnt[slot]
                o.needs_inc = True
            elif o.needs_inc:
                cnt[o.eng] += 1
                o.val = cnt[o.eng]
                o.sem = ("eng", o.eng)

    def emit(self, nc, es):
        self.analyze()
        sems = {}
        for e in ENGS:
            sems[("eng", e)] = es.enter_context(nc.semaphore("s_" + e))
        for i in range(self.n_dma_sems):
            sems[("dma", i)] = es.enter_context(nc.semaphore("s_dma%d" % i))
        per = {e: [] for e in ENGS}
        for o in self.ops:
            per[o.eng].append(o)
        ops = self.ops
        final = self.final
        block = es.enter_context(nc.Block())

        def run(engine_obj, lst, ename):
            for o in lst:
                for d in o.deps:
                    p = ops[d]
                    engine_obj.wait_ge(sems[p.sem], p.val)
                if o.barrier:
                    continue
                ins = o.fn(engine_obj)
                if o.needs_inc:
                    ins.then_inc(sems[o.sem], 16 if o.is_dma else 1)
            for d in final:
                p = ops[d]
                if p.eng == ename:
                    engine_obj.wait_ge(sems[p.sem], p.val)

        @block.tensor
        def _(e):
            run(e, per["pe"], "pe")

        @block.scalar
        def _(e):
            run(e, per["act"], "act")

        @block.vector
        def _(e):
            run(e, per["dve"], "dve")

        @block.gpsimd
        def _(e):
            run(e, per["pool"], "pool")

        @block.sync
        def _(e):
            run(e, per["sp"], "sp")


def _C(name, *a, **k):
    def f(e):
        return getattr(e, name)(*a, **k)
    return f


_ARENA_HI = 0


class Arena:
    def __init__(self, big, nel, base=0):
        self.big = big
        self.nel = nel
        self.off = base
        self.hi = base

    def take(self, shape, dtype):
        global _ARENA_HI
        n = 1
        for s in shape[1:]:
            n *= s
        nb = n * (4 if dtype == F32 else 2)
        nb = (nb + 63) // 64 * 64
        el = nb // 2
        a = self.off
        self.off += el
        self.hi = max(self.hi, self.off)
        assert self.off <= self.nel, ("arena overflow", self.off, self.nel)
        _ARENA_HI = max(_ARENA_HI, self.off)
        v = self.big[:, a:a + el]
        if dtype == F32:
            v = v.bitcast(F32)[:, 0:n]
        else:
            v = v[:, 0:n]
        if len(shape) == 3:
            v = v.rearrange("p (a b) -> p a b", a=shape[1])
        elif len(shape) == 4:
            v = v.rearrange("p (a b c) -> p a b c", a=shape[1], b=shape[2])
        return v


def _na_r0(r):
    return min(max(r - 4, 0), 24)


def na_blocks(i):
    res = []
    for j in range(NB):
        inval = []
        anyv = False
        for a in range(2):
            r = 2 * i + a
            for ak in range(2):
                rk = 2 * j + ak
                ok = _na_r0(r) <= rk <= _na_r0(r) + 7
                if ok:
                    anyv = True
                else:
                    inval.append((ak, a))
        if anyv:
            res.append((j, inval))
    return res


NF_NA = 0
NF_SW = 112
NF_DA = 160
NF = 160 + 256

ACC = {}
AW = 66
for _h in range(4):
    ACC[("sw", _h)] = (0, _h * AW)
for _h in range(3):
    ACC[("na", _h)] = (0, 4 * AW + _h * AW)
for _h in range(4):
    ACC[("sw", 4 + _h)] = (1, _h * AW)
ACC[("na", 3)] = (1, 4 * AW)
ACC[("da", 0)] = (1, 5 * AW)
ACC[("da", 1)] = (1, 6 * AW)
for _u in range(6):
    ACC[("da", 2 + _u)] = (2, _u * AW)


def build_program(n_layers=L, do_attn=True, do_ffn=True, taps=(), a1_blocks=NB, a2_blocks=NB, do_mod=True):
    nc = bass.Bass("TRN2", target_bir_lowering=False)
    S = Sched()
    taps = set(taps)
    dbg_outs = {}

    def din(name, shape, dt=F32):
        return nc.dram_tensor(name, list(shape), dt, kind="ExternalInput").ap()

    x_d = din("x", [2048, D])
    cvec_d = din("cvecT", [128, 8])
    cosb_d = din("cosb", [128, NB, 32])
    sinb_d = din("sinb", [128, NB, 32])
    cosc_d = din("cosc", [128, NB, 16])
    sinc_d = din("sinc", [128, NB, 16])
    bias_d = din("biascol", [128, NF])
    swm_d = din("swmask", [128, 2, 128], BF16)
    nam_d = din("namask", [128, 16, 64], BF16)
    rpb_d = din("rpbpad", [L, RPBLEN])
    ctxone_d = din("ctxone", [128, 1])
    pflag_d = din("pflag", [128, 1])
    ck_d = din("ck", [L, 512, 640])
    cv_d = din("cv", [L, 512, 640])
    wmod_d = din("w_mod", [L, D, 6 * D])
    wkv_d = din("w_kv", [L, D, 1280])
    wq_d = din("w_q", [L, D, 1024])
    wout_d = din("w_out", [L, D, D])
    wup_d = din("w_up", [L, D, 2 * DFF])
    wdn_d = din("w_down", [L, DFF, D])
    bmod_d = din("bmodT", [128, L * 48])
    gattn_d = din("gattnT", [128, L * 8])
    gffn_d = din("gffnT", [128, L * 8])
    convw_d = din("convwT", [128, L * 3 * 44])
    convb_d = din("convbT", [128, L * 44])
    small_d = din("small", [L, 520])
    identf_d = din("identf", [128, 128])
    identb_d = din("identb", [128, 128], BF16)
    permb_d = din("permb", [128, 128], BF16)
    onesb_d = din("onesb", [128, 128], BF16)
    rowmask_d = din("rowmask", [128, 6])

    y_d = nc.dram_tensor("y", [2048, D], F32, kind="ExternalOutput").ap()
    okv_d = nc.dram_tensor("okv", [L, 2048, 1280], F32, kind="ExternalOutput").ap()

    with ExitStack() as es:
        def sb(name, shape, dt=F32):
            return es.enter_context(nc.sbuf_tensor(name, list(shape), dt))

        XT = sb("XT", [128, KC, 2048])
        identf = sb("identf_s", [128, 128])
        identb = sb("identb_s", [128, 128], BF16)
        permb = sb("permb_s", [128, 128], BF16)
        onesb = sb("onesb_s", [128, 128], BF16)
        rowmask = sb("rowmask_s", [128, 6])
        cosb = sb("cosb_s", [128, NB, 32])
        sinb = sb("sinb_s", [128, NB, 32])
        cosc = sb("cosc_s", [128, NB, 16])
        sinc = sb("sinc_s", [128, NB, 16])
        biascol = sb("biascol_s", [128, NF])
        swmask = sb("swmask_s", [128, 2, 128], BF16)
        namask = sb("namask_s", [128, 16, 64], BF16)
        ctxone = sb("ctxone_s", [128, 1])
        pflag = sb("pflag_s", [128, 1])
        cvecT = sb("cvecT_s", [128, 8])
        silub = sb("silub", [128, 8], BF16)
        bmodT = sb("bmodT_s", [128, L * 48])
        modsb = sb("modsb", [128, L * 48])
        gattnT = sb("gattnT_s", [128, L * 8])
        gffnT = sb("gffnT_s", [128, L * 8])
        G1 = sb("G1", [128, L * 8])
        G2 = sb("G2", [128, L * 8])
        convwT = sb("convwT_s", [128, L * 3 * 44])
        convbT = sb("convbT_s", [128, L * 44])
        NBIG = 64000
        BIG = sb("BIG", [128, NBIG], BF16)
        banks = [es.enter_context(nc.psum_tensor("bank%d" % i, [128, 512], F32)) for i in range(8)]

        def BK(i):
            return "B%d" % i

        def tap(name, ap, shape, reads, dt=F32):
            if name not in taps:
                return
            d = nc.dram_tensor("dbg_" + name, list(shape), dt, kind="ExternalOutput").ap()
            dbg_outs[name] = d
            S.dma("sp", _C("dma_start", out=d, in_=ap), reads=reads, final_wait=True)

        def ld(dst, src, key):
            S.dma("sp", _C("dma_start", out=dst, in_=src), writes=[key])

        ld(identf[:], identf_d, "identf")
        ld(identb[:], identb_d, "identb")
        ld(permb[:], permb_d, "permb")
        ld(onesb[:], onesb_d, "onesb")
        ld(rowmask[:], rowmask_d, "rowmask")
        ld(cosb[:], cosb_d, "rope")
        ld(sinb[:], sinb_d, "rope")
        ld(cosc[:], cosc_d, "rope")
        ld(sinc[:], sinc_d, "rope")
        ld(biascol[:], bias_d, "biascol")
        ld(swmask[:], swm_d, "swmask")
        ld(namask[:], nam_d, "namask")
        ld(ctxone[:], ctxone_d, "ctxone")
        ld(pflag[:], pflag_d, "pflag")
        ld(cvecT[:], cvec_d, "cvecT")
        ld(bmodT[:], bmod_d, "bmodT")
        ld(gattnT[:], gattn_d, "gattnT")
        ld(gffnT[:], gffn_d, "gffnT")
        ld(convwT[:], convw_d, "convwT")
        ld(convbT[:], convb_d, "convbT")

        ar = Arena(BIG, NBIG)
        xs = [ar.take([128, 1024], F32) for _ in range(2)]
        wm = [ar.take([128, 8, 512], BF16) for _ in range(2)]
        for t in range(NB):
            b = t % 2
            S.dma("sp", _C("dma_start", out=xs[b], in_=x_d[t * 128:(t + 1) * 128, :]),
                  writes=["xs%d" % b])
            for half in range(2):
                bank = (2 * t + half) % 4
                for q in range(4):
                    c = half * 4 + q
                    S.op("pe", _C("transpose",
                        banks[bank][:, q * 128:(q + 1) * 128], xs[b][:, c * 128:(c + 1) * 128], identf[:]),
                        reads=["xs%d" % b, "identf"], writes=[BK(bank)])
                if half == 0:
                    S.op("act", _C("activation",
                        out=XT[:, 0:4, t * 128:(t + 1) * 128],
                        in_=banks[bank][:].rearrange("p (c n) -> p c n", c=4), func=AF.Copy),
                        reads=[BK(bank)], writes=["XT%d" % t])
                else:
                    S.op("dve", _C("tensor_copy",
                        out=XT[:, 4:8, t * 128:(t + 1) * 128],
                        in_=banks[bank][:].rearrange("p (c n) -> p c n", c=4)),
                        reads=[BK(bank)], writes=["XT%d" % t])

        S.op("act", _C("activation", out=silub[:], in_=cvecT[:], func=AF.Silu),
             reads=["cvecT"], writes=["silub"])
        MB = 4
        first_mod = True
        for l in range(n_layers if do_mod else 0):
            for piece in range(12):
                b = (l * 12 + piece) % 2
                S.dma("pool", _C("dma_start",
                    out=wm[b], in_=wmod_d[l, :, piece * 512:(piece + 1) * 512].rearrange("(kc p) n -> p kc n", p=128)),
                    writes=["wm%d" % b])
                for oc in range(4):
                    col = l * 48 + piece * 4 + oc
                    for kc in range(KC):
                        S.op("pe", _C("matmul",
                            banks[MB][:, col:col + 1], lhsT=wm[b][:, kc, oc * 128:(oc + 1) * 128],
                            rhs=silub[:, kc:kc + 1], start=first_mod, stop=(kc == KC - 1), skip_group_check=True),
                            reads=["wm%d" % b, "silub"], writes=[BK(MB)])
                        first_mod = False
        nm = n_layers * 48
        S.op("dve", _C("tensor_tensor", out=modsb[:, 0:nm], in0=banks[MB][:, 0:nm], in1=bmodT[:, 0:nm], op=ALU.add),
             reads=[BK(MB), "bmodT"], writes=["modsb"])
        for l in range(n_layers):
            S.op("dve", _C("scalar_tensor_tensor",
                out=G1[:, l * 8:(l + 1) * 8], in0=modsb[:, l * 48 + 8:l * 48 + 16], scalar=1.0,
                in1=gattnT[:, l * 8:(l + 1) * 8], op0=ALU.add, op1=ALU.mult),
                reads=["modsb", "gattnT"], writes=["G"])
            S.op("dve", _C("scalar_tensor_tensor",
                out=G2[:, l * 8:(l + 1) * 8], in0=modsb[:, l * 48 + 32:l * 48 + 40], scalar=1.0,
                in1=gffnT[:, l * 8:(l + 1) * 8], op0=ALU.add, op1=ALU.mult),
                reads=["modsb", "gffnT"], writes=["G"])
        tap("modsb", modsb[:], [128, L * 48], ["modsb"])
        tap("G1", G1[:], [128, L * 8], ["G"])

        def mcol(l, k, c):
            i0 = l * 48 + k * 8 + c
            return modsb[:, i0:i0 + 1]

        def adaln(dst_fn, tok0, w, Gt, l, kshift, sqbuf, rsbuf, tmps, sbank, tagp, xkeys=("XTh0", "XTh1")):
            xkeys = list(xkeys)
            for c in range(KC):
                S.op("act", _C("activation", out=sqbuf[c % 2][:, 0:w], in_=XT[:, c, tok0:tok0 + w], func=AF.Square),
                     reads=xkeys, writes=[tagp + "sq%d" % (c % 2)])
                S.op("pe", _C("matmul", banks[sbank][:, 0:w], lhsT=onesb[:], rhs=sqbuf[c % 2][:, 0:w],
                                                   start=(c == 0), stop=(c == KC - 1)),
                     reads=[tagp + "sq%d" % (c % 2), "onesb"], writes=[BK(sbank)])
            S.op("act", _C("activation", out=rsbuf[:, 0:w], in_=banks[sbank][:, 0:w], func=AF.Ln, scale=1.0 / D, bias=EPS),
                 reads=[BK(sbank)], writes=[tagp + "rs"])
            S.op("act", _C("activation", out=rsbuf[:, 0:w], in_=rsbuf[:, 0:w], func=AF.Exp, scale=-0.5),
                 reads=[tagp + "rs"], writes=[tagp + "rs"])
            for c in range(KC):
                tb = tmps[c % 2]
                S.op("dve", _C("scalar_tensor_tensor",
                    out=tb[:, 0:w], in0=XT[:, c, tok0:tok0 + w], scalar=Gt[:, l * 8 + c:l * 8 + c + 1],
                    in1=rsbuf[:, 0:w], op0=ALU.mult, op1=ALU.mult),
                    reads=xkeys + ["G", tagp + "rs"], writes=[tagp + "tmp%d" % (c % 2)])
                S.op("act", _C("activation",
                    out=dst_fn(c), in_=tb[:, 0:w], func=AF.Identity, bias=mcol(l, kshift, c), scale=1.0),
                    reads=[tagp + "tmp%d" % (c % 2), "modsb"], writes=[tagp + "h"])

        def adaln_blk(hdst, hkey, tok0, Gt, l, kshift, sq8, rsbuf, tmp8, sbank, tagp, tmpkey=None, offload=False, affine_dve=False, scol=0):
            w = 128
            if offload:
                S.op("dve", _C("tensor_tensor", out=sq8, in0=XT[:, :, tok0:tok0 + w], in1=XT[:, :, tok0:tok0 + w], op=ALU.mult),
                     reads=["XTh0", "XTh1"], writes=[tagp + "sq8"])
            else:
                S.op("act", _C("activation", out=sq8, in_=XT[:, :, tok0:tok0 + w], func=AF.Square),
                     reads=["XTh0", "XTh1"], writes=[tagp + "sq8"])
            for c in range(KC):
                S.op("pe", _C("matmul", banks[sbank][:, scol:scol + w], lhsT=onesb[:], rhs=sq8[:, c, :],
                              start=(c == 0), stop=(c == KC - 1)),
                     reads=[tagp + "sq8", "onesb"], writes=[BK(sbank)])
            S.op("act", _C("activation", out=rsbuf[:, 0:w], in_=banks[sbank][:, scol:scol + w], func=AF.Ln, scale=1.0 / D, bias=EPS),
                 reads=[BK(sbank)], writes=[tagp + "rs"])
            S.op("act", _C("activation", out=rsbuf[:, 0:w], in_=rsbuf[:, 0:w], func=AF.Exp, scale=-0.5),
                 reads=[tagp + "rs"], writes=[tagp + "rs"])
            S.op("dve", _C("tensor_tensor", out=tmp8, in0=XT[:, :, tok0:tok0 + w],
                           in1=rsbuf[:, 0:w].unsqueeze(1).broadcast_to([128, KC, w]), op=ALU.mult),
                 reads=["XTh0", "XTh1", tagp + "rs"], writes=[tmpkey or (tagp + "tmp8")])
            for c in range(KC):
                if offload or affine_dve:
                    S.op("dve", _C("tensor_scalar", out=hdst[:, c, :], in0=tmp8[:, c, :], scalar1=Gt[:, l * 8 + c:l * 8 + c + 1],
                                   scalar2=mcol(l, kshift, c), op0=ALU.mult, op1=ALU.add),
                         reads=[tmpkey or (tagp + "tmp8"), "modsb", "G"], writes=[hkey])
                else:
                    S.op("act", _C("activation", out=hdst[:, c, :], in_=tmp8[:, c, :], func=AF.Identity,
                                   bias=mcol(l, kshift, c), scale=Gt[:, l * 8 + c:l * 8 + c + 1]),
                         reads=[tmpkey or (tagp + "tmp8"), "modsb", "G"], writes=[hkey])

        for l in range(n_layers):
            lam_init = 0.8 - 0.6 * math.exp(-0.3 * l)
            S.barrier()
            ar = Arena(BIG, NBIG)
            KT = ar.take([128, 5, 2048], BF16)
            V = ar.take([128, NB, 10, 66], BF16)
            CTXKT = ar.take([128, 5, 512], BF16)
            CTXV = ar.take([128, 4, 10, 66], BF16)
            TAB = ar.take([128, 4, 16, 64], BF16)
            WA = ar.take([128, 8, 1280], BF16)
            sq8 = ar.take([128, KC, 128], BF16)
            hT = ar.take([128, KC, 128], BF16)
            rsb = ar.take([128, 128], F32)
            SM = ar.take([128, 520], F32)
            esink = ar.take([128, 8], F32)
            lamt = ar.take([128, 8], F32)
            SG = ar.take([128, 64], F32)
            smalls = ar.take([128, 64], F32)
            base_shared = ar.off
            kcats = [ar.take([128, 640], F32) for _ in range(2)]
            vcats = [ar.take([128, 640], F32) for _ in range(2)]
            sqks = [ar.take([128, 640], F32) for _ in range(2)]
            rts = [[ar.take([128, 64], F32) for _ in range(4)] for _ in range(2)]
            rt2s = [[ar.take([128, 128], F32) for _ in range(4)] for _ in range(2)]
            kbs = [ar.take([128, 640], BF16) for _ in range(2)]
            smks = [ar.take([128, 64], F32) for _ in range(2)]
            a0_base = ar.off
            CKs = ar.take([128, 4, 640], BF16)
            TABF = ar.take([128, 16, 64], F32)
            a0_hi = ar.off
            ar.off = a0_base
            tmp8as = [ar.take([128, KC, 128], F32) for _ in range(2)]
            sq8s = [sq8, ar.take([128, KC, 128], BF16)]
            rsbs = [rsb, ar.take([128, 128], F32)]
            hTs = [hT, ar.take([128, KC, 128], BF16)]
            ar.off = max(ar.off, a0_hi)
            hiA1 = ar.off
            ar.off = base_shared
            qf = ar.take([128, 1024], F32)
            sqq = ar.take([128, 1024], F32)
            rtq = [sqq[:, k_ * 256:(k_ + 1) * 256] for k_ in range(4)]
            qb = ar.take([128, 1024], BF16)
            QTs = [ar.take([128, 20, 128], BF16) for _ in range(2)]
            PT = [ar.take([128, 512], BF16) for _ in range(4)]
            Ofin = sqq[:, 0:512].rearrange("p (a d) -> p a d", a=8)
            ddt = sqq[:, 512:768].rearrange("p (a d) -> p a d", a=4)
            sqd = sqq[:, 768:1024].rearrange("p (a d) -> p a d", a=4)
            Omix = ar.take([128, 1024], BF16)
            rtq2 = [Omix[:, k_ * 256:(k_ + 1) * 256].bitcast(F32) for k_ in range(4)]
            OT = ar.take([128, 8, 256], BF16)
            AWW = 7 * AW
            Oacc = ar.take([128, 3, AWW], F32)
            wo = [WA[:, :, 1024 + 128 * k_:1024 + 128 * (k_ + 1)] for k_ in range(2)]
            wqv = WA[:, :, 0:1024]
            tmp8q = sqq.rearrange("p (c n) -> p c n", c=KC)

            if do_attn:
                S.dma("sp", _C("dma_start", out=SM, in_=small_d[l, :].partition_broadcast(128)), writes=["SM"])
                S.op("act", _C("activation", out=esink, in_=SM[:, 320:328], func=AF.Exp), reads=["SM"], writes=["esink"])
                lp = SM[:, 328:456].rearrange("p (a b d) -> p a b d", a=2, b=2)
                S.op("dve", _C("tensor_tensor", out=smalls[:, 0:64].rearrange("p (a d) -> p a d", a=2),
                                                      in0=lp[:, :, 0, :], in1=lp[:, :, 1, :], op=ALU.mult),
                     reads=["SM"], writes=["smalls"])
                S.op("dve", _C("tensor_reduce", out=lamt[:, 0:2], in_=smalls[:, 0:64].rearrange("p (a d) -> p a d", a=2),
                                                      axis=AX.X, op=ALU.add),
                     reads=["smalls"], writes=["lamt"])
                S.op("act", _C("activation", out=lamt[:, 2:4], in_=lamt[:, 0:2], func=AF.Exp), reads=["lamt"], writes=["lamt2"])
                S.op("dve", _C("tensor_tensor", out=lamt[:, 4:5], in0=lamt[:, 3:4], in1=lamt[:, 2:3], op=ALU.subtract),
                     reads=["lamt2"], writes=["lamt3"])
                S.op("dve", _C("tensor_scalar", out=lamt[:, 5:6], in0=lamt[:, 4:5], scalar1=-lam_init, scalar2=None, op0=ALU.add),
                     reads=["lamt3"], writes=["neglam"])
                S.op("dve", _C("tensor_scalar", out=SG, in0=SM[:, 456:520], scalar1=1.0 - lam_init, scalar2=None, op0=ALU.mult),
                     reads=["SM"], writes=["SG"])
                neglam = lamt[:, 5:6]
                S.dma("pool", _C("dma_start", out=CKs, in_=ck_d[l].rearrange("(b p) f -> p b f", p=128)), writes=["CKs"])
                for b4 in range(4):
                    bank = 6 + (b4 % 2)
                    pv = banks[bank][:].bitcast(BF16)
                    for ch in range(5):
                        S.op("pe", _C("transpose", pv[:, ch * 128:(ch + 1) * 128], CKs[:, b4, ch * 128:(ch + 1) * 128], identb[:]),
                             reads=["CKs", "identb"], writes=[BK(bank)])
                    S.op("act", _C("activation", out=CTXKT[:, :, b4 * 128:(b4 + 1) * 128],
                                                                   in_=pv[:, 0:640].rearrange("p (c n) -> p c n", c=5), func=AF.Copy),
                         reads=[BK(bank)], writes=["CTXKT"])
                for b4 in range(4):
                    S.dma("pool", _C("dma_start",
                        out=CTXV[:, b4, :, 0:64], in_=cv_d[l, b4 * 128:(b4 + 1) * 128, :].rearrange("p (h d) -> p h d", h=10)),
                        writes=["CTXVd"])
                S.op("pool", _C("tensor_copy", out=CTXV[:, :, :, 64], in_=ctxone[:, 0:1].unsqueeze(2).broadcast_to([128, 4, 10])),
                     reads=["ctxone"], writes=["CTXVo"])
                S.op("pool", _C("memset", V[:, :, :, 64], 1.0), writes=["Vones"])
                for h in range(4):
                    for ak in range(2):
                        off = PADOFF + h * 465 + (ak - 1) * 31 - 48
                        src = bass.AP(rpb_d.tensor, l * RPBLEN + off, [[1, 64], [31, 16], [1, 64]])
                        S.dma("sp", _C("dma_start", out=TABF[ak * 64:(ak + 1) * 64, :, :], in_=src),
                              writes=["TABF"])
                    S.op("act", _C("activation", out=TAB[:, h, :, :], in_=TABF, func=AF.Exp), reads=["TABF"], writes=["TAB"])
                    S.op("dve", _C("tensor_tensor", out=TAB[:, h, :, :], in0=TAB[:, h, :, :], in1=namask[:], op=ALU.mult),
                         reads=["TAB", "namask"], writes=["TAB"])
                if l == 0:
                    tap("TAB", TAB, [128, 4, 16, 64], ["TAB"], BF16)
                    tap("CTXKT", CTXKT, [128, 5, 512], ["CTXKT"], BF16)
                GN = SM
                S.dma("pool", _C("dma_start", out=WA, in_=wkv_d[l].rearrange("(kc p) n -> p kc n", p=128)), writes=["WA"])
                S.barrier()
                def rope2(eng, groups, tkey):
                    seqs = []
                    for gi, (view, H, half, cs, sn, key, tl, xr, xw) in enumerate(groups):
                        x1 = view[:, :, :, 0]
                        x2 = view[:, :, :, 1]
                        cb_ = cs.unsqueeze(1).broadcast_to([128, H, half])
                        sb_ = sn.unsqueeze(1).broadcast_to([128, H, half])
                        n_ = H * half
                        tv = [tm[:, 0:n_].rearrange("p (h d) -> p h d", h=H) for tm in tl]
                        tk = ["%s_%d_%d" % (tkey, gi, k_) for k_ in range(4)]
                        seqs.append([
                            (_C("tensor_tensor", out=tv[0], in0=x1, in1=cb_, op=ALU.mult), [key, "rope"] + xr, [tk[0]] + xw),
                            (_C("tensor_tensor", out=tv[1], in0=x2, in1=sb_, op=ALU.mult), [key, "rope"] + xr, [tk[1]] + xw),
                            (_C("tensor_tensor", out=tv[2], in0=x1, in1=sb_, op=ALU.mult), [key, "rope"] + xr, [tk[2]] + xw),
                            (_C("tensor_tensor", out=tv[3], in0=x2, in1=cb_, op=ALU.mult), [key, "rope"] + xr, [tk[3]] + xw),
                            (_C("tensor_tensor", out=x1, in0=tv[0], in1=tv[1], op=ALU.subtract), [tk[0], tk[1]], [key]),
                            (_C("tensor_tensor", out=x2, in0=tv[2], in1=tv[3], op=ALU.add), [tk[2], tk[3]], [key]),
                        ])
                    for k_ in range(6):
                        for sq_ in seqs:
                            fn_, rd_, wr_ = sq_[k_]
                            S.op(eng, fn_, reads=rd_, writes=wr_)

                def a1_block(t):
                    tok0 = t * 128
                    kcat = kcats[t % 2]
                    vcat = vcats[t % 2]
                    sfx = str(t % 2)
                    sqk = sqks[t % 2]
                    kb = kbs[t % 2]
                    rt = rts[t % 2]
                    rt2 = rt2s[t % 2]
                    smk_ = smks[t % 2]
                    pb = 0 if t % 2 == 0 else 3
                    hT_ = hTs[t % 2]
                    adaln_blk(hT_, "ah" + sfx, tok0, G1, l, 0, sq8s[t % 2], rsbs[t % 2], tmp8as[t % 2], pb + 2, "a" + sfx, affine_dve=True, scol=256)
                    for nt, (n0, w) in enumerate(((0, 512), (512, 512), (1024, 256))):
                        for kc in range(KC):
                            S.op("pe", _C("matmul",
                                banks[pb + nt][:, 0:w], lhsT=hT_[:, kc, :], rhs=WA[:, kc, n0:n0 + w],
                                start=(kc == 0), stop=(kc == KC - 1)),
                                reads=["ah" + sfx, "WA"], writes=[BK(pb + nt)])
                    S.op("act", _C("activation", out=kcat[:, 0:512], in_=banks[pb][:, :], func=AF.Copy),
                         reads=[BK(pb)], writes=["kc_a" + sfx, "kc_b" + sfx, "kc_c" + sfx])
                    S.op("act", _C("activation", out=kcat[:, 512:640], in_=banks[pb + 1][:, 0:128], func=AF.Copy),
                         reads=[BK(pb + 1)], writes=["kc_c" + sfx])
                    S.op("act", _C("activation", out=vcat[:, 0:384], in_=banks[pb + 1][:, 128:512], func=AF.Copy),
                         reads=[BK(pb + 1)], writes=["vcat" + sfx])
                    S.op("dve", _C("tensor_copy", out=vcat[:, 384:640], in_=banks[pb + 2][:, 0:256]),
                         reads=[BK(pb + 2)], writes=["vcat" + sfx])
                    a1_mark[0] = len(S.ops)
                    S.op("act", _C("activation", out=V[:, t, :, 0:64], in_=vcat.rearrange("p (h d) -> p h d", h=10), func=AF.Copy),
                         reads=["vcat" + sfx], writes=["V"])
                    S.dma("sp", _C("dma_start", out=okv_d[l, tok0:tok0 + 128, 640:1280], in_=vcat),
                          reads=["vcat" + sfx], final_wait=True)
                    S.op("act", _C("activation", out=sqk, in_=kcat, func=AF.Square), reads=["kc_a" + sfx, "kc_b" + sfx, "kc_c" + sfx], writes=["sqk" + sfx])
                    S.op("dve", _C("tensor_reduce", out=smk_[:, 0:6], in_=sqk[:, 0:384].rearrange("p (h d) -> p h d", h=6), axis=AX.X, op=ALU.add),
                         reads=["sqk" + sfx], writes=["smk" + sfx])
                    S.op("dve", _C("tensor_reduce", out=smk_[:, 6:14], in_=sqk[:, 384:640].rearrange("p (h d) -> p h d", h=8), axis=AX.X, op=ALU.add),
                         reads=["sqk" + sfx], writes=["smk" + sfx])
                    S.op("act", _C("activation", out=smk_[:, 16:22], in_=smk_[:, 0:6], func=AF.Ln, scale=1.0 / 64, bias=EPS),
                         reads=["smk" + sfx], writes=["smk" + sfx])
                    S.op("act", _C("activation", out=smk_[:, 22:30], in_=smk_[:, 6:14], func=AF.Ln, scale=1.0 / 32, bias=EPS),
                         reads=["smk" + sfx], writes=["smk" + sfx])
                    S.op("act", _C("activation", out=smk_[:, 32:46], in_=smk_[:, 16:30], func=AF.Exp, scale=-0.5),
                         reads=["smk" + sfx], writes=["smk" + sfx])
                    k64 = kcat[:, 0:384].rearrange("p (h d) -> p h d", h=6)
                    k32 = kcat[:, 384:640].rearrange("p (h d) -> p h d", h=8)
                    S.op("dve", _C("tensor_tensor", out=k64, in0=k64, in1=smk_[:, 32:38].unsqueeze(2).broadcast_to([128, 6, 64]), op=ALU.mult),
                         reads=["smk" + sfx, "kc_a" + sfx, "kc_b" + sfx], writes=["kc_a" + sfx, "kc_b" + sfx])
                    S.op("dve", _C("tensor_tensor", out=k32, in0=k32, in1=smk_[:, 38:46].unsqueeze(2).broadcast_to([128, 8, 32]), op=ALU.mult),
                         reads=["smk" + sfx, "kc_c" + sfx], writes=["kc_c" + sfx])
                    kna = kcat[:, 0:256].rearrange("p (h d) -> p h d", h=4)
                    ksw = kcat[:, 256:384].rearrange("p (h d) -> p h d", h=2)
                    S.op("dve", _C("tensor_tensor", out=kna, in0=kna, in1=GN[:, 64:128].unsqueeze(1).broadcast_to([128, 4, 64]), op=ALU.mult),
                         reads=["SM", "kc_a" + sfx], writes=["kc_a" + sfx])
                    S.op("dve", _C("tensor_tensor", out=ksw, in0=ksw, in1=GN[:, 192:256].unsqueeze(1).broadcast_to([128, 2, 64]), op=ALU.mult),
                         reads=["SM", "kc_b" + sfx], writes=["kc_b" + sfx])
                    S.op("dve", _C("tensor_tensor", out=k32, in0=k32, in1=GN[:, 288:320].unsqueeze(1).broadcast_to([128, 8, 32]), op=ALU.mult),
                         reads=["SM", "kc_c" + sfx], writes=["kc_c" + sfx])

                    rope2("dve", [
                        (kcat[:, 256:384].rearrange("p (h d two) -> p h d two", h=2, two=2), 2, 32, cosb[:, t, :], sinb[:, t, :], "kc_b" + sfx, rt, [], []),
                        (kcat[:, 384:640].rearrange("p (h d two) -> p h d two", h=8, two=2), 8, 16, cosc[:, t, :], sinc[:, t, :], "kc_c" + sfx, rt2, [], []),
                    ], "rtk" + sfx)
                    S.dma("sp", _C("dma_start", out=okv_d[l, tok0:tok0 + 128, 0:640], in_=kcat),
                          reads=["kc_a" + sfx, "kc_b" + sfx, "kc_c" + sfx], final_wait=True)
                    S.op("act", _C("activation", out=kb, in_=kcat, func=AF.Copy), reads=["kc_a" + sfx, "kc_b" + sfx, "kc_c" + sfx], writes=["kb" + sfx])
                    tb_ = 6 + t % 2
                    pv = banks[tb_][:].bitcast(BF16)
                    for ch in range(5):
                        S.op("pe", _C("transpose", pv[:, ch * 128:(ch + 1) * 128], kb[:, ch * 128:(ch + 1) * 128], identb[:]),
                             reads=["kb" + sfx, "identb"], writes=[BK(tb_)])
                    S.op("act", _C("activation", out=KT[:, :, tok0:tok0 + 128], in_=pv[:, 0:640].rearrange("p (c n) -> p c n", c=5), func=AF.Copy),
                         reads=[BK(tb_)], writes=["KT"])

                a1_mark = [0]

                def cap_a1(t):
                    saved = S.ops
                    S.ops = []
                    a1_block(t)
                    got = S.ops
                    S.ops = saved
                    return got[:a1_mark[0]], got[a1_mark[0]:]

                st_a1 = [cap_a1(t) for t in range(a1_blocks)]
                for t in range(0, a1_blocks, 2):
                    if t + 1 < a1_blocks:
                        la = st_a1[t][0] + st_a1[t][1]
                        lb = st_a1[t + 1][0] + st_a1[t + 1][1]
                        for k_ in range(max(len(la), len(lb))):
                            if k_ < len(la):
                                S.ops.append(la[k_])
                            if k_ < len(lb):
                                S.ops.append(lb[k_])
                    else:
                        S.ops.extend(st_a1[t][0] + st_a1[t][1])
                if l == 0:
                    tap("KT", KT, [128, 5, 2048], ["KT"], BF16)
                    tap("V", V, [128, NB, 10, 66], ["V", "Vones"], BF16)

                S.barrier()
                S.dma("pool", _C("dma_start", out=wqv, in_=wq_d[l].rearrange("(kc p) n -> p kc n", p=128)), writes=["WA"])
                sctr = [0]
                pctr = [0]
                woctr = [0]
                def front(i):
                    tok0 = i * 128
                    QT = QTs[i % 2]
                    qk_ = "QT%d" % (i % 2)
                    adaln_blk(hT, "ah", tok0, G1, l, 0, sq8, rsb, tmp8q, 0, "a", tmpkey="sqq", offload=True)
                    for nt in range(2):
                        for kc in range(KC):
                            S.op("pe", _C("matmul",
                                banks[nt][:, :], lhsT=hT[:, kc, :], rhs=wqv[:, kc, nt * 512:(nt + 1) * 512],
                                start=(kc == 0), stop=(kc == KC - 1)),
                                reads=["ah", "WA"], writes=[BK(nt)])
                    S.op("dve", _C("tensor_copy", out=qf[:, 0:512], in_=banks[0][:, :]), reads=[BK(0)], writes=["qf_a", "qf_b"])
                    S.op("dve", _C("tensor_copy", out=qf[:, 512:1024], in_=banks[1][:, :]), reads=[BK(1)], writes=["qf_b", "qf_c"])
                    S.op("dve", _C("tensor_tensor", out=sqq, in0=qf, in1=qf, op=ALU.mult), reads=["qf_a", "qf_b", "qf_c"], writes=["sqq"])
                    S.op("dve", _C("tensor_reduce", out=smalls[:, 0:12], in_=sqq[:, 0:768].rearrange("p (h d) -> p h d", h=12), axis=AX.X, op=ALU.add),
                         reads=["sqq"], writes=["smalls"])
                    S.op("dve", _C("tensor_reduce", out=smalls[:, 12:20], in_=sqq[:, 768:1024].rearrange("p (h d) -> p h d", h=8), axis=AX.X, op=ALU.add),
                         reads=["sqq"], writes=["smalls"])
                    S.op("act", _C("activation", out=smalls[:, 20:32], in_=smalls[:, 0:12], func=AF.Ln, scale=1.0 / 64, bias=EPS),
                         reads=["smalls"], writes=["smalls"])
                    S.op("act", _C("activation", out=smalls[:, 32:40], in_=smalls[:, 12:20], func=AF.Ln, scale=1.0 / 32, bias=EPS),
                         reads=["smalls"], writes=["smalls"])
                    S.op("act", _C("activation", out=smalls[:, 40:60], in_=smalls[:, 20:40], func=AF.Exp, scale=-0.5),
                         reads=["smalls"], writes=["smalls"])
                    q64 = qf[:, 0:768].rearrange("p (h d) -> p h d", h=12)
                    q32 = qf[:, 768:1024].rearrange("p (h d) -> p h d", h=8)
                    S.op("dve", _C("tensor_tensor", out=q64, in0=q64, in1=smalls[:, 40:52].unsqueeze(2).broadcast_to([128, 12, 64]), op=ALU.mult),
                         reads=["smalls", "qf_a", "qf_b"], writes=["qf_a", "qf_b"])
                    S.op("dve", _C("tensor_tensor", out=q32, in0=q32, in1=smalls[:, 52:60].unsqueeze(2).broadcast_to([128, 8, 32]), op=ALU.mult),
                         reads=["smalls", "qf_c"], writes=["qf_c"])
                    qna = qf[:, 0:256].rearrange("p (h d) -> p h d", h=4)
                    qsw = qf[:, 256:768].rearrange("p (h d) -> p h d", h=8)
                    S.op("dve", _C("tensor_tensor", out=qna, in0=qna, in1=GN[:, 0:64].unsqueeze(1).broadcast_to([128, 4, 64]), op=ALU.mult),
                         reads=["SM", "qf_a"], writes=["qf_a"])
                    S.op("dve", _C("tensor_tensor", out=qsw, in0=qsw, in1=GN[:, 128:192].unsqueeze(1).broadcast_to([128, 8, 64]), op=ALU.mult),
                         reads=["SM", "qf_b"], writes=["qf_b"])
                    S.op("dve", _C("tensor_tensor", out=q32, in0=q32, in1=GN[:, 256:288].unsqueeze(1).broadcast_to([128, 8, 32]), op=ALU.mult),
                         reads=["SM", "qf_c"], writes=["qf_c"])
                    rope2("dve", [
                        (qf[:, 256:768].rearrange("p (h d two) -> p h d two", h=8, two=2), 8, 32, cosb[:, i, :], sinb[:, i, :], "qf_b", rtq, ["sqq"], []),
                        (qf[:, 768:1024].rearrange("p (h d two) -> p h d two", h=8, two=2), 8, 16, cosc[:, i, :], sinc[:, i, :], "qf_c", rtq2, [], ["Omix"]),
                    ], "rtq")
                    S.op("dve", _C("tensor_copy", out=qb, in_=qf), reads=["qf_a", "qf_b", "qf_c"], writes=["qb"])
                    if l == 0 and i == 1:
                        tap("qf", qf, [128, 1024], ["qf_a", "qf_b", "qf_c"])
                    for ch in range(8):
                        bk = ch // 4
                        S.op("pe", _C("matmul",
                            banks[bk][:, (ch % 4) * 128:(ch % 4 + 1) * 128], lhsT=qb[:, ch * 128:(ch + 1) * 128],
                            rhs=(permb[:] if ch < 2 else identb[:]), start=True, stop=True, skip_group_check=True),
                            reads=["qb", "permb", "identb"], writes=[BK(bk)])
                    b0 = banks[0][:].rearrange("p (c n) -> p c n", c=4)
                    b1 = banks[1][:].rearrange("p (c n) -> p c n", c=4)
                    QTn = QT[:, 0:4, :].rearrange("p (c two) n -> p c two n", two=2)
                    for hl in range(2):
                        S.op("dve", _C("tensor_scalar", out=QTn[:, :, hl, :], in0=b0[:, 0:2, :], scalar1=rowmask[:, hl:hl + 1], scalar2=None, op0=ALU.mult),
                             reads=[BK(0), "rowmask"], writes=[qk_])
                    for g in range(2):
                        S.op("dve", _C("tensor_scalar", out=QT[:, 4 + 4 * g:6 + 4 * g, :], in0=b0[:, 2:4, :], scalar1=rowmask[:, g:g + 1], scalar2=None, op0=ALU.mult),
                             reads=[BK(0), "rowmask"], writes=[qk_])
                        S.op("dve", _C("tensor_scalar", out=QT[:, 6 + 4 * g:8 + 4 * g, :], in0=b1[:, 0:2, :], scalar1=rowmask[:, g:g + 1], scalar2=None, op0=ALU.mult),
                             reads=[BK(1), "rowmask"], writes=[qk_])
                    QTd = QT[:, 12:20, :].rearrange("p (hf u) n -> p hf u n", u=4)
                    for u in range(4):
                        S.op("dve", _C("tensor_scalar", out=QTd[:, :, u, :], in0=b1[:, 2:4, :], scalar1=rowmask[:, 2 + u:3 + u], scalar2=None, op0=ALU.mult),
                             reads=[BK(1), "rowmask"], writes=[qk_])

                def steps_and_back(i, fe_next):
                    QT = QTs[i % 2]
                    qk_ = "QT%d" % (i % 2)
                    steps = []
                    for j in range(NB):
                        for hf in range(2):
                            steps.append(("da", j, hf))
                    for b4 in range(4):
                        for hf in range(2):
                            steps.append(("dac", b4, hf))
                    for g in range(2):
                        for j in (i - 1, i, i + 1):
                            if 0 <= j < NB:
                                steps.append(("sw", j, g))
                        for b4 in range(4):
                            steps.append(("swc", b4, g))
                    for (j, inval) in na_blocks(i):
                        steps.append(("na", j, inval))
                    for b4 in range(4):
                        steps.append(("nac", b4, None))

                    acc_first = [True, True, True]

                    def emit_qk(st, sbk):
                        kind, j, x = st
                        if kind in ("na", "nac"):
                            for hp in range(2):
                                if kind == "na":
                                    lh = KT[:, hp, j * 128:(j + 1) * 128]
                                    rk_ = ["KT"]
                                else:
                                    lh = CTXKT[:, hp, j * 128:(j + 1) * 128]
                                    rk_ = ["CTXKT"]
                                S.op("pe", _C("matmul",
                                    banks[sbk][:, hp * 256:(hp + 1) * 256], lhsT=lh, rhs=QT[:, 2 * hp:2 * hp + 2, :],
                                    start=True, stop=True, skip_group_check=True),
                                    reads=rk_ + [qk_], writes=[BK(sbk)])
                        elif kind in ("sw", "swc"):
                            g = x
                            if kind == "sw":
                                lh = KT[:, 2, j * 128:(j + 1) * 128]
                                rk_ = ["KT"]
                            else:
                                lh = CTXKT[:, 2, j * 128:(j + 1) * 128]
                                rk_ = ["CTXKT"]
                            S.op("pe", _C("matmul",
                                banks[sbk][:, :], lhsT=lh, rhs=QT[:, 4 + 4 * g:8 + 4 * g, :],
                                start=True, stop=True, skip_group_check=True),
                                reads=rk_ + [qk_], writes=[BK(sbk)])
                        else:
                            hf = x
                            if kind == "da":
                                lh = KT[:, 3 + hf, j * 128:(j + 1) * 128]
                                rk_ = ["KT"]
                            else:
                                lh = CTXKT[:, 3 + hf, j * 128:(j + 1) * 128]
                                rk_ = ["CTXKT"]
                            S.op("pe", _C("matmul",
                                banks[sbk][:, :], lhsT=lh, rhs=QT[:, 12 + 4 * hf:16 + 4 * hf, :],
                                start=True, stop=True, skip_group_check=True),
                                reads=rk_ + [qk_], writes=[BK(sbk)])

                    def emit_exp(st, sbk, pbi):
                        kind, j, x = st
                        P = PT[pbi]
                        pk = "PT%d" % pbi
                        if kind == "na":
                            bc = biascol[:, NF_NA + i * 7 + (j - i + 3):NF_NA + i * 7 + (j - i + 3) + 1]
                            sc = 0.125
                        elif kind == "sw":
                            bc = biascol[:, NF_SW + i * 3 + (j - i + 1):NF_SW + i * 3 + (j - i + 1) + 1]
                            sc = 0.125
                        elif kind == "da":
                            bc = biascol[:, NF_DA + i * 16 + j:NF_DA + i * 16 + j + 1]
                            sc = 32 ** -0.5
                        elif kind == "dac":
                            bc = 0.0
                            sc = 32 ** -0.5
                        else:
                            bc = 0.0
                            sc = 0.125
                        S.op("act", _C("activation", out=P, in_=banks[sbk][:, :], func=AF.Exp, bias=bc, scale=sc),
                             reads=[BK(sbk), "biascol"], writes=[pk])
                        if kind == "na":
                            e0 = 2 * (j - i) + 7
                            Pv = P.rearrange("p (h n) -> p h n", h=4)
                            tv = TAB[:, :, e0:e0 + 2, :].rearrange("p h a c -> p h (a c)")
                            S.op("dve", _C("tensor_tensor", out=Pv, in0=Pv, in1=tv, op=ALU.mult),
                                 reads=[pk, "TAB"], writes=[pk])
                            for (ak, a) in x:
                                arr = 1 - a
                                S.op("dve", _C("memset",
                                    P[ak * 64:(ak + 1) * 64, :].rearrange("p (h n) -> p h n", h=4)[:, :, arr * 64:(arr + 1) * 64], 0.0),
                                    reads=[pk], writes=[pk])
                        elif kind == "sw" and j != i:
                            which = 0 if j < i else 1
                            Pv = P.rearrange("p (h n) -> p h n", h=4)
                            mv = swmask[:, which, :].unsqueeze(1).broadcast_to([128, 4, 128])
                            S.op("dve", _C("tensor_tensor", out=Pv, in0=Pv, in1=mv, op=ALU.mult),
                                 reads=[pk, "swmask"], writes=[pk])

                    def emit_pv(st, pbi):
                        kind, j, x = st
                        P = PT[pbi]
                        pk = "PT%d" % pbi
                        for m in range(4):
                            if kind in ("na", "nac"):
                                key = ("na", m)
                                vh = m
                            elif kind in ("sw", "swc"):
                                key = ("sw", 4 * x + m)
                                vh = 4 + x
                            else:
                                hh_ = 2 * x + m // 2
                                key = ("da", 4 * x + m)
                                vh = 6 + hh_
                            slot, col = ACC[key]
                            bk = 2 + slot
                            if kind in ("na", "sw", "da"):
                                rv = V[:, j, vh, 0:65]
                                rk_ = ["V", "Vones"]
                            else:
                                rv = CTXV[:, j, vh, 0:65]
                                rk_ = ["CTXVd", "CTXVo"]
                            st_ = acc_first[slot]
                            acc_first[slot] = False
                            S.op("pe", _C("matmul",
                                banks[bk][:, col:col + 65], lhsT=P[:, m * 128:(m + 1) * 128], rhs=rv,
                                start=st_, stop=False, skip_group_check=True),
                                reads=[pk] + rk_, writes=[BK(bk)])

                    nst = len(steps)
                    sb_of = []
                    pb_of = []
                    for s_ in range(nst):
                        sb_of.append(5 + (sctr[0] % 3))
                        sctr[0] += 1
                        pb_of.append(pctr[0] % 4)
                        pctr[0] += 1
                    LA = 2
                    for s_ in range(min(LA, nst)):
                        emit_qk(steps[s_], sb_of[s_])
                        emit_exp(steps[s_], sb_of[s_], pb_of[s_])
                    per = (len(fe_next) + nst - 1) // nst if fe_next else 0
                    fpos = 0
                    for s_ in range(nst):
                        if s_ + LA < nst:
                            emit_qk(steps[s_ + LA], sb_of[s_ + LA])
                            emit_exp(steps[s_ + LA], sb_of[s_ + LA], pb_of[s_ + LA])
                        emit_pv(steps[s_], pb_of[s_])
                        if fpos < len(fe_next):
                            S.ops.extend(fe_next[fpos:fpos + per])
                            fpos += per
                    S.ops.extend(fe_next[fpos:])

                    S.op("dve", _C("tensor_copy", out=Oacc[:, 0, :], in_=banks[2][:, 0:AWW]), reads=[BK(2)], writes=["Oacc"])
                    S.op("act", _C("activation", out=Oacc[:, 1, :], in_=banks[3][:, 0:AWW], func=AF.Copy), reads=[BK(3)], writes=["Oacc"])
                    S.op("dve", _C("tensor_copy", out=Oacc[:, 2, :], in_=banks[4][:, 0:AWW]), reads=[BK(4)], writes=["Oacc"])

                def back(i):
                    def accv(kind, lo, n):
                        slot, col = ACC[(kind, lo)]
                        return Oacc[:, slot, col:col + AW * n].rearrange("p (h d) -> p h d", h=n), "Oacc"

                    runs = [("sw", 0, 4, 256), ("na", 0, 3, 0), ("sw", 4, 4, 512), ("na", 3, 1, 192), ("da", 0, 2, None), ("da", 2, 6, None)]
                    rec = smalls
                    ri = 0
                    for (kind, lo, n, ocol) in runs:
                        av, bkey = accv(kind, lo, n)
                        rr = rec[:, ri:ri + n]
                        if kind == "sw":
                            S.op("dve", _C("tensor_tensor", out=rr, in0=av[:, :, 64], in1=esink[:, lo:lo + n], op=ALU.add),
                                 reads=[bkey, "esink"], writes=["smalls"])
                            S.op("dve", _C("reciprocal", out=rr, in_=rr), reads=["smalls"], writes=["smalls"])
                        else:
                            S.op("dve", _C("reciprocal", out=rr, in_=av[:, :, 64]), reads=[bkey], writes=["smalls"])
                        if kind == "da":
                            ov = Ofin[:, lo:lo + n, :]
                            okey = "sqq"
                        else:
                            ov = Omix[:, ocol:ocol + 64 * n].rearrange("p (h d) -> p h d", h=n)
                            okey = "Omix"
                        S.op("dve", _C("tensor_tensor",
                            out=ov, in0=av[:, :, 0:64], in1=rr.unsqueeze(2).broadcast_to([128, n, 64]), op=ALU.mult),
                            reads=[bkey, "smalls"], writes=[okey])
                        ri += n
                    O4 = Ofin.rearrange("p (h c) d -> p h c d", c=2)
                    S.op("dve", _C("scalar_tensor_tensor", out=ddt, in0=O4[:, :, 1, :], scalar=neglam, in1=O4[:, :, 0, :], op0=ALU.mult, op1=ALU.add),
                         reads=["sqq", "neglam"], writes=["sqq"])
                    S.op("act", _C("activation", out=sqd, in_=ddt, func=AF.Square), reads=["sqq"], writes=["sqq"])
                    S.op("dve", _C("tensor_reduce", out=rec[:, 24:28], in_=sqd, axis=AX.X, op=ALU.add), reads=["sqq"], writes=["smalls"])
                    S.op("act", _C("activation", out=rec[:, 28:32], in_=rec[:, 24:28], func=AF.Ln, scale=1.0 / 64, bias=EPS), reads=["smalls"], writes=["smalls"])
                    S.op("act", _C("activation", out=rec[:, 32:36], in_=rec[:, 28:32], func=AF.Exp, scale=-0.5), reads=["smalls"], writes=["smalls"])
                    S.op("dve", _C("tensor_tensor", out=ddt, in0=ddt, in1=rec[:, 32:36].unsqueeze(2).broadcast_to([128, 4, 64]), op=ALU.mult),
                         reads=["smalls", "sqq"], writes=["sqq"])
                    S.op("dve", _C("tensor_tensor", out=Omix[:, 768:1024].rearrange("p (h d) -> p h d", h=4), in0=ddt,
                                                          in1=SG.unsqueeze(1).broadcast_to([128, 4, 64]), op=ALU.mult),
                         reads=["sqq", "SG"], writes=["Omix"])
                    if l == 0 and i == 1:
                        tap("Omix", Omix, [128, 1024], ["Omix"], BF16)
                    iq = i % 2
                    for rnd in range(2):
                        for q4 in range(4):
                            ch = rnd * 4 + q4
                            S.op("pe", _C("matmul",
                                banks[0][:, q4 * 128:(q4 + 1) * 128], lhsT=Omix[:, ch * 128:(ch + 1) * 128],
                                rhs=(permb[:] if ch < 2 else identb[:]), start=True, stop=True, skip_group_check=True),
                                reads=["Omix", "permb", "identb"], writes=[BK(0)])
                        S.op("dve", _C("tensor_copy",
                            out=OT[:, rnd * 4:(rnd + 1) * 4, iq * 128:(iq + 1) * 128],
                            in_=banks[0][:].rearrange("p (c n) -> p c n", c=4)),
                            reads=[BK(0)], writes=["OT"])
                    if iq == 1:
                        t0p = (i - 1) * 128
                        for co in range(KC):
                            wb_ = woctr[0] % 2
                            woctr[0] += 1
                            S.dma("pool", _C("dma_start",
                                out=wo[wb_], in_=wout_d[l, :, co * 128:(co + 1) * 128].rearrange("(kc p) n -> p kc n", p=128)),
                                writes=["wo%d" % wb_])
                            for kc in range(KC):
                                S.op("pe", _C("matmul",
                                    banks[1][:, 0:256], lhsT=wo[wb_][:, kc, :], rhs=OT[:, kc, :],
                                    start=(kc == 0), stop=(kc == KC - 1)),
                                    reads=["wo%d" % wb_, "OT"], writes=[BK(1)])
                            S.op("dve", _C("scalar_tensor_tensor",
                                out=XT[:, co, t0p:t0p + 256], in0=banks[1][:, 0:256], scalar=mcol(l, 2, co),
                                in1=XT[:, co, t0p:t0p + 256], op0=ALU.mult, op1=ALU.add),
                                reads=[BK(1), "modsb", "XTh0", "XTh1"], writes=["XTh0", "XTh1"])


                def capture(i):
                    saved = S.ops
                    S.ops = []
                    front(i)
                    got = S.ops
                    S.ops = saved
                    return got

                def capture_back(i):
                    saved = S.ops
                    S.ops = []
                    back(i)
                    got = S.ops
                    S.ops = saved
                    return got

                if a2_blocks > 0:
                    S.ops.extend(capture(0))
                pending = []
                for i in range(a2_blocks):
                    fe_next = capture(i + 1) if i + 1 < a2_blocks else []
                    steps_and_back(i, pending + fe_next)
                    pending = capture_back(i)
                S.ops.extend(pending)

            if do_ffn:
                S.barrier()
                ar = Arena(BIG, NBIG)
                ACTT = ar.take([128, NJ, 1024], BF16)
                H2T = ar.take([128, KC, 1026], BF16)
                wg = [ar.take([128, 8, 256], BF16) for _ in range(2)]
                wv = [ar.take([128, 8, 256], BF16) for _ in range(2)]
                wd = [ar.take([128, NJ, 128], BF16) for _ in range(2)]
                sqf = [ar.take([128, 512], BF16) for _ in range(2)]
                rs2 = ar.take([128, 512], F32)
                tmpf = [ar.take([128, 512], F32) for _ in range(2)]
                ugs2 = [ar.take([128, 1026], F32) for _ in range(2)]
                uvs2 = [ar.take([128, 1026], F32) for _ in range(2)]
                usets = [(0, 1), (2, 3), (5, 6)]
                uctr = [0]
                prev_j = [None]
                ygs = [ar.take([128, 1024], F32) for _ in range(2)]
                yv = ar.take([128, 1024], F32)
                cwn = ar.take([128, 2, 44], F32)
                halo_save = ar.take([128, KC, 1], BF16)
                for tapi, slot in ((0, 0), (2, 1)):
                    c0 = (l * 3 + tapi) * 44
                    S.op("dve", _C("tensor_scalar",
                        out=cwn[:, slot, :], in0=convwT[:, c0:c0 + 44], scalar1=pflag[:, 0:1], scalar2=-1.0, op0=ALU.mult, op1=ALU.mult),
                        reads=["convwT", "pflag"], writes=["cwn"])
                pctr2 = [0]
                def ffn_f1(hh):
                    h0 = hh * 1024
                    if hh == 0:
                        segs = [(1, 513), (513, 1025), (1025, 1026)]
                        zc = 0
                    else:
                        segs = [(0, 1), (1, 513), (513, 1025)]
                        zc = 1025
                    S.op("pool", _C("memset", H2T[:, :, zc:zc + 1], 0.0), writes=["fh"])
                    for (n0, n1) in segs:
                        w = n1 - n0
                        tok0 = h0 - 1 + n0
                        if hh == 1 and n0 == 0:
                            S.op("pool", _C("tensor_copy", out=H2T[:, :, 0:1], in_=halo_save), reads=["halo_save"], writes=["fh"])
                            continue
                        adaln(lambda c, n0=n0, n1=n1: H2T[:, c, n0:n1], tok0, w, G2, l, 3, sqf, rs2, tmpf, 5, "f", xkeys=(["XTh1"] if hh == 1 else ["XTh0", "XTh1"]))
                    if hh == 0:
                        S.op("pool", _C("tensor_copy", out=halo_save, in_=H2T[:, :, 1024:1025]), reads=["fh"], writes=["halo_save"])
                def ffn_f2(hh):
                    h0 = hh * 1024
                    def ffn_post(j):
                        ub = j % 2
                        yg = ygs[j % 2]
                        ygk = "yg%d" % (j % 2)
                        for (us, ukey, yy, ykey, chn) in ((ugs2[ub], "ugs%d" % ub, yg, ygk, j), (uvs2[ub], "uvs%d" % ub, yv, "yv", NJ + j)):
                            cw0 = convwT[:, (l * 3 + 0) * 44 + chn:(l * 3 + 0) * 44 + chn + 1]
                            cw1 = convwT[:, (l * 3 + 1) * 44 + chn:(l * 3 + 1) * 44 + chn + 1]
                            cw2 = convwT[:, (l * 3 + 2) * 44 + chn:(l * 3 + 2) * 44 + chn + 1]
                            cbb = convbT[:, l * 44 + chn:l * 44 + chn + 1]
                            S.op("act", _C("activation", out=yy, in_=us[:, 1:1025], func=AF.Identity, bias=cbb, scale=cw1),
                                reads=[ukey, "convwT", "convbT"], writes=[ykey])
                            S.op("dve", _C("scalar_tensor_tensor",
                                out=yy, in0=us[:, 0:1024], scalar=cw0, in1=yy, op0=ALU.mult, op1=ALU.add),
                                reads=[ukey, "convwT", ykey], writes=[ykey])
                            S.op("dve", _C("scalar_tensor_tensor",
                                out=yy, in0=us[:, 2:1026], scalar=cw2, in1=yy, op0=ALU.mult, op1=ALU.add),
                                reads=[ukey, "convwT", ykey], writes=[ykey])
                            y4 = yy.rearrange("p (s n) -> p s n", s=4)
                            u0 = us[:, 0:1024].rearrange("p (s n) -> p s n", s=4)
                            u2 = us[:, 2:1026].rearrange("p (s n) -> p s n", s=4)
                            S.op("dve", _C("scalar_tensor_tensor",
                                out=y4[:, :, 0:1], in0=u0[:, :, 0:1], scalar=cwn[:, 0, chn:chn + 1], in1=y4[:, :, 0:1], op0=ALU.mult, op1=ALU.add),
                                reads=[ukey, "cwn", ykey], writes=[ykey])
                            S.op("dve", _C("scalar_tensor_tensor",
                                out=y4[:, :, 255:256], in0=u2[:, :, 255:256], scalar=cwn[:, 1, chn:chn + 1], in1=y4[:, :, 255:256], op0=ALU.mult, op1=ALU.add),
                                reads=[ukey, "cwn", ykey], writes=[ykey])
                        S.op("act", _C("activation", out=yg, in_=yg, func=AF.Silu), reads=[ygk], writes=[ygk])
                        S.op("dve", _C("tensor_tensor", out=ACTT[:, j, :], in0=yg, in1=yv, op=ALU.mult),
                             reads=[ygk, "yv"], writes=["ACTT"])

                    for c_ in range(2):
                        S.dma("pool", _C("dma_start",
                            out=wd[c_], in_=wdn_d[l, :, c_ * 128:(c_ + 1) * 128].rearrange("(j q) n -> q j n", q=128)),
                            writes=["wd%d" % c_])
                    for p in range(11):
                        b = pctr2[0] % 2
                        pctr2[0] += 1
                        S.dma("pool", _C("dma_start",
                            out=wg[b], in_=wup_d[l, :, p * 256:(p + 1) * 256].rearrange("(kc q) n -> q kc n", q=128)),
                            writes=["wg%d" % b])
                        S.dma("pool", _C("dma_start",
                            out=wv[b], in_=wup_d[l, :, DFF + p * 256:DFF + (p + 1) * 256].rearrange("(kc q) n -> q kc n", q=128)),
                            writes=["wv%d" % b])
                        for sub in range(2):
                            j = 2 * p + sub
                            ub = j % 2
                            for (wt, wkey, tgt, tkey, hb) in ((wg[b], "wg%d" % b, ugs2[ub], "ugs%d" % ub, 4), (wv[b], "wv%d" % b, uvs2[ub], "uvs%d" % ub, 7)):
                                bset = usets[uctr[0] % 3]
                                hc = 2 * (uctr[0] % 8)
                                uctr[0] += 1
                                for part in range(3):
                                    if part < 2:
                                        ob = banks[bset[part]][:, :]
                                        okey = BK(bset[part])
                                        rsl = (part * 512, part * 512 + 512)
                                    else:
                                        ob = banks[hb][:, hc:hc + 2]
                                        okey = BK(hb)
                                        rsl = (1024, 1026)
                                    for kc in range(KC):
                                        S.op("pe", _C("matmul",
                                            ob, lhsT=wt[:, kc, sub * 128:(sub + 1) * 128], rhs=H2T[:, kc, rsl[0]:rsl[1]],
                                            start=(kc == 0), stop=(kc == KC - 1), skip_group_check=True),
                                            reads=[wkey, "fh"], writes=[okey])
                                S.op("act", _C("activation", out=tgt[:, 0:512], in_=banks[bset[0]][:, :], func=AF.Copy), reads=[BK(bset[0])], writes=[tkey])
                                S.op("act", _C("activation", out=tgt[:, 512:1024], in_=banks[bset[1]][:, :], func=AF.Copy), reads=[BK(bset[1])], writes=[tkey])
                                S.op("act", _C("activation", out=tgt[:, 1024:1026], in_=banks[hb][:, hc:hc + 2], func=AF.Copy), reads=[BK(hb)], writes=[tkey])
                            if prev_j[0] is not None:
                                ffn_post(prev_j[0])
                            prev_j[0] = j
                    ffn_post(prev_j[0])
                    prev_j[0] = None
                def ffn_f3(hh):
                    h0 = hh * 1024
                    for c in range(KC):
                        b = c % 2
                        if c >= 2:
                          S.dma("pool", _C("dma_start",
                            out=wd[b], in_=wdn_d[l, :, c * 128:(c + 1) * 128].rearrange("(j q) n -> q j n", q=128)),
                            writes=["wd%d" % b])
                        for tg in range(2):
                            ob = 6 + tg
                            for j in range(NJ):
                                S.op("pe", _C("matmul",
                                    banks[ob][:, :], lhsT=wd[b][:, j, :], rhs=ACTT[:, j, tg * 512:(tg + 1) * 512],
                                    start=(j == 0), stop=(j == NJ - 1)),
                                    reads=["wd%d" % b, "ACTT"], writes=[BK(ob)])
                            ta = h0 + tg * 512
                            S.op("dve", _C("scalar_tensor_tensor",
                                out=XT[:, c, ta:ta + 512], in0=banks[ob][:, :], scalar=mcol(l, 5, c),
                                in1=XT[:, c, ta:ta + 512], op0=ALU.mult, op1=ALU.add),
                                reads=[BK(ob), "modsb", "XTh%d" % hh], writes=["XTh%d" % hh])


                def cap(fn, hh):
                    saved = S.ops
                    S.ops = []
                    fn(hh)
                    got = S.ops
                    S.ops = saved
                    return got

                ffn_f1(0)
                ffn_f2(0)
                fa = cap(ffn_f1, 1)
                fb = cap(ffn_f3, 0)
                per_ = max(1, len(fb) // max(1, len(fa)))
                ia = 0
                for k_, o_ in enumerate(fb):
                    S.ops.append(o_)
                    if k_ % per_ == per_ - 1 and ia < len(fa):
                        S.ops.append(fa[ia])
                        ia += 1
                S.ops.extend(fa[ia:])
                ffn_f2(1)
                ffn_f3(1)

        S.barrier()
        ar = Arena(BIG, NBIG)
        ys = [ar.take([128, 1024], F32) for _ in range(2)]
        for t in range(NB):
            b = t % 2
            for half in range(2):
                bank = (2 * t + half) % 4
                for q in range(4):
                    c = half * 4 + q
                    S.op("pe", _C("transpose",
                        banks[bank][:, q * 128:(q + 1) * 128], XT[:, c, t * 128:(t + 1) * 128], identf[:]),
                        reads=["XTh0", "XTh1", "XT%d" % t, "identf"], writes=[BK(bank)])
                if half == 0:
                    S.op("act", _C("activation", out=ys[b][:, 0:512], in_=banks[bank][:, :], func=AF.Copy),
                         reads=[BK(bank)], writes=["ys%d_0" % b])
                else:
                    S.op("dve", _C("tensor_copy", out=ys[b][:, 512:1024], in_=banks[bank][:, :]),
                         reads=[BK(bank)], writes=["ys%d_1" % b])
            S.dma("sp", _C("dma_start", out=y_d[t * 128:(t + 1) * 128, :], in_=ys[b]),
                  reads=["ys%d_0" % b, "ys%d_1" % b], final_wait=True)
        S.emit(nc, es)
    return nc, dbg_outs


def _rope_tables(dim):
    n = dim // 4
    inv = (1.0 / (10000.0 ** (np.arange(n, dtype=np.float32) / np.float32(n)))).astype(np.float32)
    t = np.arange(2048)
    row = (t // 64).astype(np.float32)
    col = (t % 64).astype(np.float32)
    ang = np.concatenate([row[:, None] * inv, col[:, None] * inv], axis=-1).astype(np.float32)
    return np.cos(ang).astype(np.float32), np.sin(ang).astype(np.float32)


def _tok_major(a):
    return np.ascontiguousarray(a.reshape(NB, 128, -1).transpose(1, 0, 2))


def _static_tables():
    bf = ml_dtypes.bfloat16
    st = {}
    cb, sb_ = _rope_tables(64)
    cc, sc = _rope_tables(32)
    st["rope_s"] = [_tok_major(cb), _tok_major(sb_), _tok_major(cc), _tok_major(sc)]
    st["rope_p"] = [np.ones((128, NB, 32), np.float32), np.zeros((128, NB, 32), np.float32),
                    np.ones((128, NB, 16), np.float32), np.zeros((128, NB, 16), np.float32)]
    bs = np.zeros((128, NF), np.float32)
    bp = np.zeros((128, NF), np.float32)
    for i in range(NB):
        for dj in range(7):
            j = i + dj - 3
            ok_p = 0 <= j < NB and (j // 2 == i // 2)
            bp[:, NF_NA + i * 7 + dj] = 0.0 if ok_p else NEG
        for dj in range(3):
            j = i + dj - 1
            ok_p = 0 <= j < NB and (j // 2 == i // 2)
            bp[:, NF_SW + i * 3 + dj] = 0.0 if ok_p else NEG
        for j in range(NB):
            bp[:, NF_DA + i * 16 + j] = 0.0 if (j // 2 == i // 2) else NEG
    st["bias_s"] = bs
    st["bias_p"] = bp
    k = np.arange(128)[:, None]
    q = np.arange(128)[None, :]
    sm = np.zeros((128, 2, 128), np.float32)
    sm[:, 0, :] = (k >= q)
    sm[:, 1, :] = (k <= q)
    st["swm_s"] = sm.astype(bf)
    st["swm_p"] = np.ones((128, 2, 128), np.float32).astype(bf)
    nm = np.zeros((128, 16, 64), np.float32)
    for p in range(128):
        ak, ck_ = p // 64, p % 64
        for e2 in range(16):
            dr = e2 - 8 + ak
            if abs(dr) > 7:
                continue
            for cr in range(64):
                c = 63 - cr
                qs = min(max(c - 8, 0), 48)
                if qs <= ck_ < qs + 16:
                    nm[p, e2, cr] = 1.0
    st["nam_s"] = nm.astype(bf)
    st["nam_p"] = np.ones((128, 16, 64), np.float32).astype(bf)
    st["identf"] = np.eye(128, dtype=np.float32)
    st["identb"] = np.eye(128, dtype=np.float32).astype(bf)
    st["permb"] = np.eye(128, dtype=np.float32)[:, ::-1].copy().astype(bf)
    st["onesb"] = np.ones((128, 128), np.float32).astype(bf)
    rm = np.zeros((128, 6), np.float32)
    rm[0:64, 0] = 1.0
    rm[64:128, 1] = 1.0
    for u in range(4):
        rm[32 * u:32 * u + 32, 2 + u] = 1.0
    st["rowmask"] = rm
    return st


_SW_Q_ORDER = [0, 4, 1, 5, 2, 6, 3, 7]


def make_in_maps(inp):
    st = _static_tables()
    f = lambda a: np.ascontiguousarray(np.asarray(a, dtype=np.float32))
    w_in = f(inp["w_in"])
    o = 0
    seg = {}
    for name, wdt in (("naq", 256), ("nak", 256), ("nav", 256), ("swq", 512), ("swk", 128), ("swv", 128), ("daq", 256), ("dak", 256), ("dav", 256)):
        seg[name] = (o, o + wdt)
        o += wdt

    def cols(name):
        a, b = seg[name]
        return w_in[:, :, a:b]

    swq = cols("swq").reshape(L, D, 8, 64)[:, :, _SW_Q_ORDER, :].reshape(L, D, 512)
    w_kv = np.ascontiguousarray(np.concatenate([cols("nak"), cols("swk"), cols("dak"), cols("nav"), cols("swv"), cols("dav")], axis=-1))
    w_q = np.ascontiguousarray(np.concatenate([cols("naq"), swq, cols("daq")], axis=-1))

    def colT(v, n):
        return np.ascontiguousarray(v.reshape(L, n, 128).transpose(2, 0, 1).reshape(128, L * n))

    bmodT = colT(f(inp["b_mod"]), 48)
    gattnT = colT(f(inp["g_attn"]), 8)
    gffnT = colT(f(inp["g_ffn"]), 8)
    convwT = np.ascontiguousarray(f(inp["conv_w"]).reshape(L, 3, 44, 128).transpose(3, 0, 1, 2).reshape(128, L * 3 * 44))
    convbT = colT(f(inp["conv_b"]), 44)
    naq = f(inp["na_qk_g"])
    swg = f(inp["sw_qk_g"])
    dag = f(inp["da_qk_g"])
    small = np.concatenate([naq[:, 0], naq[:, 1], swg[:, 0], swg[:, 1], dag[:, 0], dag[:, 1],
                            f(inp["sw_sink"]), f(inp["da_lambda"]).reshape(L, 128), f(inp["da_subln_g"])], axis=-1)
    small = np.ascontiguousarray(small)
    assert small.shape == (L, 520)
    rpb = f(inp["na_rpb"]).reshape(L, 4 * 465)
    rpbpad_s = np.zeros((L, RPBLEN), np.float32)
    rpbpad_s[:, PADOFF:PADOFF + 1860] = rpb
    rpbpad_p = np.zeros((L, RPBLEN), np.float32)

    shared = dict(w_mod=f(inp["w_mod"]), w_kv=w_kv, w_q=w_q, w_out=f(inp["w_out"]), w_up=f(inp["w_up"]), w_down=f(inp["w_down"]),
                  bmodT=bmodT, gattnT=gattnT, gffnT=gffnT, convwT=convwT, convbT=convbT, small=small,
                  identf=st["identf"], identb=st["identb"], permb=st["permb"], onesb=st["onesb"], rowmask=st["rowmask"])
    xs_ = f(inp["x_sample"])
    xp = f(inp["x_prompt"])
    cc = f(inp["c"])
    cctx = f(inp["c_ctx"])
    ck_all = np.concatenate([f(inp["cache_na_k"]).reshape(4, L, 512, 256), f(inp["cache_sw_k"]).reshape(4, L, 512, 128),
                             f(inp["cache_da_k"]).reshape(4, L, 512, 256)], axis=-1)
    cv_all = np.concatenate([f(inp["cache_na_v"]).reshape(4, L, 512, 256), f(inp["cache_sw_v"]).reshape(4, L, 512, 128),
                             f(inp["cache_da_v"]).reshape(4, L, 512, 256)], axis=-1)
    zc = np.zeros((L, 512, 640), np.float32)
    maps = []
    for core in range(8):
        m = dict(shared)
        if core < 4:
            m["x"] = np.ascontiguousarray(xs_[core])
            cv_ = cc[core]
            r = st["rope_s"]
            m["biascol"] = st["bias_s"]
            m["swmask"] = st["swm_s"]
            m["namask"] = st["nam_s"]
            m["rpbpad"] = rpbpad_s
            m["ctxone"] = np.ones((128, 1), np.float32)
            m["pflag"] = np.zeros((128, 1), np.float32)
            m["ck"] = np.ascontiguousarray(ck_all[core])
            m["cv"] = np.ascontiguousarray(cv_all[core])
        else:
            g = core - 4
            m["x"] = np.ascontiguousarray(xp[8 * g:8 * g + 8].reshape(2048, D))
            cv_ = cctx
            r = st["rope_p"]
            m["biascol"] = st["bias_p"]
            m["swmask"] = st["swm_p"]
            m["namask"] = st["nam_p"]
            m["rpbpad"] = rpbpad_p
            m["ctxone"] = np.zeros((128, 1), np.float32)
            m["pflag"] = np.ones((128, 1), np.float32)
            m["ck"] = zc
            m["cv"] = zc
        m["cvecT"] = np.ascontiguousarray(cv_.reshape(8, 128).T)
        m["cosb"], m["sinb"], m["cosc"], m["sinc"] = r
        maps.append(m)
    return maps


_PROG = {}


def _get_prog(key=("full",), **kw):
    if key not in _PROG:
        _PROG[key] = build_program(**kw)
    return _PROG[key]


def assemble(results):
    y_s = np.stack([results[c]["y"] for c in range(4)], axis=0)
    y_p = np.concatenate([results[c]["y"].reshape(8, 256, D) for c in range(4, 8)], axis=0)
    okv = np.concatenate([results[c]["okv"].reshape(L, 8, 256, 1280).transpose(1, 0, 2, 3) for c in range(4, 8)], axis=0)
    nk = np.ascontiguousarray(okv[..., 0:256]).reshape(32, L, 256, 4, 64)
    sk = np.ascontiguousarray(okv[..., 256:384]).reshape(32, L, 256, 2, 64)
    dk = np.ascontiguousarray(okv[..., 384:640]).reshape(32, L, 256, 4, 2, 32)
    nv = np.ascontiguousarray(okv[..., 640:896]).reshape(32, L, 256, 4, 64)
    sv = np.ascontiguousarray(okv[..., 896:1024]).reshape(32, L, 256, 2, 64)
    dv = np.ascontiguousarray(okv[..., 1024:1280]).reshape(32, L, 256, 4, 64)
    return (np.ascontiguousarray(y_p), np.ascontiguousarray(y_s), nk, nv, sk, sv, dk, dv)


def kernel(**inputs):
    nc, _ = _get_prog()
    maps = make_in_maps(inputs)
    res = run_bass_kernel_spmd(nc, maps, core_ids=list(range(8)))
    return assemble(res.results)
```

```python
import math
from contextlib import ExitStack

import numpy as np
import ml_dtypes

import concourse.bass as bass
import concourse.mybir as mybir
from concourse.bass_utils import run_bass_kernel_spmd

F32 = mybir.dt.float32
BF16 = mybir.dt.bfloat16
AF = mybir.ActivationFunctionType
ALU = mybir.AluOpType
AX = mybir.AxisListType

L = 4
NB = 16
D = 1024
KC = 8
DFF = 2816
NJ = 22
EPS = 1e-6
NEG = -30000.0
PADOFF = 128
RPBLEN = 2176
ENGS = ("pe", "act", "dve", "pool", "sp")


class _Op:
    __slots__ = ("eng", "fn", "reads", "writes", "is_dma", "deps", "needs_inc",
                 "sem", "val", "idx", "final_wait", "barrier")

    def __init__(self, eng, fn, reads, writes, is_dma, final_wait):
        self.eng = eng
        self.fn = fn
        self.reads = reads
        self.writes = writes
        self.is_dma = is_dma
        self.deps = []
        self.needs_inc = False
        self.sem = None
        self.val = 0
        self.final_wait = final_wait
        self.barrier = False


class Sched:
    def __init__(self, same_engine_sync=True, n_dma_sems=32):
        self.ops = []
        self.same_engine_sync = same_engine_sync
        self.n_dma_sems = n_dma_sems

    def op(self, eng, fn, reads=(), writes=()):
        o = _Op(eng, fn, tuple(reads), tuple(writes), False, False)
        self.ops.append(o)
        return o

    def dma(self, eng, fn, reads=(), writes=(), final_wait=False):
        o = _Op(eng, fn, tuple(reads), tuple(writes), True, final_wait)
        self.ops.append(o)
        return o

    def barrier(self):
        for e in ENGS:
            o = _Op(e, None, (), (), False, False)
            o.barrier = True
            self.ops.append(o)

    def analyze(self):
        last_w = {}
        readers = {}
        waited = {e: {s: -1 for s in ENGS} for e in ENGS}
        waited_dma = {e: set() for e in ENGS}
        dma_slot_last = [None] * self.n_dma_sems
        dma_ctr = {"sp": 0, "pool": 0, "act": 0}
        half = self.n_dma_sems // 2
        last_compute = {e: None for e in ENGS}
        all_dma = []
        for idx, o in enumerate(self.ops):
            o.idx = idx
            deps = set()
            if o.barrier:
                for e in ENGS:
                    if last_compute[e] is not None and e != o.eng:
                        deps.add(last_compute[e])
                    if e == o.eng and last_compute[e] is not None and e != "pe":
                        deps.add(last_compute[e])
                for d in all_dma:
                    if d not in waited_dma[o.eng]:
                        deps.add(d)
            raw = set()
            for r in o.reads:
                w = last_w.get(r)
                if w is not None:
                    deps.add(w)
                    raw.add(w)
            for wkey in o.writes:
                w = last_w.get(wkey)
                if w is not None:
                    deps.add(w)
                for rd in readers.get(wkey, ()):
                    deps.add(rd)
            if o.is_dma:
                if o.eng == "pool":
                    slot = half + dma_ctr["pool"] % (self.n_dma_sems - half)
                else:
                    slot = dma_ctr["sp"] % half
                dma_ctr["pool" if o.eng == "pool" else "sp"] += 1
                prev = dma_slot_last[slot]
                if prev is not None:
                    deps.add(prev)
                dma_slot_last[slot] = idx
                o.sem = ("dma", slot)
                all_dma.append(idx)
            deps.discard(idx)
            best = {}
            out = []
            for d in sorted(deps):
                p = self.ops[d]
                if p.is_dma:
                    if d in waited_dma[o.eng]:
                        continue
                    waited_dma[o.eng].add(d)
                    out.append(d)
                else:
                    if p.eng == o.eng and not o.is_dma and not o.barrier and \
                            (p.eng == "pe" or not self.same_engine_sync):
                        continue
                    if waited[o.eng][p.eng] >= d:
                        continue
                    best[p.eng] = max(best.get(p.eng, -1), d)
            for e, d in best.items():
                waited[o.eng][e] = d
                out.append(d)
            o.deps = out
            for d in out:
                self.ops[d].needs_inc = True
            if not o.barrier:
                for r in o.reads:
                    readers.setdefault(r, []).append(idx)
                for wkey in o.writes:
                    last_w[wkey] = idx
                    readers[wkey] = []
                if not o.is_dma:
                    last_compute[o.eng] = idx
        self.final = [o.idx for o in self.ops if o.final_wait]
        cnt = {e: 0 for e in ENGS}
        dma_cnt = [0] * self.n_dma_sems
        for o in self.ops:
            if o.barrier:
                continue
            if o.is_dma:
                slot = o.sem[1]
                dma_cnt[slot] += 16
                o.val = dma_cnt[slot]
                o.needs_inc = True
            elif o.needs_inc:
                cnt[o.eng] += 1
                o.val = cnt[o.eng]
                o.sem = ("eng", o.eng)

    def emit(self, nc, es):
        self.analyze()
        sems = {}
        for e in ENGS:
            sems[("eng", e)] = es.enter_context(nc.semaphore("s_" + e))
        for i in range(self.n_dma_sems):
            sems[("dma", i)] = es.enter_context(nc.semaphore("s_dma%d" % i))
        per = {e: [] for e in ENGS}
        for o in self.ops:
            per[o.eng].append(o)
        ops = self.ops
        final = self.final
        block = es.enter_context(nc.Block())

        def run(engine_obj, lst, ename):
            for o in lst:
                for d in o.deps:
                    p = ops[d]
                    engine_obj.wait_ge(sems[p.sem], p.val)
                if o.barrier:
                    continue
                ins = o.fn(engine_obj)
                if o.needs_inc:
                    ins.then_inc(sems[o.sem], 16 if o.is_dma else 1)
            for d in final:
                p = ops[d]
                if p.eng == ename:
                    engine_obj.wait_ge(sems[p.sem], p.val)

        @block.tensor
        def _(e):
            run(e, per["pe"], "pe")

        @block.scalar
        def _(e):
            run(e, per["act"], "act")

        @block.vector
        def _(e):
            run(e, per["dve"], "dve")

        @block.gpsimd
        def _(e):
            run(e, per["pool"], "pool")

        @block.sync
        def _(e):
            run(e, per["sp"], "sp")


def _C(name, *a, **k):
    def f(e):
        return getattr(e, name)(*a, **k)
    return f


_ARENA_HI = 0


class Arena:
    def __init__(self, big, nel, base=0):
        self.big = big
        self.nel = nel
        self.off = base
        self.hi = base

    def take(self, shape, dtype):
        global _ARENA_HI
        n = 1
        for s in shape[1:]:
            n *= s
        nb = n * (4 if dtype == F32 else 2)
        nb = (nb + 63) // 64 * 64
        el = nb // 2
        a = self.off
        self.off += el
        self.hi = max(self.hi, self.off)
        assert self.off <= self.nel, ("arena overflow", self.off, self.nel)
        _ARENA_HI = max(_ARENA_HI, self.off)
        v = self.big[:, a:a + el]
        if dtype == F32:
            v = v.bitcast(F32)[:, 0:n]
        else:
            v = v[:, 0:n]
        if len(shape) == 3:
            v = v.rearrange("p (a b) -> p a b", a=shape[1])
        elif len(shape) == 4:
            v = v.rearrange("p (a b c) -> p a b c", a=shape[1], b=shape[2])
        return v


def _na_r0(r):
    return min(max(r - 4, 0), 24)


def na_blocks(i):
    res = []
    for j in range(NB):
        inval = []
        anyv = False
        for a in range(2):
            r = 2 * i + a
            for ak in range(2):
                rk = 2 * j + ak
                ok = _na_r0(r) <= rk <= _na_r0(r) + 7
                if ok:
                    anyv = True
                else:
                    inval.append((ak, a))
        if anyv:
            res.append((j, inval))
    return res


NF_NA = 0
NF_SW = 112
NF_DA = 160
NF = 160 + 256

ACC = {}
AW = 66
for _h in range(4):
    ACC[("sw", _h)] = (0, _h * AW)
for _h in range(3):
    ACC[("na", _h)] = (0, 4 * AW + _h * AW)
for _h in range(4):
    ACC[("sw", 4 + _h)] = (1, _h * AW)
ACC[("na", 3)] = (1, 4 * AW)
ACC[("da", 0)] = (1, 5 * AW)
ACC[("da", 1)] = (1, 6 * AW)
for _u in range(6):
    ACC[("da", 2 + _u)] = (2, _u * AW)


def build_program(n_layers=L, do_attn=True, do_ffn=True, taps=(), a1_blocks=NB, a2_blocks=NB, do_mod=True):
    nc = bass.Bass("TRN2", target_bir_lowering=False)
    S = Sched()
    taps = set(taps)
    dbg_outs = {}

    def din(name, shape, dt=F32):
        return nc.dram_tensor(name, list(shape), dt, kind="ExternalInput").ap()

    x_d = din("x", [2048, D])
    cvec_d = din("cvecT", [128, 8])
    cosb_d = din("cosb", [128, NB, 32])
    sinb_d = din("sinb", [128, NB, 32])
    cosc_d = din("cosc", [128, NB, 16])
    sinc_d = din("sinc", [128, NB, 16])
    bias_d = din("biascol", [128, NF])
    swm_d = din("swmask", [128, 2, 128], BF16)
    nam_d = din("namask", [128, 16, 64], BF16)
    rpb_d = din("rpbpad", [L, RPBLEN])
    ctxone_d = din("ctxone", [128, 1])
    pflag_d = din("pflag", [128, 1])
    ck_d = din("ck", [L, 512, 640])
    cv_d = din("cv", [L, 512, 640])
    wmod_d = din("w_mod", [L, D, 6 * D])
    wkv_d = din("w_kv", [L, D, 1280])
    wq_d = din("w_q", [L, D, 1024])
    wout_d = din("w_out", [L, D, D])
    wup_d = din("w_up", [L, D, 2 * DFF])
    wdn_d = din("w_down", [L, DFF, D])
    bmod_d = din("bmodT", [128, L * 48])
    gattn_d = din("gattnT", [128, L * 8])
    gffn_d = din("gffnT", [128, L * 8])
    convw_d = din("convwT", [128, L * 3 * 44])
    convb_d = din("convbT", [128, L * 44])
    small_d = din("small", [L, 520])
    identf_d = din("identf", [128, 128])
    identb_d = din("identb", [128, 128], BF16)
    permb_d = din("permb", [128, 128], BF16)
    onesb_d = din("onesb", [128, 128], BF16)
    rowmask_d = din("rowmask", [128, 6])

    y_d = nc.dram_tensor("y", [2048, D], F32, kind="ExternalOutput").ap()
    okv_d = nc.dram_tensor("okv", [L, 2048, 1280], F32, kind="ExternalOutput").ap()

    with ExitStack() as es:
        def sb(name, shape, dt=F32):
            return es.enter_context(nc.sbuf_tensor(name, list(shape), dt))

        XT = sb("XT", [128, KC, 2048])
        identf = sb("identf_s", [128, 128])
        identb = sb("identb_s", [128, 128], BF16)
        permb = sb("permb_s", [128, 128], BF16)
        onesb = sb("onesb_s", [128, 128], BF16)
        rowmask = sb("rowmask_s", [128, 6])
        cosb = sb("cosb_s", [128, NB, 32])
        sinb = sb("sinb_s", [128, NB, 32])
        cosc = sb("cosc_s", [128, NB, 16])
        sinc = sb("sinc_s", [128, NB, 16])
        biascol = sb("biascol_s", [128, NF])
        swmask = sb("swmask_s", [128, 2, 128], BF16)
        namask = sb("namask_s", [128, 16, 64], BF16)
        ctxone = sb("ctxone_s", [128, 1])
        pflag = sb("pflag_s", [128, 1])
        cvecT = sb("cvecT_s", [128, 8])
        silub = sb("silub", [128, 8], BF16)
        bmodT = sb("bmodT_s", [128, L * 48])
        modsb = sb("modsb", [128, L * 48])
        gattnT = sb("gattnT_s", [128, L * 8])
        gffnT = sb("gffnT_s", [128, L * 8])
        G1 = sb("G1", [128, L * 8])
        G2 = sb("G2", [128, L * 8])
        convwT = sb("convwT_s", [128, L * 3 * 44])
        convbT = sb("convbT_s", [128, L * 44])
        NBIG = 64000
        BIG = sb("BIG", [128, NBIG], BF16)
        banks = [es.enter_context(nc.psum_tensor("bank%d" % i, [128, 512], F32)) for i in range(8)]

        def BK(i):
            return "B%d" % i

        def tap(name, ap, shape, reads, dt=F32):
            if name not in taps:
                return
            d = nc.dram_tensor("dbg_" + name, list(shape), dt, kind="ExternalOutput").ap()
            dbg_outs[name] = d
            S.dma("sp", _C("dma_start", out=d, in_=ap), reads=reads, final_wait=True)

        def ld(dst, src, key):
            S.dma("sp", _C("dma_start", out=dst, in_=src), writes=[key])

        ld(identf[:], identf_d, "identf")
        ld(identb[:], identb_d, "identb")
        ld(permb[:], permb_d, "permb")
        ld(onesb[:], onesb_d, "onesb")
        ld(rowmask[:], rowmask_d, "rowmask")
        ld(cosb[:], cosb_d, "rope")
        ld(sinb[:], sinb_d, "rope")
        ld(cosc[:], cosc_d, "rope")
        ld(sinc[:], sinc_d, "rope")
        ld(biascol[:], bias_d, "biascol")
        ld(swmask[:], swm_d, "swmask")
        ld(namask[:], nam_d, "namask")
        ld(ctxone[:], ctxone_d, "ctxone")
        ld(pflag[:], pflag_d, "pflag")
        ld(cvecT[:], cvec_d, "cvecT")
        ld(bmodT[:], bmod_d, "bmodT")
        ld(gattnT[:], gattn_d, "gattnT")
        ld(gffnT[:], gffn_d, "gffnT")
        ld(convwT[:], convw_d, "convwT")
        ld(convbT[:], convb_d, "convbT")

        ar = Arena(BIG, NBIG)
        xs = [ar.take([128, 1024], F32) for _ in range(2)]
        wm = [ar.take([128, 8, 512], BF16) for _ in range(2)]
        for t in range(NB):
            b = t % 2
            S.dma("sp", _C("dma_start", out=xs[b], in_=x_d[t * 128:(t + 1) * 128, :]),
                  writes=["xs%d" % b])
            for half in range(2):
                bank = (2 * t + half) % 4
                for q in range(4):
                    c = half * 4 + q
                    S.op("pe", _C("transpose",
                        banks[bank][:, q * 128:(q + 1) * 128], xs[b][:, c * 128:(c + 1) * 128], identf[:]),
                        reads=["xs%d" % b, "identf"], writes=[BK(bank)])
                if half == 0:
                    S.op("act", _C("activation",
                        out=XT[:, 0:4, t * 128:(t + 1) * 128],
                        in_=banks[bank][:].rearrange("p (c n) -> p c n", c=4), func=AF.Copy),
                        reads=[BK(bank)], writes=["XT%d" % t])
                else:
                    S.op("dve", _C("tensor_copy",
                        out=XT[:, 4:8, t * 128:(t + 1) * 128],
                        in_=banks[bank][:].rearrange("p (c n) -> p c n", c=4)),
                        reads=[BK(bank)], writes=["XT%d" % t])

        S.op("act", _C("activation", out=silub[:], in_=cvecT[:], func=AF.Silu),
             reads=["cvecT"], writes=["silub"])
        MB = 4
        first_mod = True
        for l in range(n_layers if do_mod else 0):
            for piece in range(12):
                b = (l * 12 + piece) % 2
                S.dma("pool", _C("dma_start",
                    out=wm[b], in_=wmod_d[l, :, piece * 512:(piece + 1) * 512].rearrange("(kc p) n -> p kc n", p=128)),
                    writes=["wm%d" % b])
                for oc in range(4):
                    col = l * 48 + piece * 4 + oc
                    for kc in range(KC):
                        S.op("pe", _C("matmul",
                            banks[MB][:, col:col + 1], lhsT=wm[b][:, kc, oc * 128:(oc + 1) * 128],
                            rhs=silub[:, kc:kc + 1], start=first_mod, stop=(kc == KC - 1), skip_group_check=True),
                            reads=["wm%d" % b, "silub"], writes=[BK(MB)])
                        first_mod = False
        nm = n_layers * 48
        S.op("dve", _C("tensor_tensor", out=modsb[:, 0:nm], in0=banks[MB][:, 0:nm], in1=bmodT[:, 0:nm], op=ALU.add),
             reads=[BK(MB), "bmodT"], writes=["modsb"])
        for l in range(n_layers):
            S.op("dve", _C("scalar_tensor_tensor",
                out=G1[:, l * 8:(l + 1) * 8], in0=modsb[:, l * 48 + 8:l * 48 + 16], scalar=1.0,
                in1=gattnT[:, l * 8:(l + 1) * 8], op0=ALU.add, op1=ALU.mult),
                reads=["modsb", "gattnT"], writes=["G"])
            S.op("dve", _C("scalar_tensor_tensor",
                out=G2[:, l * 8:(l + 1) * 8], in0=modsb[:, l * 48 + 32:l * 48 + 40], scalar=1.0,
                in1=gffnT[:, l * 8:(l + 1) * 8], op0=ALU.add, op1=ALU.mult),
                reads=["modsb", "gffnT"], writes=["G"])
        tap("modsb", modsb[:], [128, L * 48], ["modsb"])
        tap("G1", G1[:], [128, L * 8], ["G"])

        def mcol(l, k, c):
            i0 = l * 48 + k * 8 + c
            return modsb[:, i0:i0 + 1]

        def adaln(dst_fn, tok0, w, Gt, l, kshift, sqbuf, rsbuf, tmps, sbank, tagp, xkeys=("XTh0", "XTh1")):
            xkeys = list(xkeys)
            for c in range(KC):
                S.op("act", _C("activation", out=sqbuf[c % 2][:, 0:w], in_=XT[:, c, tok0:tok0 + w], func=AF.Square),
                     reads=xkeys, writes=[tagp + "sq%d" % (c % 2)])
                S.op("pe", _C("matmul", banks[sbank][:, 0:w], lhsT=onesb[:], rhs=sqbuf[c % 2][:, 0:w],
                                                   start=(c == 0), stop=(c == KC - 1)),
                     reads=[tagp + "sq%d" % (c % 2), "onesb"], writes=[BK(sbank)])
            S.op("act", _C("activation", out=rsbuf[:, 0:w], in_=banks[sbank][:, 0:w], func=AF.Ln, scale=1.0 / D, bias=EPS),
                 reads=[BK(sbank)], writes=[tagp + "rs"])
            S.op("act", _C("activation", out=rsbuf[:, 0:w], in_=rsbuf[:, 0:w], func=AF.Exp, scale=-0.5),
                 reads=[tagp + "rs"], writes=[tagp + "rs"])
            for c in range(KC):
                tb = tmps[c % 2]
                S.op("dve", _C("scalar_tensor_tensor",
                    out=tb[:, 0:w], in0=XT[:, c, tok0:tok0 + w], scalar=Gt[:, l * 8 + c:l * 8 + c + 1],
                    in1=rsbuf[:, 0:w], op0=ALU.mult, op1=ALU.mult),
                    reads=xkeys + ["G", tagp + "rs"], writes=[tagp + "tmp%d" % (c % 2)])
                S.op("act", _C("activation",
                    out=dst_fn(c), in_=tb[:, 0:w], func=AF.Identity, bias=mcol(l, kshift, c), scale=1.0),
                    reads=[tagp + "tmp%d" % (c % 2), "modsb"], writes=[tagp + "h"])

        def adaln_blk(hdst, hkey, tok0, Gt, l, kshift, sq8, rsbuf, tmp8, sbank, tagp, tmpkey=None, offload=False, affine_dve=False, scol=0):
            w = 128
            if offload:
                S.op("dve", _C("tensor_tensor", out=sq8, in0=XT[:, :, tok0:tok0 + w], in1=XT[:, :, tok0:tok0 + w], op=ALU.mult),
                     reads=["XTh0", "XTh1"], writes=[tagp + "sq8"])
            else:
                S.op("act", _C("activation", out=sq8, in_=XT[:, :, tok0:tok0 + w], func=AF.Square),
                     reads=["XTh0", "XTh1"], writes=[tagp + "sq8"])
            for c in range(KC):
                S.op("pe", _C("matmul", banks[sbank][:, scol:scol + w], lhsT=onesb[:], rhs=sq8[:, c, :],
                              start=(c == 0), stop=(c == KC - 1)),
                     reads=[tagp + "sq8", "onesb"], writes=[BK(sbank)])
            S.op("act", _C("activation", out=rsbuf[:, 0:w], in_=banks[sbank][:, scol:scol + w], func=AF.Ln, scale=1.0 / D, bias=EPS),
                 reads=[BK(sbank)], writes=[tagp + "rs"])
            S.op("act", _C("activation", out=rsbuf[:, 0:w], in_=rsbuf[:, 0:w], func=AF.Exp, scale=-0.5),
                 reads=[tagp + "rs"], writes=[tagp + "rs"])
            S.op("dve", _C("tensor_tensor", out=tmp8, in0=XT[:, :, tok0:tok0 + w],
                           in1=rsbuf[:, 0:w].unsqueeze(1).broadcast_to([128, KC, w]), op=ALU.mult),
                 reads=["XTh0", "XTh1", tagp + "rs"], writes=[tmpkey or (tagp + "tmp8")])
            for c in range(KC):
                if offload or affine_dve:
                    S.op("dve", _C("tensor_scalar", out=hdst[:, c, :], in0=tmp8[:, c, :], scalar1=Gt[:, l * 8 + c:l * 8 + c + 1],
                                   scalar2=mcol(l, kshift, c), op0=ALU.mult, op1=ALU.add),
                         reads=[tmpkey or (tagp + "tmp8"), "modsb", "G"], writes=[hkey])
                else:
                    S.op("act", _C("activation", out=hdst[:, c, :], in_=tmp8[:, c, :], func=AF.Identity,
                                   bias=mcol(l, kshift, c), scale=Gt[:, l * 8 + c:l * 8 + c + 1]),
                         reads=[tmpkey or (tagp + "tmp8"), "modsb", "G"], writes=[hkey])

        for l in range(n_layers):
            lam_init = 0.8 - 0.6 * math.exp(-0.3 * l)
            S.barrier()
            ar = Arena(BIG, NBIG)
            KT = ar.take([128, 5, 2048], BF16)
            V = ar.take([128, NB, 10, 66], BF16)
            CTXKT = ar.take([128, 5, 512], BF16)
            CTXV = ar.take([128, 4, 10, 66], BF16)
            TAB = ar.take([128, 4, 16, 64], BF16)
            WA = ar.take([128, 8, 1280], BF16)
            sq8 = ar.take([128, KC, 128], BF16)
            hT = ar.take([128, KC, 128], BF16)
            rsb = ar.take([128, 128], F32)
            SM = ar.take([128, 520], F32)
            esink = ar.take([128, 8], F32)
            lamt = ar.take([128, 8], F32)
            SG = ar.take([128, 64], F32)
            smalls = ar.take([128, 64], F32)
            base_shared = ar.off
            kcats = [ar.take([128, 640], F32) for _ in range(2)]
            vcats = [ar.take([128, 640], F32) for _ in range(2)]
            sqks = [ar.take([128, 640], F32) for _ in range(2)]
            rts = [[ar.take([128, 64], F32) for _ in range(4)] for _ in range(2)]
            rt2s = [[ar.take([128, 128], F32) for _ in range(4)] for _ in range(2)]
            kbs = [ar.take([128, 640], BF16) for _ in range(2)]
            smks = [ar.take([128, 64], F32) for _ in range(2)]
            a0_base = ar.off
            CKs = ar.take([128, 4, 640], BF16)
            TABF = ar.take([128, 16, 64], F32)
            a0_hi = ar.off
            ar.off = a0_base
            tmp8as = [ar.take([128, KC, 128], F32) for _ in range(2)]
            sq8s = [sq8, ar.take([128, KC, 128], BF16)]
            rsbs = [rsb, ar.take([128, 128], F32)]
            hTs = [hT, ar.take([128, KC, 128], BF16)]
            ar.off = max(ar.off, a0_hi)
            hiA1 = ar.off
            ar.off = base_shared
            qf = ar.take([128, 1024], F32)
            sqq = ar.take([128, 1024], F32)
            rtq = [sqq[:, k_ * 256:(k_ + 1) * 256] for k_ in range(4)]
            qb = ar.take([128, 1024], BF16)
            QTs = [ar.take([128, 20, 128], BF16) for _ in range(2)]
            PT = [ar.take([128, 512], BF16) for _ in range(4)]
            Ofin = sqq[:, 0:512].rearrange("p (a d) -> p a d", a=8)
            ddt = sqq[:, 512:768].rearrange("p (a d) -> p a d", a=4)
            sqd = sqq[:, 768:1024].rearrange("p (a d) -> p a d", a=4)
            Omix = ar.take([128, 1024], BF16)
            rtq2 = [Omix[:, k_ * 256:(k_ + 1) * 256].bitcast(F32) for k_ in range(4)]
            OT = ar.take([128, 8, 256], BF16)
            AWW = 7 * AW
            Oacc = ar.take([128, 3, AWW], F32)
            wo = [WA[:, :, 1024 + 128 * k_:1024 + 128 * (k_ + 1)] for k_ in range(2)]
            wqv = WA[:, :, 0:1024]
            tmp8q = sqq.rearrange("p (c n) -> p c n", c=KC)

            if do_attn:
                S.dma("sp", _C("dma_start", out=SM, in_=small_d[l, :].partition_broadcast(128)), writes=["SM"])
                S.op("act", _C("activation", out=esink, in_=SM[:, 320:328], func=AF.Exp), reads=["SM"], writes=["esink"])
                lp = SM[:, 328:456].rearrange("p (a b d) -> p a b d", a=2, b=2)
                S.op("dve", _C("tensor_tensor", out=smalls[:, 0:64].rearrange("p (a d) -> p a d", a=2),
                                                      in0=lp[:, :, 0, :], in1=lp[:, :, 1, :], op=ALU.mult),
                     reads=["SM"], writes=["smalls"])
                S.op("dve", _C("tensor_reduce", out=lamt[:, 0:2], in_=smalls[:, 0:64].rearrange("p (a d) -> p a d", a=2),
                                                      axis=AX.X, op=ALU.add),
                     reads=["smalls"], writes=["lamt"])
                S.op("act", _C("activation", out=lamt[:, 2:4], in_=lamt[:, 0:2], func=AF.Exp), reads=["lamt"], writes=["lamt2"])
                S.op("dve", _C("tensor_tensor", out=lamt[:, 4:5], in0=lamt[:, 3:4], in1=lamt[:, 2:3], op=ALU.subtract),
                     reads=["lamt2"], writes=["lamt3"])
                S.op("dve", _C("tensor_scalar", out=lamt[:, 5:6], in0=lamt[:, 4:5], scalar1=-lam_init, scalar2=None, op0=ALU.add),
                     reads=["lamt3"], writes=["neglam"])
                S.op("dve", _C("tensor_scalar", out=SG, in0=SM[:, 456:520], scalar1=1.0 - lam_init, scalar2=None, op0=ALU.mult),
                     reads=["SM"], writes=["SG"])
                neglam = lamt[:, 5:6]
                S.dma("pool", _C("dma_start", out=CKs, in_=ck_d[l].rearrange("(b p) f -> p b f", p=128)), writes=["CKs"])
                for b4 in range(4):
                    bank = 6 + (b4 % 2)
                    pv = banks[bank][:].bitcast(BF16)
                    for ch in range(5):
                        S.op("pe", _C("transpose", pv[:, ch * 128:(ch + 1) * 128], CKs[:, b4, ch * 128:(ch + 1) * 128], identb[:]),
                             reads=["CKs", "identb"], writes=[BK(bank)])
                    S.op("act", _C("activation", out=CTXKT[:, :, b4 * 128:(b4 + 1) * 128],
                                                                   in_=pv[:, 0:640].rearrange("p (c n) -> p c n", c=5), func=AF.Copy),
                         reads=[BK(bank)], writes=["CTXKT"])
                for b4 in range(4):
                    S.dma("pool", _C("dma_start",
                        out=CTXV[:, b4, :, 0:64], in_=cv_d[l, b4 * 128:(b4 + 1) * 128, :].rearrange("p (h d) -> p h d", h=10)),
                        writes=["CTXVd"])
                S.op("pool", _C("tensor_copy", out=CTXV[:, :, :, 64], in_=ctxone[:, 0:1].unsqueeze(2).broadcast_to([128, 4, 10])),
                     reads=["ctxone"], writes=["CTXVo"])
                S.op("pool", _C("memset", V[:, :, :, 64], 1.0), writes=["Vones"])
                for h in range(4):
                    for ak in range(2):
                        off = PADOFF + h * 465 + (ak - 1) * 31 - 48
                        src = bass.AP(rpb_d.tensor, l * RPBLEN + off, [[1, 64], [31, 16], [1, 64]])
                        S.dma("sp", _C("dma_start", out=TABF[ak * 64:(ak + 1) * 64, :, :], in_=src),
                              writes=["TABF"])
                    S.op("act", _C("activation", out=TAB[:, h, :, :], in_=TABF, func=AF.Exp), reads=["TABF"], writes=["TAB"])
                    S.op("dve", _C("tensor_tensor", out=TAB[:, h, :, :], in0=TAB[:, h, :, :], in1=namask[:], op=ALU.mult),
                         reads=["TAB", "namask"], writes=["TAB"])
                if l == 0:
                    tap("TAB", TAB, [128, 4, 16, 64], ["TAB"], BF16)
                    tap("CTXKT", CTXKT, [128, 5, 512], ["CTXKT"], BF16)
                GN = SM
                S.dma("pool", _C("dma_start", out=WA, in_=wkv_d[l].rearrange("(kc p) n -> p kc n", p=128)), writes=["WA"])
                S.barrier()
                def rope2(eng, groups, tkey):
                    seqs = []
                    for gi, (view, H, half, cs, sn, key, tl, xr, xw) in enumerate(groups):
                        x1 = view[:, :, :, 0]
                        x2 = view[:, :, :, 1]
                        cb_ = cs.unsqueeze(1).broadcast_to([128, H, half])
                        sb_ = sn.unsqueeze(1).broadcast_to([128, H, half])
                        n_ = H * half
                        tv = [tm[:, 0:n_].rearrange("p (h d) -> p h d", h=H) for tm in tl]
                        tk = ["%s_%d_%d" % (tkey, gi, k_) for k_ in range(4)]
                        seqs.append([
                            (_C("tensor_tensor", out=tv[0], in0=x1, in1=cb_, op=ALU.mult), [key, "rope"] + xr, [tk[0]] + xw),
                            (_C("tensor_tensor", out=tv[1], in0=x2, in1=sb_, op=ALU.mult), [key, "rope"] + xr, [tk[1]] + xw),
                            (_C("tensor_tensor", out=tv[2], in0=x1, in1=sb_, op=ALU.mult), [key, "rope"] + xr, [tk[2]] + xw),
                            (_C("tensor_tensor", out=tv[3], in0=x2, in1=cb_, op=ALU.mult), [key, "rope"] + xr, [tk[3]] + xw),
                            (_C("tensor_tensor", out=x1, in0=tv[0], in1=tv[1], op=ALU.subtract), [tk[0], tk[1]], [key]),
                            (_C("tensor_tensor", out=x2, in0=tv[2], in1=tv[3], op=ALU.add), [tk[2], tk[3]], [key]),
                        ])
                    for k_ in range(6):
                        for sq_ in seqs:
                            fn_, rd_, wr_ = sq_[k_]
                            S.op(eng, fn_, reads=rd_, writes=wr_)

                def a1_block(t):
                    tok0 = t * 128
                    kcat = kcats[t % 2]
                    vcat = vcats[t % 2]
                    sfx = str(t % 2)
                    sqk = sqks[t % 2]
                    kb = kbs[t % 2]
                    rt = rts[t % 2]
                    rt2 = rt2s[t % 2]
                    smk_ = smks[t % 2]
                    pb = 0 if t % 2 == 0 else 3
                    hT_ = hTs[t % 2]
                    adaln_blk(hT_, "ah" + sfx, tok0, G1, l, 0, sq8s[t % 2], rsbs[t % 2], tmp8as[t % 2], pb + 2, "a" + sfx, affine_dve=True, scol=256)
                    for nt, (n0, w) in enumerate(((0, 512), (512, 512), (1024, 256))):
                        for kc in range(KC):
                            S.op("pe", _C("matmul",
                                banks[pb + nt][:, 0:w], lhsT=hT_[:, kc, :], rhs=WA[:, kc, n0:n0 + w],
                                start=(kc == 0), stop=(kc == KC - 1)),
                                reads=["ah" + sfx, "WA"], writes=[BK(pb + nt)])
                    S.op("act", _C("activation", out=kcat[:, 0:512], in_=banks[pb][:, :], func=AF.Copy),
                         reads=[BK(pb)], writes=["kc_a" + sfx, "kc_b" + sfx, "kc_c" + sfx])
                    S.op("act", _C("activation", out=kcat[:, 512:640], in_=banks[pb + 1][:, 0:128], func=AF.Copy),
                         reads=[BK(pb + 1)], writes=["kc_c" + sfx])
                    S.op("act", _C("activation", out=vcat[:, 0:384], in_=banks[pb + 1][:, 128:512], func=AF.Copy),
                         reads=[BK(pb + 1)], writes=["vcat" + sfx])
                    S.op("dve", _C("tensor_copy", out=vcat[:, 384:640], in_=banks[pb + 2][:, 0:256]),
                         reads=[BK(pb + 2)], writes=["vcat" + sfx])
                    a1_mark[0] = len(S.ops)
                    S.op("act", _C("activation", out=V[:, t, :, 0:64], in_=vcat.rearrange("p (h d) -> p h d", h=10), func=AF.Copy),
                         reads=["vcat" + sfx], writes=["V"])
                    S.dma("sp", _C("dma_start", out=okv_d[l, tok0:tok0 + 128, 640:1280], in_=vcat),
                          reads=["vcat" + sfx], final_wait=True)
                    S.op("act", _C("activation", out=sqk, in_=kcat, func=AF.Square), reads=["kc_a" + sfx, "kc_b" + sfx, "kc_c" + sfx], writes=["sqk" + sfx])
                    S.op("dve", _C("tensor_reduce", out=smk_[:, 0:6], in_=sqk[:, 0:384].rearrange("p (h d) -> p h d", h=6), axis=AX.X, op=ALU.add),
                         reads=["sqk" + sfx], writes=["smk" + sfx])
                    S.op("dve", _C("tensor_reduce", out=smk_[:, 6:14], in_=sqk[:, 384:640].rearrange("p (h d) -> p h d", h=8), axis=AX.X, op=ALU.add),
                         reads=["sqk" + sfx], writes=["smk" + sfx])
                    S.op("act", _C("activation", out=smk_[:, 16:22], in_=smk_[:, 0:6], func=AF.Ln, scale=1.0 / 64, bias=EPS),
                         reads=["smk" + sfx], writes=["smk" + sfx])
                    S.op("act", _C("activation", out=smk_[:, 22:30], in_=smk_[:, 6:14], func=AF.Ln, scale=1.0 / 32, bias=EPS),
                         reads=["smk" + sfx], writes=["smk" + sfx])
                    S.op("act", _C("activation", out=smk_[:, 32:46], in_=smk_[:, 16:30], func=AF.Exp, scale=-0.5),
                         reads=["smk" + sfx], writes=["smk" + sfx])
                    k64 = kcat[:, 0:384].rearrange("p (h d) -> p h d", h=6)
                    k32 = kcat[:, 384:640].rearrange("p (h d) -> p h d", h=8)
                    S.op("dve", _C("tensor_tensor", out=k64, in0=k64, in1=smk_[:, 32:38].unsqueeze(2).broadcast_to([128, 6, 64]), op=ALU.mult),
                         reads=["smk" + sfx, "kc_a" + sfx, "kc_b" + sfx], writes=["kc_a" + sfx, "kc_b" + sfx])
                    S.op("dve", _C("tensor_tensor", out=k32, in0=k32, in1=smk_[:, 38:46].unsqueeze(2).broadcast_to([128, 8, 32]), op=ALU.mult),
                         reads=["smk" + sfx, "kc_c" + sfx], writes=["kc_c" + sfx])
                    kna = kcat[:, 0:256].rearrange("p (h d) -> p h d", h=4)
                    ksw = kcat[:, 256:384].rearrange("p (h d) -> p h d", h=2)
                    S.op("dve", _C("tensor_tensor", out=kna, in0=kna, in1=GN[:, 64:128].unsqueeze(1).broadcast_to([128, 4, 64]), op=ALU.mult),
                         reads=["SM", "kc_a" + sfx], writes=["kc_a" + sfx])
                    S.op("dve", _C("tensor_tensor", out=ksw, in0=ksw, in1=GN[:, 192:256].unsqueeze(1).broadcast_to([128, 2, 64]), op=ALU.mult),
                         reads=["SM", "kc_b" + sfx], writes=["kc_b" + sfx])
                    S.op("dve", _C("tensor_tensor", out=k32, in0=k32, in1=GN[:, 288:320].unsqueeze(1).broadcast_to([128, 8, 32]), op=ALU.mult),
                         reads=["SM", "kc_c" + sfx], writes=["kc_c" + sfx])

                    rope2("dve", [
                        (kcat[:, 256:384].rearrange("p (h d two) -> p h d two", h=2, two=2), 2, 32, cosb[:, t, :], sinb[:, t, :], "kc_b" + sfx, rt, [], []),
                        (kcat[:, 384:640].rearrange("p (h d two) -> p h d two", h=8, two=2), 8, 16, cosc[:, t, :], sinc[:, t, :], "kc_c" + sfx, rt2, [], []),
                    ], "rtk" + sfx)
                    S.dma("sp", _C("dma_start", out=okv_d[l, tok0:tok0 + 128, 0:640], in_=kcat),
                          reads=["kc_a" + sfx, "kc_b" + sfx, "kc_c" + sfx], final_wait=True)
                    S.op("act", _C("activation", out=kb, in_=kcat, func=AF.Copy), reads=["kc_a" + sfx, "kc_b" + sfx, "kc_c" + sfx], writes=["kb" + sfx])
                    tb_ = 6 + t % 2
                    pv = banks[tb_][:].bitcast(BF16)
                    for ch in range(5):
                        S.op("pe", _C("transpose", pv[:, ch * 128:(ch + 1) * 128], kb[:, ch * 128:(ch + 1) * 128], identb[:]),
                             reads=["kb" + sfx, "identb"], writes=[BK(tb_)])
                    S.op("act", _C("activation", out=KT[:, :, tok0:tok0 + 128], in_=pv[:, 0:640].rearrange("p (c n) -> p c n", c=5), func=AF.Copy),
                         reads=[BK(tb_)], writes=["KT"])

                a1_mark = [0]

                def cap_a1(t):
                    saved = S.ops
                    S.ops = []
                    a1_block(t)
                    got = S.ops
                    S.ops = saved
                    return got[:a1_mark[0]], got[a1_mark[0]:]

                st_a1 = [cap_a1(t) for t in range(a1_blocks)]
                for t in range(0, a1_blocks, 2):
                    if t + 1 < a1_blocks:
                        la = st_a1[t][0] + st_a1[t][1]
                        lb = st_a1[t + 1][0] + st_a1[t + 1][1]
                        for k_ in range(max(len(la), len(lb))):
                            if k_ < len(la):
                                S.ops.append(la[k_])
                            if k_ < len(lb):
                                S.ops.append(lb[k_])
                    else:
                        S.ops.extend(st_a1[t][0] + st_a1[t][1])
                if l == 0:
                    tap("KT", KT, [128, 5, 2048], ["KT"], BF16)
                    tap("V", V, [128, NB, 10, 66], ["V", "Vones"], BF16)

                S.barrier()
                S.dma("pool", _C("dma_start", out=wqv, in_=wq_d[l].rearrange("(kc p) n -> p kc n", p=128)), writes=["WA"])
                sctr = [0]
                pctr = [0]
                woctr = [0]
                def front(i):
                    tok0 = i * 128
                    QT = QTs[i % 2]
                    qk_ = "QT%d" % (i % 2)
                    adaln_blk(hT, "ah", tok0, G1, l, 0, sq8, rsb, tmp8q, 0, "a", tmpkey="sqq", offload=True)
                    for nt in range(2):
                        for kc in range(KC):
                            S.op("pe", _C("matmul",
                                banks[nt][:, :], lhsT=hT[:, kc, :], rhs=wqv[:, kc, nt * 512:(nt + 1) * 512],
                                start=(kc == 0), stop=(kc == KC - 1)),
                                reads=["ah", "WA"], writes=[BK(nt)])
                    S.op("dve", _C("tensor_copy", out=qf[:, 0:512], in_=banks[0][:, :]), reads=[BK(0)], writes=["qf_a", "qf_b"])
                    S.op("dve", _C("tensor_copy", out=qf[:, 512:1024], in_=banks[1][:, :]), reads=[BK(1)], writes=["qf_b", "qf_c"])
                    S.op("dve", _C("tensor_tensor", out=sqq, in0=qf, in1=qf, op=ALU.mult), reads=["qf_a", "qf_b", "qf_c"], writes=["sqq"])
                    S.op("dve", _C("tensor_reduce", out=smalls[:, 0:12], in_=sqq[:, 0:768].rearrange("p (h d) -> p h d", h=12), axis=AX.X, op=ALU.add),
                         reads=["sqq"], writes=["smalls"])
                    S.op("dve", _C("tensor_reduce", out=smalls[:, 12:20], in_=sqq[:, 768:1024].rearrange("p (h d) -> p h d", h=8), axis=AX.X, op=ALU.add),
                         reads=["sqq"], writes=["smalls"])
                    S.op("act", _C("activation", out=smalls[:, 20:32], in_=smalls[:, 0:12], func=AF.Ln, scale=1.0 / 64, bias=EPS),
                         reads=["smalls"], writes=["smalls"])
                    S.op("act", _C("activation", out=smalls[:, 32:40], in_=smalls[:, 12:20], func=AF.Ln, scale=1.0 / 32, bias=EPS),
                         reads=["smalls"], writes=["smalls"])
                    S.op("act", _C("activation", out=smalls[:, 40:60], in_=smalls[:, 20:40], func=AF.Exp, scale=-0.5),
                         reads=["smalls"], writes=["smalls"])
                    q64 = qf[:, 0:768].rearrange("p (h d) -> p h d", h=12)
                    q32 = qf[:, 768:1024].rearrange("p (h d) -> p h d", h=8)
                    S.op("dve", _C("tensor_tensor", out=q64, in0=q64, in1=smalls[:, 40:52].unsqueeze(2).broadcast_to([128, 12, 64]), op=ALU.mult),
                         reads=["smalls", "qf_a", "qf_b"], writes=["qf_a", "qf_b"])
                    S.op("dve", _C("tensor_tensor", out=q32, in0=q32, in1=smalls[:, 52:60].unsqueeze(2).broadcast_to([128, 8, 32]), op=ALU.mult),
                         reads=["smalls", "qf_c"], writes=["qf_c"])
                    qna = qf[:, 0:256].rearrange("p (h d) -> p h d", h=4)
                    qsw = qf[:, 256:768].rearrange("p (h d) -> p h d", h=8)
                    S.op("dve", _C("tensor_tensor", out=qna, in0=qna, in1=GN[:, 0:64].unsqueeze(1).broadcast_to([128, 4, 64]), op=ALU.mult),
                         reads=["SM", "qf_a"], writes=["qf_a"])
                    S.op("dve", _C("tensor_tensor", out=qsw, in0=qsw, in1=GN[:, 128:192].unsqueeze(1).broadcast_to([128, 8, 64]), op=ALU.mult),
                         reads=["SM", "qf_b"], writes=["qf_b"])
                    S.op("dve", _C("tensor_tensor", out=q32, in0=q32, in1=GN[:, 256:288].unsqueeze(1).broadcast_to([128, 8, 32]), op=ALU.mult),
                         reads=["SM", "qf_c"], writes=["qf_c"])
                    rope2("dve", [
                        (qf[:, 256:768].rearrange("p (h d two) -> p h d two", h=8, two=2), 8, 32, cosb[:, i, :], sinb[:, i, :], "qf_b", rtq, ["sqq"], []),
                        (qf[:, 768:1024].rearrange("p (h d two) -> p h d two", h=8, two=2), 8, 16, cosc[:, i, :], sinc[:, i, :], "qf_c", rtq2, [], ["Omix"]),
                    ], "rtq")
                    S.op("dve", _C("tensor_copy", out=qb, in_=qf), reads=["qf_a", "qf_b", "qf_c"], writes=["qb"])
                    if l == 0 and i == 1:
                        tap("qf", qf, [128, 1024], ["qf_a", "qf_b", "qf_c"])
                    for ch in range(8):
                        bk = ch // 4
                        S.op("pe", _C("matmul",
                            banks[bk][:, (ch % 4) * 128:(ch % 4 + 1) * 128], lhsT=qb[:, ch * 128:(ch + 1) * 128],
                            rhs=(permb[:] if ch < 2 else identb[:]), start=True, stop=True, skip_group_check=True),
                            reads=["qb", "permb", "identb"], writes=[BK(bk)])
                    b0 = banks[0][:].rearrange("p (c n) -> p c n", c=4)
                    b1 = banks[1][:].rearrange("p (c n) -> p c n", c=4)
                    QTn = QT[:, 0:4, :].rearrange("p (c two) n -> p c two n", two=2)
                    for hl in range(2):
                        S.op("dve", _C("tensor_scalar", out=QTn[:, :, hl, :], in0=b0[:, 0:2, :], scalar1=rowmask[:, hl:hl + 1], scalar2=None, op0=ALU.mult),
                             reads=[BK(0), "rowmask"], writes=[qk_])
                    for g in range(2):
                        S.op("dve", _C("tensor_scalar", out=QT[:, 4 + 4 * g:6 + 4 * g, :], in0=b0[:, 2:4, :], scalar1=rowmask[:, g:g + 1], scalar2=None, op0=ALU.mult),
                             reads=[BK(0), "rowmask"], writes=[qk_])
                        S.op("dve", _C("tensor_scalar", out=QT[:, 6 + 4 * g:8 + 4 * g, :], in0=b1[:, 0:2, :], scalar1=rowmask[:, g:g + 1], scalar2=None, op0=ALU.mult),
                             reads=[BK(1), "rowmask"], writes=[qk_])
                    QTd = QT[:, 12:20, :].rearrange("p (hf u) n -> p hf u n", u=4)
                    for u in range(4):
                        S.op("dve", _C("tensor_scalar", out=QTd[:, :, u, :], in0=b1[:, 2:4, :], scalar1=rowmask[:, 2 + u:3 + u], scalar2=None, op0=ALU.mult),
                             reads=[BK(1), "rowmask"], writes=[qk_])

                def block_pre(i):
                    QT = QTs[i % 2]
                    qk_ = "QT%d" % (i % 2)
                    steps = []
                    for (j, inval) in na_blocks(i):
                        steps.append(("na", j, inval))
                    for b4 in range(4):
                        steps.append(("nac", b4, None))
                    for g in range(2):
                        for j in (i - 1, i, i + 1):
                            if 0 <= j < NB:
                                steps.append(("sw", j, g))
                        for b4 in range(4):
                            steps.append(("swc", b4, g))
                    for j in range(NB):
                        for hf in range(2):
                            steps.append(("da", j, hf))
                    for b4 in range(4):
                        for hf in range(2):
                            steps.append(("dac", b4, hf))

                    acc_first = [True, True, True]

                    def emit_qk(st, sbk):
                        kind, j, x = st
                        if kind in ("na", "nac"):
                            for hp in range(2):
                                if kind == "na":
                                    lh = KT[:, hp, j * 128:(j + 1) * 128]
                                    rk_ = ["KT"]
                                else:
                                    lh = CTXKT[:, hp, j * 128:(j + 1) * 128]
                                    rk_ = ["CTXKT"]
                                S.op("pe", _C("matmul",
                                    banks[sbk][:, hp * 256:(hp + 1) * 256], lhsT=lh, rhs=QT[:, 2 * hp:2 * hp + 2, :],
                                    start=True, stop=True, skip_group_check=True),
                                    reads=rk_ + [qk_], writes=[BK(sbk)])
                        elif kind in ("sw", "swc"):
                            g = x
                            if kind == "sw":
                                lh = KT[:, 2, j * 128:(j + 1) * 128]
                                rk_ = ["KT"]
                            else:
                                lh = CTXKT[:, 2, j * 128:(j + 1) * 128]
                                rk_ = ["CTXKT"]
                            S.op("pe", _C("matmul",
                                banks[sbk][:, :], lhsT=lh, rhs=QT[:, 4 + 4 * g:8 + 4 * g, :],
                                start=True, stop=True, skip_group_check=True),
                                reads=rk_ + [qk_], writes=[BK(sbk)])
                        else:
                            hf = x
                            if kind == "da":
                                lh = KT[:, 3 + hf, j * 128:(j + 1) * 128]
                                rk_ = ["KT"]
                            else:
                                lh = CTXKT[:, 3 + hf, j * 128:(j + 1) * 128]
                                rk_ = ["CTXKT"]
                            S.op("pe", _C("matmul",
                                banks[sbk][:, :], lhsT=lh, rhs=QT[:, 12 + 4 * hf:16 + 4 * hf, :],
                                start=True, stop=True, skip_group_check=True),
                                reads=rk_ + [qk_], writes=[BK(sbk)])

                    def emit_exp(st, sbk, pbi):
                        kind, j, x = st
                        P = PT[pbi]
                        pk = "PT%d" % pbi
                        if kind == "na":
                            bc = biascol[:, NF_NA + i * 7 + (j - i + 3):NF_NA + i * 7 + (j - i + 3) + 1]
                            sc = 0.125
                        elif kind == "sw":
                            bc = biascol[:, NF_SW + i * 3 + (j - i + 1):NF_SW + i * 3 + (j - i + 1) + 1]
                            sc = 0.125
                        elif kind == "da":
                            bc = biascol[:, NF_DA + i * 16 + j:NF_DA + i * 16 + j + 1]
                            sc = 32 ** -0.5
                        elif kind == "dac":
                            bc = 0.0
                            sc = 32 ** -0.5
                        else:
                            bc = 0.0
                            sc = 0.125
                        S.op("act", _C("activation", out=P, in_=banks[sbk][:, :], func=AF.Exp, bias=bc, scale=sc),
                             reads=[BK(sbk), "biascol"], writes=[pk])
                        if kind == "na":
                            e0 = 2 * (j - i) + 7
                            Pv = P.rearrange("p (h n) -> p h n", h=4)
                            tv = TAB[:, :, e0:e0 + 2, :].rearrange("p h a c -> p h (a c)")
                            S.op("dve", _C("tensor_tensor", out=Pv, in0=Pv, in1=tv, op=ALU.mult),
                                 reads=[pk, "TAB"], writes=[pk])
                            for (ak, a) in x:
                                arr = 1 - a
                                S.op("dve", _C("memset",
                                    P[ak * 64:(ak + 1) * 64, :].rearrange("p (h n) -> p h n", h=4)[:, :, arr * 64:(arr + 1) * 64], 0.0),
                                    reads=[pk], writes=[pk])
                        elif kind == "sw" and j != i:
                            which = 0 if j < i else 1
                            Pv = P.rearrange("p (h n) -> p h n", h=4)
                            mv = swmask[:, which, :].unsqueeze(1).broadcast_to([128, 4, 128])
                            S.op("dve", _C("tensor_tensor", out=Pv, in0=Pv, in1=mv, op=ALU.mult),
                                 reads=[pk, "swmask"], writes=[pk])

                    def emit_pv(st, pbi):
                        kind, j, x = st
                        P = PT[pbi]
                        pk = "PT%d" % pbi
                        for m in range(4):
                            if kind in ("na", "nac"):
                                key = ("na", m)
                                vh = m
                            elif kind in ("sw", "swc"):
                                key = ("sw", 4 * x + m)
                                vh = 4 + x
                            else:
                                hh_ = 2 * x + m // 2
                                key = ("da", 4 * x + m)
                                vh = 6 + hh_
                            slot, col = ACC[key]
                            bk = 2 + slot
                            if kind in ("na", "sw", "da"):
                                rv = V[:, j, vh, 0:65]
                                rk_ = ["V", "Vones"]
                            else:
                                rv = CTXV[:, j, vh, 0:65]
                                rk_ = ["CTXVd", "CTXVo"]
                            st_ = acc_first[slot]
                            acc_first[slot] = False
                            S.op("pe", _C("matmul",
                                banks[bk][:, col:col + 65], lhsT=P[:, m * 128:(m + 1) * 128], rhs=rv,
                                start=st_, stop=False, skip_group_check=True),
                                reads=[pk] + rk_, writes=[BK(bk)])

                    nst = len(steps)
                    sb_of = []
                    pb_of = []
                    for s_ in range(nst):
                        sb_of.append(5 + (sctr[0] % 3))
                        sctr[0] += 1
                        pb_of.append(pctr[0] % 4)
                        pctr[0] += 1
                    LA = 2
                    for s_ in range(min(LA, nst)):
                        emit_qk(steps[s_], sb_of[s_])
                        emit_exp(steps[s_], sb_of[s_], pb_of[s_])
                    blk_pre[i] = (steps, sb_of, pb_of, emit_qk, emit_exp, emit_pv, nst, LA)

                def steps_and_back(i, fe_next):
                    steps, sb_of, pb_of, emit_qk, emit_exp, emit_pv, nst, LA = blk_pre.pop(i)
                    per = (len(fe_next) + nst - 1) // nst if fe_next else 0
                    fpos = 0
                    for s_ in range(nst):
                        if s_ + LA < nst:
                            emit_qk(steps[s_ + LA], sb_of[s_ + LA])
                            emit_exp(steps[s_ + LA], sb_of[s_ + LA], pb_of[s_ + LA])
                        emit_pv(steps[s_], pb_of[s_])
                        if fpos < len(fe_next):
                            S.ops.extend(fe_next[fpos:fpos + per])
                            fpos += per
                    S.ops.extend(fe_next[fpos:])
                    if i + 1 < a2_blocks:
                        block_pre(i + 1)

                    S.op("dve", _C("tensor_copy", out=Oacc[:, 0, :], in_=banks[2][:, 0:AWW]), reads=[BK(2)], writes=["Oacc"])
                    S.op("act", _C("activation", out=Oacc[:, 1, :], in_=banks[3][:, 0:AWW], func=AF.Copy), reads=[BK(3)], writes=["Oacc"])
                    S.op("dve", _C("tensor_copy", out=Oacc[:, 2, :], in_=banks[4][:, 0:AWW]), reads=[BK(4)], writes=["Oacc"])

                def back(i):
                    def accv(kind, lo, n):
                        slot, col = ACC[(kind, lo)]
                        return Oacc[:, slot, col:col + AW * n].rearrange("p (h d) -> p h d", h=n), "Oacc"

                    runs = [("sw", 0, 4, 256), ("na", 0, 3, 0), ("sw", 4, 4, 512), ("na", 3, 1, 192), ("da", 0, 2, None), ("da", 2, 6, None)]
                    rec = smalls
                    ri = 0
                    for (kind, lo, n, ocol) in runs:
                        av, bkey = accv(kind, lo, n)
                        rr = rec[:, ri:ri + n]
                        if kind == "sw":
                            S.op("dve", _C("tensor_tensor", out=rr, in0=av[:, :, 64], in1=esink[:, lo:lo + n], op=ALU.add),
                                 reads=[bkey, "esink"], writes=["smalls"])
                            S.op("dve", _C("reciprocal", out=rr, in_=rr), reads=["smalls"], writes=["smalls"])
                        else:
                            S.op("dve", _C("reciprocal", out=rr, in_=av[:, :, 64]), reads=[bkey], writes=["smalls"])
                        if kind == "da":
                            ov = Ofin[:, lo:lo + n, :]
                            okey = "sqq"
                        else:
                            ov = Omix[:, ocol:ocol + 64 * n].rearrange("p (h d) -> p h d", h=n)
                            okey = "Omix"
                        S.op("dve", _C("tensor_tensor",
                            out=ov, in0=av[:, :, 0:64], in1=rr.unsqueeze(2).broadcast_to([128, n, 64]), op=ALU.mult),
                            reads=[bkey, "smalls"], writes=[okey])
                        ri += n
                    O4 = Ofin.rearrange("p (h c) d -> p h c d", c=2)
                    S.op("dve", _C("scalar_tensor_tensor", out=ddt, in0=O4[:, :, 1, :], scalar=neglam, in1=O4[:, :, 0, :], op0=ALU.mult, op1=ALU.add),
                         reads=["sqq", "neglam"], writes=["sqq"])
                    S.op("act", _C("activation", out=sqd, in_=ddt, func=AF.Square), reads=["sqq"], writes=["sqq"])
                    S.op("dve", _C("tensor_reduce", out=rec[:, 24:28], in_=sqd, axis=AX.X, op=ALU.add), reads=["sqq"], writes=["smalls"])
                    S.op("act", _C("activation", out=rec[:, 28:32], in_=rec[:, 24:28], func=AF.Ln, scale=1.0 / 64, bias=EPS), reads=["smalls"], writes=["smalls"])
                    S.op("act", _C("activation", out=rec[:, 32:36], in_=rec[:, 28:32], func=AF.Exp, scale=-0.5), reads=["smalls"], writes=["smalls"])
                    S.op("dve", _C("tensor_tensor", out=ddt, in0=ddt, in1=rec[:, 32:36].unsqueeze(2).broadcast_to([128, 4, 64]), op=ALU.mult),
                         reads=["smalls", "sqq"], writes=["sqq"])
                    S.op("dve", _C("tensor_tensor", out=Omix[:, 768:1024].rearrange("p (h d) -> p h d", h=4), in0=ddt,
                                                          in1=SG.unsqueeze(1).broadcast_to([128, 4, 64]), op=ALU.mult),
                         reads=["sqq", "SG"], writes=["Omix"])
                    if l == 0 and i == 1:
                        tap("Omix", Omix, [128, 1024], ["Omix"], BF16)
                    iq = i % 2
                    for rnd in range(2):
                        for q4 in range(4):
                            ch = rnd * 4 + q4
                            S.op("pe", _C("matmul",
                                banks[0][:, q4 * 128:(q4 + 1) * 128], lhsT=Omix[:, ch * 128:(ch + 1) * 128],
                                rhs=(permb[:] if ch < 2 else identb[:]), start=True, stop=True, skip_group_check=True),
                                reads=["Omix", "permb", "identb"], writes=[BK(0)])
                        S.op("dve", _C("tensor_copy",
                            out=OT[:, rnd * 4:(rnd + 1) * 4, iq * 128:(iq + 1) * 128],
                            in_=banks[0][:].rearrange("p (c n) -> p c n", c=4)),
                            reads=[BK(0)], writes=["OT"])
                    if iq == 1:
                        t0p = (i - 1) * 128
                        for co in range(KC):
                            wb_ = woctr[0] % 2
                            woctr[0] += 1
                            S.dma("pool", _C("dma_start",
                                out=wo[wb_], in_=wout_d[l, :, co * 128:(co + 1) * 128].rearrange("(kc p) n -> p kc n", p=128)),
                                writes=["wo%d" % wb_])
                            for kc in range(KC):
                                S.op("pe", _C("matmul",
                                    banks[1][:, 0:256], lhsT=wo[wb_][:, kc, :], rhs=OT[:, kc, :],
                                    start=(kc == 0), stop=(kc == KC - 1)),
                                    reads=["wo%d" % wb_, "OT"], writes=[BK(1)])
                            S.op("dve", _C("scalar_tensor_tensor",
                                out=XT[:, co, t0p:t0p + 256], in0=banks[1][:, 0:256], scalar=mcol(l, 2, co),
                                in1=XT[:, co, t0p:t0p + 256], op0=ALU.mult, op1=ALU.add),
                                reads=[BK(1), "modsb", "XTh0", "XTh1"], writes=["XTh0", "XTh1"])


                def capture(i):
                    saved = S.ops
                    S.ops = []
                    front(i)
                    got = S.ops
                    S.ops = saved
                    return got

                def capture_back(i):
                    saved = S.ops
                    S.ops = []
                    back(i)
                    got = S.ops
                    S.ops = saved
                    return got

                blk_pre = {}
                if a2_blocks > 0:
                    S.ops.extend(capture(0))
                    block_pre(0)
                pending = []
                for i in range(a2_blocks):
                    fe_next = capture(i + 1) if i + 1 < a2_blocks else []
                    steps_and_back(i, pending + fe_next)
                    pending = capture_back(i)
                S.ops.extend(pending)

            if do_ffn:
                S.barrier()
                ar = Arena(BIG, NBIG)
                ACTT = ar.take([128, NJ, 1024], BF16)
                H2T = ar.take([128, KC, 1026], BF16)
                wg = [ar.take([128, 8, 256], BF16) for _ in range(2)]
                wv = [ar.take([128, 8, 256], BF16) for _ in range(2)]
                wd = [ar.take([128, NJ, 128], BF16) for _ in range(2)]
                sqf = [ar.take([128, 512], BF16) for _ in range(2)]
                rs2 = ar.take([128, 512], F32)
                tmpf = [ar.take([128, 512], F32) for _ in range(2)]
                ugs2 = [ar.take([128, 1026], F32) for _ in range(2)]
                uvs2 = [ar.take([128, 1026], F32) for _ in range(2)]
                usets = [(0, 1), (2, 3), (5, 6)]
                uctr = [0]
                prev_j = [None]
                ygs = [ar.take([128, 1024], F32) for _ in range(2)]
                yv = ar.take([128, 1024], F32)
                cwn = ar.take([128, 2, 44], F32)
                halo_save = ar.take([128, KC, 1], BF16)
                for tapi, slot in ((0, 0), (2, 1)):
                    c0 = (l * 3 + tapi) * 44
                    S.op("dve", _C("tensor_scalar",
                        out=cwn[:, slot, :], in0=convwT[:, c0:c0 + 44], scalar1=pflag[:, 0:1], scalar2=-1.0, op0=ALU.mult, op1=ALU.mult),
                        reads=["convwT", "pflag"], writes=["cwn"])
                pctr2 = [0]
                def ffn_f1(hh):
                    h0 = hh * 1024
                    if hh == 0:
                        segs = [(1, 513), (513, 1025), (1025, 1026)]
                        zc = 0
                    else:
                        segs = [(0, 1), (1, 513), (513, 1025)]
                        zc = 1025
                    S.op("pool", _C("memset", H2T[:, :, zc:zc + 1], 0.0), writes=["fh"])
                    for (n0, n1) in segs:
                        w = n1 - n0
                        tok0 = h0 - 1 + n0
                        if hh == 1 and n0 == 0:
                            S.op("pool", _C("tensor_copy", out=H2T[:, :, 0:1], in_=halo_save), reads=["halo_save"], writes=["fh"])
                            continue
                        adaln(lambda c, n0=n0, n1=n1: H2T[:, c, n0:n1], tok0, w, G2, l, 3, sqf, rs2, tmpf, 5, "f", xkeys=(["XTh1"] if hh == 1 else ["XTh0", "XTh1"]))
                    if hh == 0:
                        S.op("pool", _C("tensor_copy", out=halo_save, in_=H2T[:, :, 1024:1025]), reads=["fh"], writes=["halo_save"])
                def ffn_f2(hh):
                    h0 = hh * 1024
                    def ffn_post(j):
                        ub = j % 2
                        yg = ygs[j % 2]
                        ygk = "yg%d" % (j % 2)
                        for (us, ukey, yy, ykey, chn) in ((ugs2[ub], "ugs%d" % ub, yg, ygk, j), (uvs2[ub], "uvs%d" % ub, yv, "yv", NJ + j)):
                            cw0 = convwT[:, (l * 3 + 0) * 44 + chn:(l * 3 + 0) * 44 + chn + 1]
                            cw1 = convwT[:, (l * 3 + 1) * 44 + chn:(l * 3 + 1) * 44 + chn + 1]
                            cw2 = convwT[:, (l * 3 + 2) * 44 + chn:(l * 3 + 2) * 44 + chn + 1]
                            cbb = convbT[:, l * 44 + chn:l * 44 + chn + 1]
                            S.op("act", _C("activation", out=yy, in_=us[:, 1:1025], func=AF.Identity, bias=cbb, scale=cw1),
                                reads=[ukey, "convwT", "convbT"], writes=[ykey])
                            S.op("dve", _C("scalar_tensor_tensor",
                                out=yy, in0=us[:, 0:1024], scalar=cw0, in1=yy, op0=ALU.mult, op1=ALU.add),
                                reads=[ukey, "convwT", ykey], writes=[ykey])
                            S.op("dve", _C("scalar_tensor_tensor",
                                out=yy, in0=us[:, 2:1026], scalar=cw2, in1=yy, op0=ALU.mult, op1=ALU.add),
                                reads=[ukey, "convwT", ykey], writes=[ykey])
                            y4 = yy.rearrange("p (s n) -> p s n", s=4)
                            u0 = us[:, 0:1024].rearrange("p (s n) -> p s n", s=4)
                            u2 = us[:, 2:1026].rearrange("p (s n) -> p s n", s=4)
                            S.op("dve", _C("scalar_tensor_tensor",
                                out=y4[:, :, 0:1], in0=u0[:, :, 0:1], scalar=cwn[:, 0, chn:chn + 1], in1=y4[:, :, 0:1], op0=ALU.mult, op1=ALU.add),
                                reads=[ukey, "cwn", ykey], writes=[ykey])
                            S.op("dve", _C("scalar_tensor_tensor",
                                out=y4[:, :, 255:256], in0=u2[:, :, 255:256], scalar=cwn[:, 1, chn:chn + 1], in1=y4[:, :, 255:256], op0=ALU.mult, op1=ALU.add),
                                reads=[ukey, "cwn", ykey], writes=[ykey])
                        S.op("act", _C("activation", out=yg, in_=yg, func=AF.Silu), reads=[ygk], writes=[ygk])
                        S.op("dve", _C("tensor_tensor", out=ACTT[:, j, :], in0=yg, in1=yv, op=ALU.mult),
                             reads=[ygk, "yv"], writes=["ACTT"])

                    for c_ in range(2):
                        S.dma("pool", _C("dma_start",
                            out=wd[c_], in_=wdn_d[l, :, c_ * 128:(c_ + 1) * 128].rearrange("(j q) n -> q j n", q=128)),
                            writes=["wd%d" % c_])
                    for p in range(11):
                        b = pctr2[0] % 2
                        pctr2[0] += 1
                        S.dma("pool", _C("dma_start",
                            out=wg[b], in_=wup_d[l, :, p * 256:(p + 1) * 256].rearrange("(kc q) n -> q kc n", q=128)),
                            writes=["wg%d" % b])
                        S.dma("pool", _C("dma_start",
                            out=wv[b], in_=wup_d[l, :, DFF + p * 256:DFF + (p + 1) * 256].rearrange("(kc q) n -> q kc n", q=128)),
                            writes=["wv%d" % b])
                        for sub in range(2):
                            j = 2 * p + sub
                            ub = j % 2
                            for (wt, wkey, tgt, tkey, hb) in ((wg[b], "wg%d" % b, ugs2[ub], "ugs%d" % ub, 4), (wv[b], "wv%d" % b, uvs2[ub], "uvs%d" % ub, 7)):
                                bset = usets[uctr[0] % 3]
                                hc = 2 * (uctr[0] % 8)
                                uctr[0] += 1
                                for part in range(3):
                                    if part < 2:
                                        ob = banks[bset[part]][:, :]
                                        okey = BK(bset[part])
                                        rsl = (part * 512, part * 512 + 512)
                                    else:
                                        ob = banks[hb][:, hc:hc + 2]
                                        okey = BK(hb)
                                        rsl = (1024, 1026)
                                    for kc in range(KC):
                                        S.op("pe", _C("matmul",
                                            ob, lhsT=wt[:, kc, sub * 128:(sub + 1) * 128], rhs=H2T[:, kc, rsl[0]:rsl[1]],
                                            start=(kc == 0), stop=(kc == KC - 1), skip_group_check=True),
                                            reads=[wkey, "fh"], writes=[okey])
                                S.op("act", _C("activation", out=tgt[:, 0:512], in_=banks[bset[0]][:, :], func=AF.Copy), reads=[BK(bset[0])], writes=[tkey])
                                S.op("act", _C("activation", out=tgt[:, 512:1024], in_=banks[bset[1]][:, :], func=AF.Copy), reads=[BK(bset[1])], writes=[tkey])
                                S.op("act", _C("activation", out=tgt[:, 1024:1026], in_=banks[hb][:, hc:hc + 2], func=AF.Copy), reads=[BK(hb)], writes=[tkey])
                            if prev_j[0] is not None:
                                ffn_post(prev_j[0])
                            prev_j[0] = j
                    ffn_post(prev_j[0])
                    prev_j[0] = None
                def ffn_f3(hh):
                    h0 = hh * 1024
                    for c in range(KC):
                        b = c % 2
                        if c >= 2:
                          S.dma("pool", _C("dma_start",
                            out=wd[b], in_=wdn_d[l, :, c * 128:(c + 1) * 128].rearrange("(j q) n -> q j n", q=128)),
                            writes=["wd%d" % b])
                        for tg in range(2):
                            ob = 6 + tg
                            for j in range(NJ):
                                S.op("pe", _C("matmul",
                                    banks[ob][:, :], lhsT=wd[b][:, j, :], rhs=ACTT[:, j, tg * 512:(tg + 1) * 512],
                                    start=(j == 0), stop=(j == NJ - 1)),
                                    reads=["wd%d" % b, "ACTT"], writes=[BK(ob)])
                            ta = h0 + tg * 512
                            S.op("dve", _C("scalar_tensor_tensor",
                                out=XT[:, c, ta:ta + 512], in0=banks[ob][:, :], scalar=mcol(l, 5, c),
                                in1=XT[:, c, ta:ta + 512], op0=ALU.mult, op1=ALU.add),
                                reads=[BK(ob), "modsb", "XTh%d" % hh], writes=["XTh%d" % hh])


                def cap(fn, hh):
                    saved = S.ops
                    S.ops = []
                    fn(hh)
                    got = S.ops
                    S.ops = saved
                    return got

                ffn_f1(0)
                ffn_f2(0)
                fa = cap(ffn_f1, 1)
                fb = cap(ffn_f3, 0)
                per_ = max(1, len(fb) // max(1, len(fa)))
                ia = 0
                for k_, o_ in enumerate(fb):
                    S.ops.append(o_)
                    if k_ % per_ == per_ - 1 and ia < len(fa):
                        S.ops.append(fa[ia])
                        ia += 1
                S.ops.extend(fa[ia:])
                ffn_f2(1)
                ffn_f3(1)

        S.barrier()
        ar = Arena(BIG, NBIG)
        ys = [ar.take([128, 1024], F32) for _ in range(2)]
        for t in range(NB):
            b = t % 2
            for half in range(2):
                bank = (2 * t + half) % 4
                for q in range(4):
                    c = half * 4 + q
                    S.op("pe", _C("transpose",
                        banks[bank][:, q * 128:(q + 1) * 128], XT[:, c, t * 128:(t + 1) * 128], identf[:]),
                        reads=["XTh0", "XTh1", "XT%d" % t, "identf"], writes=[BK(bank)])
                if half == 0:
                    S.op("act", _C("activation", out=ys[b][:, 0:512], in_=banks[bank][:, :], func=AF.Copy),
                         reads=[BK(bank)], writes=["ys%d_0" % b])
                else:
                    S.op("dve", _C("tensor_copy", out=ys[b][:, 512:1024], in_=banks[bank][:, :]),
                         reads=[BK(bank)], writes=["ys%d_1" % b])
            S.dma("sp", _C("dma_start", out=y_d[t * 128:(t + 1) * 128, :], in_=ys[b]),
                  reads=["ys%d_0" % b, "ys%d_1" % b], final_wait=True)
        S.emit(nc, es)
    return nc, dbg_outs


def _rope_tables(dim):
    n = dim // 4
    inv = (1.0 / (10000.0 ** (np.arange(n, dtype=np.float32) / np.float32(n)))).astype(np.float32)
    t = np.arange(2048)
    row = (t // 64).astype(np.float32)
    col = (t % 64).astype(np.float32)
    ang = np.concatenate([row[:, None] * inv, col[:, None] * inv], axis=-1).astype(np.float32)
    return np.cos(ang).astype(np.float32), np.sin(ang).astype(np.float32)


def _tok_major(a):
    return np.ascontiguousarray(a.reshape(NB, 128, -1).transpose(1, 0, 2))


def _static_tables():
    bf = ml_dtypes.bfloat16
    st = {}
    cb, sb_ = _rope_tables(64)
    cc, sc = _rope_tables(32)
    st["rope_s"] = [_tok_major(cb), _tok_major(sb_), _tok_major(cc), _tok_major(sc)]
    st["rope_p"] = [np.ones((128, NB, 32), np.float32), np.zeros((128, NB, 32), np.float32),
                    np.ones((128, NB, 16), np.float32), np.zeros((128, NB, 16), np.float32)]
    bs = np.zeros((128, NF), np.float32)
    bp = np.zeros((128, NF), np.float32)
    for i in range(NB):
        for dj in range(7):
            j = i + dj - 3
            ok_p = 0 <= j < NB and (j // 2 == i // 2)
            bp[:, NF_NA + i * 7 + dj] = 0.0 if ok_p else NEG
        for dj in range(3):
            j = i + dj - 1
            ok_p = 0 <= j < NB and (j // 2 == i // 2)
            bp[:, NF_SW + i * 3 + dj] = 0.0 if ok_p else NEG
        for j in range(NB):
            bp[:, NF_DA + i * 16 + j] = 0.0 if (j // 2 == i // 2) else NEG
    st["bias_s"] = bs
    st["bias_p"] = bp
    k = np.arange(128)[:, None]
    q = np.arange(128)[None, :]
    sm = np.zeros((128, 2, 128), np.float32)
    sm[:, 0, :] = (k >= q)
    sm[:, 1, :] = (k <= q)
    st["swm_s"] = sm.astype(bf)
    st["swm_p"] = np.ones((128, 2, 128), np.float32).astype(bf)
    nm = np.zeros((128, 16, 64), np.float32)
    for p in range(128):
        ak, ck_ = p // 64, p % 64
        for e2 in range(16):
            dr = e2 - 8 + ak
            if abs(dr) > 7:
                continue
            for cr in range(64):
                c = 63 - cr
                qs = min(max(c - 8, 0), 48)
                if qs <= ck_ < qs + 16:
                    nm[p, e2, cr] = 1.0
    st["nam_s"] = nm.astype(bf)
    st["nam_p"] = np.ones((128, 16, 64), np.float32).astype(bf)
    st["identf"] = np.eye(128, dtype=np.float32)
    st["identb"] = np.eye(128, dtype=np.float32).astype(bf)
    st["permb"] = np.eye(128, dtype=np.float32)[:, ::-1].copy().astype(bf)
    st["onesb"] = np.ones((128, 128), np.float32).astype(bf)
    rm = np.zeros((128, 6), np.float32)
    rm[0:64, 0] = 1.0
    rm[64:128, 1] = 1.0
    for u in range(4):
        rm[32 * u:32 * u + 32, 2 + u] = 1.0
    st["rowmask"] = rm
    return st


_SW_Q_ORDER = [0, 4, 1, 5, 2, 6, 3, 7]


def make_in_maps(inp):
    st = _static_tables()
    f = lambda a: np.ascontiguousarray(np.asarray(a, dtype=np.float32))
    w_in = f(inp["w_in"])
    o = 0
    seg = {}
    for name, wdt in (("naq", 256), ("nak", 256), ("nav", 256), ("swq", 512), ("swk", 128), ("swv", 128), ("daq", 256), ("dak", 256), ("dav", 256)):
        seg[name] = (o, o + wdt)
        o += wdt

    def cols(name):
        a, b = seg[name]
        return w_in[:, :, a:b]

    swq = cols("swq").reshape(L, D, 8, 64)[:, :, _SW_Q_ORDER, :].reshape(L, D, 512)
    w_kv = np.ascontiguousarray(np.concatenate([cols("nak"), cols("swk"), cols("dak"), cols("nav"), cols("swv"), cols("dav")], axis=-1))
    w_q = np.ascontiguousarray(np.concatenate([cols("naq"), swq, cols("daq")], axis=-1))

    def colT(v, n):
        return np.ascontiguousarray(v.reshape(L, n, 128).transpose(2, 0, 1).reshape(128, L * n))

    bmodT = colT(f(inp["b_mod"]), 48)
    gattnT = colT(f(inp["g_attn"]), 8)
    gffnT = colT(f(inp["g_ffn"]), 8)
    convwT = np.ascontiguousarray(f(inp["conv_w"]).reshape(L, 3, 44, 128).transpose(3, 0, 1, 2).reshape(128, L * 3 * 44))
    convbT = colT(f(inp["conv_b"]), 44)
    naq = f(inp["na_qk_g"])
    swg = f(inp["sw_qk_g"])
    dag = f(inp["da_qk_g"])
    small = np.concatenate([naq[:, 0], naq[:, 1], swg[:, 0], swg[:, 1], dag[:, 0], dag[:, 1],
                            f(inp["sw_sink"]), f(inp["da_lambda"]).reshape(L, 128), f(inp["da_subln_g"])], axis=-1)
    small = np.ascontiguousarray(small)
    assert small.shape == (L, 520)
    rpb = f(inp["na_rpb"]).reshape(L, 4 * 465)
    rpbpad_s = np.zeros((L, RPBLEN), np.float32)
    rpbpad_s[:, PADOFF:PADOFF + 1860] = rpb
    rpbpad_p = np.zeros((L, RPBLEN), np.float32)

    shared = dict(w_mod=f(inp["w_mod"]), w_kv=w_kv, w_q=w_q, w_out=f(inp["w_out"]), w_up=f(inp["w_up"]), w_down=f(inp["w_down"]),
                  bmodT=bmodT, gattnT=gattnT, gffnT=gffnT, convwT=convwT, convbT=convbT, small=small,
                  identf=st["identf"], identb=st["identb"], permb=st["permb"], onesb=st["onesb"], rowmask=st["rowmask"])
    xs_ = f(inp["x_sample"])
    xp = f(inp["x_prompt"])
    cc = f(inp["c"])
    cctx = f(inp["c_ctx"])
    ck_all = np.concatenate([f(inp["cache_na_k"]).reshape(4, L, 512, 256), f(inp["cache_sw_k"]).reshape(4, L, 512, 128),
                             f(inp["cache_da_k"]).reshape(4, L, 512, 256)], axis=-1)
    cv_all = np.concatenate([f(inp["cache_na_v"]).reshape(4, L, 512, 256), f(inp["cache_sw_v"]).reshape(4, L, 512, 128),
                             f(inp["cache_da_v"]).reshape(4, L, 512, 256)], axis=-1)
    zc = np.zeros((L, 512, 640), np.float32)
    maps = []
    for core in range(8):
        m = dict(shared)
        if core < 4:
            m["x"] = np.ascontiguousarray(xs_[core])
            cv_ = cc[core]
            r = st["rope_s"]
            m["biascol"] = st["bias_s"]
            m["swmask"] = st["swm_s"]
            m["namask"] = st["nam_s"]
            m["rpbpad"] = rpbpad_s
            m["ctxone"] = np.ones((128, 1), np.float32)
            m["pflag"] = np.zeros((128, 1), np.float32)
            m["ck"] = np.ascontiguousarray(ck_all[core])
            m["cv"] = np.ascontiguousarray(cv_all[core])
        else:
            g = core - 4
            m["x"] = np.ascontiguousarray(xp[8 * g:8 * g + 8].reshape(2048, D))
            cv_ = cctx
            r = st["rope_p"]
            m["biascol"] = st["bias_p"]
            m["swmask"] = st["swm_p"]
            m["namask"] = st["nam_p"]
            m["rpbpad"] = rpbpad_p
            m["ctxone"] = np.zeros((128, 1), np.float32)
            m["pflag"] = np.ones((128, 1), np.float32)
            m["ck"] = zc
            m["cv"] = zc
        m["cvecT"] = np.ascontiguousarray(cv_.reshape(8, 128).T)
        m["cosb"], m["sinb"], m["cosc"], m["sinc"] = r
        maps.append(m)
    return maps


_PROG = {}


def _get_prog(key=("full",), **kw):
    if key not in _PROG:
        _PROG[key] = build_program(**kw)
    return _PROG[key]


def assemble(results):
    y_s = np.stack([results[c]["y"] for c in range(4)], axis=0)
    y_p = np.concatenate([results[c]["y"].reshape(8, 256, D) for c in range(4, 8)], axis=0)
    okv = np.concatenate([results[c]["okv"].reshape(L, 8, 256, 1280).transpose(1, 0, 2, 3) for c in range(4, 8)], axis=0)
    nk = np.ascontiguousarray(okv[..., 0:256]).reshape(32, L, 256, 4, 64)
    sk = np.ascontiguousarray(okv[..., 256:384]).reshape(32, L, 256, 2, 64)
    dk = np.ascontiguousarray(okv[..., 384:640]).reshape(32, L, 256, 4, 2, 32)
    nv = np.ascontiguousarray(okv[..., 640:896]).reshape(32, L, 256, 4, 64)
    sv = np.ascontiguousarray(okv[..., 896:1024]).reshape(32, L, 256, 2, 64)
    dv = np.ascontiguousarray(okv[..., 1024:1280]).reshape(32, L, 256, 4, 64)
    return (np.ascontiguousarray(y_p), np.ascontiguousarray(y_s), nk, nv, sk, sv, dk, dv)


def kernel(**inputs):
    nc, _ = _get_prog()
    maps = make_in_maps(inputs)
    res = run_bass_kernel_spmd(nc, maps, core_ids=list(range(8)))
    return assemble(res.results)
```

```python
import math
from contextlib import ExitStack

import numpy as np
import ml_dtypes

import concourse.bass as bass
import concourse.mybir as mybir
from concourse.bass_utils import run_bass_kernel_spmd

F32 = mybir.dt.float32
BF16 = mybir.dt.bfloat16
AF = mybir.ActivationFunctionType
ALU = mybir.AluOpType
AX = mybir.AxisListType

L = 4
NB = 16
D = 1024
KC = 8
DFF = 2816
NJ = 22
EPS = 1e-6
NEG = -30000.0
PADOFF = 128
RPBLEN = 2176
ENGS = ("pe", "act", "dve", "pool", "sp")


class _Op:
    __slots__ = ("eng", "fn", "reads", "writes", "is_dma", "deps", "needs_inc",
                 "sem", "val", "idx", "final_wait", "barrier")

    def __init__(self, eng, fn, reads, writes, is_dma, final_wait):
        self.eng = eng
        self.fn = fn
        self.reads = reads
        self.writes = writes
        self.is_dma = is_dma
        self.deps = []
        self.needs_inc = False
        self.sem = None
        self.val = 0
        self.final_wait = final_wait
        self.barrier = False


class Sched:
    def __init__(self, same_engine_sync=True, n_dma_sems=32):
        self.ops = []
        self.same_engine_sync = same_engine_sync
        self.n_dma_sems = n_dma_sems

    def op(self, eng, fn, reads=(), writes=()):
        o = _Op(eng, fn, tuple(reads), tuple(writes), False, False)
        self.ops.append(o)
        return o

    def dma(self, eng, fn, reads=(), writes=(), final_wait=False):
        o = _Op(eng, fn, tuple(reads), tuple(writes), True, final_wait)
        self.ops.append(o)
        return o

    def barrier(self):
        for e in ENGS:
            o = _Op(e, None, (), (), False, False)
            o.barrier = True
            self.ops.append(o)

    def analyze(self):
        last_w = {}
        readers = {}
        waited = {e: {s: -1 for s in ENGS} for e in ENGS}
        waited_dma = {e: set() for e in ENGS}
        dma_slot_last = [None] * self.n_dma_sems
        dma_ctr = {"sp": 0, "pool": 0, "act": 0}
        half = self.n_dma_sems // 2
        last_compute = {e: None for e in ENGS}
        all_dma = []
        for idx, o in enumerate(self.ops):
            o.idx = idx
            deps = set()
            if o.barrier:
                for e in ENGS:
                    if last_compute[e] is not None and e != o.eng:
                        deps.add(last_compute[e])
                    if e == o.eng and last_compute[e] is not None and e != "pe":
                        deps.add(last_compute[e])
                for d in all_dma:
                    if d not in waited_dma[o.eng]:
                        deps.add(d)
            raw = set()
            for r in o.reads:
                w = last_w.get(r)
                if w is not None:
                    deps.add(w)
                    raw.add(w)
            for wkey in o.writes:
                w = last_w.get(wkey)
                if w is not None:
                    deps.add(w)
                for rd in readers.get(wkey, ()):
                    deps.add(rd)
            if o.is_dma:
                if o.eng == "pool":
                    slot = half + dma_ctr["pool"] % (self.n_dma_sems - half)
                else:
                    slot = dma_ctr["sp"] % half
                dma_ctr["pool" if o.eng == "pool" else "sp"] += 1
                prev = dma_slot_last[slot]
                if prev is not None:
                    deps.add(prev)
                dma_slot_last[slot] = idx
                o.sem = ("dma", slot)
                all_dma.append(idx)
            deps.discard(idx)
            best = {}
            out = []
            for d in sorted(deps):
                p = self.ops[d]
                if p.is_dma:
                    if d in waited_dma[o.eng]:
                        continue
                    waited_dma[o.eng].add(d)
                    out.append(d)
                else:
                    if p.eng == o.eng and not o.is_dma and not o.barrier and \
                            (p.eng == "pe" or not self.same_engine_sync):
                        continue
                    if waited[o.eng][p.eng] >= d:
                        continue
                    best[p.eng] = max(best.get(p.eng, -1), d)
            for e, d in best.items():
                waited[o.eng][e] = d
                out.append(d)
            o.deps = out
            for d in out:
                self.ops[d].needs_inc = True
            if not o.barrier:
                for r in o.reads:
                    readers.setdefault(r, []).append(idx)
                for wkey in o.writes:
                    last_w[wkey] = idx
                    readers[wkey] = []
                if not o.is_dma:
                    last_compute[o.eng] = idx
        self.final = [o.idx for o in self.ops if o.final_wait]
        cnt = {e: 0 for e in ENGS}
        dma_cnt = [0] * self.n_dma_sems
        for o in self.ops:
            if o.barrier:
                continue
            if o.is_dma:
                slot = o.sem[1]
                dma_cnt[slot] += 16
                o.val = dma_cnt[slot]
                o.needs_inc = True
            elif o.needs_inc:
                cnt[o.eng] += 1
                o.val = cnt[o.eng]
                o.sem = ("eng", o.eng)

    def emit(self, nc, es):
        self.analyze()
        sems = {}
        for e in ENGS:
            sems[("eng", e)] = es.enter_context(nc.semaphore("s_" + e))
        for i in range(self.n_dma_sems):
            sems[("dma", i)] = es.enter_context(nc.semaphore("s_dma%d" % i))
        per = {e: [] for e in ENGS}
        for o in self.ops:
            per[o.eng].append(o)
        ops = self.ops
        final = self.final
        block = es.enter_context(nc.Block())

        def run(engine_obj, lst, ename):
            for o in lst:
                for d in o.deps:
                    p = ops[d]
                    engine_obj.wait_ge(sems[p.sem], p.val)
                if o.barrier:
                    continue
                ins = o.fn(engine_obj)
                if o.needs_inc:
                    ins.then_inc(sems[o.sem], 16 if o.is_dma else 1)
            for d in final:
                p = ops[d]
                if p.eng == ename:
                    engine_obj.wait_ge(sems[p.sem], p.val)

        @block.tensor
        def _(e):
            run(e, per["pe"], "pe")

        @block.scalar
        def _(e):
            run(e, per["act"], "act")

        @block.vector
        def _(e):
            run(e, per["dve"], "dve")

        @block.gpsimd
        def _(e):
            run(e, per["pool"], "pool")

        @block.sync
        def _(e):
            run(e, per["sp"], "sp")


def _C(name, *a, **k):
    def f(e):
        return getattr(e, name)(*a, **k)
    return f


_ARENA_HI = 0


class Arena:
    def __init__(self, big, nel, base=0):
        self.big = big
        self.nel = nel
        self.off = base
        self.hi = base

    def take(self, shape, dtype):
        global _ARENA_HI
        n = 1
        for s in shape[1:]:
            n *= s
        nb = n * (4 if dtype == F32 else 2)
        nb = (nb + 63) // 64 * 64
        el = nb // 2
        a = self.off
        self.off += el
        self.hi = max(self.hi, self.off)
        assert self.off <= self.nel, ("arena overflow", self.off, self.nel)
        _ARENA_HI = max(_ARENA_HI, self.off)
        v = self.big[:, a:a + el]
        if dtype == F32:
            v = v.bitcast(F32)[:, 0:n]
        else:
            v = v[:, 0:n]
        if len(shape) == 3:
            v = v.rearrange("p (a b) -> p a b", a=shape[1])
        elif len(shape) == 4:
            v = v.rearrange("p (a b c) -> p a b c", a=shape[1], b=shape[2])
        return v


def _na_r0(r):
    return min(max(r - 4, 0), 24)


def na_blocks(i):
    res = []
    for j in range(NB):
        inval = []
        anyv = False
        for a in range(2):
            r = 2 * i + a
            for ak in range(2):
                rk = 2 * j + ak
                ok = _na_r0(r) <= rk <= _na_r0(r) + 7
                if ok:
                    anyv = True
                else:
                    inval.append((ak, a))
        if anyv:
            res.append((j, inval))
    return res


NF_NA = 0
NF_SW = 112
NF_DA = 160
NF = 160 + 256

ACC = {}
AW = 66
for _h in range(4):
    ACC[("sw", _h)] = (0, _h * AW)
for _h in range(3):
    ACC[("na", _h)] = (0, 4 * AW + _h * AW)
for _h in range(4):
    ACC[("sw", 4 + _h)] = (1, _h * AW)
ACC[("na", 3)] = (1, 4 * AW)
ACC[("da", 0)] = (1, 5 * AW)
ACC[("da", 1)] = (1, 6 * AW)
for _u in range(6):
    ACC[("da", 2 + _u)] = (2, _u * AW)


def build_program(n_layers=L, do_attn=True, do_ffn=True, taps=(), a1_blocks=NB, a2_blocks=NB, do_mod=True):
    nc = bass.Bass("TRN2", target_bir_lowering=False)
    S = Sched()
    taps = set(taps)
    dbg_outs = {}

    def din(name, shape, dt=F32):
        return nc.dram_tensor(name, list(shape), dt, kind="ExternalInput").ap()

    x_d = din("x", [2048, D])
    cvec_d = din("cvecT", [128, 8])
    cosb_d = din("cosb", [128, NB, 32])
    sinb_d = din("sinb", [128, NB, 32])
    cosc_d = din("cosc", [128, NB, 16])
    sinc_d = din("sinc", [128, NB, 16])
    bias_d = din("biascol", [128, NF])
    swm_d = din("swmask", [128, 2, 128], BF16)
    nam_d = din("namask", [128, 16, 64], BF16)
    rpb_d = din("rpbpad", [L, RPBLEN])
    ctxone_d = din("ctxone", [128, 1])
    pflag_d = din("pflag", [128, 1])
    ck_d = din("ck", [L, 512, 640])
    cv_d = din("cv", [L, 512, 640])
    wmod_d = din("w_mod", [L, D, 6 * D])
    wkv_d = din("w_kv", [L, D, 1280])
    wq_d = din("w_q", [L, D, 1024])
    wout_d = din("w_out", [L, D, D])
    wup_d = din("w_up", [L, D, 2 * DFF])
    wdn_d = din("w_down", [L, DFF, D])
    bmod_d = din("bmodT", [128, L * 48])
    gattn_d = din("gattnT", [128, L * 8])
    gffn_d = din("gffnT", [128, L * 8])
    convw_d = din("convwT", [128, L * 3 * 44])
    convb_d = din("convbT", [128, L * 44])
    small_d = din("small", [L, 520])
    identf_d = din("identf", [128, 128])
    identb_d = din("identb", [128, 128], BF16)
    permb_d = din("permb", [128, 128], BF16)
    onesb_d = din("onesb", [128, 128], BF16)
    rowmask_d = din("rowmask", [128, 6])

    y_d = nc.dram_tensor("y", [2048, D], F32, kind="ExternalOutput").ap()
    okv_d = nc.dram_tensor("okv", [L, 2048, 1280], F32, kind="ExternalOutput").ap()

    with ExitStack() as es:
        def sb(name, shape, dt=F32):
            return es.enter_context(nc.sbuf_tensor(name, list(shape), dt))

        XT = sb("XT", [128, KC, 2048])
        identf = sb("identf_s", [128, 128])
        identb = sb("identb_s", [128, 128], BF16)
        permb = sb("permb_s", [128, 128], BF16)
        onesb = sb("onesb_s", [128, 128], BF16)
        rowmask = sb("rowmask_s", [128, 6])
        cosb = sb("cosb_s", [128, NB, 32])
        sinb = sb("sinb_s", [128, NB, 32])
        cosc = sb("cosc_s", [128, NB, 16])
        sinc = sb("sinc_s", [128, NB, 16])
        biascol = sb("biascol_s", [128, NF])
        swmask = sb("swmask_s", [128, 2, 128], BF16)
        namask = sb("namask_s", [128, 16, 64], BF16)
        ctxone = sb("ctxone_s", [128, 1])
        pflag = sb("pflag_s", [128, 1])
        cvecT = sb("cvecT_s", [128, 8])
        silub = sb("silub", [128, 8], BF16)
        bmodT = sb("bmodT_s", [128, L * 48])
        modsb = sb("modsb", [128, L * 48])
        gattnT = sb("gattnT_s", [128, L * 8])
        gffnT = sb("gffnT_s", [128, L * 8])
        G1 = sb("G1", [128, L * 8])
        G2 = sb("G2", [128, L * 8])
        convwT = sb("convwT_s", [128, L * 3 * 44])
        convbT = sb("convbT_s", [128, L * 44])
        NBIG = 64000
        BIG = sb("BIG", [128, NBIG], BF16)
        banks = [es.enter_context(nc.psum_tensor("bank%d" % i, [128, 512], F32)) for i in range(8)]

        def BK(i):
            return "B%d" % i

        def tap(name, ap, shape, reads, dt=F32):
            if name not in taps:
                return
            d = nc.dram_tensor("dbg_" + name, list(shape), dt, kind="ExternalOutput").ap()
            dbg_outs[name] = d
            S.dma("sp", _C("dma_start", out=d, in_=ap), reads=reads, final_wait=True)

        def ld(dst, src, key):
            S.dma("sp", _C("dma_start", out=dst, in_=src), writes=[key])

        ld(identf[:], identf_d, "identf")
        ld(identb[:], identb_d, "identb")
        ld(permb[:], permb_d, "permb")
        ld(onesb[:], onesb_d, "onesb")
        ld(rowmask[:], rowmask_d, "rowmask")
        ld(cosb[:], cosb_d, "rope")
        ld(sinb[:], sinb_d, "rope")
        ld(cosc[:], cosc_d, "rope")
        ld(sinc[:], sinc_d, "rope")
        ld(biascol[:], bias_d, "biascol")
        ld(swmask[:], swm_d, "swmask")
        ld(namask[:], nam_d, "namask")
        ld(ctxone[:], ctxone_d, "ctxone")
        ld(pflag[:], pflag_d, "pflag")
        ld(cvecT[:], cvec_d, "cvecT")
        ld(bmodT[:], bmod_d, "bmodT")
        ld(gattnT[:], gattn_d, "gattnT")
        ld(gffnT[:], gffn_d, "gffnT")
        ld(convwT[:], convw_d, "convwT")
        ld(convbT[:], convb_d, "convbT")

        ar = Arena(BIG, NBIG)
        xs = [ar.take([128, 1024], F32) for _ in range(2)]
        wm = [ar.take([128, 8, 512], BF16) for _ in range(2)]
        for t in range(NB):
            b = t % 2
            S.dma("sp", _C("dma_start", out=xs[b], in_=x_d[t * 128:(t + 1) * 128, :]),
                  writes=["xs%d" % b])
            for half in range(2):
                bank = (2 * t + half) % 4
                for q in range(4):
                    c = half * 4 + q
                    S.op("pe", _C("transpose",
                        banks[bank][:, q * 128:(q + 1) * 128], xs[b][:, c * 128:(c + 1) * 128], identf[:]),
                        reads=["xs%d" % b, "identf"], writes=[BK(bank)])
                if half == 0:
                    S.op("act", _C("activation",
                        out=XT[:, 0:4, t * 128:(t + 1) * 128],
                        in_=banks[bank][:].rearrange("p (c n) -> p c n", c=4), func=AF.Copy),
                        reads=[BK(bank)], writes=["XT%d" % t])
                else:
                    S.op("dve", _C("tensor_copy",
                        out=XT[:, 4:8, t * 128:(t + 1) * 128],
                        in_=banks[bank][:].rearrange("p (c n) -> p c n", c=4)),
                        reads=[BK(bank)], writes=["XT%d" % t])

        S.op("act", _C("activation", out=silub[:], in_=cvecT[:], func=AF.Silu),
             reads=["cvecT"], writes=["silub"])
        MB = 4
        first_mod = True
        for l in range(n_layers if do_mod else 0):
            for piece in range(12):
                b = (l * 12 + piece) % 2
                S.dma("pool", _C("dma_start",
                    out=wm[b], in_=wmod_d[l, :, piece * 512:(piece + 1) * 512].rearrange("(kc p) n -> p kc n", p=128)),
                    writes=["wm%d" % b])
                for oc in range(4):
                    col = l * 48 + piece * 4 + oc
                    for kc in range(KC):
                        S.op("pe", _C("matmul",
                            banks[MB][:, col:col + 1], lhsT=wm[b][:, kc, oc * 128:(oc + 1) * 128],
                            rhs=silub[:, kc:kc + 1], start=first_mod, stop=(kc == KC - 1), skip_group_check=True),
                            reads=["wm%d" % b, "silub"], writes=[BK(MB)])
                        first_mod = False
        nm = n_layers * 48
        S.op("dve", _C("tensor_tensor", out=modsb[:, 0:nm], in0=banks[MB][:, 0:nm], in1=bmodT[:, 0:nm], op=ALU.add),
             reads=[BK(MB), "bmodT"], writes=["modsb"])
        for l in range(n_layers):
            S.op("dve", _C("scalar_tensor_tensor",
                out=G1[:, l * 8:(l + 1) * 8], in0=modsb[:, l * 48 + 8:l * 48 + 16], scalar=1.0,
                in1=gattnT[:, l * 8:(l + 1) * 8], op0=ALU.add, op1=ALU.mult),
                reads=["modsb", "gattnT"], writes=["G"])
            S.op("dve", _C("scalar_tensor_tensor",
                out=G2[:, l * 8:(l + 1) * 8], in0=modsb[:, l * 48 + 32:l * 48 + 40], scalar=1.0,
                in1=gffnT[:, l * 8:(l + 1) * 8], op0=ALU.add, op1=ALU.mult),
                reads=["modsb", "gffnT"], writes=["G"])
        tap("modsb", modsb[:], [128, L * 48], ["modsb"])
        tap("G1", G1[:], [128, L * 8], ["G"])

        def mcol(l, k, c):
            i0 = l * 48 + k * 8 + c
            return modsb[:, i0:i0 + 1]

        def adaln(dst_fn, tok0, w, Gt, l, kshift, sqbuf, rsbuf, tmps, sbank, tagp, xkeys=("XTh0", "XTh1")):
            xkeys = list(xkeys)
            for c in range(KC):
                S.op("act", _C("activation", out=sqbuf[c % 2][:, 0:w], in_=XT[:, c, tok0:tok0 + w], func=AF.Square),
                     reads=xkeys, writes=[tagp + "sq%d" % (c % 2)])
                S.op("pe", _C("matmul", banks[sbank][:, 0:w], lhsT=onesb[:], rhs=sqbuf[c % 2][:, 0:w],
                                                   start=(c == 0), stop=(c == KC - 1)),
                     reads=[tagp + "sq%d" % (c % 2), "onesb"], writes=[BK(sbank)])
            S.op("act", _C("activation", out=rsbuf[:, 0:w], in_=banks[sbank][:, 0:w], func=AF.Ln, scale=1.0 / D, bias=EPS),
                 reads=[BK(sbank)], writes=[tagp + "rs"])
            S.op("act", _C("activation", out=rsbuf[:, 0:w], in_=rsbuf[:, 0:w], func=AF.Exp, scale=-0.5),
                 reads=[tagp + "rs"], writes=[tagp + "rs"])
            for c in range(KC):
                tb = tmps[c % 2]
                S.op("dve", _C("scalar_tensor_tensor",
                    out=tb[:, 0:w], in0=XT[:, c, tok0:tok0 + w], scalar=Gt[:, l * 8 + c:l * 8 + c + 1],
                    in1=rsbuf[:, 0:w], op0=ALU.mult, op1=ALU.mult),
                    reads=xkeys + ["G", tagp + "rs"], writes=[tagp + "tmp%d" % (c % 2)])
                S.op("act", _C("activation",
                    out=dst_fn(c), in_=tb[:, 0:w], func=AF.Identity, bias=mcol(l, kshift, c), scale=1.0),
                    reads=[tagp + "tmp%d" % (c % 2), "modsb"], writes=[tagp + "h"])

        def adaln_blk(hdst, hkey, tok0, Gt, l, kshift, sq8, rsbuf, tmp8, sbank, tagp, tmpkey=None, offload=False, affine_dve=False, scol=0):
            w = 128
            if offload:
                S.op("dve", _C("tensor_tensor", out=sq8, in0=XT[:, :, tok0:tok0 + w], in1=XT[:, :, tok0:tok0 + w], op=ALU.mult),
                     reads=["XTh0", "XTh1"], writes=[tagp + "sq8"])
            else:
                S.op("act", _C("activation", out=sq8, in_=XT[:, :, tok0:tok0 + w], func=AF.Square),
                     reads=["XTh0", "XTh1"], writes=[tagp + "sq8"])
            for c in range(KC):
                S.op("pe", _C("matmul", banks[sbank][:, scol:scol + w], lhsT=onesb[:], rhs=sq8[:, c, :],
                              start=(c == 0), stop=(c == KC - 1)),
                     reads=[tagp + "sq8", "onesb"], writes=[BK(sbank)])
            S.op("act", _C("activation", out=rsbuf[:, 0:w], in_=banks[sbank][:, scol:scol + w], func=AF.Ln, scale=1.0 / D, bias=EPS),
                 reads=[BK(sbank)], writes=[tagp + "rs"])
            S.op("act", _C("activation", out=rsbuf[:, 0:w], in_=rsbuf[:, 0:w], func=AF.Exp, scale=-0.5),
                 reads=[tagp + "rs"], writes=[tagp + "rs"])
            S.op("dve", _C("tensor_tensor", out=tmp8, in0=XT[:, :, tok0:tok0 + w],
                           in1=rsbuf[:, 0:w].unsqueeze(1).broadcast_to([128, KC, w]), op=ALU.mult),
                 reads=["XTh0", "XTh1", tagp + "rs"], writes=[tmpkey or (tagp + "tmp8")])
            for c in range(KC):
                if offload or affine_dve:
                    S.op("dve", _C("tensor_scalar", out=hdst[:, c, :], in0=tmp8[:, c, :], scalar1=Gt[:, l * 8 + c:l * 8 + c + 1],
                                   scalar2=mcol(l, kshift, c), op0=ALU.mult, op1=ALU.add),
                         reads=[tmpkey or (tagp + "tmp8"), "modsb", "G"], writes=[hkey])
                else:
                    S.op("act", _C("activation", out=hdst[:, c, :], in_=tmp8[:, c, :], func=AF.Identity,
                                   bias=mcol(l, kshift, c), scale=Gt[:, l * 8 + c:l * 8 + c + 1]),
                         reads=[tmpkey or (tagp + "tmp8"), "modsb", "G"], writes=[hkey])

        for l in range(n_layers):
            lam_init = 0.8 - 0.6 * math.exp(-0.3 * l)
            S.barrier()
            ar = Arena(BIG, NBIG)
            KT = ar.take([128, 5, 2048], BF16)
            V = ar.take([128, NB, 10, 66], BF16)
            CTXKT = ar.take([128, 5, 512], BF16)
            CTXV = ar.take([128, 4, 10, 66], BF16)
            TAB = ar.take([128, 4, 16, 64], BF16)
            WA = ar.take([128, 8, 1280], BF16)
            sq8 = ar.take([128, KC, 128], BF16)
            hT = ar.take([128, KC, 128], BF16)
            rsb = ar.take([128, 128], F32)
            SM = ar.take([128, 520], F32)
            esink = ar.take([128, 8], F32)
            lamt = ar.take([128, 8], F32)
            SG = ar.take([128, 64], F32)
            smalls = ar.take([128, 64], F32)
            base_shared = ar.off
            kcats = [ar.take([128, 640], F32) for _ in range(2)]
            vcats = [ar.take([128, 640], F32) for _ in range(2)]
            sqks = [ar.take([128, 640], F32) for _ in range(2)]
            rts = [[ar.take([128, 64], F32) for _ in range(4)] for _ in range(2)]
            rt2s = [[ar.take([128, 128], F32) for _ in range(4)] for _ in range(2)]
            kbs = [ar.take([128, 640], BF16) for _ in range(2)]
            smks = [ar.take([128, 64], F32) for _ in range(2)]
            a0_base = ar.off
            CKs = ar.take([128, 4, 640], BF16)
            TABF = ar.take([128, 16, 64], F32)
            a0_hi = ar.off
            ar.off = a0_base
            tmp8as = [ar.take([128, KC, 128], F32) for _ in range(2)]
            sq8s = [sq8, ar.take([128, KC, 128], BF16)]
            rsbs = [rsb, ar.take([128, 128], F32)]
            hTs = [hT, ar.take([128, KC, 128], BF16)]
            ar.off = max(ar.off, a0_hi)
            hiA1 = ar.off
            ar.off = base_shared
            qf = ar.take([128, 1024], F32)
            sqq = ar.take([128, 1024], F32)
            rtq = [sqq[:, k_ * 256:(k_ + 1) * 256] for k_ in range(4)]
            qb = ar.take([128, 1024], BF16)
            QTs = [ar.take([128, 20, 128], BF16) for _ in range(2)]
            PT = [ar.take([128, 512], BF16) for _ in range(4)]
            Ofin = sqq[:, 0:512].rearrange("p (a d) -> p a d", a=8)
            ddt = sqq[:, 512:768].rearrange("p (a d) -> p a d", a=4)
            sqd = sqq[:, 768:1024].rearrange("p (a d) -> p a d", a=4)
            Omix = ar.take([128, 1024], BF16)
            rtq2 = [Omix[:, k_ * 256:(k_ + 1) * 256].bitcast(F32) for k_ in range(4)]
            OT = ar.take([128, 8, 256], BF16)
            AWW = 7 * AW
            Oacc = ar.take([128, 3, AWW], F32)
            wo = [WA[:, :, 1024 + 128 * k_:1024 + 128 * (k_ + 1)] for k_ in range(2)]
            wqv = WA[:, :, 0:1024]
            tmp8q = sqq.rearrange("p (c n) -> p c n", c=KC)

            if do_attn:
                S.dma("sp", _C("dma_start", out=SM, in_=small_d[l, :].partition_broadcast(128)), writes=["SM"])
                S.op("act", _C("activation", out=esink, in_=SM[:, 320:328], func=AF.Exp), reads=["SM"], writes=["esink"])
                lp = SM[:, 328:456].rearrange("p (a b d) -> p a b d", a=2, b=2)
                S.op("dve", _C("tensor_tensor", out=smalls[:, 0:64].rearrange("p (a d) -> p a d", a=2),
                                                      in0=lp[:, :, 0, :], in1=lp[:, :, 1, :], op=ALU.mult),
                     reads=["SM"], writes=["smalls"])
                S.op("dve", _C("tensor_reduce", out=lamt[:, 0:2], in_=smalls[:, 0:64].rearrange("p (a d) -> p a d", a=2),
                                                      axis=AX.X, op=ALU.add),
                     reads=["smalls"], writes=["lamt"])
                S.op("act", _C("activation", out=lamt[:, 2:4], in_=lamt[:, 0:2], func=AF.Exp), reads=["lamt"], writes=["lamt2"])
                S.op("dve", _C("tensor_tensor", out=lamt[:, 4:5], in0=lamt[:, 3:4], in1=lamt[:, 2:3], op=ALU.subtract),
                     reads=["lamt2"], writes=["lamt3"])
                S.op("dve", _C("tensor_scalar", out=lamt[:, 5:6], in0=lamt[:, 4:5], scalar1=-lam_init, scalar2=None, op0=ALU.add),
                     reads=["lamt3"], writes=["neglam"])
                S.op("dve", _C("tensor_scalar", out=SG, in0=SM[:, 456:520], scalar1=1.0 - lam_init, scalar2=None, op0=ALU.mult),
                     reads=["SM"], writes=["SG"])
                neglam = lamt[:, 5:6]
                S.dma("pool", _C("dma_start", out=CKs, in_=ck_d[l].rearrange("(b p) f -> p b f", p=128)), writes=["CKs"])
                for b4 in range(4):
                    bank = 6 + (b4 % 2)
                    pv = banks[bank][:].bitcast(BF16)
                    for ch in range(5):
                        S.op("pe", _C("transpose", pv[:, ch * 128:(ch + 1) * 128], CKs[:, b4, ch * 128:(ch + 1) * 128], identb[:]),
                             reads=["CKs", "identb"], writes=[BK(bank)])
                    S.op("act", _C("activation", out=CTXKT[:, :, b4 * 128:(b4 + 1) * 128],
                                                                   in_=pv[:, 0:640].rearrange("p (c n) -> p c n", c=5), func=AF.Copy),
                         reads=[BK(bank)], writes=["CTXKT"])
                for b4 in range(4):
                    S.dma("pool", _C("dma_start",
                        out=CTXV[:, b4, :, 0:64], in_=cv_d[l, b4 * 128:(b4 + 1) * 128, :].rearrange("p (h d) -> p h d", h=10)),
                        writes=["CTXVd"])
                S.op("pool", _C("tensor_copy", out=CTXV[:, :, :, 64], in_=ctxone[:, 0:1].unsqueeze(2).broadcast_to([128, 4, 10])),
                     reads=["ctxone"], writes=["CTXVo"])
                S.op("pool", _C("memset", V[:, :, :, 64], 1.0), writes=["Vones"])
                for h in range(4):
                    for ak in range(2):
                        off = PADOFF + h * 465 + (ak - 1) * 31 - 48
                        src = bass.AP(rpb_d.tensor, l * RPBLEN + off, [[1, 64], [31, 16], [1, 64]])
                        S.dma("sp", _C("dma_start", out=TABF[ak * 64:(ak + 1) * 64, :, :], in_=src),
                              writes=["TABF"])
                    S.op("act", _C("activation", out=TAB[:, h, :, :], in_=TABF, func=AF.Exp), reads=["TABF"], writes=["TAB"])
                    S.op("dve", _C("tensor_tensor", out=TAB[:, h, :, :], in0=TAB[:, h, :, :], in1=namask[:], op=ALU.mult),
                         reads=["TAB", "namask"], writes=["TAB"])
                if l == 0:
                    tap("TAB", TAB, [128, 4, 16, 64], ["TAB"], BF16)
                    tap("CTXKT", CTXKT, [128, 5, 512], ["CTXKT"], BF16)
                GN = SM
                S.dma("pool", _C("dma_start", out=WA, in_=wkv_d[l].rearrange("(kc p) n -> p kc n", p=128)), writes=["WA"])
                S.barrier()
                def rope2(eng, groups, tkey):
                    seqs = []
                    for gi, (view, H, half, cs, sn, key, tl, xr, xw) in enumerate(groups):
                        x1 = view[:, :, :, 0]
                        x2 = view[:, :, :, 1]
                        cb_ = cs.unsqueeze(1).broadcast_to([128, H, half])
                        sb_ = sn.unsqueeze(1).broadcast_to([128, H, half])
                        n_ = H * half
                        tv = [tm[:, 0:n_].rearrange("p (h d) -> p h d", h=H) for tm in tl]
                        tk = ["%s_%d_%d" % (tkey, gi, k_) for k_ in range(4)]
                        seqs.append([
                            (_C("tensor_tensor", out=tv[0], in0=x1, in1=cb_, op=ALU.mult), [key, "rope"] + xr, [tk[0]] + xw),
                            (_C("tensor_tensor", out=tv[1], in0=x2, in1=sb_, op=ALU.mult), [key, "rope"] + xr, [tk[1]] + xw),
                            (_C("tensor_tensor", out=tv[2], in0=x1, in1=sb_, op=ALU.mult), [key, "rope"] + xr, [tk[2]] + xw),
                            (_C("tensor_tensor", out=tv[3], in0=x2, in1=cb_, op=ALU.mult), [key, "rope"] + xr, [tk[3]] + xw),
                            (_C("tensor_tensor", out=x1, in0=tv[0], in1=tv[1], op=ALU.subtract), [tk[0], tk[1]], [key]),
                            (_C("tensor_tensor", out=x2, in0=tv[2], in1=tv[3], op=ALU.add), [tk[2], tk[3]], [key]),
                        ])
                    for k_ in range(6):
                        for sq_ in seqs:
                            fn_, rd_, wr_ = sq_[k_]
                            S.op(eng, fn_, reads=rd_, writes=wr_)

                def a1_block(t):
                    tok0 = t * 128
                    kcat = kcats[t % 2]
                    vcat = vcats[t % 2]
                    sfx = str(t % 2)
                    sqk = sqks[t % 2]
                    kb = kbs[t % 2]
                    rt = rts[t % 2]
                    rt2 = rt2s[t % 2]
                    smk_ = smks[t % 2]
                    pb = 0 if t % 2 == 0 else 3
                    hT_ = hTs[t % 2]
                    adaln_blk(hT_, "ah" + sfx, tok0, G1, l, 0, sq8s[t % 2], rsbs[t % 2], tmp8as[t % 2], pb + 2, "a" + sfx, affine_dve=True, scol=256)
                    for nt, (n0, w) in enumerate(((0, 512), (512, 512), (1024, 256))):
                        for kc in range(KC):
                            S.op("pe", _C("matmul",
                                banks[pb + nt][:, 0:w], lhsT=hT_[:, kc, :], rhs=WA[:, kc, n0:n0 + w],
                                start=(kc == 0), stop=(kc == KC - 1)),
                                reads=["ah" + sfx, "WA"], writes=[BK(pb + nt)])
                    S.op("act", _C("activation", out=kcat[:, 0:512], in_=banks[pb][:, :], func=AF.Copy),
                         reads=[BK(pb)], writes=["kc_a" + sfx, "kc_b" + sfx, "kc_c" + sfx])
                    S.op("act", _C("activation", out=kcat[:, 512:640], in_=banks[pb + 1][:, 0:128], func=AF.Copy),
                         reads=[BK(pb + 1)], writes=["kc_c" + sfx])
                    S.op("act", _C("activation", out=vcat[:, 0:384], in_=banks[pb + 1][:, 128:512], func=AF.Copy),
                         reads=[BK(pb + 1)], writes=["vcat" + sfx])
                    S.op("dve", _C("tensor_copy", out=vcat[:, 384:640], in_=banks[pb + 2][:, 0:256]),
                         reads=[BK(pb + 2)], writes=["vcat" + sfx])
                    a1_mark[0] = len(S.ops)
                    S.op("act", _C("activation", out=V[:, t, :, 0:64], in_=vcat.rearrange("p (h d) -> p h d", h=10), func=AF.Copy),
                         reads=["vcat" + sfx], writes=["V"])
                    S.dma("sp", _C("dma_start", out=okv_d[l, tok0:tok0 + 128, 640:1280], in_=vcat),
                          reads=["vcat" + sfx], final_wait=True)
                    S.op("act", _C("activation", out=sqk, in_=kcat, func=AF.Square), reads=["kc_a" + sfx, "kc_b" + sfx, "kc_c" + sfx], writes=["sqk" + sfx])
                    S.op("dve", _C("tensor_reduce", out=smk_[:, 0:6], in_=sqk[:, 0:384].rearrange("p (h d) -> p h d", h=6), axis=AX.X, op=ALU.add),
                         reads=["sqk" + sfx], writes=["smk" + sfx])
                    S.op("dve", _C("tensor_reduce", out=smk_[:, 6:14], in_=sqk[:, 384:640].rearrange("p (h d) -> p h d", h=8), axis=AX.X, op=ALU.add),
                         reads=["sqk" + sfx], writes=["smk" + sfx])
                    S.op("act", _C("activation", out=smk_[:, 16:22], in_=smk_[:, 0:6], func=AF.Ln, scale=1.0 / 64, bias=EPS),
                         reads=["smk" + sfx], writes=["smk" + sfx])
                    S.op("act", _C("activation", out=smk_[:, 22:30], in_=smk_[:, 6:14], func=AF.Ln, scale=1.0 / 32, bias=EPS),
                         reads=["smk" + sfx], writes=["smk" + sfx])
                    S.op("act", _C("activation", out=smk_[:, 32:46], in_=smk_[:, 16:30], func=AF.Exp, scale=-0.5),
                         reads=["smk" + sfx], writes=["smk" + sfx])
                    k64 = kcat[:, 0:384].rearrange("p (h d) -> p h d", h=6)
                    k32 = kcat[:, 384:640].rearrange("p (h d) -> p h d", h=8)
                    S.op("dve", _C("tensor_tensor", out=k64, in0=k64, in1=smk_[:, 32:38].unsqueeze(2).broadcast_to([128, 6, 64]), op=ALU.mult),
                         reads=["smk" + sfx, "kc_a" + sfx, "kc_b" + sfx], writes=["kc_a" + sfx, "kc_b" + sfx])
                    S.op("dve", _C("tensor_tensor", out=k32, in0=k32, in1=smk_[:, 38:46].unsqueeze(2).broadcast_to([128, 8, 32]), op=ALU.mult),
                         reads=["smk" + sfx, "kc_c" + sfx], writes=["kc_c" + sfx])
                    kna = kcat[:, 0:256].rearrange("p (h d) -> p h d", h=4)
                    ksw = kcat[:, 256:384].rearrange("p (h d) -> p h d", h=2)
                    S.op("dve", _C("tensor_tensor", out=kna, in0=kna, in1=GN[:, 64:128].unsqueeze(1).broadcast_to([128, 4, 64]), op=ALU.mult),
                         reads=["SM", "kc_a" + sfx], writes=["kc_a" + sfx])
                    S.op("dve", _C("tensor_tensor", out=ksw, in0=ksw, in1=GN[:, 192:256].unsqueeze(1).broadcast_to([128, 2, 64]), op=ALU.mult),
                         reads=["SM", "kc_b" + sfx], writes=["kc_b" + sfx])
                    S.op("dve", _C("tensor_tensor", out=k32, in0=k32, in1=GN[:, 288:320].unsqueeze(1).broadcast_to([128, 8, 32]), op=ALU.mult),
                         reads=["SM", "kc_c" + sfx], writes=["kc_c" + sfx])

                    rope2("dve", [
                        (kcat[:, 256:384].rearrange("p (h d two) -> p h d two", h=2, two=2), 2, 32, cosb[:, t, :], sinb[:, t, :], "kc_b" + sfx, rt, [], []),
                        (kcat[:, 384:640].rearrange("p (h d two) -> p h d two", h=8, two=2), 8, 16, cosc[:, t, :], sinc[:, t, :], "kc_c" + sfx, rt2, [], []),
                    ], "rtk" + sfx)
                    S.dma("sp", _C("dma_start", out=okv_d[l, tok0:tok0 + 128, 0:640], in_=kcat),
                          reads=["kc_a" + sfx, "kc_b" + sfx, "kc_c" + sfx], final_wait=True)
                    S.op("act", _C("activation", out=kb, in_=kcat, func=AF.Copy), reads=["kc_a" + sfx, "kc_b" + sfx, "kc_c" + sfx], writes=["kb" + sfx])
                    tb_ = 6 + t % 2
                    pv = banks[tb_][:].bitcast(BF16)
                    for ch in range(5):
                        S.op("pe", _C("transpose", pv[:, ch * 128:(ch + 1) * 128], kb[:, ch * 128:(ch + 1) * 128], identb[:]),
                             reads=["kb" + sfx, "identb"], writes=[BK(tb_)])
                    S.op("act", _C("activation", out=KT[:, :, tok0:tok0 + 128], in_=pv[:, 0:640].rearrange("p (c n) -> p c n", c=5), func=AF.Copy),
                         reads=[BK(tb_)], writes=["KT"])

                a1_mark = [0]

                def cap_a1(t):
                    saved = S.ops
                    S.ops = []
                    a1_block(t)
                    got = S.ops
                    S.ops = saved
                    return got[:a1_mark[0]], got[a1_mark[0]:]

                st_a1 = [cap_a1(t) for t in range(a1_blocks)]
                for t in range(0, a1_blocks, 2):
                    if t + 1 < a1_blocks:
                        la = st_a1[t][0] + st_a1[t][1]
                        lb = st_a1[t + 1][0] + st_a1[t + 1][1]
                        for k_ in range(max(len(la), len(lb))):
                            if k_ < len(la):
                                S.ops.append(la[k_])
                            if k_ < len(lb):
                                S.ops.append(lb[k_])
                    else:
                        S.ops.extend(st_a1[t][0] + st_a1[t][1])
                if l == 0:
                    tap("KT", KT, [128, 5, 2048], ["KT"], BF16)
                    tap("V", V, [128, NB, 10, 66], ["V", "Vones"], BF16)

                S.barrier()
                S.dma("pool", _C("dma_start", out=wqv, in_=wq_d[l].rearrange("(kc p) n -> p kc n", p=128)), writes=["WA"])
                sctr = [0]
                pctr = [0]
                woctr = [0]
                def front(i):
                    tok0 = i * 128
                    QT = QTs[i % 2]
                    qk_ = "QT%d" % (i % 2)
                    adaln_blk(hT, "ah", tok0, G1, l, 0, sq8, rsb, tmp8q, 0, "a", tmpkey="sqq", offload=True)
                    for nt in range(2):
                        for kc in range(KC):
                            S.op("pe", _C("matmul",
                                banks[nt][:, :], lhsT=hT[:, kc, :], rhs=wqv[:, kc, nt * 512:(nt + 1) * 512],
                                start=(kc == 0), stop=(kc == KC - 1)),
                                reads=["ah", "WA"], writes=[BK(nt)])
                    S.op("dve", _C("tensor_copy", out=qf[:, 0:512], in_=banks[0][:, :]), reads=[BK(0)], writes=["qf_a", "qf_b"])
                    S.op("dve", _C("tensor_copy", out=qf[:, 512:1024], in_=banks[1][:, :]), reads=[BK(1)], writes=["qf_b", "qf_c"])
                    S.op("dve", _C("tensor_tensor", out=sqq, in0=qf, in1=qf, op=ALU.mult), reads=["qf_a", "qf_b", "qf_c"], writes=["sqq"])
                    S.op("dve", _C("tensor_reduce", out=smalls[:, 0:12], in_=sqq[:, 0:768].rearrange("p (h d) -> p h d", h=12), axis=AX.X, op=ALU.add),
                         reads=["sqq"], writes=["smalls"])
                    S.op("dve", _C("tensor_reduce", out=smalls[:, 12:20], in_=sqq[:, 768:1024].rearrange("p (h d) -> p h d", h=8), axis=AX.X, op=ALU.add),
                         reads=["sqq"], writes=["smalls"])
                    S.op("act", _C("activation", out=smalls[:, 20:32], in_=smalls[:, 0:12], func=AF.Ln, scale=1.0 / 64, bias=EPS),
                         reads=["smalls"], writes=["smalls"])
                    S.op("act", _C("activation", out=smalls[:, 32:40], in_=smalls[:, 12:20], func=AF.Ln, scale=1.0 / 32, bias=EPS),
                         reads=["smalls"], writes=["smalls"])
                    S.op("act", _C("activation", out=smalls[:, 40:60], in_=smalls[:, 20:40], func=AF.Exp, scale=-0.5),
                         reads=["smalls"], writes=["smalls"])
                    q64 = qf[:, 0:768].rearrange("p (h d) -> p h d", h=12)
                    q32 = qf[:, 768:1024].rearrange("p (h d) -> p h d", h=8)
                    S.op("dve", _C("tensor_tensor", out=q64, in0=q64, in1=smalls[:, 40:52].unsqueeze(2).broadcast_to([128, 12, 64]), op=ALU.mult),
                         reads=["smalls", "qf_a", "qf_b"], writes=["qf_a", "qf_b"])
                    S.op("dve", _C("tensor_tensor", out=q32, in0=q32, in1=smalls[:, 52:60].unsqueeze(2).broadcast_to([128, 8, 32]), op=ALU.mult),
                         reads=["smalls", "qf_c"], writes=["qf_c"])
                    qna = qf[:, 0:256].rearrange("p (h d) -> p h d", h=4)
                    qsw = qf[:, 256:768].rearrange("p (h d) -> p h d", h=8)
                    S.op("dve", _C("tensor_tensor", out=qna, in0=qna, in1=GN[:, 0:64].unsqueeze(1).broadcast_to([128, 4, 64]), op=ALU.mult),
                         reads=["SM", "qf_a"], writes=["qf_a"])
                    S.op("dve", _C("tensor_tensor", out=qsw, in0=qsw, in1=GN[:, 128:192].unsqueeze(1).broadcast_to([128, 8, 64]), op=ALU.mult),
                         reads=["SM", "qf_b"], writes=["qf_b"])
                    S.op("dve", _C("tensor_tensor", out=q32, in0=q32, in1=GN[:, 256:288].unsqueeze(1).broadcast_to([128, 8, 32]), op=ALU.mult),
                         reads=["SM", "qf_c"], writes=["qf_c"])
                    rope2("dve", [
                        (qf[:, 256:768].rearrange("p (h d two) -> p h d two", h=8, two=2), 8, 32, cosb[:, i, :], sinb[:, i, :], "qf_b", rtq, ["sqq"], []),
                        (qf[:, 768:1024].rearrange("p (h d two) -> p h d two", h=8, two=2), 8, 16, cosc[:, i, :], sinc[:, i, :], "qf_c", rtq2, [], ["Omix"]),
                    ], "rtq")
                    S.op("dve", _C("tensor_copy", out=qb, in_=qf), reads=["qf_a", "qf_b", "qf_c"], writes=["qb"])
                    if l == 0 and i == 1:
                        tap("qf", qf, [128, 1024], ["qf_a", "qf_b", "qf_c"])
                    for ch in range(8):
                        bk = ch // 4
                        S.op("pe", _C("matmul",
                            banks[bk][:, (ch % 4) * 128:(ch % 4 + 1) * 128], lhsT=qb[:, ch * 128:(ch + 1) * 128],
                            rhs=(permb[:] if ch < 2 else identb[:]), start=True, stop=True, skip_group_check=True),
                            reads=["qb", "permb", "identb"], writes=[BK(bk)])
                    b0 = banks[0][:].rearrange("p (c n) -> p c n", c=4)
                    b1 = banks[1][:].rearrange("p (c n) -> p c n", c=4)
                    QTn = QT[:, 0:4, :].rearrange("p (c two) n -> p c two n", two=2)
                    for hl in range(2):
                        S.op("dve", _C("tensor_scalar", out=QTn[:, :, hl, :], in0=b0[:, 0:2, :], scalar1=rowmask[:, hl:hl + 1], scalar2=None, op0=ALU.mult),
                             reads=[BK(0), "rowmask"], writes=[qk_])
                    for g in range(2):
                        S.op("dve", _C("tensor_scalar", out=QT[:, 4 + 4 * g:6 + 4 * g, :], in0=b0[:, 2:4, :], scalar1=rowmask[:, g:g + 1], scalar2=None, op0=ALU.mult),
                             reads=[BK(0), "rowmask"], writes=[qk_])
                        S.op("dve", _C("tensor_scalar", out=QT[:, 6 + 4 * g:8 + 4 * g, :], in0=b1[:, 0:2, :], scalar1=rowmask[:, g:g + 1], scalar2=None, op0=ALU.mult),
                             reads=[BK(1), "rowmask"], writes=[qk_])
                    QTd = QT[:, 12:20, :].rearrange("p (hf u) n -> p hf u n", u=4)
                    for u in range(4):
                        S.op("dve", _C("tensor_scalar", out=QTd[:, :, u, :], in0=b1[:, 2:4, :], scalar1=rowmask[:, 2 + u:3 + u], scalar2=None, op0=ALU.mult),
                             reads=[BK(1), "rowmask"], writes=[qk_])

                def steps_and_back(i, fe_next):
                    QT = QTs[i % 2]
                    qk_ = "QT%d" % (i % 2)
                    steps = []
                    for (j, inval) in na_blocks(i):
                        steps.append(("na", j, inval))
                    for b4 in range(4):
                        steps.append(("nac", b4, None))
                    for g in range(2):
                        for j in (i - 1, i, i + 1):
                            if 0 <= j < NB:
                                steps.append(("sw", j, g))
                        for b4 in range(4):
                            steps.append(("swc", b4, g))
                    for j in range(NB):
                        for hf in range(2):
                            steps.append(("da", j, hf))
                    for b4 in range(4):
                        for hf in range(2):
                            steps.append(("dac", b4, hf))

                    acc_first = [True, True, True]

                    def emit_qk(st, sbk):
                        kind, j, x = st
                        if kind in ("na", "nac"):
                            for hp in range(2):
                                if kind == "na":
                                    lh = KT[:, hp, j * 128:(j + 1) * 128]
                                    rk_ = ["KT"]
                                else:
                                    lh = CTXKT[:, hp, j * 128:(j + 1) * 128]
                                    rk_ = ["CTXKT"]
                                S.op("pe", _C("matmul",
                                    banks[sbk][:, hp * 256:(hp + 1) * 256], lhsT=lh, rhs=QT[:, 2 * hp:2 * hp + 2, :],
                                    start=True, stop=True, skip_group_check=True),
                                    reads=rk_ + [qk_], writes=[BK(sbk)])
                        elif kind in ("sw", "swc"):
                            g = x
                            if kind == "sw":
                                lh = KT[:, 2, j * 128:(j + 1) * 128]
                                rk_ = ["KT"]
                            else:
                                lh = CTXKT[:, 2, j * 128:(j + 1) * 128]
                                rk_ = ["CTXKT"]
                            S.op("pe", _C("matmul",
                                banks[sbk][:, :], lhsT=lh, rhs=QT[:, 4 + 4 * g:8 + 4 * g, :],
                                start=True, stop=True, skip_group_check=True),
                                reads=rk_ + [qk_], writes=[BK(sbk)])
                        else:
                            hf = x
                            if kind == "da":
                                lh = KT[:, 3 + hf, j * 128:(j + 1) * 128]
                                rk_ = ["KT"]
                            else:
                                lh = CTXKT[:, 3 + hf, j * 128:(j + 1) * 128]
                                rk_ = ["CTXKT"]
                            S.op("pe", _C("matmul",
                                banks[sbk][:, :], lhsT=lh, rhs=QT[:, 12 + 4 * hf:16 + 4 * hf, :],
                                start=True, stop=True, skip_group_check=True),
                                reads=rk_ + [qk_], writes=[BK(sbk)])

                    def emit_exp(st, sbk, pbi):
                        kind, j, x = st
                        P = PT[pbi]
                        pk = "PT%d" % pbi
                        if kind == "na":
                            bc = biascol[:, NF_NA + i * 7 + (j - i + 3):NF_NA + i * 7 + (j - i + 3) + 1]
                            sc = 0.125
                        elif kind == "sw":
                            bc = biascol[:, NF_SW + i * 3 + (j - i + 1):NF_SW + i * 3 + (j - i + 1) + 1]
                            sc = 0.125
                        elif kind == "da":
                            bc = biascol[:, NF_DA + i * 16 + j:NF_DA + i * 16 + j + 1]
                            sc = 32 ** -0.5
                        elif kind == "dac":
                            bc = 0.0
                            sc = 32 ** -0.5
                        else:
                            bc = 0.0
                            sc = 0.125
                        S.op("act", _C("activation", out=P, in_=banks[sbk][:, :], func=AF.Exp, bias=bc, scale=sc),
                             reads=[BK(sbk), "biascol"], writes=[pk])
                        if kind == "na":
                            e0 = 2 * (j - i) + 7
                            Pv = P.rearrange("p (h n) -> p h n", h=4)
                            tv = TAB[:, :, e0:e0 + 2, :].rearrange("p h a c -> p h (a c)")
                            S.op("dve", _C("tensor_tensor", out=Pv, in0=Pv, in1=tv, op=ALU.mult),
                                 reads=[pk, "TAB"], writes=[pk])
                            for (ak, a) in x:
                                arr = 1 - a
                                S.op("dve", _C("memset",
                                    P[ak * 64:(ak + 1) * 64, :].rearrange("p (h n) -> p h n", h=4)[:, :, arr * 64:(arr + 1) * 64], 0.0),
                                    reads=[pk], writes=[pk])
                        elif kind == "sw" and j != i:
                            which = 0 if j < i else 1
                            Pv = P.rearrange("p (h n) -> p h n", h=4)
                            mv = swmask[:, which, :].unsqueeze(1).broadcast_to([128, 4, 128])
                            S.op("dve", _C("tensor_tensor", out=Pv, in0=Pv, in1=mv, op=ALU.mult),
                                 reads=[pk, "swmask"], writes=[pk])

                    def emit_pv(st, pbi):
                        kind, j, x = st
                        P = PT[pbi]
                        pk = "PT%d" % pbi
                        for m in range(4):
                            if kind in ("na", "nac"):
                                key = ("na", m)
                                vh = m
                            elif kind in ("sw", "swc"):
                                key = ("sw", 4 * x + m)
                                vh = 4 + x
                            else:
                                hh_ = 2 * x + m // 2
                                key = ("da", 4 * x + m)
                                vh = 6 + hh_
                            slot, col = ACC[key]
                            bk = 2 + slot
                            if kind in ("na", "sw", "da"):
                                rv = V[:, j, vh, 0:65]
                                rk_ = ["V", "Vones"]
                            else:
                                rv = CTXV[:, j, vh, 0:65]
                                rk_ = ["CTXVd", "CTXVo"]
                            st_ = acc_first[slot]
                            acc_first[slot] = False
                            S.op("pe", _C("matmul",
                                banks[bk][:, col:col + 65], lhsT=P[:, m * 128:(m + 1) * 128], rhs=rv,
                                start=st_, stop=False, skip_group_check=True),
                                reads=[pk] + rk_, writes=[BK(bk)])

                    nst = len(steps)
                    sb_of = []
                    pb_of = []
                    for s_ in range(nst):
                        sb_of.append(5 + (sctr[0] % 3))
                        sctr[0] += 1
                        pb_of.append(pctr[0] % 4)
                        pctr[0] += 1
                    LA = 2
                    for s_ in range(min(LA, nst)):
                        emit_qk(steps[s_], sb_of[s_])
                        emit_exp(steps[s_], sb_of[s_], pb_of[s_])
                    per = (len(fe_next) + nst - 1) // nst if fe_next else 0
                    fpos = 0
                    for s_ in range(nst):
                        if s_ + LA < nst:
                            emit_qk(steps[s_ + LA], sb_of[s_ + LA])
                            emit_exp(steps[s_ + LA], sb_of[s_ + LA], pb_of[s_ + LA])
                        emit_pv(steps[s_], pb_of[s_])
                        if fpos < len(fe_next):
                            S.ops.extend(fe_next[fpos:fpos + per])
                            fpos += per
                    S.ops.extend(fe_next[fpos:])

                    S.op("dve", _C("tensor_copy", out=Oacc[:, 0, :], in_=banks[2][:, 0:AWW]), reads=[BK(2)], writes=["Oacc"])
                    S.op("act", _C("activation", out=Oacc[:, 1, :], in_=banks[3][:, 0:AWW], func=AF.Copy), reads=[BK(3)], writes=["Oacc"])
                    S.op("dve", _C("tensor_copy", out=Oacc[:, 2, :], in_=banks[4][:, 0:AWW]), reads=[BK(4)], writes=["Oacc"])

                def back(i):
                    def accv(kind, lo, n):
                        slot, col = ACC[(kind, lo)]
                        return Oacc[:, slot, col:col + AW * n].rearrange("p (h d) -> p h d", h=n), "Oacc"

                    runs = [("sw", 0, 4, 256), ("na", 0, 3, 0), ("sw", 4, 4, 512), ("na", 3, 1, 192), ("da", 0, 2, None), ("da", 2, 6, None)]
                    rec = smalls
                    ri = 0
                    for (kind, lo, n, ocol) in runs:
                        av, bkey = accv(kind, lo, n)
                        rr = rec[:, ri:ri + n]
                        if kind == "sw":
                            S.op("dve", _C("tensor_tensor", out=rr, in0=av[:, :, 64], in1=esink[:, lo:lo + n], op=ALU.add),
                                 reads=[bkey, "esink"], writes=["smalls"])
                            S.op("dve", _C("reciprocal", out=rr, in_=rr), reads=["smalls"], writes=["smalls"])
                        else:
                            S.op("dve", _C("reciprocal", out=rr, in_=av[:, :, 64]), reads=[bkey], writes=["smalls"])
                        if kind == "da":
                            ov = Ofin[:, lo:lo + n, :]
                            okey = "sqq"
                        else:
                            ov = Omix[:, ocol:ocol + 64 * n].rearrange("p (h d) -> p h d", h=n)
                            okey = "Omix"
                        S.op("dve", _C("tensor_tensor",
                            out=ov, in0=av[:, :, 0:64], in1=rr.unsqueeze(2).broadcast_to([128, n, 64]), op=ALU.mult),
                            reads=[bkey, "smalls"], writes=[okey])
                        ri += n
                    O4 = Ofin.rearrange("p (h c) d -> p h c d", c=2)
                    S.op("dve", _C("scalar_tensor_tensor", out=ddt, in0=O4[:, :, 1, :], scalar=neglam, in1=O4[:, :, 0, :], op0=ALU.mult, op1=ALU.add),
                         reads=["sqq", "neglam"], writes=["sqq"])
                    S.op("act", _C("activation", out=sqd, in_=ddt, func=AF.Square), reads=["sqq"], writes=["sqq"])
                    S.op("dve", _C("tensor_reduce", out=rec[:, 24:28], in_=sqd, axis=AX.X, op=ALU.add), reads=["sqq"], writes=["smalls"])
                    S.op("act", _C("activation", out=rec[:, 28:32], in_=rec[:, 24:28], func=AF.Ln, scale=1.0 / 64, bias=EPS), reads=["smalls"], writes=["smalls"])
                    S.op("act", _C("activation", out=rec[:, 32:36], in_=rec[:, 28:32], func=AF.Exp, scale=-0.5), reads=["smalls"], writes=["smalls"])
                    S.op("dve", _C("tensor_tensor", out=ddt, in0=ddt, in1=rec[:, 32:36].unsqueeze(2).broadcast_to([128, 4, 64]), op=ALU.mult),
                         reads=["smalls", "sqq"], writes=["sqq"])
                    S.op("dve", _C("tensor_tensor", out=Omix[:, 768:1024].rearrange("p (h d) -> p h d", h=4), in0=ddt,
                                                          in1=SG.unsqueeze(1).broadcast_to([128, 4, 64]), op=ALU.mult),
                         reads=["sqq", "SG"], writes=["Omix"])
                    if l == 0 and i == 1:
                        tap("Omix", Omix, [128, 1024], ["Omix"], BF16)
                    iq = i % 2
                    for rnd in range(2):
                        for q4 in range(4):
                            ch = rnd * 4 + q4
                            S.op("pe", _C("matmul",
                                banks[0][:, q4 * 128:(q4 + 1) * 128], lhsT=Omix[:, ch * 128:(ch + 1) * 128],
                                rhs=(permb[:] if ch < 2 else identb[:]), start=True, stop=True, skip_group_check=True),
                                reads=["Omix", "permb", "identb"], writes=[BK(0)])
                        S.op("dve", _C("tensor_copy",
                            out=OT[:, rnd * 4:(rnd + 1) * 4, iq * 128:(iq + 1) * 128],
                            in_=banks[0][:].rearrange("p (c n) -> p c n", c=4)),
                            reads=[BK(0)], writes=["OT"])
                    if iq == 1:
                        t0p = (i - 1) * 128
                        for co in range(KC):
                            wb_ = woctr[0] % 2
                            woctr[0] += 1
                            S.dma("pool", _C("dma_start",
                                out=wo[wb_], in_=wout_d[l, :, co * 128:(co + 1) * 128].rearrange("(kc p) n -> p kc n", p=128)),
                                writes=["wo%d" % wb_])
                            for kc in range(KC):
                                S.op("pe", _C("matmul",
                                    banks[1][:, 0:256], lhsT=wo[wb_][:, kc, :], rhs=OT[:, kc, :],
                                    start=(kc == 0), stop=(kc == KC - 1)),
                                    reads=["wo%d" % wb_, "OT"], writes=[BK(1)])
                            S.op("dve", _C("scalar_tensor_tensor",
                                out=XT[:, co, t0p:t0p + 256], in0=banks[1][:, 0:256], scalar=mcol(l, 2, co),
                                in1=XT[:, co, t0p:t0p + 256], op0=ALU.mult, op1=ALU.add),
                                reads=[BK(1), "modsb", "XTh0", "XTh1"], writes=["XTh0", "XTh1"])


                def capture(i):
                    saved = S.ops
                    S.ops = []
                    front(i)
                    got = S.ops
                    S.ops = saved
                    return got

                def capture_back(i):
                    saved = S.ops
                    S.ops = []
                    back(i)
                    got = S.ops
                    S.ops = saved
                    return got

                if a2_blocks > 0:
                    S.ops.extend(capture(0))
                pending = []
                for i in range(a2_blocks):
                    fe_next = capture(i + 1) if i + 1 < a2_blocks else []
                    steps_and_back(i, fe_next + pending)
                    pending = capture_back(i)
                S.ops.extend(pending)

            if do_ffn:
                S.barrier()
                ar = Arena(BIG, NBIG)
                ACTT = ar.take([128, NJ, 1024], BF16)
                H2T = ar.take([128, KC, 1026], BF16)
                wg = [ar.take([128, 8, 256], BF16) for _ in range(2)]
                wv = [ar.take([128, 8, 256], BF16) for _ in range(2)]
                wd = [ar.take([128, NJ, 128], BF16) for _ in range(2)]
                sqf = [ar.take([128, 512], BF16) for _ in range(2)]
                rs2 = ar.take([128, 512], F32)
                tmpf = [ar.take([128, 512], F32) for _ in range(2)]
                ugs2 = [ar.take([128, 1026], F32) for _ in range(2)]
                uvs2 = [ar.take([128, 1026], F32) for _ in range(2)]
                usets = [(0, 1), (2, 3), (5, 6)]
                uctr = [0]
                prev_j = [None]
                ygs = [ar.take([128, 1024], F32) for _ in range(2)]
                yv = ar.take([128, 1024], F32)
                cwn = ar.take([128, 2, 44], F32)
                halo_save = ar.take([128, KC, 1], BF16)
                for tapi, slot in ((0, 0), (2, 1)):
                    c0 = (l * 3 + tapi) * 44
                    S.op("dve", _C("tensor_scalar",
                        out=cwn[:, slot, :], in0=convwT[:, c0:c0 + 44], scalar1=pflag[:, 0:1], scalar2=-1.0, op0=ALU.mult, op1=ALU.mult),
                        reads=["convwT", "pflag"], writes=["cwn"])
                pctr2 = [0]
                def ffn_f1(hh):
                    h0 = hh * 1024
                    if hh == 0:
                        segs = [(1, 513), (513, 1025), (1025, 1026)]
                        zc = 0
                    else:
                        segs = [(0, 1), (1, 513), (513, 1025)]
                        zc = 1025
                    S.op("pool", _C("memset", H2T[:, :, zc:zc + 1], 0.0), writes=["fh"])
                    for (n0, n1) in segs:
                        w = n1 - n0
                        tok0 = h0 - 1 + n0
                        if hh == 1 and n0 == 0:
                            S.op("pool", _C("tensor_copy", out=H2T[:, :, 0:1], in_=halo_save), reads=["halo_save"], writes=["fh"])
                            continue
                        adaln(lambda c, n0=n0, n1=n1: H2T[:, c, n0:n1], tok0, w, G2, l, 3, sqf, rs2, tmpf, 5, "f", xkeys=(["XTh1"] if hh == 1 else ["XTh0", "XTh1"]))
                    if hh == 0:
                        S.op("pool", _C("tensor_copy", out=halo_save, in_=H2T[:, :, 1024:1025]), reads=["fh"], writes=["halo_save"])
                def ffn_f2(hh):
                    h0 = hh * 1024
                    def ffn_post(j):
                        ub = j % 2
                        yg = ygs[j % 2]
                        ygk = "yg%d" % (j % 2)
                        for (us, ukey, yy, ykey, chn) in ((ugs2[ub], "ugs%d" % ub, yg, ygk, j), (uvs2[ub], "uvs%d" % ub, yv, "yv", NJ + j)):
                            cw0 = convwT[:, (l * 3 + 0) * 44 + chn:(l * 3 + 0) * 44 + chn + 1]
                            cw1 = convwT[:, (l * 3 + 1) * 44 + chn:(l * 3 + 1) * 44 + chn + 1]
                            cw2 = convwT[:, (l * 3 + 2) * 44 + chn:(l * 3 + 2) * 44 + chn + 1]
                            cbb = convbT[:, l * 44 + chn:l * 44 + chn + 1]
                            S.op("act", _C("activation", out=yy, in_=us[:, 1:1025], func=AF.Identity, bias=cbb, scale=cw1),
                                reads=[ukey, "convwT", "convbT"], writes=[ykey])
                            S.op("dve", _C("scalar_tensor_tensor",
                                out=yy, in0=us[:, 0:1024], scalar=cw0, in1=yy, op0=ALU.mult, op1=ALU.add),
                                reads=[ukey, "convwT", ykey], writes=[ykey])
                            S.op("dve", _C("scalar_tensor_tensor",
                                out=yy, in0=us[:, 2:1026], scalar=cw2, in1=yy, op0=ALU.mult, op1=ALU.add),
                                reads=[ukey, "convwT", ykey], writes=[ykey])
                            y4 = yy.rearrange("p (s n) -> p s n", s=4)
                            u0 = us[:, 0:1024].rearrange("p (s n) -> p s n", s=4)
                            u2 = us[:, 2:1026].rearrange("p (s n) -> p s n", s=4)
                            S.op("dve", _C("scalar_tensor_tensor",
                                out=y4[:, :, 0:1], in0=u0[:, :, 0:1], scalar=cwn[:, 0, chn:chn + 1], in1=y4[:, :, 0:1], op0=ALU.mult, op1=ALU.add),
                                reads=[ukey, "cwn", ykey], writes=[ykey])
                            S.op("dve", _C("scalar_tensor_tensor",
                                out=y4[:, :, 255:256], in0=u2[:, :, 255:256], scalar=cwn[:, 1, chn:chn + 1], in1=y4[:, :, 255:256], op0=ALU.mult, op1=ALU.add),
                                reads=[ukey, "cwn", ykey], writes=[ykey])
                        S.op("act", _C("activation", out=yg, in_=yg, func=AF.Silu), reads=[ygk], writes=[ygk])
                        S.op("dve", _C("tensor_tensor", out=ACTT[:, j, :], in0=yg, in1=yv, op=ALU.mult),
                             reads=[ygk, "yv"], writes=["ACTT"])

                    for c_ in range(2):
                        S.dma("pool", _C("dma_start",
                            out=wd[c_], in_=wdn_d[l, :, c_ * 128:(c_ + 1) * 128].rearrange("(j q) n -> q j n", q=128)),
                            writes=["wd%d" % c_])
                    for p in range(11):
                        b = pctr2[0] % 2
                        pctr2[0] += 1
                        S.dma("pool", _C("dma_start",
                            out=wg[b], in_=wup_d[l, :, p * 256:(p + 1) * 256].rearrange("(kc q) n -> q kc n", q=128)),
                            writes=["wg%d" % b])
                        S.dma("pool", _C("dma_start",
                            out=wv[b], in_=wup_d[l, :, DFF + p * 256:DFF + (p + 1) * 256].rearrange("(kc q) n -> q kc n", q=128)),
                            writes=["wv%d" % b])
                        for sub in range(2):
                            j = 2 * p + sub
                            ub = j % 2
                            for (wt, wkey, tgt, tkey, hb) in ((wg[b], "wg%d" % b, ugs2[ub], "ugs%d" % ub, 4), (wv[b], "wv%d" % b, uvs2[ub], "uvs%d" % ub, 7)):
                                bset = usets[uctr[0] % 3]
                                hc = 2 * (uctr[0] % 8)
                                uctr[0] += 1
                                for part in range(3):
                                    if part < 2:
                                        ob = banks[bset[part]][:, :]
                                        okey = BK(bset[part])
                                        rsl = (part * 512, part * 512 + 512)
                                    else:
                                        ob = banks[hb][:, hc:hc + 2]
                                        okey = BK(hb)
                                        rsl = (1024, 1026)
                                    for kc in range(KC):
                                        S.op("pe", _C("matmul",
                                            ob, lhsT=wt[:, kc, sub * 128:(sub + 1) * 128], rhs=H2T[:, kc, rsl[0]:rsl[1]],
                                            start=(kc == 0), stop=(kc == KC - 1), skip_group_check=True),
                                            reads=[wkey, "fh"], writes=[okey])
                                S.op("act", _C("activation", out=tgt[:, 0:512], in_=banks[bset[0]][:, :], func=AF.Copy), reads=[BK(bset[0])], writes=[tkey])
                                S.op("act", _C("activation", out=tgt[:, 512:1024], in_=banks[bset[1]][:, :], func=AF.Copy), reads=[BK(bset[1])], writes=[tkey])
                                S.op("act", _C("activation", out=tgt[:, 1024:1026], in_=banks[hb][:, hc:hc + 2], func=AF.Copy), reads=[BK(hb)], writes=[tkey])
                            if prev_j[0] is not None:
                                ffn_post(prev_j[0])
                            prev_j[0] = j
                    ffn_post(prev_j[0])
                    prev_j[0] = None
                def ffn_f3(hh):
                    h0 = hh * 1024
                    for c in range(KC):
                        b = c % 2
                        if c >= 2:
                          S.dma("pool", _C("dma_start",
                            out=wd[b], in_=wdn_d[l, :, c * 128:(c + 1) * 128].rearrange("(j q) n -> q j n", q=128)),
                            writes=["wd%d" % b])
                        for tg in range(2):
                            ob = 6 + tg
                            for j in range(NJ):
                                S.op("pe", _C("matmul",
                                    banks[ob][:, :], lhsT=wd[b][:, j, :], rhs=ACTT[:, j, tg * 512:(tg + 1) * 512],
                                    start=(j == 0), stop=(j == NJ - 1)),
                                    reads=["wd%d" % b, "ACTT"], writes=[BK(ob)])
                            ta = h0 + tg * 512
                            S.op("dve", _C("scalar_tensor_tensor",
                                out=XT[:, c, ta:ta + 512], in0=banks[ob][:, :], scalar=mcol(l, 5, c),
                                in1=XT[:, c, ta:ta + 512], op0=ALU.mult, op1=ALU.add),
                                reads=[BK(ob), "modsb", "XTh%d" % hh], writes=["XTh%d" % hh])


                def cap(fn, hh):
                    saved = S.ops
                    S.ops = []
                    fn(hh)
                    got = S.ops
                    S.ops = saved
                    return got

                ffn_f1(0)
                ffn_f2(0)
                fa = cap(ffn_f1, 1)
                fb = cap(ffn_f3, 0)
                per_ = max(1, len(fb) // max(1, len(fa)))
                ia = 0
                for k_, o_ in enumerate(fb):
                    S.ops.append(o_)
                    if k_ % per_ == per_ - 1 and ia < len(fa):
                        S.ops.append(fa[ia])
                        ia += 1
                S.ops.extend(fa[ia:])
                ffn_f2(1)
                ffn_f3(1)

        S.barrier()
        ar = Arena(BIG, NBIG)
        ys = [ar.take([128, 1024], F32) for _ in range(2)]
        for t in range(NB):
            b = t % 2
            for half in range(2):
                bank = (2 * t + half) % 4
                for q in range(4):
                    c = half * 4 + q
                    S.op("pe", _C("transpose",
                        banks[bank][:, q * 128:(q + 1) * 128], XT[:, c, t * 128:(t + 1) * 128], identf[:]),
                        reads=["XTh0", "XTh1", "XT%d" % t, "identf"], writes=[BK(bank)])
                if half == 0:
                    S.op("act", _C("activation", out=ys[b][:, 0:512], in_=banks[bank][:, :], func=AF.Copy),
                         reads=[BK(bank)], writes=["ys%d_0" % b])
                else:
                    S.op("dve", _C("tensor_copy", out=ys[b][:, 512:1024], in_=banks[bank][:, :]),
                         reads=[BK(bank)], writes=["ys%d_1" % b])
            S.dma("sp", _C("dma_start", out=y_d[t * 128:(t + 1) * 128, :], in_=ys[b]),
                  reads=["ys%d_0" % b, "ys%d_1" % b], final_wait=True)
        S.emit(nc, es)
    return nc, dbg_outs


def _rope_tables(dim):
    n = dim // 4
    inv = (1.0 / (10000.0 ** (np.arange(n, dtype=np.float32) / np.float32(n)))).astype(np.float32)
    t = np.arange(2048)
    row = (t // 64).astype(np.float32)
    col = (t % 64).astype(np.float32)
    ang = np.concatenate([row[:, None] * inv, col[:, None] * inv], axis=-1).astype(np.float32)
    return np.cos(ang).astype(np.float32), np.sin(ang).astype(np.float32)


def _tok_major(a):
    return np.ascontiguousarray(a.reshape(NB, 128, -1).transpose(1, 0, 2))


def _static_tables():
    bf = ml_dtypes.bfloat16
    st = {}
    cb, sb_ = _rope_tables(64)
    cc, sc = _rope_tables(32)
    st["rope_s"] = [_tok_major(cb), _tok_major(sb_), _tok_major(cc), _tok_major(sc)]
    st["rope_p"] = [np.ones((128, NB, 32), np.float32), np.zeros((128, NB, 32), np.float32),
                    np.ones((128, NB, 16), np.float32), np.zeros((128, NB, 16), np.float32)]
    bs = np.zeros((128, NF), np.float32)
    bp = np.zeros((128, NF), np.float32)
    for i in range(NB):
        for dj in range(7):
            j = i + dj - 3
            ok_p = 0 <= j < NB and (j // 2 == i // 2)
            bp[:, NF_NA + i * 7 + dj] = 0.0 if ok_p else NEG
        for dj in range(3):
            j = i + dj - 1
            ok_p = 0 <= j < NB and (j // 2 == i // 2)
            bp[:, NF_SW + i * 3 + dj] = 0.0 if ok_p else NEG
        for j in range(NB):
            bp[:, NF_DA + i * 16 + j] = 0.0 if (j // 2 == i // 2) else NEG
    st["bias_s"] = bs
    st["bias_p"] = bp
    k = np.arange(128)[:, None]
    q = np.arange(128)[None, :]
    sm = np.zeros((128, 2, 128), np.float32)
    sm[:, 0, :] = (k >= q)
    sm[:, 1, :] = (k <= q)
    st["swm_s"] = sm.astype(bf)
    st["swm_p"] = np.ones((128, 2, 128), np.float32).astype(bf)
    nm = np.zeros((128, 16, 64), np.float32)
    for p in range(128):
        ak, ck_ = p // 64, p % 64
        for e2 in range(16):
            dr = e2 - 8 + ak
            if abs(dr) > 7:
                continue
            for cr in range(64):
                c = 63 - cr
                qs = min(max(c - 8, 0), 48)
                if qs <= ck_ < qs + 16:
                    nm[p, e2, cr] = 1.0
    st["nam_s"] = nm.astype(bf)
    st["nam_p"] = np.ones((128, 16, 64), np.float32).astype(bf)
    st["identf"] = np.eye(128, dtype=np.float32)
    st["identb"] = np.eye(128, dtype=np.float32).astype(bf)
    st["permb"] = np.eye(128, dtype=np.float32)[:, ::-1].copy().astype(bf)
    st["onesb"] = np.ones((128, 128), np.float32).astype(bf)
    rm = np.zeros((128, 6), np.float32)
    rm[0:64, 0] = 1.0
    rm[64:128, 1] = 1.0
    for u in range(4):
        rm[32 * u:32 * u + 32, 2 + u] = 1.0
    st["rowmask"] = rm
    return st


_SW_Q_ORDER = [0, 4, 1, 5, 2, 6, 3, 7]


def make_in_maps(inp):
    st = _static_tables()
    f = lambda a: np.ascontiguousarray(np.asarray(a, dtype=np.float32))
    w_in = f(inp["w_in"])
    o = 0
    seg = {}
    for name, wdt in (("naq", 256), ("nak", 256), ("nav", 256), ("swq", 512), ("swk", 128), ("swv", 128), ("daq", 256), ("dak", 256), ("dav", 256)):
        seg[name] = (o, o + wdt)
        o += wdt

    def cols(name):
        a, b = seg[name]
        return w_in[:, :, a:b]

    swq = cols("swq").reshape(L, D, 8, 64)[:, :, _SW_Q_ORDER, :].reshape(L, D, 512)
    w_kv = np.ascontiguousarray(np.concatenate([cols("nak"), cols("swk"), cols("dak"), cols("nav"), cols("swv"), cols("dav")], axis=-1))
    w_q = np.ascontiguousarray(np.concatenate([cols("naq"), swq, cols("daq")], axis=-1))

    def colT(v, n):
        return np.ascontiguousarray(v.reshape(L, n, 128).transpose(2, 0, 1).reshape(128, L * n))

    bmodT = colT(f(inp["b_mod"]), 48)
    gattnT = colT(f(inp["g_attn"]), 8)
    gffnT = colT(f(inp["g_ffn"]), 8)
    convwT = np.ascontiguousarray(f(inp["conv_w"]).reshape(L, 3, 44, 128).transpose(3, 0, 1, 2).reshape(128, L * 3 * 44))
    convbT = colT(f(inp["conv_b"]), 44)
    naq = f(inp["na_qk_g"])
    swg = f(inp["sw_qk_g"])
    dag = f(inp["da_qk_g"])
    small = np.concatenate([naq[:, 0], naq[:, 1], swg[:, 0], swg[:, 1], dag[:, 0], dag[:, 1],
                            f(inp["sw_sink"]), f(inp["da_lambda"]).reshape(L, 128), f(inp["da_subln_g"])], axis=-1)
    small = np.ascontiguousarray(small)
    assert small.shape == (L, 520)
    rpb = f(inp["na_rpb"]).reshape(L, 4 * 465)
    rpbpad_s = np.zeros((L, RPBLEN), np.float32)
    rpbpad_s[:, PADOFF:PADOFF + 1860] = rpb
    rpbpad_p = np.zeros((L, RPBLEN), np.float32)

    shared = dict(w_mod=f(inp["w_mod"]), w_kv=w_kv, w_q=w_q, w_out=f(inp["w_out"]), w_up=f(inp["w_up"]), w_down=f(inp["w_down"]),
                  bmodT=bmodT, gattnT=gattnT, gffnT=gffnT, convwT=convwT, convbT=convbT, small=small,
                  identf=st["identf"], identb=st["identb"], permb=st["permb"], onesb=st["onesb"], rowmask=st["rowmask"])
    xs_ = f(inp["x_sample"])
    xp = f(inp["x_prompt"])
    cc = f(inp["c"])
    cctx = f(inp["c_ctx"])
    ck_all = np.concatenate([f(inp["cache_na_k"]).reshape(4, L, 512, 256), f(inp["cache_sw_k"]).reshape(4, L, 512, 128),
                             f(inp["cache_da_k"]).reshape(4, L, 512, 256)], axis=-1)
    cv_all = np.concatenate([f(inp["cache_na_v"]).reshape(4, L, 512, 256), f(inp["cache_sw_v"]).reshape(4, L, 512, 128),
                             f(inp["cache_da_v"]).reshape(4, L, 512, 256)], axis=-1)
    zc = np.zeros((L, 512, 640), np.float32)
    maps = []
    for core in range(8):
        m = dict(shared)
        if core < 4:
            m["x"] = np.ascontiguousarray(xs_[core])
            cv_ = cc[core]
            r = st["rope_s"]
            m["biascol"] = st["bias_s"]
            m["swmask"] = st["swm_s"]
            m["namask"] = st["nam_s"]
            m["rpbpad"] = rpbpad_s
            m["ctxone"] = np.ones((128, 1), np.float32)
            m["pflag"] = np.zeros((128, 1), np.float32)
            m["ck"] = np.ascontiguousarray(ck_all[core])
            m["cv"] = np.ascontiguousarray(cv_all[core])
        else:
            g = core - 4
            m["x"] = np.ascontiguousarray(xp[8 * g:8 * g + 8].reshape(2048, D))
            cv_ = cctx
            r = st["rope_p"]
            m["biascol"] = st["bias_p"]
            m["swmask"] = st["swm_p"]
            m["namask"] = st["nam_p"]
            m["rpbpad"] = rpbpad_p
            m["ctxone"] = np.zeros((128, 1), np.float32)
            m["pflag"] = np.ones((128, 1), np.float32)
            m["ck"] = zc
            m["cv"] = zc
        m["cvecT"] = np.ascontiguousarray(cv_.reshape(8, 128).T)
        m["cosb"], m["sinb"], m["cosc"], m["sinc"] = r
        maps.append(m)
    return maps


_PROG = {}


def _get_prog(key=("full",), **kw):
    if key not in _PROG:
        _PROG[key] = build_program(**kw)
    return _PROG[key]


def assemble(results):
    y_s = np.stack([results[c]["y"] for c in range(4)], axis=0)
    y_p = np.concatenate([results[c]["y"].reshape(8, 256, D) for c in range(4, 8)], axis=0)
    okv = np.concatenate([results[c]["okv"].reshape(L, 8, 256, 1280).transpose(1, 0, 2, 3) for c in range(4, 8)], axis=0)
    nk = np.ascontiguousarray(okv[..., 0:256]).reshape(32, L, 256, 4, 64)
    sk = np.ascontiguousarray(okv[..., 256:384]).reshape(32, L, 256, 2, 64)
    dk = np.ascontiguousarray(okv[..., 384:640]).reshape(32, L, 256, 4, 2, 32)
    nv = np.ascontiguousarray(okv[..., 640:896]).reshape(32, L, 256, 4, 64)
    sv = np.ascontiguousarray(okv[..., 896:1024]).reshape(32, L, 256, 2, 64)
    dv = np.ascontiguousarray(okv[..., 1024:1280]).reshape(32, L, 256, 4, 64)
    return (np.ascontiguousarray(y_p), np.ascontiguousarray(y_s), nk, nv, sk, sv, dk, dv)


def kernel(**inputs):
    nc, _ = _get_prog()
    maps = make_in_maps(inputs)
    res = run_bass_kernel_spmd(nc, maps, core_ids=list(range(8)))
    return assemble(res.results)
```

```python
import math
from contextlib import ExitStack

import numpy as np
import ml_dtypes

import concourse.bass as bass
import concourse.mybir as mybir
from concourse.bass_utils import run_bass_kernel_spmd

F32 = mybir.dt.float32
BF16 = mybir.dt.bfloat16
AF = mybir.ActivationFunctionType
ALU = mybir.AluOpType
AX = mybir.AxisListType

L = 4
NB = 16
D = 1024
KC = 8
DFF = 2816
NJ = 22
EPS = 1e-6
NEG = -30000.0
PADOFF = 128
RPBLEN = 2176
ENGS = ("pe", "act", "dve", "pool", "sp")


class _Op:
    __slots__ = ("eng", "fn", "reads", "writes", "is_dma", "deps", "needs_inc",
                 "sem", "val", "idx", "final_wait", "barrier")

    def __init__(self, eng, fn, reads, writes, is_dma, final_wait):
        self.eng = eng
        self.fn = fn
        self.reads = reads
        self.writes = writes
        self.is_dma = is_dma
        self.deps = []
        self.needs_inc = False
        self.sem = None
        self.val = 0
        self.final_wait = final_wait
        self.barrier = False


class Sched:
    def __init__(self, same_engine_sync=True, n_dma_sems=32):
        self.ops = []
        self.same_engine_sync = same_engine_sync
        self.n_dma_sems = n_dma_sems

    def op(self, eng, fn, reads=(), writes=()):
        o = _Op(eng, fn, tuple(reads), tuple(writes), False, False)
        self.ops.append(o)
        return o

    def dma(self, eng, fn, reads=(), writes=(), final_wait=False):
        o = _Op(eng, fn, tuple(reads), tuple(writes), True, final_wait)
        self.ops.append(o)
        return o

    def barrier(self):
        for e in ENGS:
            o = _Op(e, None, (), (), False, False)
            o.barrier = True
            self.ops.append(o)

    def analyze(self):
        last_w = {}
        readers = {}
        waited = {e: {s: -1 for s in ENGS} for e in ENGS}
        waited_dma = {e: set() for e in ENGS}
        dma_slot_last = [None] * self.n_dma_sems
        dma_ctr = {"sp": 0, "pool": 0, "act": 0}
        half = self.n_dma_sems // 2
        last_compute = {e: None for e in ENGS}
        all_dma = []
        for idx, o in enumerate(self.ops):
            o.idx = idx
            deps = set()
            if o.barrier:
                for e in ENGS:
                    if last_compute[e] is not None and e != o.eng:
                        deps.add(last_compute[e])
                    if e == o.eng and last_compute[e] is not None and e != "pe":
                        deps.add(last_compute[e])
                for d in all_dma:
                    if d not in waited_dma[o.eng]:
                        deps.add(d)
            raw = set()
            for r in o.reads:
                w = last_w.get(r)
                if w is not None:
                    deps.add(w)
                    raw.add(w)
            for wkey in o.writes:
                w = last_w.get(wkey)
                if w is not None:
                    deps.add(w)
                for rd in readers.get(wkey, ()):
                    deps.add(rd)
            if o.is_dma:
                if o.eng == "pool":
                    slot = half + dma_ctr["pool"] % (self.n_dma_sems - half)
                else:
                    slot = dma_ctr["sp"] % half
                dma_ctr["pool" if o.eng == "pool" else "sp"] += 1
                prev = dma_slot_last[slot]
                if prev is not None:
                    deps.add(prev)
                dma_slot_last[slot] = idx
                o.sem = ("dma", slot)
                all_dma.append(idx)
            deps.discard(idx)
            best = {}
            out = []
            for d in sorted(deps):
                p = self.ops[d]
                if p.is_dma:
                    if d in waited_dma[o.eng]:
                        continue
                    waited_dma[o.eng].add(d)
                    out.append(d)
                else:
                    if p.eng == o.eng and not o.is_dma and not o.barrier and \
                            (p.eng == "pe" or not self.same_engine_sync):
                        continue
                    if waited[o.eng][p.eng] >= d:
                        continue
                    best[p.eng] = max(best.get(p.eng, -1), d)
            for e, d in best.items():
                waited[o.eng][e] = d
                out.append(d)
            o.deps = out
            for d in out:
                self.ops[d].needs_inc = True
            if not o.barrier:
                for r in o.reads:
                    readers.setdefault(r, []).append(idx)
                for wkey in o.writes:
                    last_w[wkey] = idx
                    readers[wkey] = []
                if not o.is_dma:
                    last_compute[o.eng] = idx
        self.final = [o.idx for o in self.ops if o.final_wait]
        cnt = {e: 0 for e in ENGS}
        dma_cnt = [0] * self.n_dma_sems
        for o in self.ops:
            if o.barrier:
                continue
            if o.is_dma:
                slot = o.sem[1]
                dma_cnt[slot] += 16
                o.val = dma_cnt[slot]
                o.needs_inc = True
            elif o.needs_inc:
                cnt[o.eng] += 1
                o.val = cnt[o.eng]
                o.sem = ("eng", o.eng)

    def emit(self, nc, es):
        self.analyze()
        sems = {}
        for e in ENGS:
            sems[("eng", e)] = es.enter_context(nc.semaphore("s_" + e))
        for i in range(self.n_dma_sems):
            sems[("dma", i)] = es.enter_context(nc.semaphore("s_dma%d" % i))
        per = {e: [] for e in ENGS}
        for o in self.ops:
            per[o.eng].append(o)
        ops = self.ops
        final = self.final
        block = es.enter_context(nc.Block())

        def run(engine_obj, lst, ename):
            for o in lst:
                for d in o.deps:
                    p = ops[d]
                    engine_obj.wait_ge(sems[p.sem], p.val)
                if o.barrier:
                    continue
                ins = o.fn(engine_obj)
                if o.needs_inc:
                    ins.then_inc(sems[o.sem], 16 if o.is_dma else 1)
            for d in final:
                p = ops[d]
                if p.eng == ename:
                    engine_obj.wait_ge(sems[p.sem], p.val)

        @block.tensor
        def _(e):
            run(e, per["pe"], "pe")

        @block.scalar
        def _(e):
            run(e, per["act"], "act")

        @block.vector
        def _(e):
            run(e, per["dve"], "dve")

        @block.gpsimd
        def _(e):
            run(e, per["pool"], "pool")

        @block.sync
        def _(e):
            run(e, per["sp"], "sp")


def _C(name, *a, **k):
    def f(e):
        return getattr(e, name)(*a, **k)
    return f


_ARENA_HI = 0


class Arena:
    def __init__(self, big, nel, base=0):
        self.big = big
        self.nel = nel
        self.off = base
        self.hi = base

    def take(self, shape, dtype):
        global _ARENA_HI
        n = 1
        for s in shape[1:]:
            n *= s
        nb = n * (4 if dtype == F32 else 2)
        nb = (nb + 63) // 64 * 64
        el = nb // 2
        a = self.off
        self.off += el
        self.hi = max(self.hi, self.off)
        assert self.off <= self.nel, ("arena overflow", self.off, self.nel)
        _ARENA_HI = max(_ARENA_HI, self.off)
        v = self.big[:, a:a + el]
        if dtype == F32:
            v = v.bitcast(F32)[:, 0:n]
        else:
            v = v[:, 0:n]
        if len(shape) == 3:
            v = v.rearrange("p (a b) -> p a b", a=shape[1])
        elif len(shape) == 4:
            v = v.rearrange("p (a b c) -> p a b c", a=shape[1], b=shape[2])
        return v


def _na_r0(r):
    return min(max(r - 4, 0), 24)


def na_blocks(i):
    res = []
    for j in range(NB):
        inval = []
        anyv = False
        for a in range(2):
            r = 2 * i + a
            for ak in range(2):
                rk = 2 * j + ak
                ok = _na_r0(r) <= rk <= _na_r0(r) + 7
                if ok:
                    anyv = True
                else:
                    inval.append((ak, a))
        if anyv:
            res.append((j, inval))
    return res


NF_NA = 0
NF_SW = 112
NF_DA = 160
NF = 160 + 256

ACC = {}
AW = 66
for _h in range(4):
    ACC[("sw", _h)] = (0, _h * AW)
for _h in range(3):
    ACC[("na", _h)] = (0, 4 * AW + _h * AW)
for _h in range(4):
    ACC[("sw", 4 + _h)] = (1, _h * AW)
ACC[("na", 3)] = (1, 4 * AW)
ACC[("da", 0)] = (1, 5 * AW)
ACC[("da", 1)] = (1, 6 * AW)
for _u in range(6):
    ACC[("da", 2 + _u)] = (2, _u * AW)


def build_program(n_layers=L, do_attn=True, do_ffn=True, taps=(), a1_blocks=NB, a2_blocks=NB, do_mod=True):
    nc = bass.Bass("TRN2", target_bir_lowering=False)
    S = Sched()
    taps = set(taps)
    dbg_outs = {}

    def din(name, shape, dt=F32):
        return nc.dram_tensor(name, list(shape), dt, kind="ExternalInput").ap()

    x_d = din("x", [2048, D])
    cvec_d = din("cvecT", [128, 8])
    cosb_d = din("cosb", [128, NB, 32])
    sinb_d = din("sinb", [128, NB, 32])
    cosc_d = din("cosc", [128, NB, 16])
    sinc_d = din("sinc", [128, NB, 16])
    bias_d = din("biascol", [128, NF])
    swm_d = din("swmask", [128, 2, 128], BF16)
    nam_d = din("namask", [128, 16, 64], BF16)
    rpb_d = din("rpbpad", [L, RPBLEN])
    ctxone_d = din("ctxone", [128, 1])
    pflag_d = din("pflag", [128, 1])
    ck_d = din("ck", [L, 512, 640])
    cv_d = din("cv", [L, 512, 640])
    wmod_d = din("w_mod", [L, D, 6 * D])
    wkv_d = din("w_kv", [L, D, 1280])
    wq_d = din("w_q", [L, D, 1024])
    wout_d = din("w_out", [L, D, D])
    wup_d = din("w_up", [L, D, 2 * DFF])
    wdn_d = din("w_down", [L, DFF, D])
    bmod_d = din("bmodT", [128, L * 48])
    gattn_d = din("gattnT", [128, L * 8])
    gffn_d = din("gffnT", [128, L * 8])
    convw_d = din("convwT", [128, L * 3 * 44])
    convb_d = din("convbT", [128, L * 44])
    small_d = din("small", [L, 520])
    identf_d = din("identf", [128, 128])
    identb_d = din("identb", [128, 128], BF16)
    permb_d = din("permb", [128, 128], BF16)
    onesb_d = din("onesb", [128, 128], BF16)
    rowmask_d = din("rowmask", [128, 6])

    y_d = nc.dram_tensor("y", [2048, D], F32, kind="ExternalOutput").ap()
    okv_d = nc.dram_tensor("okv", [L, 2048, 1280], F32, kind="ExternalOutput").ap()

    with ExitStack() as es:
        def sb(name, shape, dt=F32):
            return es.enter_context(nc.sbuf_tensor(name, list(shape), dt))

        XT = sb("XT", [128, KC, 2048])
        identf = sb("identf_s", [128, 128])
        identb = sb("identb_s", [128, 128], BF16)
        permb = sb("permb_s", [128, 128], BF16)
        onesb = sb("onesb_s", [128, 128], BF16)
        rowmask = sb("rowmask_s", [128, 6])
        cosb = sb("cosb_s", [128, NB, 32])
        sinb = sb("sinb_s", [128, NB, 32])
        cosc = sb("cosc_s", [128, NB, 16])
        sinc = sb("sinc_s", [128, NB, 16])
        biascol = sb("biascol_s", [128, NF])
        swmask = sb("swmask_s", [128, 2, 128], BF16)
        namask = sb("namask_s", [128, 16, 64], BF16)
        ctxone = sb("ctxone_s", [128, 1])
        pflag = sb("pflag_s", [128, 1])
        cvecT = sb("cvecT_s", [128, 8])
        silub = sb("silub", [128, 8], BF16)
        bmodT = sb("bmodT_s", [128, L * 48])
        modsb = sb("modsb", [128, L * 48])
        gattnT = sb("gattnT_s", [128, L * 8])
        gffnT = sb("gffnT_s", [128, L * 8])
        G1 = sb("G1", [128, L * 8])
        G2 = sb("G2", [128, L * 8])
        convwT = sb("convwT_s", [128, L * 3 * 44])
        convbT = sb("convbT_s", [128, L * 44])
        NBIG = 64000
        BIG = sb("BIG", [128, NBIG], BF16)
        banks = [es.enter_context(nc.psum_tensor("bank%d" % i, [128, 512], F32)) for i in range(8)]

        def BK(i):
            return "B%d" % i

        def tap(name, ap, shape, reads, dt=F32):
            if name not in taps:
                return
            d = nc.dram_tensor("dbg_" + name, list(shape), dt, kind="ExternalOutput").ap()
            dbg_outs[name] = d
            S.dma("sp", _C("dma_start", out=d, in_=ap), reads=reads, final_wait=True)

        def ld(dst, src, key):
            S.dma("sp", _C("dma_start", out=dst, in_=src), writes=[key])

        ld(identf[:], identf_d, "identf")
        ld(identb[:], identb_d, "identb")
        ld(permb[:], permb_d, "permb")
        ld(onesb[:], onesb_d, "onesb")
        ld(rowmask[:], rowmask_d, "rowmask")
        ld(cosb[:], cosb_d, "rope")
        ld(sinb[:], sinb_d, "rope")
        ld(cosc[:], cosc_d, "rope")
        ld(sinc[:], sinc_d, "rope")
        ld(biascol[:], bias_d, "biascol")
        ld(swmask[:], swm_d, "swmask")
        ld(namask[:], nam_d, "namask")
        ld(ctxone[:], ctxone_d, "ctxone")
        ld(pflag[:], pflag_d, "pflag")
        ld(cvecT[:], cvec_d, "cvecT")
        ld(bmodT[:], bmod_d, "bmodT")
        ld(gattnT[:], gattn_d, "gattnT")
        ld(gffnT[:], gffn_d, "gffnT")
        ld(convwT[:], convw_d, "convwT")
        ld(convbT[:], convb_d, "convbT")

        ar = Arena(BIG, NBIG)
        xs = [ar.take([128, 1024], F32) for _ in range(2)]
        wm = [ar.take([128, 8, 512], BF16) for _ in range(2)]
        for t in range(NB):
            b = t % 2
            S.dma("sp", _C("dma_start", out=xs[b], in_=x_d[t * 128:(t + 1) * 128, :]),
                  writes=["xs%d" % b])
            for half in range(2):
                bank = (2 * t + half) % 4
                for q in range(4):
                    c = half * 4 + q
                    S.op("pe", _C("transpose",
                        banks[bank][:, q * 128:(q + 1) * 128], xs[b][:, c * 128:(c + 1) * 128], identf[:]),
                        reads=["xs%d" % b, "identf"], writes=[BK(bank)])
                if half == 0:
                    S.op("act", _C("activation",
                        out=XT[:, 0:4, t * 128:(t + 1) * 128],
                        in_=banks[bank][:].rearrange("p (c n) -> p c n", c=4), func=AF.Copy),
                        reads=[BK(bank)], writes=["XT%d" % t])
                else:
                    S.op("dve", _C("tensor_copy",
                        out=XT[:, 4:8, t * 128:(t + 1) * 128],
                        in_=banks[bank][:].rearrange("p (c n) -> p c n", c=4)),
                        reads=[BK(bank)], writes=["XT%d" % t])

        S.op("act", _C("activation", out=silub[:], in_=cvecT[:], func=AF.Silu),
             reads=["cvecT"], writes=["silub"])
        MB = 4
        first_mod = True
        for l in range(n_layers if do_mod else 0):
            for piece in range(12):
                b = (l * 12 + piece) % 2
                S.dma("pool", _C("dma_start",
                    out=wm[b], in_=wmod_d[l, :, piece * 512:(piece + 1) * 512].rearrange("(kc p) n -> p kc n", p=128)),
                    writes=["wm%d" % b])
                for oc in range(4):
                    col = l * 48 + piece * 4 + oc
                    for kc in range(KC):
                        S.op("pe", _C("matmul",
                            banks[MB][:, col:col + 1], lhsT=wm[b][:, kc, oc * 128:(oc + 1) * 128],
                            rhs=silub[:, kc:kc + 1], start=first_mod, stop=(kc == KC - 1), skip_group_check=True),
                            reads=["wm%d" % b, "silub"], writes=[BK(MB)])
                        first_mod = False
        nm = n_layers * 48
        S.op("dve", _C("tensor_tensor", out=modsb[:, 0:nm], in0=banks[MB][:, 0:nm], in1=bmodT[:, 0:nm], op=ALU.add),
             reads=[BK(MB), "bmodT"], writes=["modsb"])
        for l in range(n_layers):
            S.op("dve", _C("scalar_tensor_tensor",
                out=G1[:, l * 8:(l + 1) * 8], in0=modsb[:, l * 48 + 8:l * 48 + 16], scalar=1.0,
                in1=gattnT[:, l * 8:(l + 1) * 8], op0=ALU.add, op1=ALU.mult),
                reads=["modsb", "gattnT"], writes=["G"])
            S.op("dve", _C("scalar_tensor_tensor",
                out=G2[:, l * 8:(l + 1) * 8], in0=modsb[:, l * 48 + 32:l * 48 + 40], scalar=1.0,
                in1=gffnT[:, l * 8:(l + 1) * 8], op0=ALU.add, op1=ALU.mult),
                reads=["modsb", "gffnT"], writes=["G"])
        tap("modsb", modsb[:], [128, L * 48], ["modsb"])
        tap("G1", G1[:], [128, L * 8], ["G"])

        def mcol(l, k, c):
            i0 = l * 48 + k * 8 + c
            return modsb[:, i0:i0 + 1]

        def adaln(dst_fn, tok0, w, Gt, l, kshift, sqbuf, rsbuf, tmps, sbank, tagp, xkeys=("XTh0", "XTh1")):
            xkeys = list(xkeys)
            for c in range(KC):
                S.op("act", _C("activation", out=sqbuf[c % 2][:, 0:w], in_=XT[:, c, tok0:tok0 + w], func=AF.Square),
                     reads=xkeys, writes=[tagp + "sq%d" % (c % 2)])
                S.op("pe", _C("matmul", banks[sbank][:, 0:w], lhsT=onesb[:], rhs=sqbuf[c % 2][:, 0:w],
                                                   start=(c == 0), stop=(c == KC - 1)),
                     reads=[tagp + "sq%d" % (c % 2), "onesb"], writes=[BK(sbank)])
            S.op("act", _C("activation", out=rsbuf[:, 0:w], in_=banks[sbank][:, 0:w], func=AF.Ln, scale=1.0 / D, bias=EPS),
                 reads=[BK(sbank)], writes=[tagp + "rs"])
            S.op("act", _C("activation", out=rsbuf[:, 0:w], in_=rsbuf[:, 0:w], func=AF.Exp, scale=-0.5),
                 reads=[tagp + "rs"], writes=[tagp + "rs"])
            for c in range(KC):
                tb = tmps[c % 2]
                S.op("dve", _C("scalar_tensor_tensor",
                    out=tb[:, 0:w], in0=XT[:, c, tok0:tok0 + w], scalar=Gt[:, l * 8 + c:l * 8 + c + 1],
                    in1=rsbuf[:, 0:w], op0=ALU.mult, op1=ALU.mult),
                    reads=xkeys + ["G", tagp + "rs"], writes=[tagp + "tmp%d" % (c % 2)])
                S.op("act", _C("activation",
                    out=dst_fn(c), in_=tb[:, 0:w], func=AF.Identity, bias=mcol(l, kshift, c), scale=1.0),
                    reads=[tagp + "tmp%d" % (c % 2), "modsb"], writes=[tagp + "h"])

        def adaln_blk(hdst, hkey, tok0, Gt, l, kshift, sq8, rsbuf, tmp8, sbank, tagp, tmpkey=None, offload=False, affine_dve=False, scol=0):
            w = 128
            if offload:
                S.op("dve", _C("tensor_tensor", out=sq8, in0=XT[:, :, tok0:tok0 + w], in1=XT[:, :, tok0:tok0 + w], op=ALU.mult),
                     reads=["XTh0", "XTh1"], writes=[tagp + "sq8"])
            else:
                S.op("act", _C("activation", out=sq8, in_=XT[:, :, tok0:tok0 + w], func=AF.Square),
                     reads=["XTh0", "XTh1"], writes=[tagp + "sq8"])
            for c in range(KC):
                S.op("pe", _C("matmul", banks[sbank][:, scol:scol + w], lhsT=onesb[:], rhs=sq8[:, c, :],
                              start=(c == 0), stop=(c == KC - 1)),
                     reads=[tagp + "sq8", "onesb"], writes=[BK(sbank)])
            S.op("act", _C("activation", out=rsbuf[:, 0:w], in_=banks[sbank][:, scol:scol + w], func=AF.Ln, scale=1.0 / D, bias=EPS),
                 reads=[BK(sbank)], writes=[tagp + "rs"])
            S.op("act", _C("activation", out=rsbuf[:, 0:w], in_=rsbuf[:, 0:w], func=AF.Exp, scale=-0.5),
                 reads=[tagp + "rs"], writes=[tagp + "rs"])
            S.op("dve", _C("tensor_tensor", out=tmp8, in0=XT[:, :, tok0:tok0 + w],
                           in1=rsbuf[:, 0:w].unsqueeze(1).broadcast_to([128, KC, w]), op=ALU.mult),
                 reads=["XTh0", "XTh1", tagp + "rs"], writes=[tmpkey or (tagp + "tmp8")])
            for c in range(KC):
                if offload or affine_dve:
                    S.op("dve", _C("tensor_scalar", out=hdst[:, c, :], in0=tmp8[:, c, :], scalar1=Gt[:, l * 8 + c:l * 8 + c + 1],
                                   scalar2=mcol(l, kshift, c), op0=ALU.mult, op1=ALU.add),
                         reads=[tmpkey or (tagp + "tmp8"), "modsb", "G"], writes=[hkey])
                else:
                    S.op("act", _C("activation", out=hdst[:, c, :], in_=tmp8[:, c, :], func=AF.Identity,
                                   bias=mcol(l, kshift, c), scale=Gt[:, l * 8 + c:l * 8 + c + 1]),
                         reads=[tmpkey or (tagp + "tmp8"), "modsb", "G"], writes=[hkey])

        for l in range(n_layers):
            lam_init = 0.8 - 0.6 * math.exp(-0.3 * l)
            S.barrier()
            ar = Arena(BIG, NBIG)
            KT = ar.take([128, 5, 2048], BF16)
            V = ar.take([128, NB, 10, 66], BF16)
            CTXKT = ar.take([128, 5, 512], BF16)
            CTXV = ar.take([128, 4, 10, 66], BF16)
            TAB = ar.take([128, 4, 16, 64], BF16)
            WA = ar.take([128, 8, 1280], BF16)
            sq8 = ar.take([128, KC, 128], BF16)
            hT = ar.take([128, KC, 128], BF16)
            rsb = ar.take([128, 128], F32)
            SM = ar.take([128, 520], F32)
            esink = ar.take([128, 8], F32)
            lamt = ar.take([128, 8], F32)
            SG = ar.take([128, 64], F32)
            smalls = ar.take([128, 64], F32)
            base_shared = ar.off
            kcats = [ar.take([128, 640], F32) for _ in range(2)]
            vcats = [ar.take([128, 640], F32) for _ in range(2)]
            sqks = [ar.take([128, 640], F32) for _ in range(2)]
            rts = [[ar.take([128, 64], F32) for _ in range(4)] for _ in range(2)]
            rt2s = [[ar.take([128, 128], F32) for _ in range(4)] for _ in range(2)]
            kbs = [ar.take([128, 640], BF16) for _ in range(2)]
            smks = [ar.take([128, 64], F32) for _ in range(2)]
            a0_base = ar.off
            CKs = ar.take([128, 4, 640], BF16)
            TABF = ar.take([128, 16, 64], F32)
            a0_hi = ar.off
            ar.off = a0_base
            tmp8as = [ar.take([128, KC, 128], F32) for _ in range(2)]
            sq8s = [sq8, ar.take([128, KC, 128], BF16)]
            rsbs = [rsb, ar.take([128, 128], F32)]
            hTs = [hT, ar.take([128, KC, 128], BF16)]
            ar.off = max(ar.off, a0_hi)
            hiA1 = ar.off
            ar.off = base_shared
            qf = ar.take([128, 1024], F32)
            sqq = ar.take([128, 1024], F32)
            rtq = [sqq[:, k_ * 256:(k_ + 1) * 256] for k_ in range(4)]
            qb = ar.take([128, 1024], BF16)
            QTs = [ar.take([128, 20, 128], BF16) for _ in range(2)]
            PT = [ar.take([128, 512], BF16) for _ in range(4)]
            Ofin = sqq[:, 0:512].rearrange("p (a d) -> p a d", a=8)
            ddt = sqq[:, 512:768].rearrange("p (a d) -> p a d", a=4)
            sqd = sqq[:, 768:1024].rearrange("p (a d) -> p a d", a=4)
            Omix = ar.take([128, 1024], BF16)
            rtq2 = [Omix[:, k_ * 256:(k_ + 1) * 256].bitcast(F32) for k_ in range(4)]
            OT = ar.take([128, 8, 256], BF16)
            AWW = 7 * AW
            Oacc = ar.take([128, 3, AWW], F32)
            wo = [WA[:, :, 1024 + 128 * k_:1024 + 128 * (k_ + 1)] for k_ in range(2)]
            wqv = WA[:, :, 0:1024]
            tmp8q = sqq.rearrange("p (c n) -> p c n", c=KC)

            if do_attn:
                S.dma("sp", _C("dma_start", out=SM, in_=small_d[l, :].partition_broadcast(128)), writes=["SM"])
                S.op("act", _C("activation", out=esink, in_=SM[:, 320:328], func=AF.Exp), reads=["SM"], writes=["esink"])
                lp = SM[:, 328:456].rearrange("p (a b d) -> p a b d", a=2, b=2)
                S.op("dve", _C("tensor_tensor", out=smalls[:, 0:64].rearrange("p (a d) -> p a d", a=2),
                                                      in0=lp[:, :, 0, :], in1=lp[:, :, 1, :], op=ALU.mult),
                     reads=["SM"], writes=["smalls"])
                S.op("dve", _C("tensor_reduce", out=lamt[:, 0:2], in_=smalls[:, 0:64].rearrange("p (a d) -> p a d", a=2),
                                                      axis=AX.X, op=ALU.add),
                     reads=["smalls"], writes=["lamt"])
                S.op("act", _C("activation", out=lamt[:, 2:4], in_=lamt[:, 0:2], func=AF.Exp), reads=["lamt"], writes=["lamt2"])
                S.op("dve", _C("tensor_tensor", out=lamt[:, 4:5], in0=lamt[:, 3:4], in1=lamt[:, 2:3], op=ALU.subtract),
                     reads=["lamt2"], writes=["lamt3"])
                S.op("dve", _C("tensor_scalar", out=lamt[:, 5:6], in0=lamt[:, 4:5], scalar1=-lam_init, scalar2=None, op0=ALU.add),
                     reads=["lamt3"], writes=["neglam"])
                S.op("dve", _C("tensor_scalar", out=SG, in0=SM[:, 456:520], scalar1=1.0 - lam_init, scalar2=None, op0=ALU.mult),
                     reads=["SM"], writes=["SG"])
                neglam = lamt[:, 5:6]
                S.dma("pool", _C("dma_start", out=CKs, in_=ck_d[l].rearrange("(b p) f -> p b f", p=128)), writes=["CKs"])
                for b4 in range(4):
                    bank = 6 + (b4 % 2)
                    pv = banks[bank][:].bitcast(BF16)
                    for ch in range(5):
                        S.op("pe", _C("transpose", pv[:, ch * 128:(ch + 1) * 128], CKs[:, b4, ch * 128:(ch + 1) * 128], identb[:]),
                             reads=["CKs", "identb"], writes=[BK(bank)])
                    S.op("act", _C("activation", out=CTXKT[:, :, b4 * 128:(b4 + 1) * 128],
                                                                   in_=pv[:, 0:640].rearrange("p (c n) -> p c n", c=5), func=AF.Copy),
                         reads=[BK(bank)], writes=["CTXKT"])
                for b4 in range(4):
                    S.dma("pool", _C("dma_start",
                        out=CTXV[:, b4, :, 0:64], in_=cv_d[l, b4 * 128:(b4 + 1) * 128, :].rearrange("p (h d) -> p h d", h=10)),
                        writes=["CTXVd"])
                S.op("pool", _C("tensor_copy", out=CTXV[:, :, :, 64], in_=ctxone[:, 0:1].unsqueeze(2).broadcast_to([128, 4, 10])),
                     reads=["ctxone"], writes=["CTXVo"])
                S.op("pool", _C("memset", V[:, :, :, 64], 1.0), writes=["Vones"])
                for h in range(4):
                    for ak in range(2):
                        off = PADOFF + h * 465 + (ak - 1) * 31 - 48
                        src = bass.AP(rpb_d.tensor, l * RPBLEN + off, [[1, 64], [31, 16], [1, 64]])
                        S.dma("sp", _C("dma_start", out=TABF[ak * 64:(ak + 1) * 64, :, :], in_=src),
                              writes=["TABF"])
                    S.op("act", _C("activation", out=TAB[:, h, :, :], in_=TABF, func=AF.Exp), reads=["TABF"], writes=["TAB"])
                    S.op("dve", _C("tensor_tensor", out=TAB[:, h, :, :], in0=TAB[:, h, :, :], in1=namask[:], op=ALU.mult),
                         reads=["TAB", "namask"], writes=["TAB"])
                if l == 0:
                    tap("TAB", TAB, [128, 4, 16, 64], ["TAB"], BF16)
                    tap("CTXKT", CTXKT, [128, 5, 512], ["CTXKT"], BF16)
                GN = SM
                S.dma("pool", _C("dma_start", out=WA, in_=wkv_d[l].rearrange("(kc p) n -> p kc n", p=128)), writes=["WA"])
                S.barrier()
                def rope2(eng, groups, tkey):
                    seqs = []
                    for gi, (view, H, half, cs, sn, key, tl, xr, xw) in enumerate(groups):
                        x1 = view[:, :, :, 0]
                        x2 = view[:, :, :, 1]
                        cb_ = cs.unsqueeze(1).broadcast_to([128, H, half])
                        sb_ = sn.unsqueeze(1).broadcast_to([128, H, half])
                        n_ = H * half
                        tv = [tm[:, 0:n_].rearrange("p (h d) -> p h d", h=H) for tm in tl]
                        tk = ["%s_%d_%d" % (tkey, gi, k_) for k_ in range(4)]
                        seqs.append([
                            (_C("tensor_tensor", out=tv[0], in0=x1, in1=cb_, op=ALU.mult), [key, "rope"] + xr, [tk[0]] + xw),
                            (_C("tensor_tensor", out=tv[1], in0=x2, in1=sb_, op=ALU.mult), [key, "rope"] + xr, [tk[1]] + xw),
                            (_C("tensor_tensor", out=tv[2], in0=x1, in1=sb_, op=ALU.mult), [key, "rope"] + xr, [tk[2]] + xw),
                            (_C("tensor_tensor", out=tv[3], in0=x2, in1=cb_, op=ALU.mult), [key, "rope"] + xr, [tk[3]] + xw),
                            (_C("tensor_tensor", out=x1, in0=tv[0], in1=tv[1], op=ALU.subtract), [tk[0], tk[1]], [key]),
                            (_C("tensor_tensor", out=x2, in0=tv[2], in1=tv[3], op=ALU.add), [tk[2], tk[3]], [key]),
                        ])
                    for k_ in range(6):
                        for sq_ in seqs:
                            fn_, rd_, wr_ = sq_[k_]
                            S.op(eng, fn_, reads=rd_, writes=wr_)

                def a1_block(t):
                    tok0 = t * 128
                    kcat = kcats[t % 2]
                    vcat = vcats[t % 2]
                    sfx = str(t % 2)
                    sqk = sqks[t % 2]
                    kb = kbs[t % 2]
                    rt = rts[t % 2]
                    rt2 = rt2s[t % 2]
                    smk_ = smks[t % 2]
                    pb = 0 if t % 2 == 0 else 3
                    hT_ = hTs[t % 2]
                    adaln_blk(hT_, "ah" + sfx, tok0, G1, l, 0, sq8s[t % 2], rsbs[t % 2], tmp8as[t % 2], pb + 2, "a" + sfx, affine_dve=True, scol=256)
                    for nt, (n0, w) in enumerate(((0, 512), (512, 512), (1024, 256))):
                        for kc in range(KC):
                            S.op("pe", _C("matmul",
                                banks[pb + nt][:, 0:w], lhsT=hT_[:, kc, :], rhs=WA[:, kc, n0:n0 + w],
                                start=(kc == 0), stop=(kc == KC - 1)),
                                reads=["ah" + sfx, "WA"], writes=[BK(pb + nt)])
                    S.op("act", _C("activation", out=kcat[:, 0:512], in_=banks[pb][:, :], func=AF.Copy),
                         reads=[BK(pb)], writes=["kc_a" + sfx, "kc_b" + sfx, "kc_c" + sfx])
                    S.op("act", _C("activation", out=kcat[:, 512:640], in_=banks[pb + 1][:, 0:128], func=AF.Copy),
                         reads=[BK(pb + 1)], writes=["kc_c" + sfx])
                    S.op("act", _C("activation", out=vcat[:, 0:384], in_=banks[pb + 1][:, 128:512], func=AF.Copy),
                         reads=[BK(pb + 1)], writes=["vcat" + sfx])
                    S.op("dve", _C("tensor_copy", out=vcat[:, 384:640], in_=banks[pb + 2][:, 0:256]),
                         reads=[BK(pb + 2)], writes=["vcat" + sfx])
                    a1_mark[0] = len(S.ops)
                    S.op("act", _C("activation", out=V[:, t, :, 0:64], in_=vcat.rearrange("p (h d) -> p h d", h=10), func=AF.Copy),
                         reads=["vcat" + sfx], writes=["V"])
                    S.dma("sp", _C("dma_start", out=okv_d[l, tok0:tok0 + 128, 640:1280], in_=vcat),
                          reads=["vcat" + sfx], final_wait=True)
                    S.op("act", _C("activation", out=sqk, in_=kcat, func=AF.Square), reads=["kc_a" + sfx, "kc_b" + sfx, "kc_c" + sfx], writes=["sqk" + sfx])
                    S.op("dve", _C("tensor_reduce", out=smk_[:, 0:6], in_=sqk[:, 0:384].rearrange("p (h d) -> p h d", h=6), axis=AX.X, op=ALU.add),
                         reads=["sqk" + sfx], writes=["smk" + sfx])
                    S.op("dve", _C("tensor_reduce", out=smk_[:, 6:14], in_=sqk[:, 384:640].rearrange("p (h d) -> p h d", h=8), axis=AX.X, op=ALU.add),
                         reads=["sqk" + sfx], writes=["smk" + sfx])
                    S.op("act", _C("activation", out=smk_[:, 16:22], in_=smk_[:, 0:6], func=AF.Ln, scale=1.0 / 64, bias=EPS),
                         reads=["smk" + sfx], writes=["smk" + sfx])
                    S.op("act", _C("activation", out=smk_[:, 22:30], in_=smk_[:, 6:14], func=AF.Ln, scale=1.0 / 32, bias=EPS),
                         reads=["smk" + sfx], writes=["smk" + sfx])
                    S.op("act", _C("activation", out=smk_[:, 32:46], in_=smk_[:, 16:30], func=AF.Exp, scale=-0.5),
                         reads=["smk" + sfx], writes=["smk" + sfx])
                    k64 = kcat[:, 0:384].rearrange("p (h d) -> p h d", h=6)
                    k32 = kcat[:, 384:640].rearrange("p (h d) -> p h d", h=8)
                    S.op("dve", _C("tensor_tensor", out=k64, in0=k64, in1=smk_[:, 32:38].unsqueeze(2).broadcast_to([128, 6, 64]), op=ALU.mult),
                         reads=["smk" + sfx, "kc_a" + sfx, "kc_b" + sfx], writes=["kc_a" + sfx, "kc_b" + sfx])
                    S.op("dve", _C("tensor_tensor", out=k32, in0=k32, in1=smk_[:, 38:46].unsqueeze(2).broadcast_to([128, 8, 32]), op=ALU.mult),
                         reads=["smk" + sfx, "kc_c" + sfx], writes=["kc_c" + sfx])
                    kna = kcat[:, 0:256].rearrange("p (h d) -> p h d", h=4)
                    ksw = kcat[:, 256:384].rearrange("p (h d) -> p h d", h=2)
                    S.op("dve", _C("tensor_tensor", out=kna, in0=kna, in1=GN[:, 64:128].unsqueeze(1).broadcast_to([128, 4, 64]), op=ALU.mult),
                         reads=["SM", "kc_a" + sfx], writes=["kc_a" + sfx])
                    S.op("dve", _C("tensor_tensor", out=ksw, in0=ksw, in1=GN[:, 192:256].unsqueeze(1).broadcast_to([128, 2, 64]), op=ALU.mult),
                         reads=["SM", "kc_b" + sfx], writes=["kc_b" + sfx])
                    S.op("dve", _C("tensor_tensor", out=k32, in0=k32, in1=GN[:, 288:320].unsqueeze(1).broadcast_to([128, 8, 32]), op=ALU.mult),
                         reads=["SM", "kc_c" + sfx], writes=["kc_c" + sfx])

                    rope2("dve", [
                        (kcat[:, 256:384].rearrange("p (h d two) -> p h d two", h=2, two=2), 2, 32, cosb[:, t, :], sinb[:, t, :], "kc_b" + sfx, rt, [], []),
                        (kcat[:, 384:640].rearrange("p (h d two) -> p h d two", h=8, two=2), 8, 16, cosc[:, t, :], sinc[:, t, :], "kc_c" + sfx, rt2, [], []),
                    ], "rtk" + sfx)
                    S.dma("sp", _C("dma_start", out=okv_d[l, tok0:tok0 + 128, 0:640], in_=kcat),
                          reads=["kc_a" + sfx, "kc_b" + sfx, "kc_c" + sfx], final_wait=True)
                    S.op("act", _C("activation", out=kb, in_=kcat, func=AF.Copy), reads=["kc_a" + sfx, "kc_b" + sfx, "kc_c" + sfx], writes=["kb" + sfx])
                    tb_ = 6 + t % 2
                    pv = banks[tb_][:].bitcast(BF16)
                    for ch in range(5):
                        S.op("pe", _C("transpose", pv[:, ch * 128:(ch + 1) * 128], kb[:, ch * 128:(ch + 1) * 128], identb[:]),
                             reads=["kb" + sfx, "identb"], writes=[BK(tb_)])
                    S.op("act", _C("activation", out=KT[:, :, tok0:tok0 + 128], in_=pv[:, 0:640].rearrange("p (c n) -> p c n", c=5), func=AF.Copy),
                         reads=[BK(tb_)], writes=["KT"])

                a1_mark = [0]

                def cap_a1(t):
                    saved = S.ops
                    S.ops = []
                    a1_block(t)
                    got = S.ops
                    S.ops = saved
                    return got[:a1_mark[0]], got[a1_mark[0]:]

                st_a1 = [cap_a1(t) for t in range(a1_blocks)]
                for t in range(0, a1_blocks, 2):
                    if t + 1 < a1_blocks:
                        la = st_a1[t][0] + st_a1[t][1]
                        lb = st_a1[t + 1][0] + st_a1[t + 1][1]
                        for k_ in range(max(len(la), len(lb))):
                            if k_ < len(la):
                                S.ops.append(la[k_])
                            if k_ < len(lb):
                                S.ops.append(lb[k_])
                    else:
                        S.ops.extend(st_a1[t][0] + st_a1[t][1])
                if l == 0:
                    tap("KT", KT, [128, 5, 2048], ["KT"], BF16)
                    tap("V", V, [128, NB, 10, 66], ["V", "Vones"], BF16)

                S.barrier()
                S.dma("pool", _C("dma_start", out=wqv, in_=wq_d[l].rearrange("(kc p) n -> p kc n", p=128)), writes=["WA"])
                sctr = [0]
                pctr = [0]
                woctr = [0]
                def front(i):
                    tok0 = i * 128
                    QT = QTs[i % 2]
                    qk_ = "QT%d" % (i % 2)
                    adaln_blk(hT, "ah", tok0, G1, l, 0, sq8, rsb, tmp8q, 0, "a", tmpkey="sqq", offload=True)
                    for nt in range(2):
                        for kc in range(KC):
                            S.op("pe", _C("matmul",
                                banks[nt][:, :], lhsT=hT[:, kc, :], rhs=wqv[:, kc, nt * 512:(nt + 1) * 512],
                                start=(kc == 0), stop=(kc == KC - 1)),
                                reads=["ah", "WA"], writes=[BK(nt)])
                    S.op("dve", _C("tensor_copy", out=qf[:, 0:512], in_=banks[0][:, :]), reads=[BK(0)], writes=["qf_a", "qf_b"])
                    S.op("dve", _C("tensor_copy", out=qf[:, 512:1024], in_=banks[1][:, :]), reads=[BK(1)], writes=["qf_b", "qf_c"])
                    S.op("dve", _C("tensor_tensor", out=sqq, in0=qf, in1=qf, op=ALU.mult), reads=["qf_a", "qf_b", "qf_c"], writes=["sqq"])
                    S.op("dve", _C("tensor_reduce", out=smalls[:, 0:12], in_=sqq[:, 0:768].rearrange("p (h d) -> p h d", h=12), axis=AX.X, op=ALU.add),
                         reads=["sqq"], writes=["smalls"])
                    S.op("dve", _C("tensor_reduce", out=smalls[:, 12:20], in_=sqq[:, 768:1024].rearrange("p (h d) -> p h d", h=8), axis=AX.X, op=ALU.add),
                         reads=["sqq"], writes=["smalls"])
                    S.op("act", _C("activation", out=smalls[:, 20:32], in_=smalls[:, 0:12], func=AF.Ln, scale=1.0 / 64, bias=EPS),
                         reads=["smalls"], writes=["smalls"])
                    S.op("act", _C("activation", out=smalls[:, 32:40], in_=smalls[:, 12:20], func=AF.Ln, scale=1.0 / 32, bias=EPS),
                         reads=["smalls"], writes=["smalls"])
                    S.op("act", _C("activation", out=smalls[:, 40:60], in_=smalls[:, 20:40], func=AF.Exp, scale=-0.5),
                         reads=["smalls"], writes=["smalls"])
                    q64 = qf[:, 0:768].rearrange("p (h d) -> p h d", h=12)
                    q32 = qf[:, 768:1024].rearrange("p (h d) -> p h d", h=8)
                    S.op("dve", _C("tensor_tensor", out=q64, in0=q64, in1=smalls[:, 40:52].unsqueeze(2).broadcast_to([128, 12, 64]), op=ALU.mult),
                         reads=["smalls", "qf_a", "qf_b"], writes=["qf_a", "qf_b"])
                    S.op("dve", _C("tensor_tensor", out=q32, in0=q32, in1=smalls[:, 52:60].unsqueeze(2).broadcast_to([128, 8, 32]), op=ALU.mult),
                         reads=["smalls", "qf_c"], writes=["qf_c"])
                    qna = qf[:, 0:256].rearrange("p (h d) -> p h d", h=4)
                    qsw = qf[:, 256:768].rearrange("p (h d) -> p h d", h=8)
                    S.op("dve", _C("tensor_tensor", out=qna, in0=qna, in1=GN[:, 0:64].unsqueeze(1).broadcast_to([128, 4, 64]), op=ALU.mult),
                         reads=["SM", "qf_a"], writes=["qf_a"])
                    S.op("dve", _C("tensor_tensor", out=qsw, in0=qsw, in1=GN[:, 128:192].unsqueeze(1).broadcast_to([128, 8, 64]), op=ALU.mult),
                         reads=["SM", "qf_b"], writes=["qf_b"])
                    S.op("dve", _C("tensor_tensor", out=q32, in0=q32, in1=GN[:, 256:288].unsqueeze(1).broadcast_to([128, 8, 32]), op=ALU.mult),
                         reads=["SM", "qf_c"], writes=["qf_c"])
                    rope2("dve", [
                        (qf[:, 256:768].rearrange("p (h d two) -> p h d two", h=8, two=2), 8, 32, cosb[:, i, :], sinb[:, i, :], "qf_b", rtq, ["sqq"], []),
                        (qf[:, 768:1024].rearrange("p (h d two) -> p h d two", h=8, two=2), 8, 16, cosc[:, i, :], sinc[:, i, :], "qf_c", rtq2, [], ["Omix"]),
                    ], "rtq")
                    S.op("dve", _C("tensor_copy", out=qb, in_=qf), reads=["qf_a", "qf_b", "qf_c"], writes=["qb"])
                    if l == 0 and i == 1:
                        tap("qf", qf, [128, 1024], ["qf_a", "qf_b", "qf_c"])
                    for ch in range(8):
                        bk = ch // 4
                        S.op("pe", _C("matmul",
                            banks[bk][:, (ch % 4) * 128:(ch % 4 + 1) * 128], lhsT=qb[:, ch * 128:(ch + 1) * 128],
                            rhs=(permb[:] if ch < 2 else identb[:]), start=True, stop=True, skip_group_check=True),
                            reads=["qb", "permb", "identb"], writes=[BK(bk)])
                    b0 = banks[0][:].rearrange("p (c n) -> p c n", c=4)
                    b1 = banks[1][:].rearrange("p (c n) -> p c n", c=4)
                    QTn = QT[:, 0:4, :].rearrange("p (c two) n -> p c two n", two=2)
                    for hl in range(2):
                        S.op("dve", _C("tensor_scalar", out=QTn[:, :, hl, :], in0=b0[:, 0:2, :], scalar1=rowmask[:, hl:hl + 1], scalar2=None, op0=ALU.mult),
                             reads=[BK(0), "rowmask"], writes=[qk_])
                    for g in range(2):
                        S.op("dve", _C("tensor_scalar", out=QT[:, 4 + 4 * g:6 + 4 * g, :], in0=b0[:, 2:4, :], scalar1=rowmask[:, g:g + 1], scalar2=None, op0=ALU.mult),
                             reads=[BK(0), "rowmask"], writes=[qk_])
                        S.op("dve", _C("tensor_scalar", out=QT[:, 6 + 4 * g:8 + 4 * g, :], in0=b1[:, 0:2, :], scalar1=rowmask[:, g:g + 1], scalar2=None, op0=ALU.mult),
                             reads=[BK(1), "rowmask"], writes=[qk_])
                    QTd = QT[:, 12:20, :].rearrange("p (hf u) n -> p hf u n", u=4)
                    for u in range(4):
                        S.op("dve", _C("tensor_scalar", out=QTd[:, :, u, :], in0=b1[:, 2:4, :], scalar1=rowmask[:, 2 + u:3 + u], scalar2=None, op0=ALU.mult),
                             reads=[BK(1), "rowmask"], writes=[qk_])

                def steps_and_back(i, fe_next):
                    QT = QTs[i % 2]
                    qk_ = "QT%d" % (i % 2)
                    steps = []
                    for (j, inval) in na_blocks(i):
                        steps.append(("na", j, inval))
                    for b4 in range(4):
                        steps.append(("nac", b4, None))
                    for g in range(2):
                        for j in (i - 1, i, i + 1):
                            if 0 <= j < NB:
                                steps.append(("sw", j, g))
                        for b4 in range(4):
                            steps.append(("swc", b4, g))
                    for j in range(NB):
                        for hf in range(2):
                            steps.append(("da", j, hf))
                    for b4 in range(4):
                        for hf in range(2):
                            steps.append(("dac", b4, hf))

                    acc_first = [True, True, True]

                    def emit_qk(st, sbk):
                        kind, j, x = st
                        if kind in ("na", "nac"):
                            for hp in range(2):
                                if kind == "na":
                                    lh = KT[:, hp, j * 128:(j + 1) * 128]
                                    rk_ = ["KT"]
                                else:
                                    lh = CTXKT[:, hp, j * 128:(j + 1) * 128]
                                    rk_ = ["CTXKT"]
                                S.op("pe", _C("matmul",
                                    banks[sbk][:, hp * 256:(hp + 1) * 256], lhsT=lh, rhs=QT[:, 2 * hp:2 * hp + 2, :],
                                    start=True, stop=True, skip_group_check=True),
                                    reads=rk_ + [qk_], writes=[BK(sbk)])
                        elif kind in ("sw", "swc"):
                            g = x
                            if kind == "sw":
                                lh = KT[:, 2, j * 128:(j + 1) * 128]
                                rk_ = ["KT"]
                            else:
                                lh = CTXKT[:, 2, j * 128:(j + 1) * 128]
                                rk_ = ["CTXKT"]
                            S.op("pe", _C("matmul",
                                banks[sbk][:, :], lhsT=lh, rhs=QT[:, 4 + 4 * g:8 + 4 * g, :],
                                start=True, stop=True, skip_group_check=True),
                                reads=rk_ + [qk_], writes=[BK(sbk)])
                        else:
                            hf = x
                            if kind == "da":
                                lh = KT[:, 3 + hf, j * 128:(j + 1) * 128]
                                rk_ = ["KT"]
                            else:
                                lh = CTXKT[:, 3 + hf, j * 128:(j + 1) * 128]
                                rk_ = ["CTXKT"]
                            S.op("pe", _C("matmul",
                                banks[sbk][:, :], lhsT=lh, rhs=QT[:, 12 + 4 * hf:16 + 4 * hf, :],
                                start=True, stop=True, skip_group_check=True),
                                reads=rk_ + [qk_], writes=[BK(sbk)])

                    def emit_exp(st, sbk, pbi):
                        kind, j, x = st
                        P = PT[pbi]
                        pk = "PT%d" % pbi
                        if kind == "na":
                            bc = biascol[:, NF_NA + i * 7 + (j - i + 3):NF_NA + i * 7 + (j - i + 3) + 1]
                            sc = 0.125
                        elif kind == "sw":
                            bc = biascol[:, NF_SW + i * 3 + (j - i + 1):NF_SW + i * 3 + (j - i + 1) + 1]
                            sc = 0.125
                        elif kind == "da":
                            bc = biascol[:, NF_DA + i * 16 + j:NF_DA + i * 16 + j + 1]
                            sc = 32 ** -0.5
                        elif kind == "dac":
                            bc = 0.0
                            sc = 32 ** -0.5
                        else:
                            bc = 0.0
                            sc = 0.125
                        S.op("act", _C("activation", out=P, in_=banks[sbk][:, :], func=AF.Exp, bias=bc, scale=sc),
                             reads=[BK(sbk), "biascol"], writes=[pk])
                        if kind == "na":
                            e0 = 2 * (j - i) + 7
                            Pv = P.rearrange("p (h n) -> p h n", h=4)
                            tv = TAB[:, :, e0:e0 + 2, :].rearrange("p h a c -> p h (a c)")
                            S.op("dve", _C("tensor_tensor", out=Pv, in0=Pv, in1=tv, op=ALU.mult),
                                 reads=[pk, "TAB"], writes=[pk])
                            for (ak, a) in x:
                                arr = 1 - a
                                S.op("dve", _C("memset",
                                    P[ak * 64:(ak + 1) * 64, :].rearrange("p (h n) -> p h n", h=4)[:, :, arr * 64:(arr + 1) * 64], 0.0),
                                    reads=[pk], writes=[pk])
                        elif kind == "sw" and j != i:
                            which = 0 if j < i else 1
                            Pv = P.rearrange("p (h n) -> p h n", h=4)
                            mv = swmask[:, which, :].unsqueeze(1).broadcast_to([128, 4, 128])
                            S.op("dve", _C("tensor_tensor", out=Pv, in0=Pv, in1=mv, op=ALU.mult),
                                 reads=[pk, "swmask"], writes=[pk])

                    def emit_pv(st, pbi):
                        kind, j, x = st
                        P = PT[pbi]
                        pk = "PT%d" % pbi
                        for m in range(4):
                            if kind in ("na", "nac"):
                                key = ("na", m)
                                vh = m
                            elif kind in ("sw", "swc"):
                                key = ("sw", 4 * x + m)
                                vh = 4 + x
                            else:
                                hh_ = 2 * x + m // 2
                                key = ("da", 4 * x + m)
                                vh = 6 + hh_
                            slot, col = ACC[key]
                            bk = 2 + slot
                            if kind in ("na", "sw", "da"):
                                rv = V[:, j, vh, 0:65]
                                rk_ = ["V", "Vones"]
                            else:
                                rv = CTXV[:, j, vh, 0:65]
                                rk_ = ["CTXVd", "CTXVo"]
                            st_ = acc_first[slot]
                            acc_first[slot] = False
                            S.op("pe", _C("matmul",
                                banks[bk][:, col:col + 65], lhsT=P[:, m * 128:(m + 1) * 128], rhs=rv,
                                start=st_, stop=False, skip_group_check=True),
                                reads=[pk] + rk_, writes=[BK(bk)])

                    nst = len(steps)
                    sb_of = []
                    pb_of = []
                    for s_ in range(nst):
                        sb_of.append(5 + (sctr[0] % 3))
                        sctr[0] += 1
                        pb_of.append(pctr[0] % 4)
                        pctr[0] += 1
                    LA = 2
                    for s_ in range(min(LA, nst)):
                        emit_qk(steps[s_], sb_of[s_])
                        emit_exp(steps[s_], sb_of[s_], pb_of[s_])
                    per = (len(fe_next) + nst - 1) // nst if fe_next else 0
                    fpos = 0
                    for s_ in range(nst):
                        if s_ + LA < nst:
                            emit_qk(steps[s_ + LA], sb_of[s_ + LA])
                            emit_exp(steps[s_ + LA], sb_of[s_ + LA], pb_of[s_ + LA])
                        emit_pv(steps[s_], pb_of[s_])
                        if fpos < len(fe_next):
                            S.ops.extend(fe_next[fpos:fpos + per])
                            fpos += per
                    S.ops.extend(fe_next[fpos:])

                    S.op("dve", _C("tensor_copy", out=Oacc[:, 0, :], in_=banks[2][:, 0:AWW]), reads=[BK(2)], writes=["Oacc"])
                    S.op("act", _C("activation", out=Oacc[:, 1, :], in_=banks[3][:, 0:AWW], func=AF.Copy), reads=[BK(3)], writes=["Oacc"])
                    S.op("dve", _C("tensor_copy", out=Oacc[:, 2, :], in_=banks[4][:, 0:AWW]), reads=[BK(4)], writes=["Oacc"])

                def back(i):
                    def accv(kind, lo, n):
                        slot, col = ACC[(kind, lo)]
                        return Oacc[:, slot, col:col + AW * n].rearrange("p (h d) -> p h d", h=n), "Oacc"

                    runs = [("sw", 0, 4, 256), ("na", 0, 3, 0), ("sw", 4, 4, 512), ("na", 3, 1, 192), ("da", 0, 2, None), ("da", 2, 6, None)]
                    rec = smalls
                    ri = 0
                    for (kind, lo, n, ocol) in runs:
                        av, bkey = accv(kind, lo, n)
                        rr = rec[:, ri:ri + n]
                        if kind == "sw":
                            S.op("dve", _C("tensor_tensor", out=rr, in0=av[:, :, 64], in1=esink[:, lo:lo + n], op=ALU.add),
                                 reads=[bkey, "esink"], writes=["smalls"])
                            S.op("dve", _C("reciprocal", out=rr, in_=rr), reads=["smalls"], writes=["smalls"])
                        else:
                            S.op("dve", _C("reciprocal", out=rr, in_=av[:, :, 64]), reads=[bkey], writes=["smalls"])
                        if kind == "da":
                            ov = Ofin[:, lo:lo + n, :]
                            okey = "sqq"
                        else:
                            ov = Omix[:, ocol:ocol + 64 * n].rearrange("p (h d) -> p h d", h=n)
                            okey = "Omix"
                        S.op("dve", _C("tensor_tensor",
                            out=ov, in0=av[:, :, 0:64], in1=rr.unsqueeze(2).broadcast_to([128, n, 64]), op=ALU.mult),
                            reads=[bkey, "smalls"], writes=[okey])
                        ri += n
                    O4 = Ofin.rearrange("p (h c) d -> p h c d", c=2)
                    S.op("dve", _C("scalar_tensor_tensor", out=ddt, in0=O4[:, :, 1, :], scalar=neglam, in1=O4[:, :, 0, :], op0=ALU.mult, op1=ALU.add),
                         reads=["sqq", "neglam"], writes=["sqq"])
                    S.op("act", _C("activation", out=sqd, in_=ddt, func=AF.Square), reads=["sqq"], writes=["sqq"])
                    S.op("dve", _C("tensor_reduce", out=rec[:, 24:28], in_=sqd, axis=AX.X, op=ALU.add), reads=["sqq"], writes=["smalls"])
                    S.op("act", _C("activation", out=rec[:, 28:32], in_=rec[:, 24:28], func=AF.Ln, scale=1.0 / 64, bias=EPS), reads=["smalls"], writes=["smalls"])
                    S.op("act", _C("activation", out=rec[:, 32:36], in_=rec[:, 28:32], func=AF.Exp, scale=-0.5), reads=["smalls"], writes=["smalls"])
                    S.op("dve", _C("tensor_tensor", out=ddt, in0=ddt, in1=rec[:, 32:36].unsqueeze(2).broadcast_to([128, 4, 64]), op=ALU.mult),
                         reads=["smalls", "sqq"], writes=["sqq"])
                    S.op("dve", _C("tensor_tensor", out=Omix[:, 768:1024].rearrange("p (h d) -> p h d", h=4), in0=ddt,
                                                          in1=SG.unsqueeze(1).broadcast_to([128, 4, 64]), op=ALU.mult),
                         reads=["sqq", "SG"], writes=["Omix"])
                    if l == 0 and i == 1:
                        tap("Omix", Omix, [128, 1024], ["Omix"], BF16)
                    iq = i % 2
                    for rnd in range(2):
                        for q4 in range(4):
                            ch = rnd * 4 + q4
                            S.op("pe", _C("matmul",
                                banks[0][:, q4 * 128:(q4 + 1) * 128], lhsT=Omix[:, ch * 128:(ch + 1) * 128],
                                rhs=(permb[:] if ch < 2 else identb[:]), start=True, stop=True, skip_group_check=True),
                                reads=["Omix", "permb", "identb"], writes=[BK(0)])
                        S.op("dve", _C("tensor_copy",
                            out=OT[:, rnd * 4:(rnd + 1) * 4, iq * 128:(iq + 1) * 128],
                            in_=banks[0][:].rearrange("p (c n) -> p c n", c=4)),
                            reads=[BK(0)], writes=["OT"])
                    if iq == 1:
                        t0p = (i - 1) * 128
                        for co in range(KC):
                            wb_ = woctr[0] % 2
                            woctr[0] += 1
                            S.dma("pool", _C("dma_start",
                                out=wo[wb_], in_=wout_d[l, :, co * 128:(co + 1) * 128].rearrange("(kc p) n -> p kc n", p=128)),
                                writes=["wo%d" % wb_])
                            for kc in range(KC):
                                S.op("pe", _C("matmul",
                                    banks[1][:, 0:256], lhsT=wo[wb_][:, kc, :], rhs=OT[:, kc, :],
                                    start=(kc == 0), stop=(kc == KC - 1)),
                                    reads=["wo%d" % wb_, "OT"], writes=[BK(1)])
                            S.op("dve", _C("scalar_tensor_tensor",
                                out=XT[:, co, t0p:t0p + 256], in0=banks[1][:, 0:256], scalar=mcol(l, 2, co),
                                in1=XT[:, co, t0p:t0p + 256], op0=ALU.mult, op1=ALU.add),
                                reads=[BK(1), "modsb", "XTh0", "XTh1"], writes=["XTh0", "XTh1"])


                def capture(i):
                    saved = S.ops
                    S.ops = []
                    front(i)
                    got = S.ops
                    S.ops = saved
                    return got

                def capture_back(i):
                    saved = S.ops
                    S.ops = []
                    back(i)
                    got = S.ops
                    S.ops = saved
                    return got

                if a2_blocks > 0:
                    S.ops.extend(capture(0))
                pending = []
                for i in range(a2_blocks):
                    fe_next = capture(i + 1) if i + 1 < a2_blocks else []
                    steps_and_back(i, pending + fe_next)
                    pending = capture_back(i)
                S.ops.extend(pending)

            if do_ffn:
                S.barrier()
                ar = Arena(BIG, NBIG)
                ACTT = ar.take([128, NJ, 1024], BF16)
                H2T = ar.take([128, KC, 1026], BF16)
                wg = [ar.take([128, 8, 256], BF16) for _ in range(2)]
                wv = [ar.take([128, 8, 256], BF16) for _ in range(2)]
                wd = [ar.take([128, NJ, 128], BF16) for _ in range(2)]
                sqf = [ar.take([128, 512], BF16) for _ in range(2)]
                rs2 = ar.take([128, 512], F32)
                tmpf = [ar.take([128, 512], F32) for _ in range(2)]
                ugs2 = [ar.take([128, 1026], F32) for _ in range(2)]
                uvs2 = [ar.take([128, 1026], F32) for _ in range(2)]
                usets = [(0, 1), (2, 3), (5, 6)]
                uctr = [0]
                prev_j = [None]
                g_done = set()
                ygs = [ar.take([128, 1024], F32) for _ in range(2)]
                yv = ar.take([128, 1024], F32)
                cwn = ar.take([128, 2, 44], F32)
                halo_save = ar.take([128, KC, 1], BF16)
                for tapi, slot in ((0, 0), (2, 1)):
                    c0 = (l * 3 + tapi) * 44
                    S.op("dve", _C("tensor_scalar",
                        out=cwn[:, slot, :], in0=convwT[:, c0:c0 + 44], scalar1=pflag[:, 0:1], scalar2=-1.0, op0=ALU.mult, op1=ALU.mult),
                        reads=["convwT", "pflag"], writes=["cwn"])
                pctr2 = [0]
                def ffn_f1(hh):
                    h0 = hh * 1024
                    if hh == 0:
                        segs = [(1, 513), (513, 1025), (1025, 1026)]
                        zc = 0
                    else:
                        segs = [(0, 1), (1, 513), (513, 1025)]
                        zc = 1025
                    S.op("pool", _C("memset", H2T[:, :, zc:zc + 1], 0.0), writes=["fh"])
                    for (n0, n1) in segs:
                        w = n1 - n0
                        tok0 = h0 - 1 + n0
                        if hh == 1 and n0 == 0:
                            S.op("pool", _C("tensor_copy", out=H2T[:, :, 0:1], in_=halo_save), reads=["halo_save"], writes=["fh"])
                            continue
                        adaln(lambda c, n0=n0, n1=n1: H2T[:, c, n0:n1], tok0, w, G2, l, 3, sqf, rs2, tmpf, 5, "f", xkeys=(["XTh1"] if hh == 1 else ["XTh0", "XTh1"]))
                    if hh == 0:
                        S.op("pool", _C("tensor_copy", out=halo_save, in_=H2T[:, :, 1024:1025]), reads=["fh"], writes=["halo_save"])
                def ffn_f2(hh):
                    h0 = hh * 1024
                    g_done.clear()
                    def ffn_post(j):
                        ub = j % 2
                        yg = ygs[j % 2]
                        ygk = "yg%d" % (j % 2)
                        for (us, ukey, yy, ykey, chn) in ((ugs2[ub], "ugs%d" % ub, yg, ygk, j), (uvs2[ub], "uvs%d" % ub, yv, "yv", NJ + j)):
                            cw0 = convwT[:, (l * 3 + 0) * 44 + chn:(l * 3 + 0) * 44 + chn + 1]
                            cw1 = convwT[:, (l * 3 + 1) * 44 + chn:(l * 3 + 1) * 44 + chn + 1]
                            cw2 = convwT[:, (l * 3 + 2) * 44 + chn:(l * 3 + 2) * 44 + chn + 1]
                            cbb = convbT[:, l * 44 + chn:l * 44 + chn + 1]
                            if not (ykey != "yv" and j in g_done):
                                S.op("act", _C("activation", out=yy, in_=us[:, 1:1025], func=AF.Identity, bias=cbb, scale=cw1),
                                    reads=[ukey, "convwT", "convbT"], writes=[ykey])
                            S.op("dve", _C("scalar_tensor_tensor",
                                out=yy, in0=us[:, 0:1024], scalar=cw0, in1=yy, op0=ALU.mult, op1=ALU.add),
                                reads=[ukey, "convwT", ykey], writes=[ykey])
                            S.op("dve", _C("scalar_tensor_tensor",
                                out=yy, in0=us[:, 2:1026], scalar=cw2, in1=yy, op0=ALU.mult, op1=ALU.add),
                                reads=[ukey, "convwT", ykey], writes=[ykey])
                            y4 = yy.rearrange("p (s n) -> p s n", s=4)
                            u0 = us[:, 0:1024].rearrange("p (s n) -> p s n", s=4)
                            u2 = us[:, 2:1026].rearrange("p (s n) -> p s n", s=4)
                            S.op("dve", _C("scalar_tensor_tensor",
                                out=y4[:, :, 0:1], in0=u0[:, :, 0:1], scalar=cwn[:, 0, chn:chn + 1], in1=y4[:, :, 0:1], op0=ALU.mult, op1=ALU.add),
                                reads=[ukey, "cwn", ykey], writes=[ykey])
                            S.op("dve", _C("scalar_tensor_tensor",
                                out=y4[:, :, 255:256], in0=u2[:, :, 255:256], scalar=cwn[:, 1, chn:chn + 1], in1=y4[:, :, 255:256], op0=ALU.mult, op1=ALU.add),
                                reads=[ukey, "cwn", ykey], writes=[ykey])
                        S.op("act", _C("activation", out=yg, in_=yg, func=AF.Silu), reads=[ygk], writes=[ygk])
                        S.op("dve", _C("tensor_tensor", out=ACTT[:, j, :], in0=yg, in1=yv, op=ALU.mult),
                             reads=[ygk, "yv"], writes=["ACTT"])
                        if j + 1 < NJ and prev_j[0] is not None and prev_j[0] == j:
                            jn = j + 1
                            ubn = jn % 2
                            S.op("act", _C("activation", out=ygs[jn % 2], in_=ugs2[ubn][:, 1:1025], func=AF.Identity,
                                           bias=convbT[:, l * 44 + jn:l * 44 + jn + 1],
                                           scale=convwT[:, (l * 3 + 1) * 44 + jn:(l * 3 + 1) * 44 + jn + 1]),
                                 reads=["ugs%d" % ubn, "convwT", "convbT"], writes=["yg%d" % (jn % 2)])
                            g_done.add(jn)

                    for c_ in range(2):
                        S.dma("pool", _C("dma_start",
                            out=wd[c_], in_=wdn_d[l, :, c_ * 128:(c_ + 1) * 128].rearrange("(j q) n -> q j n", q=128)),
                            writes=["wd%d" % c_])
                    for p in range(11):
                        b = pctr2[0] % 2
                        pctr2[0] += 1
                        S.dma("pool", _C("dma_start",
                            out=wg[b], in_=wup_d[l, :, p * 256:(p + 1) * 256].rearrange("(kc q) n -> q kc n", q=128)),
                            writes=["wg%d" % b])
                        S.dma("pool", _C("dma_start",
                            out=wv[b], in_=wup_d[l, :, DFF + p * 256:DFF + (p + 1) * 256].rearrange("(kc q) n -> q kc n", q=128)),
                            writes=["wv%d" % b])
                        for sub in range(2):
                            j = 2 * p + sub
                            ub = j % 2
                            for (wt, wkey, tgt, tkey, hb) in ((wg[b], "wg%d" % b, ugs2[ub], "ugs%d" % ub, 4), (wv[b], "wv%d" % b, uvs2[ub], "uvs%d" % ub, 7)):
                                bset = usets[uctr[0] % 3]
                                hc = 2 * (uctr[0] % 8)
                                uctr[0] += 1
                                for part in range(3):
                                    if part < 2:
                                        ob = banks[bset[part]][:, :]
                                        okey = BK(bset[part])
                                        rsl = (part * 512, part * 512 + 512)
                                    else:
                                        ob = banks[hb][:, hc:hc + 2]
                                        okey = BK(hb)
                                        rsl = (1024, 1026)
                                    for kc in range(KC):
                                        S.op("pe", _C("matmul",
                                            ob, lhsT=wt[:, kc, sub * 128:(sub + 1) * 128], rhs=H2T[:, kc, rsl[0]:rsl[1]],
                                            start=(kc == 0), stop=(kc == KC - 1), skip_group_check=True),
                                            reads=[wkey, "fh"], writes=[okey])
                                S.op("act", _C("activation", out=tgt[:, 0:512], in_=banks[bset[0]][:, :], func=AF.Copy), reads=[BK(bset[0])], writes=[tkey])
                                S.op("act", _C("activation", out=tgt[:, 512:1024], in_=banks[bset[1]][:, :], func=AF.Copy), reads=[BK(bset[1])], writes=[tkey])
                                S.op("act", _C("activation", out=tgt[:, 1024:1026], in_=banks[hb][:, hc:hc + 2], func=AF.Copy), reads=[BK(hb)], writes=[tkey])
                            if prev_j[0] is not None:
                                ffn_post(prev_j[0])
                            prev_j[0] = j
                    ffn_post(prev_j[0])
                    prev_j[0] = None
                def ffn_f3(hh):
                    h0 = hh * 1024
                    for c in range(KC):
                        b = c % 2
                        if c >= 2:
                          S.dma("pool", _C("dma_start",
                            out=wd[b], in_=wdn_d[l, :, c * 128:(c + 1) * 128].rearrange("(j q) n -> q j n", q=128)),
                            writes=["wd%d" % b])
                        for tg in range(2):
                            ob = 6 + tg
                            for j in range(NJ):
                                S.op("pe", _C("matmul",
                                    banks[ob][:, :], lhsT=wd[b][:, j, :], rhs=ACTT[:, j, tg * 512:(tg + 1) * 512],
                                    start=(j == 0), stop=(j == NJ - 1)),
                                    reads=["wd%d" % b, "ACTT"], writes=[BK(ob)])
                            ta = h0 + tg * 512
                            S.op("dve", _C("scalar_tensor_tensor",
                                out=XT[:, c, ta:ta + 512], in0=banks[ob][:, :], scalar=mcol(l, 5, c),
                                in1=XT[:, c, ta:ta + 512], op0=ALU.mult, op1=ALU.add),
                                reads=[BK(ob), "modsb", "XTh%d" % hh], writes=["XTh%d" % hh])


                def cap(fn, hh):
                    saved = S.ops
                    S.ops = []
                    fn(hh)
                    got = S.ops
                    S.ops = saved
                    return got

                ffn_f1(0)
                ffn_f2(0)
                fa = cap(ffn_f1, 1)
                fb = cap(ffn_f3, 0)
                per_ = max(1, len(fb) // max(1, len(fa)))
                ia = 0
                for k_, o_ in enumerate(fb):
                    S.ops.append(o_)
                    if k_ % per_ == per_ - 1 and ia < len(fa):
                        S.ops.append(fa[ia])
                        ia += 1
                S.ops.extend(fa[ia:])
                ffn_f2(1)
                ffn_f3(1)

        S.barrier()
        ar = Arena(BIG, NBIG)
        ys = [ar.take([128, 1024], F32) for _ in range(2)]
        for t in range(NB):
            b = t % 2
            for half in range(2):
                bank = (2 * t + half) % 4
                for q in range(4):
                    c = half * 4 + q
                    S.op("pe", _C("transpose",
                        banks[bank][:, q * 128:(q + 1) * 128], XT[:, c, t * 128:(t + 1) * 128], identf[:]),
                        reads=["XTh0", "XTh1", "XT%d" % t, "identf"], writes=[BK(bank)])
                if half == 0:
                    S.op("act", _C("activation", out=ys[b][:, 0:512], in_=banks[bank][:, :], func=AF.Copy),
                         reads=[BK(bank)], writes=["ys%d_0" % b])
                else:
                    S.op("dve", _C("tensor_copy", out=ys[b][:, 512:1024], in_=banks[bank][:, :]),
                         reads=[BK(bank)], writes=["ys%d_1" % b])
            S.dma("sp", _C("dma_start", out=y_d[t * 128:(t + 1) * 128, :], in_=ys[b]),
                  reads=["ys%d_0" % b, "ys%d_1" % b], final_wait=True)
        S.emit(nc, es)
    return nc, dbg_outs


def _rope_tables(dim):
    n = dim // 4
    inv = (1.0 / (10000.0 ** (np.arange(n, dtype=np.float32) / np.float32(n)))).astype(np.float32)
    t = np.arange(2048)
    row = (t // 64).astype(np.float32)
    col = (t % 64).astype(np.float32)
    ang = np.concatenate([row[:, None] * inv, col[:, None] * inv], axis=-1).astype(np.float32)
    return np.cos(ang).astype(np.float32), np.sin(ang).astype(np.float32)


def _tok_major(a):
    return np.ascontiguousarray(a.reshape(NB, 128, -1).transpose(1, 0, 2))


def _static_tables():
    bf = ml_dtypes.bfloat16
    st = {}
    cb, sb_ = _rope_tables(64)
    cc, sc = _rope_tables(32)
    st["rope_s"] = [_tok_major(cb), _tok_major(sb_), _tok_major(cc), _tok_major(sc)]
    st["rope_p"] = [np.ones((128, NB, 32), np.float32), np.zeros((128, NB, 32), np.float32),
                    np.ones((128, NB, 16), np.float32), np.zeros((128, NB, 16), np.float32)]
    bs = np.zeros((128, NF), np.float32)
    bp = np.zeros((128, NF), np.float32)
    for i in range(NB):
        for dj in range(7):
            j = i + dj - 3
            ok_p = 0 <= j < NB and (j // 2 == i // 2)
            bp[:, NF_NA + i * 7 + dj] = 0.0 if ok_p else NEG
        for dj in range(3):
            j = i + dj - 1
            ok_p = 0 <= j < NB and (j // 2 == i // 2)
            bp[:, NF_SW + i * 3 + dj] = 0.0 if ok_p else NEG
        for j in range(NB):
            bp[:, NF_DA + i * 16 + j] = 0.0 if (j // 2 == i // 2) else NEG
    st["bias_s"] = bs
    st["bias_p"] = bp
    k = np.arange(128)[:, None]
    q = np.arange(128)[None, :]
    sm = np.zeros((128, 2, 128), np.float32)
    sm[:, 0, :] = (k >= q)
    sm[:, 1, :] = (k <= q)
    st["swm_s"] = sm.astype(bf)
    st["swm_p"] = np.ones((128, 2, 128), np.float32).astype(bf)
    nm = np.zeros((128, 16, 64), np.float32)
    for p in range(128):
        ak, ck_ = p // 64, p % 64
        for e2 in range(16):
            dr = e2 - 8 + ak
            if abs(dr) > 7:
                continue
            for cr in range(64):
                c = 63 - cr
                qs = min(max(c - 8, 0), 48)
                if qs <= ck_ < qs + 16:
                    nm[p, e2, cr] = 1.0
    st["nam_s"] = nm.astype(bf)
    st["nam_p"] = np.ones((128, 16, 64), np.float32).astype(bf)
    st["identf"] = np.eye(128, dtype=np.float32)
    st["identb"] = np.eye(128, dtype=np.float32).astype(bf)
    st["permb"] = np.eye(128, dtype=np.float32)[:, ::-1].copy().astype(bf)
    st["onesb"] = np.ones((128, 128), np.float32).astype(bf)
    rm = np.zeros((128, 6), np.float32)
    rm[0:64, 0] = 1.0
    rm[64:128, 1] = 1.0
    for u in range(4):
        rm[32 * u:32 * u + 32, 2 + u] = 1.0
    st["rowmask"] = rm
    return st


_SW_Q_ORDER = [0, 4, 1, 5, 2, 6, 3, 7]


def make_in_maps(inp):
    st = _static_tables()
    f = lambda a: np.ascontiguousarray(np.asarray(a, dtype=np.float32))
    w_in = f(inp["w_in"])
    o = 0
    seg = {}
    for name, wdt in (("naq", 256), ("nak", 256), ("nav", 256), ("swq", 512), ("swk", 128), ("swv", 128), ("daq", 256), ("dak", 256), ("dav", 256)):
        seg[name] = (o, o + wdt)
        o += wdt

    def cols(name):
        a, b = seg[name]
        return w_in[:, :, a:b]

    swq = cols("swq").reshape(L, D, 8, 64)[:, :, _SW_Q_ORDER, :].reshape(L, D, 512)
    w_kv = np.ascontiguousarray(np.concatenate([cols("nak"), cols("swk"), cols("dak"), cols("nav"), cols("swv"), cols("dav")], axis=-1))
    w_q = np.ascontiguousarray(np.concatenate([cols("naq"), swq, cols("daq")], axis=-1))

    def colT(v, n):
        return np.ascontiguousarray(v.reshape(L, n, 128).transpose(2, 0, 1).reshape(128, L * n))

    bmodT = colT(f(inp["b_mod"]), 48)
    gattnT = colT(f(inp["g_attn"]), 8)
    gffnT = colT(f(inp["g_ffn"]), 8)
    convwT = np.ascontiguousarray(f(inp["conv_w"]).reshape(L, 3, 44, 128).transpose(3, 0, 1, 2).reshape(128, L * 3 * 44))
    convbT = colT(f(inp["conv_b"]), 44)
    naq = f(inp["na_qk_g"])
    swg = f(inp["sw_qk_g"])
    dag = f(inp["da_qk_g"])
    small = np.concatenate([naq[:, 0], naq[:, 1], swg[:, 0], swg[:, 1], dag[:, 0], dag[:, 1],
                            f(inp["sw_sink"]), f(inp["da_lambda"]).reshape(L, 128), f(inp["da_subln_g"])], axis=-1)
    small = np.ascontiguousarray(small)
    assert small.shape == (L, 520)
    rpb = f(inp["na_rpb"]).reshape(L, 4 * 465)
    rpbpad_s = np.zeros((L, RPBLEN), np.float32)
    rpbpad_s[:, PADOFF:PADOFF + 1860] = rpb
    rpbpad_p = np.zeros((L, RPBLEN), np.float32)

    shared = dict(w_mod=f(inp["w_mod"]), w_kv=w_kv, w_q=w_q, w_out=f(inp["w_out"]), w_up=f(inp["w_up"]), w_down=f(inp["w_down"]),
                  bmodT=bmodT, gattnT=gattnT, gffnT=gffnT, convwT=convwT, convbT=convbT, small=small,
                  identf=st["identf"], identb=st["identb"], permb=st["permb"], onesb=st["onesb"], rowmask=st["rowmask"])
    xs_ = f(inp["x_sample"])
    xp = f(inp["x_prompt"])
    cc = f(inp["c"])
    cctx = f(inp["c_ctx"])
    ck_all = np.concatenate([f(inp["cache_na_k"]).reshape(4, L, 512, 256), f(inp["cache_sw_k"]).reshape(4, L, 512, 128),
                             f(inp["cache_da_k"]).reshape(4, L, 512, 256)], axis=-1)
    cv_all = np.concatenate([f(inp["cache_na_v"]).reshape(4, L, 512, 256), f(inp["cache_sw_v"]).reshape(4, L, 512, 128),
                             f(inp["cache_da_v"]).reshape(4, L, 512, 256)], axis=-1)
    zc = np.zeros((L, 512, 640), np.float32)
    maps = []
    for core in range(8):
        m = dict(shared)
        if core < 4:
            m["x"] = np.ascontiguousarray(xs_[core])
            cv_ = cc[core]
            r = st["rope_s"]
            m["biascol"] = st["bias_s"]
            m["swmask"] = st["swm_s"]
            m["namask"] = st["nam_s"]
            m["rpbpad"] = rpbpad_s
            m["ctxone"] = np.ones((128, 1), np.float32)
            m["pflag"] = np.zeros((128, 1), np.float32)
            m["ck"] = np.ascontiguousarray(ck_all[core])
            m["cv"] = np.ascontiguousarray(cv_all[core])
        else:
            g = core - 4
            m["x"] = np.ascontiguousarray(xp[8 * g:8 * g + 8].reshape(2048, D))
            cv_ = cctx
            r = st["rope_p"]
            m["biascol"] = st["bias_p"]
            m["swmask"] = st["swm_p"]
            m["namask"] = st["nam_p"]
            m["rpbpad"] = rpbpad_p
            m["ctxone"] = np.zeros((128, 1), np.float32)
            m["pflag"] = np.ones((128, 1), np.float32)
            m["ck"] = zc
            m["cv"] = zc
        m["cvecT"] = np.ascontiguousarray(cv_.reshape(8, 128).T)
        m["cosb"], m["sinb"], m["cosc"], m["sinc"] = r
        maps.append(m)
    return maps


_PROG = {}


def _get_prog(key=("full",), **kw):
    if key not in _PROG:
        _PROG[key] = build_program(**kw)
    return _PROG[key]


def assemble(results):
    y_s = np.stack([results[c]["y"] for c in range(4)], axis=0)
    y_p = np.concatenate([results[c]["y"].reshape(8, 256, D) for c in range(4, 8)], axis=0)
    okv = np.concatenate([results[c]["okv"].reshape(L, 8, 256, 1280).transpose(1, 0, 2, 3) for c in range(4, 8)], axis=0)
    nk = np.ascontiguousarray(okv[..., 0:256]).reshape(32, L, 256, 4, 64)
    sk = np.ascontiguousarray(okv[..., 256:384]).reshape(32, L, 256, 2, 64)
    dk = np.ascontiguousarray(okv[..., 384:640]).reshape(32, L, 256, 4, 2, 32)
    nv = np.ascontiguousarray(okv[..., 640:896]).reshape(32, L, 256, 4, 64)
    sv = np.ascontiguousarray(okv[..., 896:1024]).reshape(32, L, 256, 2, 64)
    dv = np.ascontiguousarray(okv[..., 1024:1280]).reshape(32, L, 256, 4, 64)
    return (np.ascontiguousarray(y_p), np.ascontiguousarray(y_s), nk, nv, sk, sv, dk, dv)


def kernel(**inputs):
    nc, _ = _get_prog()
    maps = make_in_maps(inputs)
    res = run_bass_kernel_spmd(nc, maps, core_ids=list(range(8)))
    return assemble(res.results)
```

```python
import math
from contextlib import ExitStack

import numpy as np
import ml_dtypes

import concourse.bass as bass
import concourse.mybir as mybir
from concourse.bass_utils import run_bass_kernel_spmd

F32 = mybir.dt.float32
BF16 = mybir.dt.bfloat16
AF = mybir.ActivationFunctionType
ALU = mybir.AluOpType
AX = mybir.AxisListType

L = 4
NB = 16
D = 1024
KC = 8
DFF = 2816
NJ = 22
EPS = 1e-6
NEG = -30000.0
PADOFF = 128
RPBLEN = 2176
ENGS = ("pe", "act", "dve", "pool", "sp")


class _Op:
    __slots__ = ("eng", "fn", "reads", "writes", "is_dma", "deps", "needs_inc",
                 "sem", "val", "idx", "final_wait", "barrier")

    def __init__(self, eng, fn, reads, writes, is_dma, final_wait):
        self.eng = eng
        self.fn = fn
        self.reads = reads
        self.writes = writes
        self.is_dma = is_dma
        self.deps = []
        self.needs_inc = False
        self.sem = None
        self.val = 0
        self.final_wait = final_wait
        self.barrier = False


class Sched:
    def __init__(self, same_engine_sync=True, n_dma_sems=32):
        self.ops = []
        self.same_engine_sync = same_engine_sync
        self.n_dma_sems = n_dma_sems

    def op(self, eng, fn, reads=(), writes=()):
        o = _Op(eng, fn, tuple(reads), tuple(writes), False, False)
        self.ops.append(o)
        return o

    def dma(self, eng, fn, reads=(), writes=(), final_wait=False):
        o = _Op(eng, fn, tuple(reads), tuple(writes), True, final_wait)
        self.ops.append(o)
        return o

    def barrier(self):
        for e in ENGS:
            o = _Op(e, None, (), (), False, False)
            o.barrier = True
            self.ops.append(o)

    def analyze(self):
        last_w = {}
        readers = {}
        waited = {e: {s: -1 for s in ENGS} for e in ENGS}
        waited_dma = {e: set() for e in ENGS}
        dma_slot_last = [None] * self.n_dma_sems
        dma_ctr = {"sp": 0, "pool": 0, "act": 0}
        half = self.n_dma_sems // 2
        last_compute = {e: None for e in ENGS}
        all_dma = []
        for idx, o in enumerate(self.ops):
            o.idx = idx
            deps = set()
            if o.barrier:
                for e in ENGS:
                    if last_compute[e] is not None and e != o.eng:
                        deps.add(last_compute[e])
                    if e == o.eng and last_compute[e] is not None and e != "pe":
                        deps.add(last_compute[e])
                for d in all_dma:
                    if d not in waited_dma[o.eng]:
                        deps.add(d)
            raw = set()
            for r in o.reads:
                w = last_w.get(r)
                if w is not None:
                    deps.add(w)
                    raw.add(w)
            for wkey in o.writes:
                w = last_w.get(wkey)
                if w is not None:
                    deps.add(w)
                for rd in readers.get(wkey, ()):
                    deps.add(rd)
            if o.is_dma:
                if o.eng == "pool":
                    slot = half + dma_ctr["pool"] % (self.n_dma_sems - half)
                else:
                    slot = dma_ctr["sp"] % half
                dma_ctr["pool" if o.eng == "pool" else "sp"] += 1
                prev = dma_slot_last[slot]
                if prev is not None:
                    deps.add(prev)
                dma_slot_last[slot] = idx
                o.sem = ("dma", slot)
                all_dma.append(idx)
            deps.discard(idx)
            best = {}
            out = []
            for d in sorted(deps):
                p = self.ops[d]
                if p.is_dma:
                    if d in waited_dma[o.eng]:
                        continue
                    waited_dma[o.eng].add(d)
                    out.append(d)
                else:
                    if p.eng == o.eng and not o.is_dma and not o.barrier and \
                            (p.eng == "pe" or not self.same_engine_sync):
                        continue
                    if waited[o.eng][p.eng] >= d:
                        continue
                    best[p.eng] = max(best.get(p.eng, -1), d)
            for e, d in best.items():
                waited[o.eng][e] = d
                out.append(d)
            o.deps = out
            for d in out:
                self.ops[d].needs_inc = True
            if not o.barrier:
                for r in o.reads:
                    readers.setdefault(r, []).append(idx)
                for wkey in o.writes:
                    last_w[wkey] = idx
                    readers[wkey] = []
                if not o.is_dma:
                    last_compute[o.eng] = idx
        self.final = [o.idx for o in self.ops if o.final_wait]
        cnt = {e: 0 for e in ENGS}
        dma_cnt = [0] * self.n_dma_sems
        for o in self.ops:
            if o.barrier:
                continue
            if o.is_dma:
                slot = o.sem[1]
                dma_cnt[slot] += 16
                o.val = dma_cnt[slot]
                o.needs_inc = True
            elif o.needs_inc:
                cnt[o.eng] += 1
                o.val = cnt[o.eng]
                o.sem = ("eng", o.eng)

    def emit(self, nc, es):
        self.analyze()
        sems = {}
        for e in ENGS:
            sems[("eng", e)] = es.enter_context(nc.semaphore("s_" + e))
        for i in range(self.n_dma_sems):
            sems[("dma", i)] = es.enter_context(nc.semaphore("s_dma%d" % i))
        per = {e: [] for e in ENGS}
        for o in self.ops:
            per[o.eng].append(o)
        ops = self.ops
        final = self.final
        block = es.enter_context(nc.Block())

        def run(engine_obj, lst, ename):
            for o in lst:
                for d in o.deps:
                    p = ops[d]
                    engine_obj.wait_ge(sems[p.sem], p.val)
                if o.barrier:
                    continue
                ins = o.fn(engine_obj)
                if o.needs_inc:
                    ins.then_inc(sems[o.sem], 16 if o.is_dma else 1)
            for d in final:
                p = ops[d]
                if p.eng == ename:
                    engine_obj.wait_ge(sems[p.sem], p.val)

        @block.tensor
        def _(e):
            run(e, per["pe"], "pe")

        @block.scalar
        def _(e):
            run(e, per["act"], "act")

        @block.vector
        def _(e):
            run(e, per["dve"], "dve")

        @block.gpsimd
        def _(e):
            run(e, per["pool"], "pool")

        @block.sync
        def _(e):
            run(e, per["sp"], "sp")


def _C(name, *a, **k):
    def f(e):
        return getattr(e, name)(*a, **k)
    return f


_ARENA_HI = 0


class Arena:
    def __init__(self, big, nel, base=0):
        self.big = big
        self.nel = nel
        self.off = base
        self.hi = base

    def take(self, shape, dtype):
        global _ARENA_HI
        n = 1
        for s in shape[1:]:
            n *= s
        nb = n * (4 if dtype == F32 else 2)
        nb = (nb + 63) // 64 * 64
        el = nb // 2
        a = self.off
        self.off += el
        self.hi = max(self.hi, self.off)
        assert self.off <= self.nel, ("arena overflow", self.off, self.nel)
        _ARENA_HI = max(_ARENA_HI, self.off)
        v = self.big[:, a:a + el]
        if dtype == F32:
            v = v.bitcast(F32)[:, 0:n]
        else:
            v = v[:, 0:n]
        if len(shape) == 3:
            v = v.rearrange("p (a b) -> p a b", a=shape[1])
        elif len(shape) == 4:
            v = v.rearrange("p (a b c) -> p a b c", a=shape[1], b=shape[2])
        return v


def _na_r0(r):
    return min(max(r - 4, 0), 24)


def na_blocks(i):
    res = []
    for j in range(NB):
        inval = []
        anyv = False
        for a in range(2):
            r = 2 * i + a
            for ak in range(2):
                rk = 2 * j + ak
                ok = _na_r0(r) <= rk <= _na_r0(r) + 7
                if ok:
                    anyv = True
                else:
                    inval.append((ak, a))
        if anyv:
            res.append((j, inval))
    return res


NF_NA = 0
NF_SW = 112
NF_DA = 160
NF = 160 + 256

ACC = {}
AW = 66
for _h in range(4):
    ACC[("sw", _h)] = (0, _h * AW)
for _h in range(3):
    ACC[("na", _h)] = (0, 4 * AW + _h * AW)
for _h in range(4):
    ACC[("sw", 4 + _h)] = (1, _h * AW)
ACC[("na", 3)] = (1, 4 * AW)
ACC[("da", 0)] = (1, 5 * AW)
ACC[("da", 1)] = (1, 6 * AW)
for _u in range(6):
    ACC[("da", 2 + _u)] = (2, _u * AW)


def build_program(n_layers=L, do_attn=True, do_ffn=True, taps=(), a1_blocks=NB, a2_blocks=NB, do_mod=True):
    nc = bass.Bass("TRN2", target_bir_lowering=False)
    S = Sched()
    taps = set(taps)
    dbg_outs = {}

    def din(name, shape, dt=F32):
        return nc.dram_tensor(name, list(shape), dt, kind="ExternalInput").ap()

    x_d = din("x", [2048, D])
    cvec_d = din("cvecT", [128, 8])
    cosb_d = din("cosb", [128, NB, 32])
    sinb_d = din("sinb", [128, NB, 32])
    cosc_d = din("cosc", [128, NB, 16])
    sinc_d = din("sinc", [128, NB, 16])
    bias_d = din("biascol", [128, NF])
    swm_d = din("swmask", [128, 2, 128], BF16)
    nam_d = din("namask", [128, 16, 64], BF16)
    rpb_d = din("rpbpad", [L, RPBLEN])
    ctxone_d = din("ctxone", [128, 1])
    pflag_d = din("pflag", [128, 1])
    ck_d = din("ck", [L, 512, 640])
    cv_d = din("cv", [L, 512, 640])
    wmod_d = din("w_mod", [L, D, 6 * D])
    wkv_d = din("w_kv", [L, D, 1280])
    wq_d = din("w_q", [L, D, 1024])
    wout_d = din("w_out", [L, D, D])
    wup_d = din("w_up", [L, D, 2 * DFF])
    wdn_d = din("w_down", [L, DFF, D])
    bmod_d = din("bmodT", [128, L * 48])
    gattn_d = din("gattnT", [128, L * 8])
    gffn_d = din("gffnT", [128, L * 8])
    convw_d = din("convwT", [128, L * 3 * 44])
    convb_d = din("convbT", [128, L * 44])
    small_d = din("small", [L, 520])
    identf_d = din("identf", [128, 128])
    identb_d = din("identb", [128, 128], BF16)
    permb_d = din("permb", [128, 128], BF16)
    onesb_d = din("onesb", [128, 128], BF16)
    rowmask_d = din("rowmask", [128, 6])

    y_d = nc.dram_tensor("y", [2048, D], F32, kind="ExternalOutput").ap()
    okv_d = nc.dram_tensor("okv", [L, 2048, 1280], F32, kind="ExternalOutput").ap()

    with ExitStack() as es:
        def sb(name, shape, dt=F32):
            return es.enter_context(nc.sbuf_tensor(name, list(shape), dt))

        XT = sb("XT", [128, KC, 2048])
        identf = sb("identf_s", [128, 128])
        identb = sb("identb_s", [128, 128], BF16)
        permb = sb("permb_s", [128, 128], BF16)
        onesb = sb("onesb_s", [128, 128], BF16)
        rowmask = sb("rowmask_s", [128, 6])
        cosb = sb("cosb_s", [128, NB, 32])
        sinb = sb("sinb_s", [128, NB, 32])
        cosc = sb("cosc_s", [128, NB, 16])
        sinc = sb("sinc_s", [128, NB, 16])
        biascol = sb("biascol_s", [128, NF])
        swmask = sb("swmask_s", [128, 2, 128], BF16)
        namask = sb("namask_s", [128, 16, 64], BF16)
        ctxone = sb("ctxone_s", [128, 1])
        pflag = sb("pflag_s", [128, 1])
        cvecT = sb("cvecT_s", [128, 8])
        silub = sb("silub", [128, 8], BF16)
        bmodT = sb("bmodT_s", [128, L * 48])
        modsb = sb("modsb", [128, L * 48])
        gattnT = sb("gattnT_s", [128, L * 8])
        gffnT = sb("gffnT_s", [128, L * 8])
        G1 = sb("G1", [128, L * 8])
        G2 = sb("G2", [128, L * 8])
        convwT = sb("convwT_s", [128, L * 3 * 44])
        convbT = sb("convbT_s", [128, L * 44])
        NBIG = 64000
        BIG = sb("BIG", [128, NBIG], BF16)
        banks = [es.enter_context(nc.psum_tensor("bank%d" % i, [128, 512], F32)) for i in range(8)]

        def BK(i):
            return "B%d" % i

        def tap(name, ap, shape, reads, dt=F32):
            if name not in taps:
                return
            d = nc.dram_tensor("dbg_" + name, list(shape), dt, kind="ExternalOutput").ap()
            dbg_outs[name] = d
            S.dma("sp", _C("dma_start", out=d, in_=ap), reads=reads, final_wait=True)

        def ld(dst, src, key):
            S.dma("sp", _C("dma_start", out=dst, in_=src), writes=[key])

        ld(identf[:], identf_d, "identf")
        ld(identb[:], identb_d, "identb")
        ld(permb[:], permb_d, "permb")
        ld(onesb[:], onesb_d, "onesb")
        ld(rowmask[:], rowmask_d, "rowmask")
        ld(cosb[:], cosb_d, "rope")
        ld(sinb[:], sinb_d, "rope")
        ld(cosc[:], cosc_d, "rope")
        ld(sinc[:], sinc_d, "rope")
        ld(biascol[:], bias_d, "biascol")
        ld(swmask[:], swm_d, "swmask")
        ld(namask[:], nam_d, "namask")
        ld(ctxone[:], ctxone_d, "ctxone")
        ld(pflag[:], pflag_d, "pflag")
        ld(cvecT[:], cvec_d, "cvecT")
        ld(bmodT[:], bmod_d, "bmodT")
        ld(gattnT[:], gattn_d, "gattnT")
        ld(gffnT[:], gffn_d, "gffnT")
        ld(convwT[:], convw_d, "convwT")
        ld(convbT[:], convb_d, "convbT")

        ar = Arena(BIG, NBIG)
        xs = [ar.take([128, 1024], F32) for _ in range(2)]
        wm = [ar.take([128, 8, 512], BF16) for _ in range(2)]
        for t in range(NB):
            b = t % 2
            S.dma("sp", _C("dma_start", out=xs[b], in_=x_d[t * 128:(t + 1) * 128, :]),
                  writes=["xs%d" % b])
            for half in range(2):
                bank = (2 * t + half) % 4
                for q in range(4):
                    c = half * 4 + q
                    S.op("pe", _C("transpose",
                        banks[bank][:, q * 128:(q + 1) * 128], xs[b][:, c * 128:(c + 1) * 128], identf[:]),
                        reads=["xs%d" % b, "identf"], writes=[BK(bank)])
                if half == 0:
                    S.op("act", _C("activation",
                        out=XT[:, 0:4, t * 128:(t + 1) * 128],
                        in_=banks[bank][:].rearrange("p (c n) -> p c n", c=4), func=AF.Copy),
                        reads=[BK(bank)], writes=["XT%d" % t])
                else:
                    S.op("dve", _C("tensor_copy",
                        out=XT[:, 4:8, t * 128:(t + 1) * 128],
                        in_=banks[bank][:].rearrange("p (c n) -> p c n", c=4)),
                        reads=[BK(bank)], writes=["XT%d" % t])

        S.op("act", _C("activation", out=silub[:], in_=cvecT[:], func=AF.Silu),
             reads=["cvecT"], writes=["silub"])
        MB = 4
        first_mod = True
        for l in range(n_layers if do_mod else 0):
            for piece in range(12):
                b = (l * 12 + piece) % 2
                S.dma("pool", _C("dma_start",
                    out=wm[b], in_=wmod_d[l, :, piece * 512:(piece + 1) * 512].rearrange("(kc p) n -> p kc n", p=128)),
                    writes=["wm%d" % b])
                for oc in range(4):
                    col = l * 48 + piece * 4 + oc
                    for kc in range(KC):
                        S.op("pe", _C("matmul",
                            banks[MB][:, col:col + 1], lhsT=wm[b][:, kc, oc * 128:(oc + 1) * 128],
                            rhs=silub[:, kc:kc + 1], start=first_mod, stop=(kc == KC - 1), skip_group_check=True),
                            reads=["wm%d" % b, "silub"], writes=[BK(MB)])
                        first_mod = False
        nm = n_layers * 48
        S.op("dve", _C("tensor_tensor", out=modsb[:, 0:nm], in0=banks[MB][:, 0:nm], in1=bmodT[:, 0:nm], op=ALU.add),
             reads=[BK(MB), "bmodT"], writes=["modsb"])
        for l in range(n_layers):
            S.op("dve", _C("scalar_tensor_tensor",
                out=G1[:, l * 8:(l + 1) * 8], in0=modsb[:, l * 48 + 8:l * 48 + 16], scalar=1.0,
                in1=gattnT[:, l * 8:(l + 1) * 8], op0=ALU.add, op1=ALU.mult),
                reads=["modsb", "gattnT"], writes=["G"])
            S.op("dve", _C("scalar_tensor_tensor",
                out=G2[:, l * 8:(l + 1) * 8], in0=modsb[:, l * 48 + 32:l * 48 + 40], scalar=1.0,
                in1=gffnT[:, l * 8:(l + 1) * 8], op0=ALU.add, op1=ALU.mult),
                reads=["modsb", "gffnT"], writes=["G"])
        tap("modsb", modsb[:], [128, L * 48], ["modsb"])
        tap("G1", G1[:], [128, L * 8], ["G"])

        def mcol(l, k, c):
            i0 = l * 48 + k * 8 + c
            return modsb[:, i0:i0 + 1]

        def adaln(dst_fn, tok0, w, Gt, l, kshift, sqbuf, rsbuf, tmps, sbank, tagp, xkeys=("XTh0", "XTh1")):
            xkeys = list(xkeys)
            for c in range(KC):
                S.op("act", _C("activation", out=sqbuf[c % 2][:, 0:w], in_=XT[:, c, tok0:tok0 + w], func=AF.Square),
                     reads=xkeys, writes=[tagp + "sq%d" % (c % 2)])
                S.op("pe", _C("matmul", banks[sbank][:, 0:w], lhsT=onesb[:], rhs=sqbuf[c % 2][:, 0:w],
                                                   start=(c == 0), stop=(c == KC - 1)),
                     reads=[tagp + "sq%d" % (c % 2), "onesb"], writes=[BK(sbank)])
            S.op("act", _C("activation", out=rsbuf[:, 0:w], in_=banks[sbank][:, 0:w], func=AF.Ln, scale=1.0 / D, bias=EPS),
                 reads=[BK(sbank)], writes=[tagp + "rs"])
            S.op("act", _C("activation", out=rsbuf[:, 0:w], in_=rsbuf[:, 0:w], func=AF.Exp, scale=-0.5),
                 reads=[tagp + "rs"], writes=[tagp + "rs"])
            for c in range(KC):
                tb = tmps[c % 2]
                S.op("dve", _C("scalar_tensor_tensor",
                    out=tb[:, 0:w], in0=XT[:, c, tok0:tok0 + w], scalar=Gt[:, l * 8 + c:l * 8 + c + 1],
                    in1=rsbuf[:, 0:w], op0=ALU.mult, op1=ALU.mult),
                    reads=xkeys + ["G", tagp + "rs"], writes=[tagp + "tmp%d" % (c % 2)])
                S.op("act", _C("activation",
                    out=dst_fn(c), in_=tb[:, 0:w], func=AF.Identity, bias=mcol(l, kshift, c), scale=1.0),
                    reads=[tagp + "tmp%d" % (c % 2), "modsb"], writes=[tagp + "h"])

        def adaln_blk(hdst, hkey, tok0, Gt, l, kshift, sq8, rsbuf, tmp8, sbank, tagp, tmpkey=None, offload=False, affine_dve=False, scol=0):
            w = 128
            if offload:
                S.op("dve", _C("tensor_tensor", out=sq8, in0=XT[:, :, tok0:tok0 + w], in1=XT[:, :, tok0:tok0 + w], op=ALU.mult),
                     reads=["XTh0", "XTh1"], writes=[tagp + "sq8"])
            else:
                S.op("act", _C("activation", out=sq8, in_=XT[:, :, tok0:tok0 + w], func=AF.Square),
                     reads=["XTh0", "XTh1"], writes=[tagp + "sq8"])
            for c in range(KC):
                S.op("pe", _C("matmul", banks[sbank][:, scol:scol + w], lhsT=onesb[:], rhs=sq8[:, c, :],
                              start=(c == 0), stop=(c == KC - 1)),
                     reads=[tagp + "sq8", "onesb"], writes=[BK(sbank)])
            S.op("act", _C("activation", out=rsbuf[:, 0:w], in_=banks[sbank][:, scol:scol + w], func=AF.Ln, scale=1.0 / D, bias=EPS),
                 reads=[BK(sbank)], writes=[tagp + "rs"])
            S.op("act", _C("activation", out=rsbuf[:, 0:w], in_=rsbuf[:, 0:w], func=AF.Exp, scale=-0.5),
                 reads=[tagp + "rs"], writes=[tagp + "rs"])
            S.op("dve", _C("tensor_tensor", out=tmp8, in0=XT[:, :, tok0:tok0 + w],
                           in1=rsbuf[:, 0:w].unsqueeze(1).broadcast_to([128, KC, w]), op=ALU.mult),
                 reads=["XTh0", "XTh1", tagp + "rs"], writes=[tmpkey or (tagp + "tmp8")])
            for c in range(KC):
                if offload or (affine_dve and c % 2 == 0):
                    S.op("dve", _C("tensor_scalar", out=hdst[:, c, :], in0=tmp8[:, c, :], scalar1=Gt[:, l * 8 + c:l * 8 + c + 1],
                                   scalar2=mcol(l, kshift, c), op0=ALU.mult, op1=ALU.add),
                         reads=[tmpkey or (tagp + "tmp8"), "modsb", "G"], writes=[hkey])
                else:
                    S.op("act", _C("activation", out=hdst[:, c, :], in_=tmp8[:, c, :], func=AF.Identity,
                                   bias=mcol(l, kshift, c), scale=Gt[:, l * 8 + c:l * 8 + c + 1]),
                         reads=[tmpkey or (tagp + "tmp8"), "modsb", "G"], writes=[hkey])

        for l in range(n_layers):
            lam_init = 0.8 - 0.6 * math.exp(-0.3 * l)
            S.barrier()
            ar = Arena(BIG, NBIG)
            KT = ar.take([128, 5, 2048], BF16)
            V = ar.take([128, NB, 10, 66], BF16)
            CTXKT = ar.take([128, 5, 512], BF16)
            CTXV = ar.take([128, 4, 10, 66], BF16)
            TAB = ar.take([128, 4, 16, 64], BF16)
            WA = ar.take([128, 8, 1280], BF16)
            sq8 = ar.take([128, KC, 128], BF16)
            hT = ar.take([128, KC, 128], BF16)
            rsb = ar.take([128, 128], F32)
            SM = ar.take([128, 520], F32)
            esink = ar.take([128, 8], F32)
            lamt = ar.take([128, 8], F32)
            SG = ar.take([128, 64], F32)
            smalls = ar.take([128, 64], F32)
            base_shared = ar.off
            kcats = [ar.take([128, 640], F32) for _ in range(2)]
            vcats = [ar.take([128, 640], F32) for _ in range(2)]
            sqks = [ar.take([128, 640], F32) for _ in range(2)]
            rts = [[ar.take([128, 64], F32) for _ in range(4)] for _ in range(2)]
            rt2s = [[ar.take([128, 128], F32) for _ in range(4)] for _ in range(2)]
            kbs = [ar.take([128, 640], BF16) for _ in range(2)]
            smks = [ar.take([128, 64], F32) for _ in range(2)]
            a0_base = ar.off
            CKs = ar.take([128, 4, 640], BF16)
            TABF = ar.take([128, 16, 64], F32)
            a0_hi = ar.off
            ar.off = a0_base
            tmp8as = [ar.take([128, KC, 128], F32) for _ in range(2)]
            sq8s = [sq8, ar.take([128, KC, 128], BF16)]
            rsbs = [rsb, ar.take([128, 128], F32)]
            hTs = [hT, ar.take([128, KC, 128], BF16)]
            ar.off = max(ar.off, a0_hi)
            hiA1 = ar.off
            ar.off = base_shared
            qf = ar.take([128, 1024], F32)
            sqq = ar.take([128, 1024], F32)
            rtq = [sqq[:, k_ * 256:(k_ + 1) * 256] for k_ in range(4)]
            qb = ar.take([128, 1024], BF16)
            QTs = [ar.take([128, 20, 128], BF16) for _ in range(2)]
            PT = [ar.take([128, 512], BF16) for _ in range(4)]
            Ofin = sqq[:, 0:512].rearrange("p (a d) -> p a d", a=8)
            ddt = sqq[:, 512:768].rearrange("p (a d) -> p a d", a=4)
            sqd = sqq[:, 768:1024].rearrange("p (a d) -> p a d", a=4)
            Omix = ar.take([128, 1024], BF16)
            rtq2 = [Omix[:, k_ * 256:(k_ + 1) * 256].bitcast(F32) for k_ in range(4)]
            OT = ar.take([128, 8, 256], BF16)
            AWW = 7 * AW
            Oacc = ar.take([128, 3, AWW], F32)
            wo = [WA[:, :, 1024 + 128 * k_:1024 + 128 * (k_ + 1)] for k_ in range(2)]
            wqv = WA[:, :, 0:1024]
            tmp8q = sqq.rearrange("p (c n) -> p c n", c=KC)

            if do_attn:
                S.dma("sp", _C("dma_start", out=SM, in_=small_d[l, :].partition_broadcast(128)), writes=["SM"])
                S.op("act", _C("activation", out=esink, in_=SM[:, 320:328], func=AF.Exp), reads=["SM"], writes=["esink"])
                lp = SM[:, 328:456].rearrange("p (a b d) -> p a b d", a=2, b=2)
                S.op("dve", _C("tensor_tensor", out=smalls[:, 0:64].rearrange("p (a d) -> p a d", a=2),
                                                      in0=lp[:, :, 0, :], in1=lp[:, :, 1, :], op=ALU.mult),
                     reads=["SM"], writes=["smalls"])
                S.op("dve", _C("tensor_reduce", out=lamt[:, 0:2], in_=smalls[:, 0:64].rearrange("p (a d) -> p a d", a=2),
                                                      axis=AX.X, op=ALU.add),
                     reads=["smalls"], writes=["lamt"])
                S.op("act", _C("activation", out=lamt[:, 2:4], in_=lamt[:, 0:2], func=AF.Exp), reads=["lamt"], writes=["lamt2"])
                S.op("dve", _C("tensor_tensor", out=lamt[:, 4:5], in0=lamt[:, 3:4], in1=lamt[:, 2:3], op=ALU.subtract),
                     reads=["lamt2"], writes=["lamt3"])
                S.op("dve", _C("tensor_scalar", out=lamt[:, 5:6], in0=lamt[:, 4:5], scalar1=-lam_init, scalar2=None, op0=ALU.add),
                     reads=["lamt3"], writes=["neglam"])
                S.op("dve", _C("tensor_scalar", out=SG, in0=SM[:, 456:520], scalar1=1.0 - lam_init, scalar2=None, op0=ALU.mult),
                     reads=["SM"], writes=["SG"])
                neglam = lamt[:, 5:6]
                S.dma("pool", _C("dma_start", out=CKs, in_=ck_d[l].rearrange("(b p) f -> p b f", p=128)), writes=["CKs"])
                for b4 in range(4):
                    bank = 6 + (b4 % 2)
                    pv = banks[bank][:].bitcast(BF16)
                    for ch in range(5):
                        S.op("pe", _C("transpose", pv[:, ch * 128:(ch + 1) * 128], CKs[:, b4, ch * 128:(ch + 1) * 128], identb[:]),
                             reads=["CKs", "identb"], writes=[BK(bank)])
                    S.op("act", _C("activation", out=CTXKT[:, :, b4 * 128:(b4 + 1) * 128],
                                                                   in_=pv[:, 0:640].rearrange("p (c n) -> p c n", c=5), func=AF.Copy),
                         reads=[BK(bank)], writes=["CTXKT"])
                for b4 in range(4):
                    S.dma("pool", _C("dma_start",
                        out=CTXV[:, b4, :, 0:64], in_=cv_d[l, b4 * 128:(b4 + 1) * 128, :].rearrange("p (h d) -> p h d", h=10)),
                        writes=["CTXVd"])
                S.op("pool", _C("tensor_copy", out=CTXV[:, :, :, 64], in_=ctxone[:, 0:1].unsqueeze(2).broadcast_to([128, 4, 10])),
                     reads=["ctxone"], writes=["CTXVo"])
                S.op("pool", _C("memset", V[:, :, :, 64], 1.0), writes=["Vones"])
                for h in range(4):
                    for ak in range(2):
                        off = PADOFF + h * 465 + (ak - 1) * 31 - 48
                        src = bass.AP(rpb_d.tensor, l * RPBLEN + off, [[1, 64], [31, 16], [1, 64]])
                        S.dma("sp", _C("dma_start", out=TABF[ak * 64:(ak + 1) * 64, :, :], in_=src),
                              writes=["TABF"])
                    S.op("act", _C("activation", out=TAB[:, h, :, :], in_=TABF, func=AF.Exp), reads=["TABF"], writes=["TAB"])
                    S.op("dve", _C("tensor_tensor", out=TAB[:, h, :, :], in0=TAB[:, h, :, :], in1=namask[:], op=ALU.mult),
                         reads=["TAB", "namask"], writes=["TAB"])
                if l == 0:
                    tap("TAB", TAB, [128, 4, 16, 64], ["TAB"], BF16)
                    tap("CTXKT", CTXKT, [128, 5, 512], ["CTXKT"], BF16)
                GN = SM
                S.dma("pool", _C("dma_start", out=WA, in_=wkv_d[l].rearrange("(kc p) n -> p kc n", p=128)), writes=["WA"])
                S.barrier()
                def rope2(eng, groups, tkey):
                    seqs = []
                    for gi, (view, H, half, cs, sn, key, tl, xr, xw) in enumerate(groups):
                        x1 = view[:, :, :, 0]
                        x2 = view[:, :, :, 1]
                        cb_ = cs.unsqueeze(1).broadcast_to([128, H, half])
                        sb_ = sn.unsqueeze(1).broadcast_to([128, H, half])
                        n_ = H * half
                        tv = [tm[:, 0:n_].rearrange("p (h d) -> p h d", h=H) for tm in tl]
                        tk = ["%s_%d_%d" % (tkey, gi, k_) for k_ in range(4)]
                        seqs.append([
                            (_C("tensor_tensor", out=tv[0], in0=x1, in1=cb_, op=ALU.mult), [key, "rope"] + xr, [tk[0]] + xw),
                            (_C("tensor_tensor", out=tv[1], in0=x2, in1=sb_, op=ALU.mult), [key, "rope"] + xr, [tk[1]] + xw),
                            (_C("tensor_tensor", out=tv[2], in0=x1, in1=sb_, op=ALU.mult), [key, "rope"] + xr, [tk[2]] + xw),
                            (_C("tensor_tensor", out=tv[3], in0=x2, in1=cb_, op=ALU.mult), [key, "rope"] + xr, [tk[3]] + xw),
                            (_C("tensor_tensor", out=x1, in0=tv[0], in1=tv[1], op=ALU.subtract), [tk[0], tk[1]], [key]),
                            (_C("tensor_tensor", out=x2, in0=tv[2], in1=tv[3], op=ALU.add), [tk[2], tk[3]], [key]),
                        ])
                    for k_ in range(6):
                        for sq_ in seqs:
                            fn_, rd_, wr_ = sq_[k_]
                            S.op(eng, fn_, reads=rd_, writes=wr_)

                def a1_block(t):
                    tok0 = t * 128
                    kcat = kcats[t % 2]
                    vcat = vcats[t % 2]
                    sfx = str(t % 2)
                    sqk = sqks[t % 2]
                    kb = kbs[t % 2]
                    rt = rts[t % 2]
                    rt2 = rt2s[t % 2]
                    smk_ = smks[t % 2]
                    pb = 0 if t % 2 == 0 else 3
                    hT_ = hTs[t % 2]
                    adaln_blk(hT_, "ah" + sfx, tok0, G1, l, 0, sq8s[t % 2], rsbs[t % 2], tmp8as[t % 2], pb + 2, "a" + sfx, affine_dve=True, scol=256)
                    for nt, (n0, w) in enumerate(((0, 512), (512, 512), (1024, 256))):
                        for kc in range(KC):
                            S.op("pe", _C("matmul",
                                banks[pb + nt][:, 0:w], lhsT=hT_[:, kc, :], rhs=WA[:, kc, n0:n0 + w],
                                start=(kc == 0), stop=(kc == KC - 1)),
                                reads=["ah" + sfx, "WA"], writes=[BK(pb + nt)])
                    S.op("act", _C("activation", out=kcat[:, 0:512], in_=banks[pb][:, :], func=AF.Copy),
                         reads=[BK(pb)], writes=["kc_a" + sfx, "kc_b" + sfx, "kc_c" + sfx])
                    S.op("act", _C("activation", out=kcat[:, 512:640], in_=banks[pb + 1][:, 0:128], func=AF.Copy),
                         reads=[BK(pb + 1)], writes=["kc_c" + sfx])
                    S.op("act", _C("activation", out=vcat[:, 0:384], in_=banks[pb + 1][:, 128:512], func=AF.Copy),
                         reads=[BK(pb + 1)], writes=["vcat" + sfx])
                    S.op("dve", _C("tensor_copy", out=vcat[:, 384:640], in_=banks[pb + 2][:, 0:256]),
                         reads=[BK(pb + 2)], writes=["vcat" + sfx])
                    a1_mark[0] = len(S.ops)
                    S.op("act", _C("activation", out=V[:, t, :, 0:64], in_=vcat.rearrange("p (h d) -> p h d", h=10), func=AF.Copy),
                         reads=["vcat" + sfx], writes=["V"])
                    S.dma("sp", _C("dma_start", out=okv_d[l, tok0:tok0 + 128, 640:1280], in_=vcat),
                          reads=["vcat" + sfx], final_wait=True)
                    S.op("act", _C("activation", out=sqk, in_=kcat, func=AF.Square), reads=["kc_a" + sfx, "kc_b" + sfx, "kc_c" + sfx], writes=["sqk" + sfx])
                    S.op("dve", _C("tensor_reduce", out=smk_[:, 0:6], in_=sqk[:, 0:384].rearrange("p (h d) -> p h d", h=6), axis=AX.X, op=ALU.add),
                         reads=["sqk" + sfx], writes=["smk" + sfx])
                    S.op("dve", _C("tensor_reduce", out=smk_[:, 6:14], in_=sqk[:, 384:640].rearrange("p (h d) -> p h d", h=8), axis=AX.X, op=ALU.add),
                         reads=["sqk" + sfx], writes=["smk" + sfx])
                    S.op("act", _C("activation", out=smk_[:, 16:22], in_=smk_[:, 0:6], func=AF.Ln, scale=1.0 / 64, bias=EPS),
                         reads=["smk" + sfx], writes=["smk" + sfx])
                    S.op("act", _C("activation", out=smk_[:, 22:30], in_=smk_[:, 6:14], func=AF.Ln, scale=1.0 / 32, bias=EPS),
                         reads=["smk" + sfx], writes=["smk" + sfx])
                    S.op("act", _C("activation", out=smk_[:, 32:46], in_=smk_[:, 16:30], func=AF.Exp, scale=-0.5),
                         reads=["smk" + sfx], writes=["smk" + sfx])
                    k64 = kcat[:, 0:384].rearrange("p (h d) -> p h d", h=6)
                    k32 = kcat[:, 384:640].rearrange("p (h d) -> p h d", h=8)
                    S.op("dve", _C("tensor_tensor", out=k64, in0=k64, in1=smk_[:, 32:38].unsqueeze(2).broadcast_to([128, 6, 64]), op=ALU.mult),
                         reads=["smk" + sfx, "kc_a" + sfx, "kc_b" + sfx], writes=["kc_a" + sfx, "kc_b" + sfx])
                    S.op("dve", _C("tensor_tensor", out=k32, in0=k32, in1=smk_[:, 38:46].unsqueeze(2).broadcast_to([128, 8, 32]), op=ALU.mult),
                         reads=["smk" + sfx, "kc_c" + sfx], writes=["kc_c" + sfx])
                    kna = kcat[:, 0:256].rearrange("p (h d) -> p h d", h=4)
                    ksw = kcat[:, 256:384].rearrange("p (h d) -> p h d", h=2)
                    S.op("dve", _C("tensor_tensor", out=kna, in0=kna, in1=GN[:, 64:128].unsqueeze(1).broadcast_to([128, 4, 64]), op=ALU.mult),
                         reads=["SM", "kc_a" + sfx], writes=["kc_a" + sfx])
                    S.op("dve", _C("tensor_tensor", out=ksw, in0=ksw, in1=GN[:, 192:256].unsqueeze(1).broadcast_to([128, 2, 64]), op=ALU.mult),
                         reads=["SM", "kc_b" + sfx], writes=["kc_b" + sfx])
                    S.op("dve", _C("tensor_tensor", out=k32, in0=k32, in1=GN[:, 288:320].unsqueeze(1).broadcast_to([128, 8, 32]), op=ALU.mult),
                         reads=["SM", "kc_c" + sfx], writes=["kc_c" + sfx])

                    rope2("dve", [
                        (kcat[:, 256:384].rearrange("p (h d two) -> p h d two", h=2, two=2), 2, 32, cosb[:, t, :], sinb[:, t, :], "kc_b" + sfx, rt, [], []),
                        (kcat[:, 384:640].rearrange("p (h d two) -> p h d two", h=8, two=2), 8, 16, cosc[:, t, :], sinc[:, t, :], "kc_c" + sfx, rt2, [], []),
                    ], "rtk" + sfx)
                    S.dma("sp", _C("dma_start", out=okv_d[l, tok0:tok0 + 128, 0:640], in_=kcat),
                          reads=["kc_a" + sfx, "kc_b" + sfx, "kc_c" + sfx], final_wait=True)
                    S.op("act", _C("activation", out=kb, in_=kcat, func=AF.Copy), reads=["kc_a" + sfx, "kc_b" + sfx, "kc_c" + sfx], writes=["kb" + sfx])
                    tb_ = 6 + t % 2
                    pv = banks[tb_][:].bitcast(BF16)
                    for ch in range(5):
                        S.op("pe", _C("transpose", pv[:, ch * 128:(ch + 1) * 128], kb[:, ch * 128:(ch + 1) * 128], identb[:]),
                             reads=["kb" + sfx, "identb"], writes=[BK(tb_)])
                    S.op("act", _C("activation", out=KT[:, :, tok0:tok0 + 128], in_=pv[:, 0:640].rearrange("p (c n) -> p c n", c=5), func=AF.Copy),
                         reads=[BK(tb_)], writes=["KT"])

                a1_mark = [0]

                def cap_a1(t):
                    saved = S.ops
                    S.ops = []
                    a1_block(t)
                    got = S.ops
                    S.ops = saved
                    return got[:a1_mark[0]], got[a1_mark[0]:]

                st_a1 = [cap_a1(t) for t in range(a1_blocks)]
                for t in range(0, a1_blocks, 2):
                    if t + 1 < a1_blocks:
                        la = st_a1[t][0] + st_a1[t][1]
                        lb = st_a1[t + 1][0] + st_a1[t + 1][1]
                        for k_ in range(max(len(la), len(lb))):
                            if k_ < len(la):
                                S.ops.append(la[k_])
                            if k_ < len(lb):
                                S.ops.append(lb[k_])
                    else:
                        S.ops.extend(st_a1[t][0] + st_a1[t][1])
                if l == 0:
                    tap("KT", KT, [128, 5, 2048], ["KT"], BF16)
                    tap("V", V, [128, NB, 10, 66], ["V", "Vones"], BF16)

                S.barrier()
                S.dma("pool", _C("dma_start", out=wqv, in_=wq_d[l].rearrange("(kc p) n -> p kc n", p=128)), writes=["WA"])
                sctr = [0]
                pctr = [0]
                woctr = [0]
                def front(i):
                    tok0 = i * 128
                    QT = QTs[i % 2]
                    qk_ = "QT%d" % (i % 2)
                    adaln_blk(hT, "ah", tok0, G1, l, 0, sq8, rsb, tmp8q, 0, "a", tmpkey="sqq", offload=True)
                    for nt in range(2):
                        for kc in range(KC):
                            S.op("pe", _C("matmul",
                                banks[nt][:, :], lhsT=hT[:, kc, :], rhs=wqv[:, kc, nt * 512:(nt + 1) * 512],
                                start=(kc == 0), stop=(kc == KC - 1)),
                                reads=["ah", "WA"], writes=[BK(nt)])
                    S.op("dve", _C("tensor_copy", out=qf[:, 0:512], in_=banks[0][:, :]), reads=[BK(0)], writes=["qf_a", "qf_b"])
                    S.op("dve", _C("tensor_copy", out=qf[:, 512:1024], in_=banks[1][:, :]), reads=[BK(1)], writes=["qf_b", "qf_c"])
                    S.op("dve", _C("tensor_tensor", out=sqq, in0=qf, in1=qf, op=ALU.mult), reads=["qf_a", "qf_b", "qf_c"], writes=["sqq"])
                    S.op("dve", _C("tensor_reduce", out=smalls[:, 0:12], in_=sqq[:, 0:768].rearrange("p (h d) -> p h d", h=12), axis=AX.X, op=ALU.add),
                         reads=["sqq"], writes=["smalls"])
                    S.op("dve", _C("tensor_reduce", out=smalls[:, 12:20], in_=sqq[:, 768:1024].rearrange("p (h d) -> p h d", h=8), axis=AX.X, op=ALU.add),
                         reads=["sqq"], writes=["smalls"])
                    S.op("act", _C("activation", out=smalls[:, 20:32], in_=smalls[:, 0:12], func=AF.Ln, scale=1.0 / 64, bias=EPS),
                         reads=["smalls"], writes=["smalls"])
                    S.op("act", _C("activation", out=smalls[:, 32:40], in_=smalls[:, 12:20], func=AF.Ln, scale=1.0 / 32, bias=EPS),
                         reads=["smalls"], writes=["smalls"])
                    S.op("act", _C("activation", out=smalls[:, 40:60], in_=smalls[:, 20:40], func=AF.Exp, scale=-0.5),
                         reads=["smalls"], writes=["smalls"])
                    q64 = qf[:, 0:768].rearrange("p (h d) -> p h d", h=12)
                    q32 = qf[:, 768:1024].rearrange("p (h d) -> p h d", h=8)
                    S.op("dve", _C("tensor_tensor", out=q64, in0=q64, in1=smalls[:, 40:52].unsqueeze(2).broadcast_to([128, 12, 64]), op=ALU.mult),
                         reads=["smalls", "qf_a", "qf_b"], writes=["qf_a", "qf_b"])
                    S.op("dve", _C("tensor_tensor", out=q32, in0=q32, in1=smalls[:, 52:60].unsqueeze(2).broadcast_to([128, 8, 32]), op=ALU.mult),
                         reads=["smalls", "qf_c"], writes=["qf_c"])
                    qna = qf[:, 0:256].rearrange("p (h d) -> p h d", h=4)
                    qsw = qf[:, 256:768].rearrange("p (h d) -> p h d", h=8)
                    S.op("dve", _C("tensor_tensor", out=qna, in0=qna, in1=GN[:, 0:64].unsqueeze(1).broadcast_to([128, 4, 64]), op=ALU.mult),
                         reads=["SM", "qf_a"], writes=["qf_a"])
                    S.op("dve", _C("tensor_tensor", out=qsw, in0=qsw, in1=GN[:, 128:192].unsqueeze(1).broadcast_to([128, 8, 64]), op=ALU.mult),
                         reads=["SM", "qf_b"], writes=["qf_b"])
                    S.op("dve", _C("tensor_tensor", out=q32, in0=q32, in1=GN[:, 256:288].unsqueeze(1).broadcast_to([128, 8, 32]), op=ALU.mult),
                         reads=["SM", "qf_c"], writes=["qf_c"])
                    rope2("dve", [
                        (qf[:, 256:768].rearrange("p (h d two) -> p h d two", h=8, two=2), 8, 32, cosb[:, i, :], sinb[:, i, :], "qf_b", rtq, ["sqq"], []),
                        (qf[:, 768:1024].rearrange("p (h d two) -> p h d two", h=8, two=2), 8, 16, cosc[:, i, :], sinc[:, i, :], "qf_c", rtq2, [], ["Omix"]),
                    ], "rtq")
                    S.op("dve", _C("tensor_copy", out=qb, in_=qf), reads=["qf_a", "qf_b", "qf_c"], writes=["qb"])
                    if l == 0 and i == 1:
                        tap("qf", qf, [128, 1024], ["qf_a", "qf_b", "qf_c"])
                    for ch in range(8):
                        bk = ch // 4
                        S.op("pe", _C("matmul",
                            banks[bk][:, (ch % 4) * 128:(ch % 4 + 1) * 128], lhsT=qb[:, ch * 128:(ch + 1) * 128],
                            rhs=(permb[:] if ch < 2 else identb[:]), start=True, stop=True, skip_group_check=True),
                            reads=["qb", "permb", "identb"], writes=[BK(bk)])
                    b0 = banks[0][:].rearrange("p (c n) -> p c n", c=4)
                    b1 = banks[1][:].rearrange("p (c n) -> p c n", c=4)
                    QTn = QT[:, 0:4, :].rearrange("p (c two) n -> p c two n", two=2)
                    for hl in range(2):
                        S.op("dve", _C("tensor_scalar", out=QTn[:, :, hl, :], in0=b0[:, 0:2, :], scalar1=rowmask[:, hl:hl + 1], scalar2=None, op0=ALU.mult),
                             reads=[BK(0), "rowmask"], writes=[qk_])
                    for g in range(2):
                        S.op("dve", _C("tensor_scalar", out=QT[:, 4 + 4 * g:6 + 4 * g, :], in0=b0[:, 2:4, :], scalar1=rowmask[:, g:g + 1], scalar2=None, op0=ALU.mult),
                             reads=[BK(0), "rowmask"], writes=[qk_])
                        S.op("dve", _C("tensor_scalar", out=QT[:, 6 + 4 * g:8 + 4 * g, :], in0=b1[:, 0:2, :], scalar1=rowmask[:, g:g + 1], scalar2=None, op0=ALU.mult),
                             reads=[BK(1), "rowmask"], writes=[qk_])
                    QTd = QT[:, 12:20, :].rearrange("p (hf u) n -> p hf u n", u=4)
                    for u in range(4):
                        S.op("dve", _C("tensor_scalar", out=QTd[:, :, u, :], in0=b1[:, 2:4, :], scalar1=rowmask[:, 2 + u:3 + u], scalar2=None, op0=ALU.mult),
                             reads=[BK(1), "rowmask"], writes=[qk_])

                def steps_and_back(i, fe_next):
                    QT = QTs[i % 2]
                    qk_ = "QT%d" % (i % 2)
                    steps = []
                    for (j, inval) in na_blocks(i):
                        steps.append(("na", j, inval))
                    for b4 in range(4):
                        steps.append(("nac", b4, None))
                    for g in range(2):
                        for j in (i - 1, i, i + 1):
                            if 0 <= j < NB:
                                steps.append(("sw", j, g))
                        for b4 in range(4):
                            steps.append(("swc", b4, g))
                    for j in range(NB):
                        for hf in range(2):
                            steps.append(("da", j, hf))
                    for b4 in range(4):
                        for hf in range(2):
                            steps.append(("dac", b4, hf))

                    acc_first = [True, True, True]

                    def emit_qk(st, sbk):
                        kind, j, x = st
                        if kind in ("na", "nac"):
                            for hp in range(2):
                                if kind == "na":
                                    lh = KT[:, hp, j * 128:(j + 1) * 128]
                                    rk_ = ["KT"]
                                else:
                                    lh = CTXKT[:, hp, j * 128:(j + 1) * 128]
                                    rk_ = ["CTXKT"]
                                S.op("pe", _C("matmul",
                                    banks[sbk][:, hp * 256:(hp + 1) * 256], lhsT=lh, rhs=QT[:, 2 * hp:2 * hp + 2, :],
                                    start=True, stop=True, skip_group_check=True),
                                    reads=rk_ + [qk_], writes=[BK(sbk)])
                        elif kind in ("sw", "swc"):
                            g = x
                            if kind == "sw":
                                lh = KT[:, 2, j * 128:(j + 1) * 128]
                                rk_ = ["KT"]
                            else:
                                lh = CTXKT[:, 2, j * 128:(j + 1) * 128]
                                rk_ = ["CTXKT"]
                            S.op("pe", _C("matmul",
                                banks[sbk][:, :], lhsT=lh, rhs=QT[:, 4 + 4 * g:8 + 4 * g, :],
                                start=True, stop=True, skip_group_check=True),
                                reads=rk_ + [qk_], writes=[BK(sbk)])
                        else:
                            hf = x
                            if kind == "da":
                                lh = KT[:, 3 + hf, j * 128:(j + 1) * 128]
                                rk_ = ["KT"]
                            else:
                                lh = CTXKT[:, 3 + hf, j * 128:(j + 1) * 128]
                                rk_ = ["CTXKT"]
                            S.op("pe", _C("matmul",
                                banks[sbk][:, :], lhsT=lh, rhs=QT[:, 12 + 4 * hf:16 + 4 * hf, :],
                                start=True, stop=True, skip_group_check=True),
                                reads=rk_ + [qk_], writes=[BK(sbk)])

                    def emit_exp(st, sbk, pbi):
                        kind, j, x = st
                        P = PT[pbi]
                        pk = "PT%d" % pbi
                        if kind == "na":
                            bc = biascol[:, NF_NA + i * 7 + (j - i + 3):NF_NA + i * 7 + (j - i + 3) + 1]
                            sc = 0.125
                        elif kind == "sw":
                            bc = biascol[:, NF_SW + i * 3 + (j - i + 1):NF_SW + i * 3 + (j - i + 1) + 1]
                            sc = 0.125
                        elif kind == "da":
                            bc = biascol[:, NF_DA + i * 16 + j:NF_DA + i * 16 + j + 1]
                            sc = 32 ** -0.5
                        elif kind == "dac":
                            bc = 0.0
                            sc = 32 ** -0.5
                        else:
                            bc = 0.0
                            sc = 0.125
                        S.op("act", _C("activation", out=P, in_=banks[sbk][:, :], func=AF.Exp, bias=bc, scale=sc),
                             reads=[BK(sbk), "biascol"], writes=[pk])
                        if kind == "na":
                            e0 = 2 * (j - i) + 7
                            Pv = P.rearrange("p (h n) -> p h n", h=4)
                            tv = TAB[:, :, e0:e0 + 2, :].rearrange("p h a c -> p h (a c)")
                            S.op("dve", _C("tensor_tensor", out=Pv, in0=Pv, in1=tv, op=ALU.mult),
                                 reads=[pk, "TAB"], writes=[pk])
                            for (ak, a) in x:
                                arr = 1 - a
                                S.op("dve", _C("memset",
                                    P[ak * 64:(ak + 1) * 64, :].rearrange("p (h n) -> p h n", h=4)[:, :, arr * 64:(arr + 1) * 64], 0.0),
                                    reads=[pk], writes=[pk])
                        elif kind == "sw" and j != i:
                            which = 0 if j < i else 1
                            Pv = P.rearrange("p (h n) -> p h n", h=4)
                            mv = swmask[:, which, :].unsqueeze(1).broadcast_to([128, 4, 128])
                            S.op("dve", _C("tensor_tensor", out=Pv, in0=Pv, in1=mv, op=ALU.mult),
                                 reads=[pk, "swmask"], writes=[pk])

                    def emit_pv(st, pbi):
                        kind, j, x = st
                        P = PT[pbi]
                        pk = "PT%d" % pbi
                        for m in range(4):
                            if kind in ("na", "nac"):
                                key = ("na", m)
                                vh = m
                            elif kind in ("sw", "swc"):
                                key = ("sw", 4 * x + m)
                                vh = 4 + x
                            else:
                                hh_ = 2 * x + m // 2
                                key = ("da", 4 * x + m)
                                vh = 6 + hh_
                            slot, col = ACC[key]
                            bk = 2 + slot
                            if kind in ("na", "sw", "da"):
                                rv = V[:, j, vh, 0:65]
                                rk_ = ["V", "Vones"]
                            else:
                                rv = CTXV[:, j, vh, 0:65]
                                rk_ = ["CTXVd", "CTXVo"]
                            st_ = acc_first[slot]
                            acc_first[slot] = False
                            S.op("pe", _C("matmul",
                                banks[bk][:, col:col + 65], lhsT=P[:, m * 128:(m + 1) * 128], rhs=rv,
                                start=st_, stop=False, skip_group_check=True),
                                reads=[pk] + rk_, writes=[BK(bk)])

                    nst = len(steps)
                    sb_of = []
                    pb_of = []
                    for s_ in range(nst):
                        sb_of.append(5 + (sctr[0] % 3))
                        sctr[0] += 1
                        pb_of.append(pctr[0] % 4)
                        pctr[0] += 1
                    LA = 2
                    for s_ in range(min(LA, nst)):
                        emit_qk(steps[s_], sb_of[s_])
                        emit_exp(steps[s_], sb_of[s_], pb_of[s_])
                    per = (len(fe_next) + nst - 1) // nst if fe_next else 0
                    fpos = 0
                    for s_ in range(nst):
                        if s_ + LA < nst:
                            emit_qk(steps[s_ + LA], sb_of[s_ + LA])
                            emit_exp(steps[s_ + LA], sb_of[s_ + LA], pb_of[s_ + LA])
                        emit_pv(steps[s_], pb_of[s_])
                        if fpos < len(fe_next):
                            S.ops.extend(fe_next[fpos:fpos + per])
                            fpos += per
                    S.ops.extend(fe_next[fpos:])

                    S.op("dve", _C("tensor_copy", out=Oacc[:, 0, :], in_=banks[2][:, 0:AWW]), reads=[BK(2)], writes=["Oacc"])
                    S.op("act", _C("activation", out=Oacc[:, 1, :], in_=banks[3][:, 0:AWW], func=AF.Copy), reads=[BK(3)], writes=["Oacc"])
                    S.op("dve", _C("tensor_copy", out=Oacc[:, 2, :], in_=banks[4][:, 0:AWW]), reads=[BK(4)], writes=["Oacc"])

                def back(i):
                    def accv(kind, lo, n):
                        slot, col = ACC[(kind, lo)]
                        return Oacc[:, slot, col:col + AW * n].rearrange("p (h d) -> p h d", h=n), "Oacc"

                    runs = [("sw", 0, 4, 256), ("na", 0, 3, 0), ("sw", 4, 4, 512), ("na", 3, 1, 192), ("da", 0, 2, None), ("da", 2, 6, None)]
                    rec = smalls
                    ri = 0
                    for (kind, lo, n, ocol) in runs:
                        av, bkey = accv(kind, lo, n)
                        rr = rec[:, ri:ri + n]
                        if kind == "sw":
                            S.op("dve", _C("tensor_tensor", out=rr, in0=av[:, :, 64], in1=esink[:, lo:lo + n], op=ALU.add),
                                 reads=[bkey, "esink"], writes=["smalls"])
                            S.op("dve", _C("reciprocal", out=rr, in_=rr), reads=["smalls"], writes=["smalls"])
                        else:
                            S.op("dve", _C("reciprocal", out=rr, in_=av[:, :, 64]), reads=[bkey], writes=["smalls"])
                        if kind == "da":
                            ov = Ofin[:, lo:lo + n, :]
                            okey = "sqq"
                        else:
                            ov = Omix[:, ocol:ocol + 64 * n].rearrange("p (h d) -> p h d", h=n)
                            okey = "Omix"
                        S.op("dve", _C("tensor_tensor",
                            out=ov, in0=av[:, :, 0:64], in1=rr.unsqueeze(2).broadcast_to([128, n, 64]), op=ALU.mult),
                            reads=[bkey, "smalls"], writes=[okey])
                        ri += n
                    O4 = Ofin.rearrange("p (h c) d -> p h c d", c=2)
                    S.op("dve", _C("scalar_tensor_tensor", out=ddt, in0=O4[:, :, 1, :], scalar=neglam, in1=O4[:, :, 0, :], op0=ALU.mult, op1=ALU.add),
                         reads=["sqq", "neglam"], writes=["sqq"])
                    S.op("act", _C("activation", out=sqd, in_=ddt, func=AF.Square), reads=["sqq"], writes=["sqq"])
                    S.op("dve", _C("tensor_reduce", out=rec[:, 24:28], in_=sqd, axis=AX.X, op=ALU.add), reads=["sqq"], writes=["smalls"])
                    S.op("act", _C("activation", out=rec[:, 28:32], in_=rec[:, 24:28], func=AF.Ln, scale=1.0 / 64, bias=EPS), reads=["smalls"], writes=["smalls"])
                    S.op("act", _C("activation", out=rec[:, 32:36], in_=rec[:, 28:32], func=AF.Exp, scale=-0.5), reads=["smalls"], writes=["smalls"])
                    S.op("dve", _C("tensor_tensor", out=ddt, in0=ddt, in1=rec[:, 32:36].unsqueeze(2).broadcast_to([128, 4, 64]), op=ALU.mult),
                         reads=["smalls", "sqq"], writes=["sqq"])
                    S.op("dve", _C("tensor_tensor", out=Omix[:, 768:1024].rearrange("p (h d) -> p h d", h=4), in0=ddt,
                                                          in1=SG.unsqueeze(1).broadcast_to([128, 4, 64]), op=ALU.mult),
                         reads=["sqq", "SG"], writes=["Omix"])
                    if l == 0 and i == 1:
                        tap("Omix", Omix, [128, 1024], ["Omix"], BF16)
                    iq = i % 2
                    for rnd in range(2):
                        for q4 in range(4):
                            ch = rnd * 4 + q4
                            S.op("pe", _C("matmul",
                                banks[0][:, q4 * 128:(q4 + 1) * 128], lhsT=Omix[:, ch * 128:(ch + 1) * 128],
                                rhs=(permb[:] if ch < 2 else identb[:]), start=True, stop=True, skip_group_check=True),
                                reads=["Omix", "permb", "identb"], writes=[BK(0)])
                        S.op("dve", _C("tensor_copy",
                            out=OT[:, rnd * 4:(rnd + 1) * 4, iq * 128:(iq + 1) * 128],
                            in_=banks[0][:].rearrange("p (c n) -> p c n", c=4)),
                            reads=[BK(0)], writes=["OT"])
                    if iq == 1:
                        t0p = (i - 1) * 128
                        for co in range(KC):
                            wb_ = woctr[0] % 2
                            woctr[0] += 1
                            S.dma("pool", _C("dma_start",
                                out=wo[wb_], in_=wout_d[l, :, co * 128:(co + 1) * 128].rearrange("(kc p) n -> p kc n", p=128)),
                                writes=["wo%d" % wb_])
                            for kc in range(KC):
                                S.op("pe", _C("matmul",
                                    banks[1][:, 0:256], lhsT=wo[wb_][:, kc, :], rhs=OT[:, kc, :],
                                    start=(kc == 0), stop=(kc == KC - 1)),
                                    reads=["wo%d" % wb_, "OT"], writes=[BK(1)])
                            S.op("dve", _C("scalar_tensor_tensor",
                                out=XT[:, co, t0p:t0p + 256], in0=banks[1][:, 0:256], scalar=mcol(l, 2, co),
                                in1=XT[:, co, t0p:t0p + 256], op0=ALU.mult, op1=ALU.add),
                                reads=[BK(1), "modsb", "XTh0", "XTh1"], writes=["XTh0", "XTh1"])


                def capture(i):
                    saved = S.ops
                    S.ops = []
                    front(i)
                    got = S.ops
                    S.ops = saved
                    return got

                def capture_back(i):
                    saved = S.ops
                    S.ops = []
                    back(i)
                    got = S.ops
                    S.ops = saved
                    return got

                if a2_blocks > 0:
                    S.ops.extend(capture(0))
                pending = []
                for i in range(a2_blocks):
                    fe_next = capture(i + 1) if i + 1 < a2_blocks else []
                    steps_and_back(i, pending + fe_next)
                    pending = capture_back(i)
                S.ops.extend(pending)

            if do_ffn:
                S.barrier()
                ar = Arena(BIG, NBIG)
                ACTT = ar.take([128, NJ, 1024], BF16)
                H2T = ar.take([128, KC, 1026], BF16)
                wg = [ar.take([128, 8, 256], BF16) for _ in range(2)]
                wv = [ar.take([128, 8, 256], BF16) for _ in range(2)]
                wd = [ar.take([128, NJ, 128], BF16) for _ in range(2)]
                sqf = [ar.take([128, 512], BF16) for _ in range(2)]
                rs2 = ar.take([128, 512], F32)
                tmpf = [ar.take([128, 512], F32) for _ in range(2)]
                ugs2 = [ar.take([128, 1026], F32) for _ in range(2)]
                uvs2 = [ar.take([128, 1026], F32) for _ in range(2)]
                usets = [(0, 1), (2, 3), (5, 6)]
                uctr = [0]
                prev_j = [None]
                g_done = set()
                ygs = [ar.take([128, 1024], F32) for _ in range(2)]
                yv = ar.take([128, 1024], F32)
                cwn = ar.take([128, 2, 44], F32)
                halo_save = ar.take([128, KC, 1], BF16)
                for tapi, slot in ((0, 0), (2, 1)):
                    c0 = (l * 3 + tapi) * 44
                    S.op("dve", _C("tensor_scalar",
                        out=cwn[:, slot, :], in0=convwT[:, c0:c0 + 44], scalar1=pflag[:, 0:1], scalar2=-1.0, op0=ALU.mult, op1=ALU.mult),
                        reads=["convwT", "pflag"], writes=["cwn"])
                pctr2 = [0]
                def ffn_f1(hh):
                    h0 = hh * 1024
                    if hh == 0:
                        segs = [(1, 513), (513, 1025), (1025, 1026)]
                        zc = 0
                    else:
                        segs = [(0, 1), (1, 513), (513, 1025)]
                        zc = 1025
                    S.op("pool", _C("memset", H2T[:, :, zc:zc + 1], 0.0), writes=["fh"])
                    for (n0, n1) in segs:
                        w = n1 - n0
                        tok0 = h0 - 1 + n0
                        if hh == 1 and n0 == 0:
                            S.op("pool", _C("tensor_copy", out=H2T[:, :, 0:1], in_=halo_save), reads=["halo_save"], writes=["fh"])
                            continue
                        adaln(lambda c, n0=n0, n1=n1: H2T[:, c, n0:n1], tok0, w, G2, l, 3, sqf, rs2, tmpf, 5, "f", xkeys=(["XTh1"] if hh == 1 else ["XTh0", "XTh1"]))
                    if hh == 0:
                        S.op("pool", _C("tensor_copy", out=halo_save, in_=H2T[:, :, 1024:1025]), reads=["fh"], writes=["halo_save"])
                def ffn_f2(hh):
                    h0 = hh * 1024
                    g_done.clear()
                    def ffn_post(j):
                        ub = j % 2
                        yg = ygs[j % 2]
                        ygk = "yg%d" % (j % 2)
                        for (us, ukey, yy, ykey, chn) in ((ugs2[ub], "ugs%d" % ub, yg, ygk, j), (uvs2[ub], "uvs%d" % ub, yv, "yv", NJ + j)):
                            cw0 = convwT[:, (l * 3 + 0) * 44 + chn:(l * 3 + 0) * 44 + chn + 1]
                            cw1 = convwT[:, (l * 3 + 1) * 44 + chn:(l * 3 + 1) * 44 + chn + 1]
                            cw2 = convwT[:, (l * 3 + 2) * 44 + chn:(l * 3 + 2) * 44 + chn + 1]
                            cbb = convbT[:, l * 44 + chn:l * 44 + chn + 1]
                            if not (ykey != "yv" and j in g_done):
                                S.op("act", _C("activation", out=yy, in_=us[:, 1:1025], func=AF.Identity, bias=cbb, scale=cw1),
                                    reads=[ukey, "convwT", "convbT"], writes=[ykey])
                            S.op("dve", _C("scalar_tensor_tensor",
                                out=yy, in0=us[:, 0:1024], scalar=cw0, in1=yy, op0=ALU.mult, op1=ALU.add),
                                reads=[ukey, "convwT", ykey], writes=[ykey])
                            S.op("dve", _C("scalar_tensor_tensor",
                                out=yy, in0=us[:, 2:1026], scalar=cw2, in1=yy, op0=ALU.mult, op1=ALU.add),
                                reads=[ukey, "convwT", ykey], writes=[ykey])
                            y4 = yy.rearrange("p (s n) -> p s n", s=4)
                            u0 = us[:, 0:1024].rearrange("p (s n) -> p s n", s=4)
                            u2 = us[:, 2:1026].rearrange("p (s n) -> p s n", s=4)
                            S.op("dve", _C("scalar_tensor_tensor",
                                out=y4[:, :, 0:1], in0=u0[:, :, 0:1], scalar=cwn[:, 0, chn:chn + 1], in1=y4[:, :, 0:1], op0=ALU.mult, op1=ALU.add),
                                reads=[ukey, "cwn", ykey], writes=[ykey])
                            S.op("dve", _C("scalar_tensor_tensor",
                                out=y4[:, :, 255:256], in0=u2[:, :, 255:256], scalar=cwn[:, 1, chn:chn + 1], in1=y4[:, :, 255:256], op0=ALU.mult, op1=ALU.add),
                                reads=[ukey, "cwn", ykey], writes=[ykey])
                        S.op("act", _C("activation", out=yg, in_=yg, func=AF.Silu), reads=[ygk], writes=[ygk])
                        S.op("dve", _C("tensor_tensor", out=ACTT[:, j, :], in0=yg, in1=yv, op=ALU.mult),
                             reads=[ygk, "yv"], writes=["ACTT"])
                        if j + 1 < NJ and prev_j[0] is not None and prev_j[0] == j:
                            jn = j + 1
                            ubn = jn % 2
                            S.op("act", _C("activation", out=ygs[jn % 2], in_=ugs2[ubn][:, 1:1025], func=AF.Identity,
                                           bias=convbT[:, l * 44 + jn:l * 44 + jn + 1],
                                           scale=convwT[:, (l * 3 + 1) * 44 + jn:(l * 3 + 1) * 44 + jn + 1]),
                                 reads=["ugs%d" % ubn, "convwT", "convbT"], writes=["yg%d" % (jn % 2)])
                            g_done.add(jn)

                    for c_ in range(2):
                        S.dma("pool", _C("dma_start",
                            out=wd[c_], in_=wdn_d[l, :, c_ * 128:(c_ + 1) * 128].rearrange("(j q) n -> q j n", q=128)),
                            writes=["wd%d" % c_])
                    for p in range(11):
                        b = pctr2[0] % 2
                        pctr2[0] += 1
                        S.dma("pool", _C("dma_start",
                            out=wg[b], in_=wup_d[l, :, p * 256:(p + 1) * 256].rearrange("(kc q) n -> q kc n", q=128)),
                            writes=["wg%d" % b])
                        S.dma("pool", _C("dma_start",
                            out=wv[b], in_=wup_d[l, :, DFF + p * 256:DFF + (p + 1) * 256].rearrange("(kc q) n -> q kc n", q=128)),
                            writes=["wv%d" % b])
                        for sub in range(2):
                            j = 2 * p + sub
                            ub = j % 2
                            for (wt, wkey, tgt, tkey, hb) in ((wg[b], "wg%d" % b, ugs2[ub], "ugs%d" % ub, 4), (wv[b], "wv%d" % b, uvs2[ub], "uvs%d" % ub, 7)):
                                bset = usets[uctr[0] % 3]
                                hc = 2 * (uctr[0] % 8)
                                uctr[0] += 1
                                for part in range(3):
                                    if part < 2:
                                        ob = banks[bset[part]][:, :]
                                        okey = BK(bset[part])
                                        rsl = (part * 512, part * 512 + 512)
                                    else:
                                        ob = banks[hb][:, hc:hc + 2]
                                        okey = BK(hb)
                                        rsl = (1024, 1026)
                                    for kc in range(KC):
                                        S.op("pe", _C("matmul",
                                            ob, lhsT=wt[:, kc, sub * 128:(sub + 1) * 128], rhs=H2T[:, kc, rsl[0]:rsl[1]],
                                            start=(kc == 0), stop=(kc == KC - 1), skip_group_check=True),
                                            reads=[wkey, "fh"], writes=[okey])
                                S.op("act", _C("activation", out=tgt[:, 0:512], in_=banks[bset[0]][:, :], func=AF.Copy), reads=[BK(bset[0])], writes=[tkey])
                                S.op("act", _C("activation", out=tgt[:, 512:1024], in_=banks[bset[1]][:, :], func=AF.Copy), reads=[BK(bset[1])], writes=[tkey])
                                S.op("act", _C("activation", out=tgt[:, 1024:1026], in_=banks[hb][:, hc:hc + 2], func=AF.Copy), reads=[BK(hb)], writes=[tkey])
                            if prev_j[0] is not None:
                                ffn_post(prev_j[0])
                            prev_j[0] = j
                    ffn_post(prev_j[0])
                    prev_j[0] = None
                def ffn_f3(hh):
                    h0 = hh * 1024
                    for c in range(KC):
                        b = c % 2
                        if c >= 2:
                          S.dma("pool", _C("dma_start",
                            out=wd[b], in_=wdn_d[l, :, c * 128:(c + 1) * 128].rearrange("(j q) n -> q j n", q=128)),
                            writes=["wd%d" % b])
                        for tg in range(2):
                            ob = 6 + tg
                            for j in range(NJ):
                                S.op("pe", _C("matmul",
                                    banks[ob][:, :], lhsT=wd[b][:, j, :], rhs=ACTT[:, j, tg * 512:(tg + 1) * 512],
                                    start=(j == 0), stop=(j == NJ - 1)),
                                    reads=["wd%d" % b, "ACTT"], writes=[BK(ob)])
                            ta = h0 + tg * 512
                            S.op("dve", _C("scalar_tensor_tensor",
                                out=XT[:, c, ta:ta + 512], in0=banks[ob][:, :], scalar=mcol(l, 5, c),
                                in1=XT[:, c, ta:ta + 512], op0=ALU.mult, op1=ALU.add),
                                reads=[BK(ob), "modsb", "XTh%d" % hh], writes=["XTh%d" % hh])


                def cap(fn, hh):
                    saved = S.ops
                    S.ops = []
                    fn(hh)
                    got = S.ops
                    S.ops = saved
                    return got

                ffn_f1(0)
                ffn_f2(0)
                fa = cap(ffn_f1, 1)
                fb = cap(ffn_f3, 0)
                per_ = max(1, len(fb) // max(1, len(fa)))
                ia = 0
                for k_, o_ in enumerate(fb):
                    S.ops.append(o_)
                    if k_ % per_ == per_ - 1 and ia < len(fa):
                        S.ops.append(fa[ia])
                        ia += 1
                S.ops.extend(fa[ia:])
                ffn_f2(1)
                ffn_f3(1)

        S.barrier()
        ar = Arena(BIG, NBIG)
        ys = [ar.take([128, 1024], F32) for _ in range(2)]
        for t in range(NB):
            b = t % 2
            for half in range(2):
                bank = (2 * t + half) % 4
                for q in range(4):
                    c = half * 4 + q
                    S.op("pe", _C("transpose",
                        banks[bank][:, q * 128:(q + 1) * 128], XT[:, c, t * 128:(t + 1) * 128], identf[:]),
                        reads=["XTh0", "XTh1", "XT%d" % t, "identf"], writes=[BK(bank)])
                if half == 0:
                    S.op("act", _C("activation", out=ys[b][:, 0:512], in_=banks[bank][:, :], func=AF.Copy),
                         reads=[BK(bank)], writes=["ys%d_0" % b])
                else:
                    S.op("dve", _C("tensor_copy", out=ys[b][:, 512:1024], in_=banks[bank][:, :]),
                         reads=[BK(bank)], writes=["ys%d_1" % b])
            S.dma("sp", _C("dma_start", out=y_d[t * 128:(t + 1) * 128, :], in_=ys[b]),
                  reads=["ys%d_0" % b, "ys%d_1" % b], final_wait=True)
        S.emit(nc, es)
    return nc, dbg_outs


def _rope_tables(dim):
    n = dim // 4
    inv = (1.0 / (10000.0 ** (np.arange(n, dtype=np.float32) / np.float32(n)))).astype(np.float32)
    t = np.arange(2048)
    row = (t // 64).astype(np.float32)
    col = (t % 64).astype(np.float32)
    ang = np.concatenate([row[:, None] * inv, col[:, None] * inv], axis=-1).astype(np.float32)
    return np.cos(ang).astype(np.float32), np.sin(ang).astype(np.float32)


def _tok_major(a):
    return np.ascontiguousarray(a.reshape(NB, 128, -1).transpose(1, 0, 2))


def _static_tables():
    bf = ml_dtypes.bfloat16
    st = {}
    cb, sb_ = _rope_tables(64)
    cc, sc = _rope_tables(32)
    st["rope_s"] = [_tok_major(cb), _tok_major(sb_), _tok_major(cc), _tok_major(sc)]
    st["rope_p"] = [np.ones((128, NB, 32), np.float32), np.zeros((128, NB, 32), np.float32),
                    np.ones((128, NB, 16), np.float32), np.zeros((128, NB, 16), np.float32)]
    bs = np.zeros((128, NF), np.float32)
    bp = np.zeros((128, NF), np.float32)
    for i in range(NB):
        for dj in range(7):
            j = i + dj - 3
            ok_p = 0 <= j < NB and (j // 2 == i // 2)
            bp[:, NF_NA + i * 7 + dj] = 0.0 if ok_p else NEG
        for dj in range(3):
            j = i + dj - 1
            ok_p = 0 <= j < NB and (j // 2 == i // 2)
            bp[:, NF_SW + i * 3 + dj] = 0.0 if ok_p else NEG
        for j in range(NB):
            bp[:, NF_DA + i * 16 + j] = 0.0 if (j // 2 == i // 2) else NEG
    st["bias_s"] = bs
    st["bias_p"] = bp
    k = np.arange(128)[:, None]
    q = np.arange(128)[None, :]
    sm = np.zeros((128, 2, 128), np.float32)
    sm[:, 0, :] = (k >= q)
    sm[:, 1, :] = (k <= q)
    st["swm_s"] = sm.astype(bf)
    st["swm_p"] = np.ones((128, 2, 128), np.float32).astype(bf)
    nm = np.zeros((128, 16, 64), np.float32)
    for p in range(128):
        ak, ck_ = p // 64, p % 64
        for e2 in range(16):
            dr = e2 - 8 + ak
            if abs(dr) > 7:
                continue
            for cr in range(64):
                c = 63 - cr
                qs = min(max(c - 8, 0), 48)
                if qs <= ck_ < qs + 16:
                    nm[p, e2, cr] = 1.0
    st["nam_s"] = nm.astype(bf)
    st["nam_p"] = np.ones((128, 16, 64), np.float32).astype(bf)
    st["identf"] = np.eye(128, dtype=np.float32)
    st["identb"] = np.eye(128, dtype=np.float32).astype(bf)
    st["permb"] = np.eye(128, dtype=np.float32)[:, ::-1].copy().astype(bf)
    st["onesb"] = np.ones((128, 128), np.float32).astype(bf)
    rm = np.zeros((128, 6), np.float32)
    rm[0:64, 0] = 1.0
    rm[64:128, 1] = 1.0
    for u in range(4):
        rm[32 * u:32 * u + 32, 2 + u] = 1.0
    st["rowmask"] = rm
    return st


_SW_Q_ORDER = [0, 4, 1, 5, 2, 6, 3, 7]


def make_in_maps(inp):
    st = _static_tables()
    f = lambda a: np.ascontiguousarray(np.asarray(a, dtype=np.float32))
    w_in = f(inp["w_in"])
    o = 0
    seg = {}
    for name, wdt in (("naq", 256), ("nak", 256), ("nav", 256), ("swq", 512), ("swk", 128), ("swv", 128), ("daq", 256), ("dak", 256), ("dav", 256)):
        seg[name] = (o, o + wdt)
        o += wdt

    def cols(name):
        a, b = seg[name]
        return w_in[:, :, a:b]

    swq = cols("swq").reshape(L, D, 8, 64)[:, :, _SW_Q_ORDER, :].reshape(L, D, 512)
    w_kv = np.ascontiguousarray(np.concatenate([cols("nak"), cols("swk"), cols("dak"), cols("nav"), cols("swv"), cols("dav")], axis=-1))
    w_q = np.ascontiguousarray(np.concatenate([cols("naq"), swq, cols("daq")], axis=-1))

    def colT(v, n):
        return np.ascontiguousarray(v.reshape(L, n, 128).transpose(2, 0, 1).reshape(128, L * n))

    bmodT = colT(f(inp["b_mod"]), 48)
    gattnT = colT(f(inp["g_attn"]), 8)
    gffnT = colT(f(inp["g_ffn"]), 8)
    convwT = np.ascontiguousarray(f(inp["conv_w"]).reshape(L, 3, 44, 128).transpose(3, 0, 1, 2).reshape(128, L * 3 * 44))
    convbT = colT(f(inp["conv_b"]), 44)
    naq = f(inp["na_qk_g"])
    swg = f(inp["sw_qk_g"])
    dag = f(inp["da_qk_g"])
    small = np.concatenate([naq[:, 0], naq[:, 1], swg[:, 0], swg[:, 1], dag[:, 0], dag[:, 1],
                            f(inp["sw_sink"]), f(inp["da_lambda"]).reshape(L, 128), f(inp["da_subln_g"])], axis=-1)
    small = np.ascontiguousarray(small)
    assert small.shape == (L, 520)
    rpb = f(inp["na_rpb"]).reshape(L, 4 * 465)
    rpbpad_s = np.zeros((L, RPBLEN), np.float32)
    rpbpad_s[:, PADOFF:PADOFF + 1860] = rpb
    rpbpad_p = np.zeros((L, RPBLEN), np.float32)

    shared = dict(w_mod=f(inp["w_mod"]), w_kv=w_kv, w_q=w_q, w_out=f(inp["w_out"]), w_up=f(inp["w_up"]), w_down=f(inp["w_down"]),
                  bmodT=bmodT, gattnT=gattnT, gffnT=gffnT, convwT=convwT, convbT=convbT, small=small,
                  identf=st["identf"], identb=st["identb"], permb=st["permb"], onesb=st["onesb"], rowmask=st["rowmask"])
    xs_ = f(inp["x_sample"])
    xp = f(inp["x_prompt"])
    cc = f(inp["c"])
    cctx = f(inp["c_ctx"])
    ck_all = np.concatenate([f(inp["cache_na_k"]).reshape(4, L, 512, 256), f(inp["cache_sw_k"]).reshape(4, L, 512, 128),
                             f(inp["cache_da_k"]).reshape(4, L, 512, 256)], axis=-1)
    cv_all = np.concatenate([f(inp["cache_na_v"]).reshape(4, L, 512, 256), f(inp["cache_sw_v"]).reshape(4, L, 512, 128),
                             f(inp["cache_da_v"]).reshape(4, L, 512, 256)], axis=-1)
    zc = np.zeros((L, 512, 640), np.float32)
    maps = []
    for core in range(8):
        m = dict(shared)
        if core < 4:
            m["x"] = np.ascontiguousarray(xs_[core])
            cv_ = cc[core]
            r = st["rope_s"]
            m["biascol"] = st["bias_s"]
            m["swmask"] = st["swm_s"]
            m["namask"] = st["nam_s"]
            m["rpbpad"] = rpbpad_s
            m["ctxone"] = np.ones((128, 1), np.float32)
            m["pflag"] = np.zeros((128, 1), np.float32)
            m["ck"] = np.ascontiguousarray(ck_all[core])
            m["cv"] = np.ascontiguousarray(cv_all[core])
        else:
            g = core - 4
            m["x"] = np.ascontiguousarray(xp[8 * g:8 * g + 8].reshape(2048, D))
            cv_ = cctx
            r = st["rope_p"]
            m["biascol"] = st["bias_p"]
            m["swmask"] = st["swm_p"]
            m["namask"] = st["nam_p"]
            m["rpbpad"] = rpbpad_p
            m["ctxone"] = np.zeros((128, 1), np.float32)
            m["pflag"] = np.ones((128, 1), np.float32)
            m["ck"] = zc
            m["cv"] = zc
        m["cvecT"] = np.ascontiguousarray(cv_.reshape(8, 128).T)
        m["cosb"], m["sinb"], m["cosc"], m["sinc"] = r
        maps.append(m)
    return maps


_PROG = {}


def _get_prog(key=("full",), **kw):
    if key not in _PROG:
        _PROG[key] = build_program(**kw)
    return _PROG[key]


def assemble(results):
    y_s = np.stack([results[c]["y"] for c in range(4)], axis=0)
    y_p = np.concatenate([results[c]["y"].reshape(8, 256, D) for c in range(4, 8)], axis=0)
    okv = np.concatenate([results[c]["okv"].reshape(L, 8, 256, 1280).transpose(1, 0, 2, 3) for c in range(4, 8)], axis=0)
    nk = np.ascontiguousarray(okv[..., 0:256]).reshape(32, L, 256, 4, 64)
    sk = np.ascontiguousarray(okv[..., 256:384]).reshape(32, L, 256, 2, 64)
    dk = np.ascontiguousarray(okv[..., 384:640]).reshape(32, L, 256, 4, 2, 32)
    nv = np.ascontiguousarray(okv[..., 640:896]).reshape(32, L, 256, 4, 64)
    sv = np.ascontiguousarray(okv[..., 896:1024]).reshape(32, L, 256, 2, 64)
    dv = np.ascontiguousarray(okv[..., 1024:1280]).reshape(32, L, 256, 4, 64)
    return (np.ascontiguousarray(y_p), np.ascontiguousarray(y_s), nk, nv, sk, sv, dk, dv)


def kernel(**inputs):
    nc, _ = _get_prog()
    maps = make_in_maps(inputs)
    res = run_bass_kernel_spmd(nc, maps, core_ids=list(range(8)))
    return assemble(res.results)
```

```python
import math
from contextlib import ExitStack

import numpy as np
import ml_dtypes

import concourse.bass as bass
import concourse.mybir as mybir
from concourse.bass_utils import run_bass_kernel_spmd

F32 = mybir.dt.float32
BF16 = mybir.dt.bfloat16
AF = mybir.ActivationFunctionType
ALU = mybir.AluOpType
AX = mybir.AxisListType

L = 4
NB = 16
D = 1024
KC = 8
DFF = 2816
NJ = 22
EPS = 1e-6
NEG = -30000.0
PADOFF = 128
RPBLEN = 2176
ENGS = ("pe", "act", "dve", "pool", "sp")


class _Op:
    __slots__ = ("eng", "fn", "reads", "writes", "is_dma", "deps", "needs_inc",
                 "sem", "val", "idx", "final_wait", "barrier")

    def __init__(self, eng, fn, reads, writes, is_dma, final_wait):
        self.eng = eng
        self.fn = fn
        self.reads = reads
        self.writes = writes
        self.is_dma = is_dma
        self.deps = []
        self.needs_inc = False
        self.sem = None
        self.val = 0
        self.final_wait = final_wait
        self.barrier = False


class Sched:
    def __init__(self, same_engine_sync=True, n_dma_sems=32):
        self.ops = []
        self.same_engine_sync = same_engine_sync
        self.n_dma_sems = n_dma_sems

    def op(self, eng, fn, reads=(), writes=()):
        o = _Op(eng, fn, tuple(reads), tuple(writes), False, False)
        self.ops.append(o)
        return o

    def dma(self, eng, fn, reads=(), writes=(), final_wait=False):
        o = _Op(eng, fn, tuple(reads), tuple(writes), True, final_wait)
        self.ops.append(o)
        return o

    def barrier(self):
        for e in ENGS:
            o = _Op(e, None, (), (), False, False)
            o.barrier = True
            self.ops.append(o)

    def analyze(self):
        last_w = {}
        readers = {}
        waited = {e: {s: -1 for s in ENGS} for e in ENGS}
        waited_dma = {e: set() for e in ENGS}
        dma_slot_last = [None] * self.n_dma_sems
        dma_ctr = {"sp": 0, "pool": 0, "act": 0}
        half = self.n_dma_sems // 2
        last_compute = {e: None for e in ENGS}
        all_dma = []
        for idx, o in enumerate(self.ops):
            o.idx = idx
            deps = set()
            if o.barrier:
                for e in ENGS:
                    if last_compute[e] is not None and e != o.eng:
                        deps.add(last_compute[e])
                    if e == o.eng and last_compute[e] is not None and e != "pe":
                        deps.add(last_compute[e])
                for d in all_dma:
                    if d not in waited_dma[o.eng]:
                        deps.add(d)
            raw = set()
            for r in o.reads:
                w = last_w.get(r)
                if w is not None:
                    deps.add(w)
                    raw.add(w)
            for wkey in o.writes:
                w = last_w.get(wkey)
                if w is not None:
                    deps.add(w)
                for rd in readers.get(wkey, ()):
                    deps.add(rd)
            if o.is_dma:
                if o.eng == "pool":
                    slot = half + dma_ctr["pool"] % (self.n_dma_sems - half)
                else:
                    slot = dma_ctr["sp"] % half
                dma_ctr["pool" if o.eng == "pool" else "sp"] += 1
                prev = dma_slot_last[slot]
                if prev is not None:
                    deps.add(prev)
                dma_slot_last[slot] = idx
                o.sem = ("dma", slot)
                all_dma.append(idx)
            deps.discard(idx)
            best = {}
            out = []
            for d in sorted(deps):
                p = self.ops[d]
                if p.is_dma:
                    if d in waited_dma[o.eng]:
                        continue
                    waited_dma[o.eng].add(d)
                    out.append(d)
                else:
                    if p.eng == o.eng and not o.is_dma and not o.barrier and \
                            (p.eng == "pe" or not self.same_engine_sync):
                        continue
                    if waited[o.eng][p.eng] >= d:
                        continue
                    best[p.eng] = max(best.get(p.eng, -1), d)
            for e, d in best.items():
                waited[o.eng][e] = d
                out.append(d)
            o.deps = out
            for d in out:
                self.ops[d].needs_inc = True
            if not o.barrier:
                for r in o.reads:
                    readers.setdefault(r, []).append(idx)
                for wkey in o.writes:
                    last_w[wkey] = idx
                    readers[wkey] = []
                if not o.is_dma:
                    last_compute[o.eng] = idx
        self.final = [o.idx for o in self.ops if o.final_wait]
        cnt = {e: 0 for e in ENGS}
        dma_cnt = [0] * self.n_dma_sems
        for o in self.ops:
            if o.barrier:
                continue
            if o.is_dma:
                slot = o.sem[1]
                dma_cnt[slot] += 16
                o.val = dma_cnt[slot]
                o.needs_inc = True
            elif o.needs_inc:
                cnt[o.eng] += 1
                o.val = cnt[o.eng]
                o.sem = ("eng", o.eng)

    def emit(self, nc, es):
        self.analyze()
        sems = {}
        for e in ENGS:
            sems[("eng", e)] = es.enter_context(nc.semaphore("s_" + e))
        for i in range(self.n_dma_sems):
            sems[("dma", i)] = es.enter_context(nc.semaphore("s_dma%d" % i))
        per = {e: [] for e in ENGS}
        for o in self.ops:
            per[o.eng].append(o)
        ops = self.ops
        final = self.final
        block = es.enter_context(nc.Block())

        def run(engine_obj, lst, ename):
            for o in lst:
                for d in o.deps:
                    p = ops[d]
                    engine_obj.wait_ge(sems[p.sem], p.val)
                if o.barrier:
                    continue
                ins = o.fn(engine_obj)
                if o.needs_inc:
                    ins.then_inc(sems[o.sem], 16 if o.is_dma else 1)
            for d in final:
                p = ops[d]
                if p.eng == ename:
                    engine_obj.wait_ge(sems[p.sem], p.val)

        @block.tensor
        def _(e):
            run(e, per["pe"], "pe")

        @block.scalar
        def _(e):
            run(e, per["act"], "act")

        @block.vector
        def _(e):
            run(e, per["dve"], "dve")

        @block.gpsimd
        def _(e):
            run(e, per["pool"], "pool")

        @block.sync
        def _(e):
            run(e, per["sp"], "sp")


def _C(name, *a, **k):
    def f(e):
        return getattr(e, name)(*a, **k)
    return f


_ARENA_HI = 0


class Arena:
    def __init__(self, big, nel, base=0):
        self.big = big
        self.nel = nel
        self.off = base
        self.hi = base

    def take(self, shape, dtype):
        global _ARENA_HI
        n = 1
        for s in shape[1:]:
            n *= s
        nb = n * (4 if dtype == F32 else 2)
        nb = (nb + 63) // 64 * 64
        el = nb // 2
        a = self.off
        self.off += el
        self.hi = max(self.hi, self.off)
        assert self.off <= self.nel, ("arena overflow", self.off, self.nel)
        _ARENA_HI = max(_ARENA_HI, self.off)
        v = self.big[:, a:a + el]
        if dtype == F32:
            v = v.bitcast(F32)[:, 0:n]
        else:
            v = v[:, 0:n]
        if len(shape) == 3:
            v = v.rearrange("p (a b) -> p a b", a=shape[1])
        elif len(shape) == 4:
            v = v.rearrange("p (a b c) -> p a b c", a=shape[1], b=shape[2])
        return v


def _na_r0(r):
    return min(max(r - 4, 0), 24)


def na_blocks(i):
    res = []
    for j in range(NB):
        inval = []
        anyv = False
        for a in range(2):
            r = 2 * i + a
            for ak in range(2):
                rk = 2 * j + ak
                ok = _na_r0(r) <= rk <= _na_r0(r) + 7
                if ok:
                    anyv = True
                else:
                    inval.append((ak, a))
        if anyv:
            res.append((j, inval))
    return res


NF_NA = 0
NF_SW = 112
NF_DA = 160
NF = 160 + 256

ACC = {}
AW = 66
for _h in range(4):
    ACC[("sw", _h)] = (0, _h * AW)
for _h in range(3):
    ACC[("na", _h)] = (0, 4 * AW + _h * AW)
for _h in range(4):
    ACC[("sw", 4 + _h)] = (1, _h * AW)
ACC[("na", 3)] = (1, 4 * AW)
ACC[("da", 0)] = (1, 5 * AW)
ACC[("da", 1)] = (1, 6 * AW)
for _u in range(6):
    ACC[("da", 2 + _u)] = (2, _u * AW)


def build_program(n_layers=L, do_attn=True, do_ffn=True, taps=(), a1_blocks=NB, a2_blocks=NB, do_mod=True):
    nc = bass.Bass("TRN2", target_bir_lowering=False)
    S = Sched()
    taps = set(taps)
    dbg_outs = {}

    def din(name, shape, dt=F32):
        return nc.dram_tensor(name, list(shape), dt, kind="ExternalInput").ap()

    x_d = din("x", [2048, D])
    cvec_d = din("cvecT", [128, 8])
    cosb_d = din("cosb", [128, NB, 32])
    sinb_d = din("sinb", [128, NB, 32])
    cosc_d = din("cosc", [128, NB, 16])
    sinc_d = din("sinc", [128, NB, 16])
    bias_d = din("biascol", [128, NF])
    swm_d = din("swmask", [128, 2, 128], BF16)
    nam_d = din("namask", [128, 16, 64], BF16)
    rpb_d = din("rpbpad", [L, RPBLEN])
    ctxone_d = din("ctxone", [128, 1])
    pflag_d = din("pflag", [128, 1])
    ck_d = din("ck", [L, 512, 640])
    cv_d = din("cv", [L, 512, 640])
    wmod_d = din("w_mod", [L, D, 6 * D])
    wkv_d = din("w_kv", [L, D, 1280])
    wq_d = din("w_q", [L, D, 1024])
    wout_d = din("w_out", [L, D, D])
    wup_d = din("w_up", [L, D, 2 * DFF])
    wdn_d = din("w_down", [L, DFF, D])
    bmod_d = din("bmodT", [128, L * 48])
    gattn_d = din("gattnT", [128, L * 8])
    gffn_d = din("gffnT", [128, L * 8])
    convw_d = din("convwT", [128, L * 3 * 44])
    convb_d = din("convbT", [128, L * 44])
    small_d = din("small", [L, 520])
    identf_d = din("identf", [128, 128])
    identb_d = din("identb", [128, 128], BF16)
    permb_d = din("permb", [128, 128], BF16)
    onesb_d = din("onesb", [128, 128], BF16)
    rowmask_d = din("rowmask", [128, 6])

    y_d = nc.dram_tensor("y", [2048, D], F32, kind="ExternalOutput").ap()
    okv_d = nc.dram_tensor("okv", [L, 2048, 1280], F32, kind="ExternalOutput").ap()

    with ExitStack() as es:
        def sb(name, shape, dt=F32):
            return es.enter_context(nc.sbuf_tensor(name, list(shape), dt))

        XT = sb("XT", [128, KC, 2048])
        identf = sb("identf_s", [128, 128])
        identb = sb("identb_s", [128, 128], BF16)
        permb = sb("permb_s", [128, 128], BF16)
        onesb = sb("onesb_s", [128, 128], BF16)
        rowmask = sb("rowmask_s", [128, 6])
        cosb = sb("cosb_s", [128, NB, 32])
        sinb = sb("sinb_s", [128, NB, 32])
        cosc = sb("cosc_s", [128, NB, 16])
        sinc = sb("sinc_s", [128, NB, 16])
        biascol = sb("biascol_s", [128, NF])
        swmask = sb("swmask_s", [128, 2, 128], BF16)
        namask = sb("namask_s", [128, 16, 64], BF16)
        ctxone = sb("ctxone_s", [128, 1])
        pflag = sb("pflag_s", [128, 1])
        cvecT = sb("cvecT_s", [128, 8])
        silub = sb("silub", [128, 8], BF16)
        bmodT = sb("bmodT_s", [128, L * 48])
        modsb = sb("modsb", [128, L * 48])
        gattnT = sb("gattnT_s", [128, L * 8])
        gffnT = sb("gffnT_s", [128, L * 8])
        G1 = sb("G1", [128, L * 8])
        G2 = sb("G2", [128, L * 8])
        convwT = sb("convwT_s", [128, L * 3 * 44])
        convbT = sb("convbT_s", [128, L * 44])
        NBIG = 64000
        BIG = sb("BIG", [128, NBIG], BF16)
        banks = [es.enter_context(nc.psum_tensor("bank%d" % i, [128, 512], F32)) for i in range(8)]

        def BK(i):
            return "B%d" % i

        def tap(name, ap, shape, reads, dt=F32):
            if name not in taps:
                return
            d = nc.dram_tensor("dbg_" + name, list(shape), dt, kind="ExternalOutput").ap()
            dbg_outs[name] = d
            S.dma("sp", _C("dma_start", out=d, in_=ap), reads=reads, final_wait=True)

        def ld(dst, src, key):
            S.dma("sp", _C("dma_start", out=dst, in_=src), writes=[key])

        ld(identf[:], identf_d, "identf")
        ld(identb[:], identb_d, "identb")
        ld(permb[:], permb_d, "permb")
        ld(onesb[:], onesb_d, "onesb")
        ld(rowmask[:], rowmask_d, "rowmask")
        ld(cosb[:], cosb_d, "rope")
        ld(sinb[:], sinb_d, "rope")
        ld(cosc[:], cosc_d, "rope")
        ld(sinc[:], sinc_d, "rope")
        ld(biascol[:], bias_d, "biascol")
        ld(swmask[:], swm_d, "swmask")
        ld(namask[:], nam_d, "namask")
        ld(ctxone[:], ctxone_d, "ctxone")
        ld(pflag[:], pflag_d, "pflag")
        ld(cvecT[:], cvec_d, "cvecT")
        ld(bmodT[:], bmod_d, "bmodT")
        ld(gattnT[:], gattn_d, "gattnT")
        ld(gffnT[:], gffn_d, "gffnT")
        ld(convwT[:], convw_d, "convwT")
        ld(convbT[:], convb_d, "convbT")

        ar = Arena(BIG, NBIG)
        xs = [ar.take([128, 1024], F32) for _ in range(2)]
        wm = [ar.take([128, 8, 512], BF16) for _ in range(2)]
        for t in range(NB):
            b = t % 2
            S.dma("sp", _C("dma_start", out=xs[b], in_=x_d[t * 128:(t + 1) * 128, :]),
                  writes=["xs%d" % b])
            for half in range(2):
                bank = (2 * t + half) % 4
                for q in range(4):
                    c = half * 4 + q
                    S.op("pe", _C("transpose",
                        banks[bank][:, q * 128:(q + 1) * 128], xs[b][:, c * 128:(c + 1) * 128], identf[:]),
                        reads=["xs%d" % b, "identf"], writes=[BK(bank)])
                if half == 0:
                    S.op("act", _C("activation",
                        out=XT[:, 0:4, t * 128:(t + 1) * 128],
                        in_=banks[bank][:].rearrange("p (c n) -> p c n", c=4), func=AF.Copy),
                        reads=[BK(bank)], writes=["XT%d" % t])
                else:
                    S.op("dve", _C("tensor_copy",
                        out=XT[:, 4:8, t * 128:(t + 1) * 128],
                        in_=banks[bank][:].rearrange("p (c n) -> p c n", c=4)),
                        reads=[BK(bank)], writes=["XT%d" % t])

        S.op("act", _C("activation", out=silub[:], in_=cvecT[:], func=AF.Silu),
             reads=["cvecT"], writes=["silub"])
        MB = 4
        first_mod = True
        for l in range(n_layers if do_mod else 0):
            for piece in range(12):
                b = (l * 12 + piece) % 2
                S.dma("pool", _C("dma_start",
                    out=wm[b], in_=wmod_d[l, :, piece * 512:(piece + 1) * 512].rearrange("(kc p) n -> p kc n", p=128)),
                    writes=["wm%d" % b])
                for oc in range(4):
                    col = l * 48 + piece * 4 + oc
                    for kc in range(KC):
                        S.op("pe", _C("matmul",
                            banks[MB][:, col:col + 1], lhsT=wm[b][:, kc, oc * 128:(oc + 1) * 128],
                            rhs=silub[:, kc:kc + 1], start=first_mod, stop=(kc == KC - 1), skip_group_check=True),
                            reads=["wm%d" % b, "silub"], writes=[BK(MB)])
                        first_mod = False
        nm = n_layers * 48
        S.op("dve", _C("tensor_tensor", out=modsb[:, 0:nm], in0=banks[MB][:, 0:nm], in1=bmodT[:, 0:nm], op=ALU.add),
             reads=[BK(MB), "bmodT"], writes=["modsb"])
        for l in range(n_layers):
            S.op("dve", _C("scalar_tensor_tensor",
                out=G1[:, l * 8:(l + 1) * 8], in0=modsb[:, l * 48 + 8:l * 48 + 16], scalar=1.0,
                in1=gattnT[:, l * 8:(l + 1) * 8], op0=ALU.add, op1=ALU.mult),
                reads=["modsb", "gattnT"], writes=["G"])
            S.op("dve", _C("scalar_tensor_tensor",
                out=G2[:, l * 8:(l + 1) * 8], in0=modsb[:, l * 48 + 32:l * 48 + 40], scalar=1.0,
                in1=gffnT[:, l * 8:(l + 1) * 8], op0=ALU.add, op1=ALU.mult),
                reads=["modsb", "gffnT"], writes=["G"])
        tap("modsb", modsb[:], [128, L * 48], ["modsb"])
        tap("G1", G1[:], [128, L * 8], ["G"])

        def mcol(l, k, c):
            i0 = l * 48 + k * 8 + c
            return modsb[:, i0:i0 + 1]

        def adaln(dst_fn, tok0, w, Gt, l, kshift, sqbuf, rsbuf, tmps, sbank, tagp, xkeys=("XTh0", "XTh1")):
            xkeys = list(xkeys)
            for c in range(KC):
                S.op("act", _C("activation", out=sqbuf[c % 2][:, 0:w], in_=XT[:, c, tok0:tok0 + w], func=AF.Square),
                     reads=xkeys, writes=[tagp + "sq%d" % (c % 2)])
                S.op("pe", _C("matmul", banks[sbank][:, 0:w], lhsT=onesb[:], rhs=sqbuf[c % 2][:, 0:w],
                                                   start=(c == 0), stop=(c == KC - 1)),
                     reads=[tagp + "sq%d" % (c % 2), "onesb"], writes=[BK(sbank)])
            S.op("act", _C("activation", out=rsbuf[:, 0:w], in_=banks[sbank][:, 0:w], func=AF.Ln, scale=1.0 / D, bias=EPS),
                 reads=[BK(sbank)], writes=[tagp + "rs"])
            S.op("act", _C("activation", out=rsbuf[:, 0:w], in_=rsbuf[:, 0:w], func=AF.Exp, scale=-0.5),
                 reads=[tagp + "rs"], writes=[tagp + "rs"])
            for c in range(KC):
                tb = tmps[c % 2]
                S.op("dve", _C("scalar_tensor_tensor",
                    out=tb[:, 0:w], in0=XT[:, c, tok0:tok0 + w], scalar=Gt[:, l * 8 + c:l * 8 + c + 1],
                    in1=rsbuf[:, 0:w], op0=ALU.mult, op1=ALU.mult),
                    reads=xkeys + ["G", tagp + "rs"], writes=[tagp + "tmp%d" % (c % 2)])
                S.op("act", _C("activation",
                    out=dst_fn(c), in_=tb[:, 0:w], func=AF.Identity, bias=mcol(l, kshift, c), scale=1.0),
                    reads=[tagp + "tmp%d" % (c % 2), "modsb"], writes=[tagp + "h"])

        def adaln_blk(hdst, hkey, tok0, Gt, l, kshift, sq8, rsbuf, tmp8, sbank, tagp, tmpkey=None, offload=False, affine_dve=False, scol=0):
            w = 128
            if offload:
                S.op("dve", _C("tensor_tensor", out=sq8, in0=XT[:, :, tok0:tok0 + w], in1=XT[:, :, tok0:tok0 + w], op=ALU.mult),
                     reads=["XTh0", "XTh1"], writes=[tagp + "sq8"])
            else:
                S.op("act", _C("activation", out=sq8, in_=XT[:, :, tok0:tok0 + w], func=AF.Square),
                     reads=["XTh0", "XTh1"], writes=[tagp + "sq8"])
            for c in range(KC):
                S.op("pe", _C("matmul", banks[sbank][:, scol:scol + w], lhsT=onesb[:], rhs=sq8[:, c, :],
                              start=(c == 0), stop=(c == KC - 1)),
                     reads=[tagp + "sq8", "onesb"], writes=[BK(sbank)])
            S.op("act", _C("activation", out=rsbuf[:, 0:w], in_=banks[sbank][:, scol:scol + w], func=AF.Ln, scale=1.0 / D, bias=EPS),
                 reads=[BK(sbank)], writes=[tagp + "rs"])
            S.op("act", _C("activation", out=rsbuf[:, 0:w], in_=rsbuf[:, 0:w], func=AF.Exp, scale=-0.5),
                 reads=[tagp + "rs"], writes=[tagp + "rs"])
            S.op("dve", _C("tensor_tensor", out=tmp8, in0=XT[:, :, tok0:tok0 + w],
                           in1=rsbuf[:, 0:w].unsqueeze(1).broadcast_to([128, KC, w]), op=ALU.mult),
                 reads=["XTh0", "XTh1", tagp + "rs"], writes=[tmpkey or (tagp + "tmp8")])
            for c in range(KC):
                if (offload or affine_dve) and c % 2 == 0:
                    S.op("dve", _C("tensor_scalar", out=hdst[:, c, :], in0=tmp8[:, c, :], scalar1=Gt[:, l * 8 + c:l * 8 + c + 1],
                                   scalar2=mcol(l, kshift, c), op0=ALU.mult, op1=ALU.add),
                         reads=[tmpkey or (tagp + "tmp8"), "modsb", "G"], writes=[hkey])
                else:
                    S.op("act", _C("activation", out=hdst[:, c, :], in_=tmp8[:, c, :], func=AF.Identity,
                                   bias=mcol(l, kshift, c), scale=Gt[:, l * 8 + c:l * 8 + c + 1]),
                         reads=[tmpkey or (tagp + "tmp8"), "modsb", "G"], writes=[hkey])

        for l in range(n_layers):
            lam_init = 0.8 - 0.6 * math.exp(-0.3 * l)
            S.barrier()
            ar = Arena(BIG, NBIG)
            KT = ar.take([128, 5, 2048], BF16)
            V = ar.take([128, NB, 10, 66], BF16)
            CTXKT = ar.take([128, 5, 512], BF16)
            CTXV = ar.take([128, 4, 10, 66], BF16)
            TAB = ar.take([128, 4, 16, 64], BF16)
            WA = ar.take([128, 8, 1280], BF16)
            sq8 = ar.take([128, KC, 128], BF16)
            hT = ar.take([128, KC, 128], BF16)
            rsb = ar.take([128, 128], F32)
            SM = ar.take([128, 520], F32)
            esink = ar.take([128, 8], F32)
            lamt = ar.take([128, 8], F32)
            SG = ar.take([128, 64], F32)
            smalls = ar.take([128, 64], F32)
            base_shared = ar.off
            kcats = [ar.take([128, 640], F32) for _ in range(2)]
            vcats = [ar.take([128, 640], F32) for _ in range(2)]
            sqks = [ar.take([128, 640], F32) for _ in range(2)]
            rts = [[ar.take([128, 64], F32) for _ in range(4)] for _ in range(2)]
            rt2s = [[ar.take([128, 128], F32) for _ in range(4)] for _ in range(2)]
            kbs = [ar.take([128, 640], BF16) for _ in range(2)]
            smks = [ar.take([128, 64], F32) for _ in range(2)]
            a0_base = ar.off
            CKs = ar.take([128, 4, 640], BF16)
            TABF = ar.take([128, 16, 64], F32)
            a0_hi = ar.off
            ar.off = a0_base
            tmp8as = [ar.take([128, KC, 128], F32) for _ in range(2)]
            sq8s = [sq8, ar.take([128, KC, 128], BF16)]
            rsbs = [rsb, ar.take([128, 128], F32)]
            hTs = [hT, ar.take([128, KC, 128], BF16)]
            ar.off = max(ar.off, a0_hi)
            hiA1 = ar.off
            ar.off = base_shared
            qf = ar.take([128, 1024], F32)
            sqq = ar.take([128, 1024], F32)
            rtq = [sqq[:, k_ * 256:(k_ + 1) * 256] for k_ in range(4)]
            qb = ar.take([128, 1024], BF16)
            QTs = [ar.take([128, 20, 128], BF16) for _ in range(2)]
            PT = [ar.take([128, 512], BF16) for _ in range(4)]
            Ofin = sqq[:, 0:512].rearrange("p (a d) -> p a d", a=8)
            ddt = sqq[:, 512:768].rearrange("p (a d) -> p a d", a=4)
            sqd = sqq[:, 768:1024].rearrange("p (a d) -> p a d", a=4)
            Omix = ar.take([128, 1024], BF16)
            rtq2 = [Omix[:, k_ * 256:(k_ + 1) * 256].bitcast(F32) for k_ in range(4)]
            OT = ar.take([128, 8, 256], BF16)
            AWW = 7 * AW
            Oacc = ar.take([128, 3, AWW], F32)
            wo = [WA[:, :, 1024 + 128 * k_:1024 + 128 * (k_ + 1)] for k_ in range(2)]
            wqv = WA[:, :, 0:1024]
            tmp8q = sqq.rearrange("p (c n) -> p c n", c=KC)

            if do_attn:
                S.dma("sp", _C("dma_start", out=SM, in_=small_d[l, :].partition_broadcast(128)), writes=["SM"])
                S.op("act", _C("activation", out=esink, in_=SM[:, 320:328], func=AF.Exp), reads=["SM"], writes=["esink"])
                lp = SM[:, 328:456].rearrange("p (a b d) -> p a b d", a=2, b=2)
                S.op("dve", _C("tensor_tensor", out=smalls[:, 0:64].rearrange("p (a d) -> p a d", a=2),
                                                      in0=lp[:, :, 0, :], in1=lp[:, :, 1, :], op=ALU.mult),
                     reads=["SM"], writes=["smalls"])
                S.op("dve", _C("tensor_reduce", out=lamt[:, 0:2], in_=smalls[:, 0:64].rearrange("p (a d) -> p a d", a=2),
                                                      axis=AX.X, op=ALU.add),
                     reads=["smalls"], writes=["lamt"])
                S.op("act", _C("activation", out=lamt[:, 2:4], in_=lamt[:, 0:2], func=AF.Exp), reads=["lamt"], writes=["lamt2"])
                S.op("dve", _C("tensor_tensor", out=lamt[:, 4:5], in0=lamt[:, 3:4], in1=lamt[:, 2:3], op=ALU.subtract),
                     reads=["lamt2"], writes=["lamt3"])
                S.op("dve", _C("tensor_scalar", out=lamt[:, 5:6], in0=lamt[:, 4:5], scalar1=-lam_init, scalar2=None, op0=ALU.add),
                     reads=["lamt3"], writes=["neglam"])
                S.op("dve", _C("tensor_scalar", out=SG, in0=SM[:, 456:520], scalar1=1.0 - lam_init, scalar2=None, op0=ALU.mult),
                     reads=["SM"], writes=["SG"])
                neglam = lamt[:, 5:6]
                S.dma("pool", _C("dma_start", out=CKs, in_=ck_d[l].rearrange("(b p) f -> p b f", p=128)), writes=["CKs"])
                for b4 in range(4):
                    bank = 6 + (b4 % 2)
                    pv = banks[bank][:].bitcast(BF16)
                    for ch in range(5):
                        S.op("pe", _C("transpose", pv[:, ch * 128:(ch + 1) * 128], CKs[:, b4, ch * 128:(ch + 1) * 128], identb[:]),
                             reads=["CKs", "identb"], writes=[BK(bank)])
                    S.op("act", _C("activation", out=CTXKT[:, :, b4 * 128:(b4 + 1) * 128],
                                                                   in_=pv[:, 0:640].rearrange("p (c n) -> p c n", c=5), func=AF.Copy),
                         reads=[BK(bank)], writes=["CTXKT"])
                for b4 in range(4):
                    S.dma("pool", _C("dma_start",
                        out=CTXV[:, b4, :, 0:64], in_=cv_d[l, b4 * 128:(b4 + 1) * 128, :].rearrange("p (h d) -> p h d", h=10)),
                        writes=["CTXVd"])
                S.op("pool", _C("tensor_copy", out=CTXV[:, :, :, 64], in_=ctxone[:, 0:1].unsqueeze(2).broadcast_to([128, 4, 10])),
                     reads=["ctxone"], writes=["CTXVo"])
                S.op("pool", _C("memset", V[:, :, :, 64], 1.0), writes=["Vones"])
                for h in range(4):
                    for ak in range(2):
                        off = PADOFF + h * 465 + (ak - 1) * 31 - 48
                        src = bass.AP(rpb_d.tensor, l * RPBLEN + off, [[1, 64], [31, 16], [1, 64]])
                        S.dma("sp", _C("dma_start", out=TABF[ak * 64:(ak + 1) * 64, :, :], in_=src),
                              writes=["TABF"])
                    S.op("act", _C("activation", out=TAB[:, h, :, :], in_=TABF, func=AF.Exp), reads=["TABF"], writes=["TAB"])
                    S.op("dve", _C("tensor_tensor", out=TAB[:, h, :, :], in0=TAB[:, h, :, :], in1=namask[:], op=ALU.mult),
                         reads=["TAB", "namask"], writes=["TAB"])
                if l == 0:
                    tap("TAB", TAB, [128, 4, 16, 64], ["TAB"], BF16)
                    tap("CTXKT", CTXKT, [128, 5, 512], ["CTXKT"], BF16)
                GN = SM
                S.dma("pool", _C("dma_start", out=WA, in_=wkv_d[l].rearrange("(kc p) n -> p kc n", p=128)), writes=["WA"])
                S.barrier()
                def rope2(eng, groups, tkey):
                    seqs = []
                    for gi, (view, H, half, cs, sn, key, tl, xr, xw) in enumerate(groups):
                        x1 = view[:, :, :, 0]
                        x2 = view[:, :, :, 1]
                        cb_ = cs.unsqueeze(1).broadcast_to([128, H, half])
                        sb_ = sn.unsqueeze(1).broadcast_to([128, H, half])
                        n_ = H * half
                        tv = [tm[:, 0:n_].rearrange("p (h d) -> p h d", h=H) for tm in tl]
                        tk = ["%s_%d_%d" % (tkey, gi, k_) for k_ in range(4)]
                        seqs.append([
                            (_C("tensor_tensor", out=tv[0], in0=x1, in1=cb_, op=ALU.mult), [key, "rope"] + xr, [tk[0]] + xw),
                            (_C("tensor_tensor", out=tv[1], in0=x2, in1=sb_, op=ALU.mult), [key, "rope"] + xr, [tk[1]] + xw),
                            (_C("tensor_tensor", out=tv[2], in0=x1, in1=sb_, op=ALU.mult), [key, "rope"] + xr, [tk[2]] + xw),
                            (_C("tensor_tensor", out=tv[3], in0=x2, in1=cb_, op=ALU.mult), [key, "rope"] + xr, [tk[3]] + xw),
                            (_C("tensor_tensor", out=x1, in0=tv[0], in1=tv[1], op=ALU.subtract), [tk[0], tk[1]], [key]),
                            (_C("tensor_tensor", out=x2, in0=tv[2], in1=tv[3], op=ALU.add), [tk[2], tk[3]], [key]),
                        ])
                    for k_ in range(6):
                        for sq_ in seqs:
                            fn_, rd_, wr_ = sq_[k_]
                            S.op(eng, fn_, reads=rd_, writes=wr_)

                def a1_block(t):
                    tok0 = t * 128
                    kcat = kcats[t % 2]
                    vcat = vcats[t % 2]
                    sfx = str(t % 2)
                    sqk = sqks[t % 2]
                    kb = kbs[t % 2]
                    rt = rts[t % 2]
                    rt2 = rt2s[t % 2]
                    smk_ = smks[t % 2]
                    pb = 0 if t % 2 == 0 else 3
                    hT_ = hTs[t % 2]
                    adaln_blk(hT_, "ah" + sfx, tok0, G1, l, 0, sq8s[t % 2], rsbs[t % 2], tmp8as[t % 2], pb + 2, "a" + sfx, affine_dve=True, scol=256)
                    for nt, (n0, w) in enumerate(((0, 512), (512, 512), (1024, 256))):
                        for kc in range(KC):
                            S.op("pe", _C("matmul",
                                banks[pb + nt][:, 0:w], lhsT=hT_[:, kc, :], rhs=WA[:, kc, n0:n0 + w],
                                start=(kc == 0), stop=(kc == KC - 1)),
                                reads=["ah" + sfx, "WA"], writes=[BK(pb + nt)])
                    S.op("act", _C("activation", out=kcat[:, 0:512], in_=banks[pb][:, :], func=AF.Copy),
                         reads=[BK(pb)], writes=["kc_a" + sfx, "kc_b" + sfx, "kc_c" + sfx])
                    S.op("act", _C("activation", out=kcat[:, 512:640], in_=banks[pb + 1][:, 0:128], func=AF.Copy),
                         reads=[BK(pb + 1)], writes=["kc_c" + sfx])
                    S.op("act", _C("activation", out=vcat[:, 0:384], in_=banks[pb + 1][:, 128:512], func=AF.Copy),
                         reads=[BK(pb + 1)], writes=["vcat" + sfx])
                    S.op("dve", _C("tensor_copy", out=vcat[:, 384:640], in_=banks[pb + 2][:, 0:256]),
                         reads=[BK(pb + 2)], writes=["vcat" + sfx])
                    a1_mark[0] = len(S.ops)
                    S.op("act", _C("activation", out=V[:, t, :, 0:64], in_=vcat.rearrange("p (h d) -> p h d", h=10), func=AF.Copy),
                         reads=["vcat" + sfx], writes=["V"])
                    S.dma("sp", _C("dma_start", out=okv_d[l, tok0:tok0 + 128, 640:1280], in_=vcat),
                          reads=["vcat" + sfx], final_wait=True)
                    S.op("act", _C("activation", out=sqk, in_=kcat, func=AF.Square), reads=["kc_a" + sfx, "kc_b" + sfx, "kc_c" + sfx], writes=["sqk" + sfx])
                    S.op("dve", _C("tensor_reduce", out=smk_[:, 0:6], in_=sqk[:, 0:384].rearrange("p (h d) -> p h d", h=6), axis=AX.X, op=ALU.add),
                         reads=["sqk" + sfx], writes=["smk" + sfx])
                    S.op("dve", _C("tensor_reduce", out=smk_[:, 6:14], in_=sqk[:, 384:640].rearrange("p (h d) -> p h d", h=8), axis=AX.X, op=ALU.add),
                         reads=["sqk" + sfx], writes=["smk" + sfx])
                    S.op("act", _C("activation", out=smk_[:, 16:22], in_=smk_[:, 0:6], func=AF.Ln, scale=1.0 / 64, bias=EPS),
                         reads=["smk" + sfx], writes=["smk" + sfx])
                    S.op("act", _C("activation", out=smk_[:, 22:30], in_=smk_[:, 6:14], func=AF.Ln, scale=1.0 / 32, bias=EPS),
                         reads=["smk" + sfx], writes=["smk" + sfx])
                    S.op("act", _C("activation", out=smk_[:, 32:46], in_=smk_[:, 16:30], func=AF.Exp, scale=-0.5),
                         reads=["smk" + sfx], writes=["smk" + sfx])
                    k64 = kcat[:, 0:384].rearrange("p (h d) -> p h d", h=6)
                    k32 = kcat[:, 384:640].rearrange("p (h d) -> p h d", h=8)
                    S.op("dve", _C("tensor_tensor", out=k64, in0=k64, in1=smk_[:, 32:38].unsqueeze(2).broadcast_to([128, 6, 64]), op=ALU.mult),
                         reads=["smk" + sfx, "kc_a" + sfx, "kc_b" + sfx], writes=["kc_a" + sfx, "kc_b" + sfx])
                    S.op("dve", _C("tensor_tensor", out=k32, in0=k32, in1=smk_[:, 38:46].unsqueeze(2).broadcast_to([128, 8, 32]), op=ALU.mult),
                         reads=["smk" + sfx, "kc_c" + sfx], writes=["kc_c" + sfx])
                    kna = kcat[:, 0:256].rearrange("p (h d) -> p h d", h=4)
                    ksw = kcat[:, 256:384].rearrange("p (h d) -> p h d", h=2)
                    S.op("dve", _C("tensor_tensor", out=kna, in0=kna, in1=GN[:, 64:128].unsqueeze(1).broadcast_to([128, 4, 64]), op=ALU.mult),
                         reads=["SM", "kc_a" + sfx], writes=["kc_a" + sfx])
                    S.op("dve", _C("tensor_tensor", out=ksw, in0=ksw, in1=GN[:, 192:256].unsqueeze(1).broadcast_to([128, 2, 64]), op=ALU.mult),
                         reads=["SM", "kc_b" + sfx], writes=["kc_b" + sfx])
                    S.op("dve", _C("tensor_tensor", out=k32, in0=k32, in1=GN[:, 288:320].unsqueeze(1).broadcast_to([128, 8, 32]), op=ALU.mult),
                         reads=["SM", "kc_c" + sfx], writes=["kc_c" + sfx])

                    rope2("dve", [
                        (kcat[:, 256:384].rearrange("p (h d two) -> p h d two", h=2, two=2), 2, 32, cosb[:, t, :], sinb[:, t, :], "kc_b" + sfx, rt, [], []),
                        (kcat[:, 384:640].rearrange("p (h d two) -> p h d two", h=8, two=2), 8, 16, cosc[:, t, :], sinc[:, t, :], "kc_c" + sfx, rt2, [], []),
                    ], "rtk" + sfx)
                    S.dma("sp", _C("dma_start", out=okv_d[l, tok0:tok0 + 128, 0:640], in_=kcat),
                          reads=["kc_a" + sfx, "kc_b" + sfx, "kc_c" + sfx], final_wait=True)
                    S.op("act", _C("activation", out=kb, in_=kcat, func=AF.Copy), reads=["kc_a" + sfx, "kc_b" + sfx, "kc_c" + sfx], writes=["kb" + sfx])
                    tb_ = 6 + t % 2
                    pv = banks[tb_][:].bitcast(BF16)
                    for ch in range(5):
                        S.op("pe", _C("transpose", pv[:, ch * 128:(ch + 1) * 128], kb[:, ch * 128:(ch + 1) * 128], identb[:]),
                             reads=["kb" + sfx, "identb"], writes=[BK(tb_)])
                    S.op("act", _C("activation", out=KT[:, :, tok0:tok0 + 128], in_=pv[:, 0:640].rearrange("p (c n) -> p c n", c=5), func=AF.Copy),
                         reads=[BK(tb_)], writes=["KT"])

                a1_mark = [0]

                def cap_a1(t):
                    saved = S.ops
                    S.ops = []
                    a1_block(t)
                    got = S.ops
                    S.ops = saved
                    return got[:a1_mark[0]], got[a1_mark[0]:]

                st_a1 = [cap_a1(t) for t in range(a1_blocks)]
                for t in range(0, a1_blocks, 2):
                    if t + 1 < a1_blocks:
                        la = st_a1[t][0] + st_a1[t][1]
                        lb = st_a1[t + 1][0] + st_a1[t + 1][1]
                        for k_ in range(max(len(la), len(lb))):
                            if k_ < len(la):
                                S.ops.append(la[k_])
                            if k_ < len(lb):
                                S.ops.append(lb[k_])
                    else:
                        S.ops.extend(st_a1[t][0] + st_a1[t][1])
                if l == 0:
                    tap("KT", KT, [128, 5, 2048], ["KT"], BF16)
                    tap("V", V, [128, NB, 10, 66], ["V", "Vones"], BF16)

                S.barrier()
                S.dma("pool", _C("dma_start", out=wqv, in_=wq_d[l].rearrange("(kc p) n -> p kc n", p=128)), writes=["WA"])
                sctr = [0]
                pctr = [0]
                woctr = [0]
                def front(i):
                    tok0 = i * 128
                    QT = QTs[i % 2]
                    qk_ = "QT%d" % (i % 2)
                    adaln_blk(hT, "ah", tok0, G1, l, 0, sq8, rsb, tmp8q, 0, "a", tmpkey="sqq", offload=True)
                    for nt in range(2):
                        for kc in range(KC):
                            S.op("pe", _C("matmul",
                                banks[nt][:, :], lhsT=hT[:, kc, :], rhs=wqv[:, kc, nt * 512:(nt + 1) * 512],
                                start=(kc == 0), stop=(kc == KC - 1)),
                                reads=["ah", "WA"], writes=[BK(nt)])
                    S.op("dve", _C("tensor_copy", out=qf[:, 0:512], in_=banks[0][:, :]), reads=[BK(0)], writes=["qf_a", "qf_b"])
                    S.op("dve", _C("tensor_copy", out=qf[:, 512:1024], in_=banks[1][:, :]), reads=[BK(1)], writes=["qf_b", "qf_c"])
                    S.op("dve", _C("tensor_tensor", out=sqq, in0=qf, in1=qf, op=ALU.mult), reads=["qf_a", "qf_b", "qf_c"], writes=["sqq"])
                    S.op("dve", _C("tensor_reduce", out=smalls[:, 0:12], in_=sqq[:, 0:768].rearrange("p (h d) -> p h d", h=12), axis=AX.X, op=ALU.add),
                         reads=["sqq"], writes=["smalls"])
                    S.op("dve", _C("tensor_reduce", out=smalls[:, 12:20], in_=sqq[:, 768:1024].rearrange("p (h d) -> p h d", h=8), axis=AX.X, op=ALU.add),
                         reads=["sqq"], writes=["smalls"])
                    S.op("act", _C("activation", out=smalls[:, 20:32], in_=smalls[:, 0:12], func=AF.Ln, scale=1.0 / 64, bias=EPS),
                         reads=["smalls"], writes=["smalls"])
                    S.op("act", _C("activation", out=smalls[:, 32:40], in_=smalls[:, 12:20], func=AF.Ln, scale=1.0 / 32, bias=EPS),
                         reads=["smalls"], writes=["smalls"])
                    S.op("act", _C("activation", out=smalls[:, 40:60], in_=smalls[:, 20:40], func=AF.Exp, scale=-0.5),
                         reads=["smalls"], writes=["smalls"])
                    q64 = qf[:, 0:768].rearrange("p (h d) -> p h d", h=12)
                    q32 = qf[:, 768:1024].rearrange("p (h d) -> p h d", h=8)
                    S.op("dve", _C("tensor_tensor", out=q64, in0=q64, in1=smalls[:, 40:52].unsqueeze(2).broadcast_to([128, 12, 64]), op=ALU.mult),
                         reads=["smalls", "qf_a", "qf_b"], writes=["qf_a", "qf_b"])
                    S.op("dve", _C("tensor_tensor", out=q32, in0=q32, in1=smalls[:, 52:60].unsqueeze(2).broadcast_to([128, 8, 32]), op=ALU.mult),
                         reads=["smalls", "qf_c"], writes=["qf_c"])
                    qna = qf[:, 0:256].rearrange("p (h d) -> p h d", h=4)
                    qsw = qf[:, 256:768].rearrange("p (h d) -> p h d", h=8)
                    S.op("dve", _C("tensor_tensor", out=qna, in0=qna, in1=GN[:, 0:64].unsqueeze(1).broadcast_to([128, 4, 64]), op=ALU.mult),
                         reads=["SM", "qf_a"], writes=["qf_a"])
                    S.op("dve", _C("tensor_tensor", out=qsw, in0=qsw, in1=GN[:, 128:192].unsqueeze(1).broadcast_to([128, 8, 64]), op=ALU.mult),
                         reads=["SM", "qf_b"], writes=["qf_b"])
                    S.op("dve", _C("tensor_tensor", out=q32, in0=q32, in1=GN[:, 256:288].unsqueeze(1).broadcast_to([128, 8, 32]), op=ALU.mult),
                         reads=["SM", "qf_c"], writes=["qf_c"])
                    rope2("dve", [
                        (qf[:, 256:768].rearrange("p (h d two) -> p h d two", h=8, two=2), 8, 32, cosb[:, i, :], sinb[:, i, :], "qf_b", rtq, ["sqq"], []),
                        (qf[:, 768:1024].rearrange("p (h d two) -> p h d two", h=8, two=2), 8, 16, cosc[:, i, :], sinc[:, i, :], "qf_c", rtq2, [], ["Omix"]),
                    ], "rtq")
                    S.op("dve", _C("tensor_copy", out=qb, in_=qf), reads=["qf_a", "qf_b", "qf_c"], writes=["qb"])
                    if l == 0 and i == 1:
                        tap("qf", qf, [128, 1024], ["qf_a", "qf_b", "qf_c"])
                    for ch in range(8):
                        bk = ch // 4
                        S.op("pe", _C("matmul",
                            banks[bk][:, (ch % 4) * 128:(ch % 4 + 1) * 128], lhsT=qb[:, ch * 128:(ch + 1) * 128],
                            rhs=(permb[:] if ch < 2 else identb[:]), start=True, stop=True, skip_group_check=True),
                            reads=["qb", "permb", "identb"], writes=[BK(bk)])
                    b0 = banks[0][:].rearrange("p (c n) -> p c n", c=4)
                    b1 = banks[1][:].rearrange("p (c n) -> p c n", c=4)
                    QTn = QT[:, 0:4, :].rearrange("p (c two) n -> p c two n", two=2)
                    for hl in range(2):
                        S.op("dve", _C("tensor_scalar", out=QTn[:, :, hl, :], in0=b0[:, 0:2, :], scalar1=rowmask[:, hl:hl + 1], scalar2=None, op0=ALU.mult),
                             reads=[BK(0), "rowmask"], writes=[qk_])
                    for g in range(2):
                        S.op("dve", _C("tensor_scalar", out=QT[:, 4 + 4 * g:6 + 4 * g, :], in0=b0[:, 2:4, :], scalar1=rowmask[:, g:g + 1], scalar2=None, op0=ALU.mult),
                             reads=[BK(0), "rowmask"], writes=[qk_])
                        S.op("dve", _C("tensor_scalar", out=QT[:, 6 + 4 * g:8 + 4 * g, :], in0=b1[:, 0:2, :], scalar1=rowmask[:, g:g + 1], scalar2=None, op0=ALU.mult),
                             reads=[BK(1), "rowmask"], writes=[qk_])
                    QTd = QT[:, 12:20, :].rearrange("p (hf u) n -> p hf u n", u=4)
                    for u in range(4):
                        S.op("dve", _C("tensor_scalar", out=QTd[:, :, u, :], in0=b1[:, 2:4, :], scalar1=rowmask[:, 2 + u:3 + u], scalar2=None, op0=ALU.mult),
                             reads=[BK(1), "rowmask"], writes=[qk_])

                def steps_and_back(i, fe_next):
                    QT = QTs[i % 2]
                    qk_ = "QT%d" % (i % 2)
                    steps = []
                    for (j, inval) in na_blocks(i):
                        steps.append(("na", j, inval))
                    for b4 in range(4):
                        steps.append(("nac", b4, None))
                    for g in range(2):
                        for j in (i - 1, i, i + 1):
                            if 0 <= j < NB:
                                steps.append(("sw", j, g))
                        for b4 in range(4):
                            steps.append(("swc", b4, g))
                    for j in range(NB):
                        for hf in range(2):
                            steps.append(("da", j, hf))
                    for b4 in range(4):
                        for hf in range(2):
                            steps.append(("dac", b4, hf))

                    acc_first = [True, True, True]

                    def emit_qk(st, sbk):
                        kind, j, x = st
                        if kind in ("na", "nac"):
                            for hp in range(2):
                                if kind == "na":
                                    lh = KT[:, hp, j * 128:(j + 1) * 128]
                                    rk_ = ["KT"]
                                else:
                                    lh = CTXKT[:, hp, j * 128:(j + 1) * 128]
                                    rk_ = ["CTXKT"]
                                S.op("pe", _C("matmul",
                                    banks[sbk][:, hp * 256:(hp + 1) * 256], lhsT=lh, rhs=QT[:, 2 * hp:2 * hp + 2, :],
                                    start=True, stop=True, skip_group_check=True),
                                    reads=rk_ + [qk_], writes=[BK(sbk)])
                        elif kind in ("sw", "swc"):
                            g = x
                            if kind == "sw":
                                lh = KT[:, 2, j * 128:(j + 1) * 128]
                                rk_ = ["KT"]
                            else:
                                lh = CTXKT[:, 2, j * 128:(j + 1) * 128]
                                rk_ = ["CTXKT"]
                            S.op("pe", _C("matmul",
                                banks[sbk][:, :], lhsT=lh, rhs=QT[:, 4 + 4 * g:8 + 4 * g, :],
                                start=True, stop=True, skip_group_check=True),
                                reads=rk_ + [qk_], writes=[BK(sbk)])
                        else:
                            hf = x
                            if kind == "da":
                                lh = KT[:, 3 + hf, j * 128:(j + 1) * 128]
                                rk_ = ["KT"]
                            else:
                                lh = CTXKT[:, 3 + hf, j * 128:(j + 1) * 128]
                                rk_ = ["CTXKT"]
                            S.op("pe", _C("matmul",
                                banks[sbk][:, :], lhsT=lh, rhs=QT[:, 12 + 4 * hf:16 + 4 * hf, :],
                                start=True, stop=True, skip_group_check=True),
                                reads=rk_ + [qk_], writes=[BK(sbk)])

                    def emit_exp(st, sbk, pbi):
                        kind, j, x = st
                        P = PT[pbi]
                        pk = "PT%d" % pbi
                        if kind == "na":
                            bc = biascol[:, NF_NA + i * 7 + (j - i + 3):NF_NA + i * 7 + (j - i + 3) + 1]
                            sc = 0.125
                        elif kind == "sw":
                            bc = biascol[:, NF_SW + i * 3 + (j - i + 1):NF_SW + i * 3 + (j - i + 1) + 1]
                            sc = 0.125
                        elif kind == "da":
                            bc = biascol[:, NF_DA + i * 16 + j:NF_DA + i * 16 + j + 1]
                            sc = 32 ** -0.5
                        elif kind == "dac":
                            bc = 0.0
                            sc = 32 ** -0.5
                        else:
                            bc = 0.0
                            sc = 0.125
                        S.op("act", _C("activation", out=P, in_=banks[sbk][:, :], func=AF.Exp, bias=bc, scale=sc),
                             reads=[BK(sbk), "biascol"], writes=[pk])
                        if kind == "na":
                            e0 = 2 * (j - i) + 7
                            Pv = P.rearrange("p (h n) -> p h n", h=4)
                            tv = TAB[:, :, e0:e0 + 2, :].rearrange("p h a c -> p h (a c)")
                            S.op("dve", _C("tensor_tensor", out=Pv, in0=Pv, in1=tv, op=ALU.mult),
                                 reads=[pk, "TAB"], writes=[pk])
                            for (ak, a) in x:
                                arr = 1 - a
                                S.op("dve", _C("memset",
                                    P[ak * 64:(ak + 1) * 64, :].rearrange("p (h n) -> p h n", h=4)[:, :, arr * 64:(arr + 1) * 64], 0.0),
                                    reads=[pk], writes=[pk])
                        elif kind == "sw" and j != i:
                            which = 0 if j < i else 1
                            Pv = P.rearrange("p (h n) -> p h n", h=4)
                            mv = swmask[:, which, :].unsqueeze(1).broadcast_to([128, 4, 128])
                            S.op("dve", _C("tensor_tensor", out=Pv, in0=Pv, in1=mv, op=ALU.mult),
                                 reads=[pk, "swmask"], writes=[pk])

                    def emit_pv(st, pbi):
                        kind, j, x = st
                        P = PT[pbi]
                        pk = "PT%d" % pbi
                        for m in range(4):
                            if kind in ("na", "nac"):
                                key = ("na", m)
                                vh = m
                            elif kind in ("sw", "swc"):
                                key = ("sw", 4 * x + m)
                                vh = 4 + x
                            else:
                                hh_ = 2 * x + m // 2
                                key = ("da", 4 * x + m)
                                vh = 6 + hh_
                            slot, col = ACC[key]
                            bk = 2 + slot
                            if kind in ("na", "sw", "da"):
                                rv = V[:, j, vh, 0:65]
                                rk_ = ["V", "Vones"]
                            else:
                                rv = CTXV[:, j, vh, 0:65]
                                rk_ = ["CTXVd", "CTXVo"]
                            st_ = acc_first[slot]
                            acc_first[slot] = False
                            S.op("pe", _C("matmul",
                                banks[bk][:, col:col + 65], lhsT=P[:, m * 128:(m + 1) * 128], rhs=rv,
                                start=st_, stop=False, skip_group_check=True),
                                reads=[pk] + rk_, writes=[BK(bk)])

                    nst = len(steps)
                    sb_of = []
                    pb_of = []
                    for s_ in range(nst):
                        sb_of.append(5 + (sctr[0] % 3))
                        sctr[0] += 1
                        pb_of.append(pctr[0] % 4)
                        pctr[0] += 1
                    LA = 2
                    for s_ in range(min(LA, nst)):
                        emit_qk(steps[s_], sb_of[s_])
                        emit_exp(steps[s_], sb_of[s_], pb_of[s_])
                    per = (len(fe_next) + nst - 1) // nst if fe_next else 0
                    fpos = 0
                    for s_ in range(nst):
                        if s_ + LA < nst:
                            emit_qk(steps[s_ + LA], sb_of[s_ + LA])
                            emit_exp(steps[s_ + LA], sb_of[s_ + LA], pb_of[s_ + LA])
                        emit_pv(steps[s_], pb_of[s_])
                        if fpos < len(fe_next):
                            S.ops.extend(fe_next[fpos:fpos + per])
                            fpos += per
                    S.ops.extend(fe_next[fpos:])

                    S.op("dve", _C("tensor_copy", out=Oacc[:, 0, :], in_=banks[2][:, 0:AWW]), reads=[BK(2)], writes=["Oacc"])
                    S.op("act", _C("activation", out=Oacc[:, 1, :], in_=banks[3][:, 0:AWW], func=AF.Copy), reads=[BK(3)], writes=["Oacc"])
                    S.op("dve", _C("tensor_copy", out=Oacc[:, 2, :], in_=banks[4][:, 0:AWW]), reads=[BK(4)], writes=["Oacc"])

                def back(i):
                    def accv(kind, lo, n):
                        slot, col = ACC[(kind, lo)]
                        return Oacc[:, slot, col:col + AW * n].rearrange("p (h d) -> p h d", h=n), "Oacc"

                    runs = [("sw", 0, 4, 256), ("na", 0, 3, 0), ("sw", 4, 4, 512), ("na", 3, 1, 192), ("da", 0, 2, None), ("da", 2, 6, None)]
                    rec = smalls
                    ri = 0
                    for (kind, lo, n, ocol) in runs:
                        av, bkey = accv(kind, lo, n)
                        rr = rec[:, ri:ri + n]
                        if kind == "sw":
                            S.op("dve", _C("tensor_tensor", out=rr, in0=av[:, :, 64], in1=esink[:, lo:lo + n], op=ALU.add),
                                 reads=[bkey, "esink"], writes=["smalls"])
                            S.op("dve", _C("reciprocal", out=rr, in_=rr), reads=["smalls"], writes=["smalls"])
                        else:
                            S.op("dve", _C("reciprocal", out=rr, in_=av[:, :, 64]), reads=[bkey], writes=["smalls"])
                        if kind == "da":
                            ov = Ofin[:, lo:lo + n, :]
                            okey = "sqq"
                        else:
                            ov = Omix[:, ocol:ocol + 64 * n].rearrange("p (h d) -> p h d", h=n)
                            okey = "Omix"
                        S.op("dve", _C("tensor_tensor",
                            out=ov, in0=av[:, :, 0:64], in1=rr.unsqueeze(2).broadcast_to([128, n, 64]), op=ALU.mult),
                            reads=[bkey, "smalls"], writes=[okey])
                        ri += n
                    O4 = Ofin.rearrange("p (h c) d -> p h c d", c=2)
                    S.op("dve", _C("scalar_tensor_tensor", out=ddt, in0=O4[:, :, 1, :], scalar=neglam, in1=O4[:, :, 0, :], op0=ALU.mult, op1=ALU.add),
                         reads=["sqq", "neglam"], writes=["sqq"])
                    S.op("act", _C("activation", out=sqd, in_=ddt, func=AF.Square), reads=["sqq"], writes=["sqq"])
                    S.op("dve", _C("tensor_reduce", out=rec[:, 24:28], in_=sqd, axis=AX.X, op=ALU.add), reads=["sqq"], writes=["smalls"])
                    S.op("act", _C("activation", out=rec[:, 28:32], in_=rec[:, 24:28], func=AF.Ln, scale=1.0 / 64, bias=EPS), reads=["smalls"], writes=["smalls"])
                    S.op("act", _C("activation", out=rec[:, 32:36], in_=rec[:, 28:32], func=AF.Exp, scale=-0.5), reads=["smalls"], writes=["smalls"])
                    S.op("dve", _C("tensor_tensor", out=ddt, in0=ddt, in1=rec[:, 32:36].unsqueeze(2).broadcast_to([128, 4, 64]), op=ALU.mult),
                         reads=["smalls", "sqq"], writes=["sqq"])
                    S.op("dve", _C("tensor_tensor", out=Omix[:, 768:1024].rearrange("p (h d) -> p h d", h=4), in0=ddt,
                                                          in1=SG.unsqueeze(1).broadcast_to([128, 4, 64]), op=ALU.mult),
                         reads=["sqq", "SG"], writes=["Omix"])
                    if l == 0 and i == 1:
                        tap("Omix", Omix, [128, 1024], ["Omix"], BF16)
                    iq = i % 2
                    for rnd in range(2):
                        for q4 in range(4):
                            ch = rnd * 4 + q4
                            S.op("pe", _C("matmul",
                                banks[0][:, q4 * 128:(q4 + 1) * 128], lhsT=Omix[:, ch * 128:(ch + 1) * 128],
                                rhs=(permb[:] if ch < 2 else identb[:]), start=True, stop=True, skip_group_check=True),
                                reads=["Omix", "permb", "identb"], writes=[BK(0)])
                        S.op("dve", _C("tensor_copy",
                            out=OT[:, rnd * 4:(rnd + 1) * 4, iq * 128:(iq + 1) * 128],
                            in_=banks[0][:].rearrange("p (c n) -> p c n", c=4)),
                            reads=[BK(0)], writes=["OT"])
                    if iq == 1:
                        t0p = (i - 1) * 128
                        for co in range(KC):
                            wb_ = woctr[0] % 2
                            woctr[0] += 1
                            S.dma("pool", _C("dma_start",
                                out=wo[wb_], in_=wout_d[l, :, co * 128:(co + 1) * 128].rearrange("(kc p) n -> p kc n", p=128)),
                                writes=["wo%d" % wb_])
                            for kc in range(KC):
                                S.op("pe", _C("matmul",
                                    banks[1][:, 0:256], lhsT=wo[wb_][:, kc, :], rhs=OT[:, kc, :],
                                    start=(kc == 0), stop=(kc == KC - 1)),
                                    reads=["wo%d" % wb_, "OT"], writes=[BK(1)])
                            S.op("dve", _C("scalar_tensor_tensor",
                                out=XT[:, co, t0p:t0p + 256], in0=banks[1][:, 0:256], scalar=mcol(l, 2, co),
                                in1=XT[:, co, t0p:t0p + 256], op0=ALU.mult, op1=ALU.add),
                                reads=[BK(1), "modsb", "XTh0", "XTh1"], writes=["XTh0", "XTh1"])


                def capture(i):
                    saved = S.ops
                    S.ops = []
                    front(i)
                    got = S.ops
                    S.ops = saved
                    return got

                def capture_back(i):
                    saved = S.ops
                    S.ops = []
                    back(i)
                    got = S.ops
                    S.ops = saved
                    return got

                if a2_blocks > 0:
                    S.ops.extend(capture(0))
                pending = []
                for i in range(a2_blocks):
                    fe_next = capture(i + 1) if i + 1 < a2_blocks else []
                    steps_and_back(i, pending + fe_next)
                    pending = capture_back(i)
                S.ops.extend(pending)

            if do_ffn:
                S.barrier()
                ar = Arena(BIG, NBIG)
                ACTT = ar.take([128, NJ, 1024], BF16)
                H2T = ar.take([128, KC, 1026], BF16)
                wg = [ar.take([128, 8, 256], BF16) for _ in range(2)]
                wv = [ar.take([128, 8, 256], BF16) for _ in range(2)]
                wd = [ar.take([128, NJ, 128], BF16) for _ in range(2)]
                sqf = [ar.take([128, 512], BF16) for _ in range(2)]
                rs2 = ar.take([128, 512], F32)
                tmpf = [ar.take([128, 512], F32) for _ in range(2)]
                ugs2 = [ar.take([128, 1026], F32) for _ in range(2)]
                uvs2 = [ar.take([128, 1026], F32) for _ in range(2)]
                usets = [(0, 1), (2, 3), (5, 6)]
                uctr = [0]
                prev_j = [None]
                g_done = set()
                ygs = [ar.take([128, 1024], F32) for _ in range(2)]
                yv = ar.take([128, 1024], F32)
                cwn = ar.take([128, 2, 44], F32)
                halo_save = ar.take([128, KC, 1], BF16)
                for tapi, slot in ((0, 0), (2, 1)):
                    c0 = (l * 3 + tapi) * 44
                    S.op("dve", _C("tensor_scalar",
                        out=cwn[:, slot, :], in0=convwT[:, c0:c0 + 44], scalar1=pflag[:, 0:1], scalar2=-1.0, op0=ALU.mult, op1=ALU.mult),
                        reads=["convwT", "pflag"], writes=["cwn"])
                pctr2 = [0]
                def ffn_f1(hh):
                    h0 = hh * 1024
                    if hh == 0:
                        segs = [(1, 513), (513, 1025), (1025, 1026)]
                        zc = 0
                    else:
                        segs = [(0, 1), (1, 513), (513, 1025)]
                        zc = 1025
                    S.op("pool", _C("memset", H2T[:, :, zc:zc + 1], 0.0), writes=["fh"])
                    for (n0, n1) in segs:
                        w = n1 - n0
                        tok0 = h0 - 1 + n0
                        if hh == 1 and n0 == 0:
                            S.op("pool", _C("tensor_copy", out=H2T[:, :, 0:1], in_=halo_save), reads=["halo_save"], writes=["fh"])
                            continue
                        adaln(lambda c, n0=n0, n1=n1: H2T[:, c, n0:n1], tok0, w, G2, l, 3, sqf, rs2, tmpf, 5, "f", xkeys=(["XTh1"] if hh == 1 else ["XTh0", "XTh1"]))
                    if hh == 0:
                        S.op("pool", _C("tensor_copy", out=halo_save, in_=H2T[:, :, 1024:1025]), reads=["fh"], writes=["halo_save"])
                def ffn_f2(hh):
                    h0 = hh * 1024
                    g_done.clear()
                    def ffn_post(j):
                        ub = j % 2
                        yg = ygs[j % 2]
                        ygk = "yg%d" % (j % 2)
                        for (us, ukey, yy, ykey, chn) in ((ugs2[ub], "ugs%d" % ub, yg, ygk, j), (uvs2[ub], "uvs%d" % ub, yv, "yv", NJ + j)):
                            cw0 = convwT[:, (l * 3 + 0) * 44 + chn:(l * 3 + 0) * 44 + chn + 1]
                            cw1 = convwT[:, (l * 3 + 1) * 44 + chn:(l * 3 + 1) * 44 + chn + 1]
                            cw2 = convwT[:, (l * 3 + 2) * 44 + chn:(l * 3 + 2) * 44 + chn + 1]
                            cbb = convbT[:, l * 44 + chn:l * 44 + chn + 1]
                            if not (ykey != "yv" and j in g_done):
                                S.op("act", _C("activation", out=yy, in_=us[:, 1:1025], func=AF.Identity, bias=cbb, scale=cw1),
                                    reads=[ukey, "convwT", "convbT"], writes=[ykey])
                            S.op("dve", _C("scalar_tensor_tensor",
                                out=yy, in0=us[:, 0:1024], scalar=cw0, in1=yy, op0=ALU.mult, op1=ALU.add),
                                reads=[ukey, "convwT", ykey], writes=[ykey])
                            S.op("dve", _C("scalar_tensor_tensor",
                                out=yy, in0=us[:, 2:1026], scalar=cw2, in1=yy, op0=ALU.mult, op1=ALU.add),
                                reads=[ukey, "convwT", ykey], writes=[ykey])
                            y4 = yy.rearrange("p (s n) -> p s n", s=4)
                            u0 = us[:, 0:1024].rearrange("p (s n) -> p s n", s=4)
                            u2 = us[:, 2:1026].rearrange("p (s n) -> p s n", s=4)
                            S.op("dve", _C("scalar_tensor_tensor",
                                out=y4[:, :, 0:1], in0=u0[:, :, 0:1], scalar=cwn[:, 0, chn:chn + 1], in1=y4[:, :, 0:1], op0=ALU.mult, op1=ALU.add),
                                reads=[ukey, "cwn", ykey], writes=[ykey])
                            S.op("dve", _C("scalar_tensor_tensor",
                                out=y4[:, :, 255:256], in0=u2[:, :, 255:256], scalar=cwn[:, 1, chn:chn + 1], in1=y4[:, :, 255:256], op0=ALU.mult, op1=ALU.add),
                                reads=[ukey, "cwn", ykey], writes=[ykey])
                        S.op("act", _C("activation", out=yg, in_=yg, func=AF.Silu), reads=[ygk], writes=[ygk])
                        S.op("dve", _C("tensor_tensor", out=ACTT[:, j, :], in0=yg, in1=yv, op=ALU.mult),
                             reads=[ygk, "yv"], writes=["ACTT"])
                        if j + 1 < NJ and prev_j[0] is not None and prev_j[0] == j:
                            jn = j + 1
                            ubn = jn % 2
                            S.op("act", _C("activation", out=ygs[jn % 2], in_=ugs2[ubn][:, 1:1025], func=AF.Identity,
                                           bias=convbT[:, l * 44 + jn:l * 44 + jn + 1],
                                           scale=convwT[:, (l * 3 + 1) * 44 + jn:(l * 3 + 1) * 44 + jn + 1]),
                                 reads=["ugs%d" % ubn, "convwT", "convbT"], writes=["yg%d" % (jn % 2)])
                            g_done.add(jn)

                    for c_ in range(2):
                        S.dma("pool", _C("dma_start",
                            out=wd[c_], in_=wdn_d[l, :, c_ * 128:(c_ + 1) * 128].rearrange("(j q) n -> q j n", q=128)),
                            writes=["wd%d" % c_])
                    for p in range(11):
                        b = pctr2[0] % 2
                        pctr2[0] += 1
                        S.dma("pool", _C("dma_start",
                            out=wg[b], in_=wup_d[l, :, p * 256:(p + 1) * 256].rearrange("(kc q) n -> q kc n", q=128)),
                            writes=["wg%d" % b])
                        S.dma("pool", _C("dma_start",
                            out=wv[b], in_=wup_d[l, :, DFF + p * 256:DFF + (p + 1) * 256].rearrange("(kc q) n -> q kc n", q=128)),
                            writes=["wv%d" % b])
                        for sub in range(2):
                            j = 2 * p + sub
                            ub = j % 2
                            for (wt, wkey, tgt, tkey, hb) in ((wg[b], "wg%d" % b, ugs2[ub], "ugs%d" % ub, 4), (wv[b], "wv%d" % b, uvs2[ub], "uvs%d" % ub, 7)):
                                bset = usets[uctr[0] % 3]
                                hc = 2 * (uctr[0] % 8)
                                uctr[0] += 1
                                for part in range(3):
                                    if part < 2:
                                        ob = banks[bset[part]][:, :]
                                        okey = BK(bset[part])
                                        rsl = (part * 512, part * 512 + 512)
                                    else:
                                        ob = banks[hb][:, hc:hc + 2]
                                        okey = BK(hb)
                                        rsl = (1024, 1026)
                                    for kc in range(KC):
                                        S.op("pe", _C("matmul",
                                            ob, lhsT=wt[:, kc, sub * 128:(sub + 1) * 128], rhs=H2T[:, kc, rsl[0]:rsl[1]],
                                            start=(kc == 0), stop=(kc == KC - 1), skip_group_check=True),
                                            reads=[wkey, "fh"], writes=[okey])
                                S.op("act", _C("activation", out=tgt[:, 0:512], in_=banks[bset[0]][:, :], func=AF.Copy), reads=[BK(bset[0])], writes=[tkey])
                                S.op("act", _C("activation", out=tgt[:, 512:1024], in_=banks[bset[1]][:, :], func=AF.Copy), reads=[BK(bset[1])], writes=[tkey])
                                S.op("act", _C("activation", out=tgt[:, 1024:1026], in_=banks[hb][:, hc:hc + 2], func=AF.Copy), reads=[BK(hb)], writes=[tkey])
                            if prev_j[0] is not None:
                                ffn_post(prev_j[0])
                            prev_j[0] = j
                    ffn_post(prev_j[0])
                    prev_j[0] = None
                def ffn_f3(hh):
                    h0 = hh * 1024
                    for c in range(KC):
                        b = c % 2
                        if c >= 2:
                          S.dma("pool", _C("dma_start",
                            out=wd[b], in_=wdn_d[l, :, c * 128:(c + 1) * 128].rearrange("(j q) n -> q j n", q=128)),
                            writes=["wd%d" % b])
                        for tg in range(2):
                            ob = 6 + tg
                            for j in range(NJ):
                                S.op("pe", _C("matmul",
                                    banks[ob][:, :], lhsT=wd[b][:, j, :], rhs=ACTT[:, j, tg * 512:(tg + 1) * 512],
                                    start=(j == 0), stop=(j == NJ - 1)),
                                    reads=["wd%d" % b, "ACTT"], writes=[BK(ob)])
                            ta = h0 + tg * 512
                            S.op("dve", _C("scalar_tensor_tensor",
                                out=XT[:, c, ta:ta + 512], in0=banks[ob][:, :], scalar=mcol(l, 5, c),
                                in1=XT[:, c, ta:ta + 512], op0=ALU.mult, op1=ALU.add),
                                reads=[BK(ob), "modsb", "XTh%d" % hh], writes=["XTh%d" % hh])


                def cap(fn, hh):
                    saved = S.ops
                    S.ops = []
                    fn(hh)
                    got = S.ops
                    S.ops = saved
                    return got

                ffn_f1(0)
                ffn_f2(0)
                fa = cap(ffn_f1, 1)
                fb = cap(ffn_f3, 0)
                per_ = max(1, len(fb) // max(1, len(fa)))
                ia = 0
                for k_, o_ in enumerate(fb):
                    S.ops.append(o_)
                    if k_ % per_ == per_ - 1 and ia < len(fa):
                        S.ops.append(fa[ia])
                        ia += 1
                S.ops.extend(fa[ia:])
                ffn_f2(1)
                ffn_f3(1)

        S.barrier()
        ar = Arena(BIG, NBIG)
        ys = [ar.take([128, 1024], F32) for _ in range(2)]
        for t in range(NB):
            b = t % 2
            for half in range(2):
                bank = (2 * t + half) % 4
                for q in range(4):
                    c = half * 4 + q
                    S.op("pe", _C("transpose",
                        banks[bank][:, q * 128:(q + 1) * 128], XT[:, c, t * 128:(t + 1) * 128], identf[:]),
                        reads=["XTh0", "XTh1", "XT%d" % t, "identf"], writes=[BK(bank)])
                if half == 0:
                    S.op("act", _C("activation", out=ys[b][:, 0:512], in_=banks[bank][:, :], func=AF.Copy),
                         reads=[BK(bank)], writes=["ys%d_0" % b])
                else:
                    S.op("dve", _C("tensor_copy", out=ys[b][:, 512:1024], in_=banks[bank][:, :]),
                         reads=[BK(bank)], writes=["ys%d_1" % b])
            S.dma("sp", _C("dma_start", out=y_d[t * 128:(t + 1) * 128, :], in_=ys[b]),
                  reads=["ys%d_0" % b, "ys%d_1" % b], final_wait=True)
        S.emit(nc, es)
    return nc, dbg_outs


def _rope_tables(dim):
    n = dim // 4
    inv = (1.0 / (10000.0 ** (np.arange(n, dtype=np.float32) / np.float32(n)))).astype(np.float32)
    t = np.arange(2048)
    row = (t // 64).astype(np.float32)
    col = (t % 64).astype(np.float32)
    ang = np.concatenate([row[:, None] * inv, col[:, None] * inv], axis=-1).astype(np.float32)
    return np.cos(ang).astype(np.float32), np.sin(ang).astype(np.float32)


def _tok_major(a):
    return np.ascontiguousarray(a.reshape(NB, 128, -1).transpose(1, 0, 2))


def _static_tables():
    bf = ml_dtypes.bfloat16
    st = {}
    cb, sb_ = _rope_tables(64)
    cc, sc = _rope_tables(32)
    st["rope_s"] = [_tok_major(cb), _tok_major(sb_), _tok_major(cc), _tok_major(sc)]
    st["rope_p"] = [np.ones((128, NB, 32), np.float32), np.zeros((128, NB, 32), np.float32),
                    np.ones((128, NB, 16), np.float32), np.zeros((128, NB, 16), np.float32)]
    bs = np.zeros((128, NF), np.float32)
    bp = np.zeros((128, NF), np.float32)
    for i in range(NB):
        for dj in range(7):
            j = i + dj - 3
            ok_p = 0 <= j < NB and (j // 2 == i // 2)
            bp[:, NF_NA + i * 7 + dj] = 0.0 if ok_p else NEG
        for dj in range(3):
            j = i + dj - 1
            ok_p = 0 <= j < NB and (j // 2 == i // 2)
            bp[:, NF_SW + i * 3 + dj] = 0.0 if ok_p else NEG
        for j in range(NB):
            bp[:, NF_DA + i * 16 + j] = 0.0 if (j // 2 == i // 2) else NEG
    st["bias_s"] = bs
    st["bias_p"] = bp
    k = np.arange(128)[:, None]
    q = np.arange(128)[None, :]
    sm = np.zeros((128, 2, 128), np.float32)
    sm[:, 0, :] = (k >= q)
    sm[:, 1, :] = (k <= q)
    st["swm_s"] = sm.astype(bf)
    st["swm_p"] = np.ones((128, 2, 128), np.float32).astype(bf)
    nm = np.zeros((128, 16, 64), np.float32)
    for p in range(128):
        ak, ck_ = p // 64, p % 64
        for e2 in range(16):
            dr = e2 - 8 + ak
            if abs(dr) > 7:
                continue
            for cr in range(64):
                c = 63 - cr
                qs = min(max(c - 8, 0), 48)
                if qs <= ck_ < qs + 16:
                    nm[p, e2, cr] = 1.0
    st["nam_s"] = nm.astype(bf)
    st["nam_p"] = np.ones((128, 16, 64), np.float32).astype(bf)
    st["identf"] = np.eye(128, dtype=np.float32)
    st["identb"] = np.eye(128, dtype=np.float32).astype(bf)
    st["permb"] = np.eye(128, dtype=np.float32)[:, ::-1].copy().astype(bf)
    st["onesb"] = np.ones((128, 128), np.float32).astype(bf)
    rm = np.zeros((128, 6), np.float32)
    rm[0:64, 0] = 1.0
    rm[64:128, 1] = 1.0
    for u in range(4):
        rm[32 * u:32 * u + 32, 2 + u] = 1.0
    st["rowmask"] = rm
    return st


_SW_Q_ORDER = [0, 4, 1, 5, 2, 6, 3, 7]


def make_in_maps(inp):
    st = _static_tables()
    f = lambda a: np.ascontiguousarray(np.asarray(a, dtype=np.float32))
    w_in = f(inp["w_in"])
    o = 0
    seg = {}
    for name, wdt in (("naq", 256), ("nak", 256), ("nav", 256), ("swq", 512), ("swk", 128), ("swv", 128), ("daq", 256), ("dak", 256), ("dav", 256)):
        seg[name] = (o, o + wdt)
        o += wdt

    def cols(name):
        a, b = seg[name]
        return w_in[:, :, a:b]

    swq = cols("swq").reshape(L, D, 8, 64)[:, :, _SW_Q_ORDER, :].reshape(L, D, 512)
    w_kv = np.ascontiguousarray(np.concatenate([cols("nak"), cols("swk"), cols("dak"), cols("nav"), cols("swv"), cols("dav")], axis=-1))
    w_q = np.ascontiguousarray(np.concatenate([cols("naq"), swq, cols("daq")], axis=-1))

    def colT(v, n):
        return np.ascontiguousarray(v.reshape(L, n, 128).transpose(2, 0, 1).reshape(128, L * n))

    bmodT = colT(f(inp["b_mod"]), 48)
    gattnT = colT(f(inp["g_attn"]), 8)
    gffnT = colT(f(inp["g_ffn"]), 8)
    convwT = np.ascontiguousarray(f(inp["conv_w"]).reshape(L, 3, 44, 128).transpose(3, 0, 1, 2).reshape(128, L * 3 * 44))
    convbT = colT(f(inp["conv_b"]), 44)
    naq = f(inp["na_qk_g"])
    swg = f(inp["sw_qk_g"])
    dag = f(inp["da_qk_g"])
    small = np.concatenate([naq[:, 0], naq[:, 1], swg[:, 0], swg[:, 1], dag[:, 0], dag[:, 1],
                            f(inp["sw_sink"]), f(inp["da_lambda"]).reshape(L, 128), f(inp["da_subln_g"])], axis=-1)
    small = np.ascontiguousarray(small)
    assert small.shape == (L, 520)
    rpb = f(inp["na_rpb"]).reshape(L, 4 * 465)
    rpbpad_s = np.zeros((L, RPBLEN), np.float32)
    rpbpad_s[:, PADOFF:PADOFF + 1860] = rpb
    rpbpad_p = np.zeros((L, RPBLEN), np.float32)

    shared = dict(w_mod=f(inp["w_mod"]), w_kv=w_kv, w_q=w_q, w_out=f(inp["w_out"]), w_up=f(inp["w_up"]), w_down=f(inp["w_down"]),
                  bmodT=bmodT, gattnT=gattnT, gffnT=gffnT, convwT=convwT, convbT=convbT, small=small,
                  identf=st["identf"], identb=st["identb"], permb=st["permb"], onesb=st["onesb"], rowmask=st["rowmask"])
    xs_ = f(inp["x_sample"])
    xp = f(inp["x_prompt"])
    cc = f(inp["c"])
    cctx = f(inp["c_ctx"])
    ck_all = np.concatenate([f(inp["cache_na_k"]).reshape(4, L, 512, 256), f(inp["cache_sw_k"]).reshape(4, L, 512, 128),
                             f(inp["cache_da_k"]).reshape(4, L, 512, 256)], axis=-1)
    cv_all = np.concatenate([f(inp["cache_na_v"]).reshape(4, L, 512, 256), f(inp["cache_sw_v"]).reshape(4, L, 512, 128),
                             f(inp["cache_da_v"]).reshape(4, L, 512, 256)], axis=-1)
    zc = np.zeros((L, 512, 640), np.float32)
    maps = []
    for core in range(8):
        m = dict(shared)
        if core < 4:
            m["x"] = np.ascontiguousarray(xs_[core])
            cv_ = cc[core]
            r = st["rope_s"]
            m["biascol"] = st["bias_s"]
            m["swmask"] = st["swm_s"]
            m["namask"] = st["nam_s"]
            m["rpbpad"] = rpbpad_s
            m["ctxone"] = np.ones((128, 1), np.float32)
            m["pflag"] = np.zeros((128, 1), np.float32)
            m["ck"] = np.ascontiguousarray(ck_all[core])
            m["cv"] = np.ascontiguousarray(cv_all[core])
        else:
            g = core - 4
            m["x"] = np.ascontiguousarray(xp[8 * g:8 * g + 8].reshape(2048, D))
            cv_ = cctx
            r = st["rope_p"]
            m["biascol"] = st["bias_p"]
            m["swmask"] = st["swm_p"]
            m["namask"] = st["nam_p"]
            m["rpbpad"] = rpbpad_p
            m["ctxone"] = np.zeros((128, 1), np.float32)
            m["pflag"] = np.ones((128, 1), np.float32)
            m["ck"] = zc
            m["cv"] = zc
        m["cvecT"] = np.ascontiguousarray(cv_.reshape(8, 128).T)
        m["cosb"], m["sinb"], m["cosc"], m["sinc"] = r
        maps.append(m)
    return maps


_PROG = {}


def _get_prog(key=("full",), **kw):
    if key not in _PROG:
        _PROG[key] = build_program(**kw)
    return _PROG[key]


def assemble(results):
    y_s = np.stack([results[c]["y"] for c in range(4)], axis=0)
    y_p = np.concatenate([results[c]["y"].reshape(8, 256, D) for c in range(4, 8)], axis=0)
    okv = np.concatenate([results[c]["okv"].reshape(L, 8, 256, 1280).transpose(1, 0, 2, 3) for c in range(4, 8)], axis=0)
    nk = np.ascontiguousarray(okv[..., 0:256]).reshape(32, L, 256, 4, 64)
    sk = np.ascontiguousarray(okv[..., 256:384]).reshape(32, L, 256, 2, 64)
    dk = np.ascontiguousarray(okv[..., 384:640]).reshape(32, L, 256, 4, 2, 32)
    nv = np.ascontiguousarray(okv[..., 640:896]).reshape(32, L, 256, 4, 64)
    sv = np.ascontiguousarray(okv[..., 896:1024]).reshape(32, L, 256, 2, 64)
    dv = np.ascontiguousarray(okv[..., 1024:1280]).reshape(32, L, 256, 4, 64)
    return (np.ascontiguousarray(y_p), np.ascontiguousarray(y_s), nk, nv, sk, sv, dk, dv)


def kernel(**inputs):
    nc, _ = _get_prog()
    maps = make_in_maps(inputs)
    res = run_bass_kernel_spmd(nc, maps, core_ids=list(range(8)))
    return assemble(res.results)
```
